# Optimizing a Trainium2 kernel written in Bass

```python
import jax, jax.numpy as jnp
from jax import lax
import numpy as np

D_MODEL = 1024
BATCH = 4
SEQ = 4096
DEPTH = 1

HEAD_DIM = 64
MOBA_HEADS = 8
MOBA_BLOCK = 256
MOBA_TOPK = 3
MOBA_Q_CHUNK = 32
DIFF_HEADS = 4
DIFF_V_DIM = 2 * HEAD_DIM
ATTN_Q_CHUNK = 128
D_FF = 2816
PLE_DIM = 256
LN_EPS = 1e-5
DEEPNORM_ALPHA = (2.0 * DEPTH) ** 0.25
DEEPNORM_BETA = (8.0 * DEPTH) ** -0.25
MOBA_W = MOBA_HEADS * HEAD_DIM
DIFF_QK_W = DIFF_HEADS * 2 * HEAD_DIM
DIFF_V_W = DIFF_HEADS * DIFF_V_DIM
IN_SIZES = (MOBA_W, MOBA_W, MOBA_W, DIFF_QK_W, DIFF_QK_W, DIFF_V_W, D_MODEL, D_MODEL)
IN_W = sum(IN_SIZES)

kernel_name = 'moba_diffattn_gated_hybrid_layer'


def alibi_slopes(n_heads):
    return jnp.asarray(2.0 ** (-8.0 * np.arange(1, n_heads + 1) / n_heads), dtype=jnp.float32)


def layer_norm(x, g, b):
    xf = x.astype(jnp.float32)
    mu = jnp.mean(xf, axis=-1, keepdims=True)
    var = jnp.mean(jnp.square(xf - mu), axis=-1, keepdims=True)
    return ((xf - mu) * lax.rsqrt(var + LN_EPS) * g + b).astype(x.dtype)


def rms_norm(x, g):
    xf = x.astype(jnp.float32)
    return (xf * lax.rsqrt(jnp.mean(jnp.square(xf), axis=-1, keepdims=True) + LN_EPS) * g).astype(x.dtype)


def swiglu(x, w_up, w_down):
    a, g = jnp.split(x @ w_up, 2, axis=-1)
    return (jax.nn.silu(a) * g) @ w_down


def moba_attention(q, k, v, slopes):
    B, H, S, dh = q.shape
    n_blk = -(-S // MOBA_BLOCK)
    s_pad = n_blk * MOBA_BLOCK
    pad = ((0, 0), (0, 0), (0, s_pad - S), (0, 0))
    kb = jnp.pad(k, pad).reshape(B, H, n_blk, MOBA_BLOCK, dh)
    vb = jnp.pad(v, pad).reshape(B, H, n_blk, MOBA_BLOCK, dh)
    k_mean = jnp.mean(kb.astype(jnp.float32), axis=3)
    n_top = min(MOBA_TOPK, n_blk)
    scale = dh ** -0.5
    blk_ids = jnp.arange(n_blk)
    in_blk = jnp.arange(MOBA_BLOCK)
    bi = jnp.arange(B)[:, None, None, None]
    hi = jnp.arange(H)[None, :, None, None]

    def chunk(c):
        t0 = c * MOBA_Q_CHUNK
        qc = lax.dynamic_slice_in_dim(q, t0, MOBA_Q_CHUNK, axis=2)
        t = t0 + jnp.arange(MOBA_Q_CHUNK)
        b_own = t0 // MOBA_BLOCK
        gate = jnp.einsum('bhqd,bhnd->bhqn', qc.astype(jnp.float32), k_mean)
        gate = jnp.where(blk_ids < b_own, gate, -jnp.inf)
        _, sel = lax.top_k(gate, n_top)
        sel_valid = sel < b_own
        ks = kb[bi, hi, sel]
        vs = vb[bi, hi, sel]
        s_past = jnp.einsum('bhqd,bhqkld->bhqkl', qc, ks).astype(jnp.float32) * scale
        pos_past = sel[..., None] * MOBA_BLOCK + in_blk
        s_past = s_past - slopes[:, None, None, None] * (t[:, None, None] - pos_past)
        s_past = jnp.where(sel_valid[..., None], s_past, -jnp.inf)
        s_past = s_past.reshape(B, H, MOBA_Q_CHUNK, n_top * MOBA_BLOCK)
        k_own = lax.dynamic_index_in_dim(kb, b_own, axis=2, keepdims=False)
        v_own = lax.dynamic_index_in_dim(vb, b_own, axis=2, keepdims=False)
        dist = t[:, None] - (b_own * MOBA_BLOCK + in_blk)[None, :]
        s_own = jnp.einsum('bhqd,bhld->bhql', qc, k_own).astype(jnp.float32) * scale
        s_own = jnp.where(dist >= 0, s_own - slopes[:, None, None] * dist, -jnp.inf)
        probs = jax.nn.softmax(jnp.concatenate([s_past, s_own], axis=-1), axis=-1)
        p_past = probs[..., :n_top * MOBA_BLOCK].reshape(B, H, MOBA_Q_CHUNK, n_top, MOBA_BLOCK)
        p_own = probs[..., n_top * MOBA_BLOCK:]
        return (jnp.einsum('bhqkl,bhqkld->bhqd', p_past.astype(v.dtype), vs)
                + jnp.einsum('bhql,bhld->bhqd', p_own.astype(v.dtype), v_own))

    outs = lax.map(chunk, jnp.arange(S // MOBA_Q_CHUNK))
    return outs.transpose(1, 2, 0, 3, 4).reshape(B, H, S, dh)


def diff_attention(q, k, v, lam, slopes):
    B, H, _, S, dh = q.shape
    scale = dh ** -0.5
    key_pos = jnp.arange(S)

    def chunk(c):
        t0 = c * ATTN_Q_CHUNK
        qc = lax.dynamic_slice_in_dim(q, t0, ATTN_Q_CHUNK, axis=3)
        s = jnp.einsum('bhmqd,bhmsd->bhmqs', qc, k).astype(jnp.float32) * scale
        dist = (t0 + jnp.arange(ATTN_Q_CHUNK))[:, None] - key_pos[None, :]
        s = jnp.where(dist >= 0, s - slopes[:, None, None, None] * dist, -jnp.inf)
        pr = jax.nn.softmax(s, axis=-1)
        a = pr[:, :, 0] - lam * pr[:, :, 1]
        return jnp.einsum('bhqs,bhsd->bhqd', a.astype(v.dtype), v)

    outs = lax.map(chunk, jnp.arange(S // ATTN_Q_CHUNK))
    return outs.transpose(1, 2, 0, 3, 4).reshape(B, H, S, v.shape[-1])


def setup_inputs(seed: int = 0) -> dict:
    key = jax.random.key(seed)
    ks = jax.random.split(key, 26)
    f32 = jnp.float32

    def nrm(k, shape, fan_in, mult=1.0):
        return jax.random.normal(k, shape, f32) * (fan_in ** -0.5) * mult

    def gain(k, shape):
        return 1.0 + 0.02 * jax.random.normal(k, shape, f32)

    def bias(k, shape):
        return 0.02 * jax.random.normal(k, shape, f32)

    col_scale = np.ones((IN_W,), np.float32)
    col_scale[2 * MOBA_W:3 * MOBA_W] = DEEPNORM_BETA
    v0 = 3 * MOBA_W + 2 * DIFF_QK_W
    col_scale[v0:v0 + DIFF_V_W] = DEEPNORM_BETA
    return {
        'x': jax.random.normal(ks[0], (BATCH, SEQ, D_MODEL), f32),
        'p': jax.random.normal(ks[1], (DEPTH, BATCH, SEQ, PLE_DIM), f32),
        'w_ffn1_up': nrm(ks[2], (DEPTH, D_MODEL, 2 * D_FF), D_MODEL),
        'w_ffn1_down': nrm(ks[3], (DEPTH, D_FF, D_MODEL), D_FF, DEEPNORM_BETA),
        'ln1_g': gain(ks[4], (DEPTH, D_MODEL)),
        'ln1_b': bias(ks[5], (DEPTH, D_MODEL)),
        'w_in': nrm(ks[6], (DEPTH, D_MODEL, IN_W), D_MODEL) * jnp.asarray(col_scale),
        'lambda_q1': 0.1 * jax.random.normal(ks[7], (DEPTH, HEAD_DIM), f32),
        'lambda_k1': 0.1 * jax.random.normal(ks[8], (DEPTH, HEAD_DIM), f32),
        'lambda_q2': 0.1 * jax.random.normal(ks[9], (DEPTH, HEAD_DIM), f32),
        'lambda_k2': 0.1 * jax.random.normal(ks[10], (DEPTH, HEAD_DIM), f32),
        'subln_g': gain(ks[11], (DEPTH, DIFF_V_DIM)),
        'w_branch_a': nrm(ks[12], (DEPTH, MOBA_W, D_MODEL), MOBA_W, DEEPNORM_BETA),
        'w_branch_b': nrm(ks[13], (DEPTH, DIFF_V_W, D_MODEL), DIFF_V_W, DEEPNORM_BETA),
        'w_out': nrm(ks[14], (DEPTH, D_MODEL, D_MODEL), D_MODEL, DEEPNORM_BETA),
        'ln2_g': gain(ks[15], (DEPTH, D_MODEL)),
        'ln2_b': bias(ks[16], (DEPTH, D_MODEL)),
        'w_ffn2_up': nrm(ks[17], (DEPTH, D_MODEL, 2 * D_FF), D_MODEL),
        'w_ffn2_down': nrm(ks[18], (DEPTH, D_FF, D_MODEL), D_FF, DEEPNORM_BETA),
        'ln3_g': gain(ks[19], (DEPTH, D_MODEL)),
        'ln3_b': bias(ks[20], (DEPTH, D_MODEL)),
        'w_ple': nrm(ks[21], (DEPTH, PLE_DIM, D_MODEL), PLE_DIM),
        'w_ple_gate': nrm(ks[22], (DEPTH, D_MODEL, D_MODEL), D_MODEL),
    }


def reference(x, p, w_ffn1_up, w_ffn1_down, ln1_g, ln1_b, w_in, lambda_q1, lambda_k1,
              lambda_q2, lambda_k2, subln_g, w_branch_a, w_branch_b, w_out, ln2_g, ln2_b,
              w_ffn2_up, w_ffn2_down, ln3_g, ln3_b, w_ple, w_ple_gate):
    B, S, _ = x.shape
    slopes_a = alibi_slopes(MOBA_HEADS)
    slopes_d = alibi_slopes(DIFF_HEADS)
    split_at = np.cumsum(IN_SIZES)[:-1].tolist()
    for i in range(DEPTH):
        x = layer_norm(DEEPNORM_ALPHA * x + 0.5 * swiglu(x, w_ffn1_up[i], w_ffn1_down[i]), ln1_g[i], ln1_b[i])
        z = x @ w_in[i]
        qa, ka, va, qd, kd, vd, ga, gb = jnp.split(z, split_at, axis=-1)
        qa_h = qa.reshape(B, S, MOBA_HEADS, HEAD_DIM).transpose(0, 2, 1, 3)
        ka_h = ka.reshape(B, S, MOBA_HEADS, HEAD_DIM).transpose(0, 2, 1, 3)
        va_h = va.reshape(B, S, MOBA_HEADS, HEAD_DIM).transpose(0, 2, 1, 3)
        ya = moba_attention(qa_h, ka_h, va_h, slopes_a).transpose(0, 2, 1, 3).reshape(B, S, MOBA_W)
        qd_h = qd.reshape(B, S, DIFF_HEADS, 2, HEAD_DIM).transpose(0, 2, 3, 1, 4)
        kd_h = kd.reshape(B, S, DIFF_HEADS, 2, HEAD_DIM).transpose(0, 2, 3, 1, 4)
        vd_h = vd.reshape(B, S, DIFF_HEADS, DIFF_V_DIM).transpose(0, 2, 1, 3)
        lam_init = 0.8 - 0.6 * float(np.exp(-0.3 * i))
        lam = (jnp.exp(jnp.sum(lambda_q1[i].astype(jnp.float32) * lambda_k1[i].astype(jnp.float32)))
               - jnp.exp(jnp.sum(lambda_q2[i].astype(jnp.float32) * lambda_k2[i].astype(jnp.float32)))
               + lam_init)
        yd = diff_attention(qd_h, kd_h, vd_h, lam, slopes_d)
        yd = rms_norm(yd, subln_g[i]) * (1.0 - lam_init)
        yd = yd.transpose(0, 2, 1, 3).reshape(B, S, DIFF_V_W)
        merged = jax.nn.sigmoid(ga) * (ya @ w_branch_a[i]) + jax.nn.sigmoid(gb) * (yd @ w_branch_b[i])
        x = layer_norm(DEEPNORM_ALPHA * x + merged @ w_out[i], ln2_g[i], ln2_b[i])
        x = layer_norm(DEEPNORM_ALPHA * x + 0.5 * swiglu(x, w_ffn2_up[i], w_ffn2_down[i]), ln3_g[i], ln3_b[i])
        x = x + jax.nn.sigmoid(x @ w_ple_gate[i]) * (p[i] @ w_ple[i])
    return x
```

```python
import contextlib
import numpy as np
import concourse.bass as bass
import concourse.mybir as mybir
from concourse.bass_utils import run_bass_kernel_spmd

F32 = mybir.dt.float32
BF16 = mybir.dt.bfloat16
AF = mybir.ActivationFunctionType
ALU = mybir.AluOpType
AX = mybir.AxisListType

D = 1024
DFF = 2816
SEQ = 4096
NFF = 22
ALPHA = float(2.0 ** 0.25)
BIGM = 262144.0
EPS = 1e-5
LAM_INIT = 0.8 - 0.6 * 1.0
N_CORES = 8
import os
FLAG_DEFER = os.environ.get("K_DEFER", "1") == "1"
FLAG_S2DEFER = os.environ.get("K_S2DEFER", "0") == "1"
FLAG_LNPIPE = os.environ.get("K_LNPIPE", "1") == "1"
FLAG_MERGEDB = os.environ.get("K_MERGEDB", "1") == "1"


class Prog:
    ENGS = ("pe", "act", "dve", "pool", "sp")

    def __init__(self, nc):
        self.nc = nc
        self.ops = []
        self.last_w = {}
        self.readers = {}
        self.dma_keys = {}
        self.last_on_eng = {}
        self.dma_since_barrier = []
        self.barrier_deps = {}

    def op(self, eng, fn, reads=(), writes=(), dma=0, key=None):
        idx = len(self.ops)
        deps = set()
        for r in reads:
            w = self.last_w.get(r)
            if w is not None:
                deps.add(w)
        for r in writes:
            w = self.last_w.get(r)
            if w is not None:
                deps.add(w)
            for rd in self.readers.get(r, {}).values():
                deps.add(rd)
        for r in reads:
            d = self.readers.setdefault(r, {})
            d[eng if not dma else ("dma", idx)] = idx
        for r in writes:
            self.last_w[r] = idx
            self.readers[r] = {}
        bd = self.barrier_deps.pop(eng, None)
        if bd:
            deps.update(bd)
        deps.discard(idx)
        o = dict(eng=eng, fn=fn, deps=deps, dma=dma, sig=False)
        if dma:
            assert key is not None
            if key not in self.dma_keys:
                self.dma_keys[key] = len(self.dma_keys)
            o["dkey"] = self.dma_keys[key]
            self.dma_since_barrier.append(idx)
        else:
            self.last_on_eng[eng] = idx
        self.ops.append(o)
        return idx

    def barrier(self):
        deps = set(self.dma_since_barrier)
        deps.update(self.last_on_eng.values())
        self.dma_since_barrier = []
        self.barrier_deps = {e: set(deps) for e in self.ENGS}

    def emit(self):
        nc = self.nc
        ops = self.ops
        for o in ops:
            for d in o["deps"]:
                p = ops[d]
                if p["eng"] == "pe" and o["eng"] == "pe" and not p["dma"] and not o["dma"]:
                    continue
                p["sig"] = True
        es = contextlib.ExitStack()
        eng_sem = {e: es.enter_context(nc.semaphore("s_" + e)) for e in self.ENGS}
        cnt = {e: 0 for e in self.ENGS}
        nk = len(self.dma_keys)
        assert nk <= 120, nk
        dma_sems = [es.enter_context(nc.semaphore("d%d" % i)) for i in range(nk)]
        dcnt = [0] * nk
        for o in ops:
            if o["dma"]:
                k = o["dkey"]
                dcnt[k] += 16 * o["dma"]
                o["sigval"] = dcnt[k]
            elif o["sig"]:
                cnt[o["eng"]] += 1
                o["sigval"] = cnt[o["eng"]]
        per_eng = {e: [] for e in self.ENGS}
        for i, o in enumerate(ops):
            per_eng[o["eng"]].append(i)
        with es, nc.Block() as block:
            def run(engname, engobj):
                waited = {}
                for i in per_eng[engname]:
                    o = ops[i]
                    for d in sorted(o["deps"]):
                        p = ops[d]
                        if p["dma"]:
                            sem = dma_sems[p["dkey"]]
                            key = ("d", p["dkey"])
                        else:
                            if p["eng"] == "pe" and engname == "pe" and not o["dma"]:
                                continue
                            sem = eng_sem[p["eng"]]
                            key = ("e", p["eng"])
                        v = p["sigval"]
                        if waited.get(key, 0) >= v:
                            continue
                        waited[key] = v
                        engobj.wait_ge(sem, v)
                    if o["dma"]:
                        o["fn"](engobj, dma_sems[o["dkey"]])
                    else:
                        inst = o["fn"](engobj)
                        if o["sig"]:
                            inst.then_inc(eng_sem[engname], 1)
                for i in per_eng[engname]:
                    o = ops[i]
                    if o["dma"]:
                        key = ("d", o["dkey"])
                        if waited.get(key, 0) < o["sigval"]:
                            waited[key] = o["sigval"]
                            engobj.wait_ge(dma_sems[o["dkey"]], o["sigval"])

            @block.tensor
            def _(e):
                run("pe", e)

            @block.scalar
            def _(e):
                run("act", e)

            @block.vector
            def _(e):
                run("dve", e)

            @block.gpsimd
            def _(e):
                run("pool", e)

            @block.sync
            def _(e):
                run("sp", e)


class Arena:
    def __init__(self, nc, st, nbytes):
        self.t = st.enter_context(nc.sbuf_tensor("arena", [128, nbytes // 4], F32))
        self.cap = nbytes
        self.off = 0

    def reset(self, off=0):
        self.off = off

    def alloc(self, shape, dtype):
        esz = 4 if dtype == F32 else 2
        n = int(np.prod(shape[1:]))
        nb = (n * esz + 63) // 64 * 64
        assert self.off + nb <= self.cap, (self.off, nb, self.cap)
        w0 = self.off // 4
        ap = self.t[0:shape[0], w0:w0 + nb // 4]
        if dtype != F32:
            ap = ap.bitcast(dtype)
        ap = ap[:, 0:n]
        if len(shape) == 3:
            ap = ap.rearrange("p (a b) -> p a b", a=shape[1])
        elif len(shape) == 4:
            ap = ap.rearrange("p (a b c) -> p a b c", a=shape[1], b=shape[2])
        self.off += nb
        return ap


def MM(P, out, lhsT, rhs, start, stop, reads, writes):
    P.op("pe", lambda e: e.matmul(out, lhsT=lhsT, rhs=rhs, start=start, stop=stop), reads, writes)


def TR(P, out, in_, ident, reads, writes):
    P.op("pe", lambda e: e.transpose(out, in_, ident), reads, writes)


def ACT(P, out, in_, func, reads, writes, scale=None, bias=None, accum_out=None):
    kw = {}
    if scale is not None:
        kw["scale"] = scale
    if bias is not None:
        kw["bias"] = bias
    if accum_out is not None:
        kw["accum_out"] = accum_out
    P.op("act", lambda e: e.activation(out=out, in_=in_, func=func, **kw), reads, writes)


def TT(P, eng, out, in0, in1, op, reads, writes):
    P.op(eng, lambda e: e.tensor_tensor(out=out, in0=in0, in1=in1, op=op), reads, writes)


def TS(P, eng, out, in0, s1, s2, op0, op1, reads, writes):
    if op1 is None:
        P.op(eng, lambda e: e.tensor_scalar(out=out, in0=in0, scalar1=s1, scalar2=None, op0=op0), reads, writes)
    else:
        P.op(eng, lambda e: e.tensor_scalar(out=out, in0=in0, scalar1=s1, scalar2=s2, op0=op0, op1=op1), reads, writes)


def STT(P, out, in0, scalar, in1, op0, op1, reads, writes):
    P.op("dve", lambda e: e.scalar_tensor_tensor(out=out, in0=in0, scalar=scalar, in1=in1, op0=op0, op1=op1), reads, writes)


def CP(P, eng, out, in_, reads, writes):
    if eng == "act":
        P.op("act", lambda e: e.activation(out=out, in_=in_, func=AF.Identity), reads, writes)
    else:
        P.op(eng, lambda e: e.tensor_copy(out=out, in_=in_), reads, writes)


def DMA(P, q, out, in_, key, reads=(), writes=()):
    P.op(q, lambda e, s: e.dma_start(out=out, in_=in_).then_inc(s, 16), reads, writes, dma=1, key=key)


class Ctx:
    pass


def build(MODE, debug=False):
    nc = bass.Bass("TRN2", target_bir_lowering=False)
    C = Ctx()
    C.nc = nc
    C.NHM, C.NHD = 8, 4
    if MODE == "P4":
        C.NQ, C.NK, C.KOFF = 4096, 4096, 0
    else:
        C.NQ, C.NK, C.KOFF = 2048, 4096, 2048
    NQ, NK = C.NQ, C.NK
    TB = 1024
    C.TB = TB

    def din(name, shape):
        return nc.dram_tensor(name, list(shape), F32, kind="ExternalInput").ap()

    I = {}
    I["x"] = din("x", [NK, D])
    I["p"] = din("p", [NQ, 256])
    I["w_ffn1_up"] = din("w_ffn1_up", [D, 2 * DFF])
    I["w_ffn1_down"] = din("w_ffn1_down", [DFF, D])
    I["w_ffn2_up"] = din("w_ffn2_up", [D, 2 * DFF])
    I["w_ffn2_down"] = din("w_ffn2_down", [DFF, D])
    I["w_in"] = din("w_in", [D, 5120])
    I["w_branch_a"] = din("w_branch_a", [512, D])
    I["w_branch_b"] = din("w_branch_b", [512, D])
    I["w_out"] = din("w_out", [D, D])
    I["w_ple"] = din("w_ple", [256, D])
    I["w_ple_gate"] = din("w_ple_gate", [D, D])
    I["lnp"] = din("lnp", [128, 6, D])
    I["sublng"] = din("sublng", [128, 128])
    I["lamv"] = din("lamv", [128, 4, 64])
    I["ident"] = din("ident", [128, 128])
    I["tri"] = din("tri", [128, 128])
    I["qc"] = din("qc", [12, 4, NQ])
    I["kc"] = din("kc", [12, 4, NK])
    I["kind"] = din("kind", [16, NK])
    I["vb"] = din("vb", [128, NQ // 512, 4, 16])
    I["lo"] = din("lo", [128, NQ // 512, 4, 16])
    C.I = I
    kind_s = "ExternalOutput" if debug else "Internal"
    C.out = nc.dram_tensor("out", [NQ, D], F32, kind="ExternalOutput").ap()
    C.X1_s = nc.dram_tensor("X1_s", [NQ, D], F32, kind=kind_s).ap()
    C.X1T_s = nc.dram_tensor("X1T_s", [D, NQ], BF16, kind=kind_s).ap()
    C.QT_s = nc.dram_tensor("QT_s", [D, NQ], BF16, kind=kind_s).ap()
    C.KT_s = nc.dram_tensor("KT_s", [D, NK], BF16, kind=kind_s).ap()
    C.V_s = nc.dram_tensor("V_s", [NK, D], BF16, kind=kind_s).ap()
    C.YT_s = nc.dram_tensor("YT_s", [D, NQ], BF16, kind=kind_s).ap()

    with contextlib.ExitStack() as st:
        C.P = P = Prog(nc)
        C.arena = Arena(nc, st, 192 * 1024)
        C.ps = [st.enter_context(nc.psum_tensor("ps%d" % i, [128, 512], F32)) for i in range(8)]
        sb = lambda n, s, d: st.enter_context(nc.sbuf_tensor(n, s, d))
        C.ident_f = sb("ident_f", [128, 128], F32)
        C.ident_b = sb("ident_b", [128, 128], BF16)
        C.tri_b = sb("tri_b", [128, 128], BF16)
        C.eps_t = sb("eps_t", [128, 1], F32)
        C.lam_t = sb("lam_t", [128, 4], F32)
        C.lamv = sb("lamv_t", [128, 4, 64], F32)
        C.sublng = sb("sublng_t", [128, 128], F32)
        C.small = sb("small_t", [128, 64], F32)

        phase0(C)
        phase1(C)
        P.barrier()
        phase2(C)
        P.barrier()
        phase3(C)
        P.emit()
    return nc


def phase0(C):
    P, I = C.P, C.I
    DMA(P, "sp", C.ident_f[:], I["ident"], "c_if", writes=["ident_f"])
    DMA(P, "pool", C.ident_b[:], I["ident"], "c_ib", writes=["ident_b"])
    DMA(P, "pool", C.tri_b[:], I["tri"], "c_tri", writes=["tri_b"])
    DMA(P, "sp", C.lamv[:], I["lamv"], "c_lamv", writes=["lamv"])
    DMA(P, "sp", C.sublng[:], I["sublng"], "c_sub", writes=["sublng"])
    P.op("dve", lambda e: e.memset(C.eps_t[:], EPS), writes=["eps"])
    sm = C.small
    TT(P, "dve", sm[:, 0:64], C.lamv[:, 0, :], C.lamv[:, 1, :], ALU.mult, ["lamv"], ["sm0"])
    P.op("dve", lambda e: e.tensor_reduce(out=C.lam_t[:, 2:3], in_=sm[:, 0:64], axis=AX.X, op=ALU.add), ["sm0"], ["l2"])
    TT(P, "dve", sm[:, 0:64], C.lamv[:, 2, :], C.lamv[:, 3, :], ALU.mult, ["lamv", "l2"], ["sm0"])
    P.op("dve", lambda e: e.tensor_reduce(out=C.lam_t[:, 3:4], in_=sm[:, 0:64], axis=AX.X, op=ALU.add), ["sm0"], ["l3"])
    ACT(P, C.lam_t[:, 2:4], C.lam_t[:, 2:4], AF.Exp, ["l2", "l3"], ["l23"])
    TT(P, "dve", C.lam_t[:, 0:1], C.lam_t[:, 2:3], C.lam_t[:, 3:4], ALU.subtract, ["l23"], ["lam0"])
    TS(P, "dve", C.lam_t[:, 0:1], C.lam_t[:, 0:1], float(LAM_INIT), None, ALU.add, None, ["lam0"], ["lam"])
    TS(P, "dve", C.lam_t[:, 1:2], C.lam_t[:, 0:1], -1.0, None, ALU.mult, None, ["lam"], ["nlam"])


def wview(w, kcn):
    return w.rearrange("(kc p) n -> p kc n", p=128)


def emit_transposes(C, xs, xT, t, evac_eng):
    P = C.P
    xb = C.xb[C.xbi % len(C.xb)]
    xbn = ("xb", C.xbi % len(C.xb))
    C.xbi += 1
    CP(P, "act", xb[:, :], xs[:, t, :], [("xs", t)], [xbn])
    for g in range(2):
        bank = C.ps[6 + g]
        for k in range(4):
            kc = g * 4 + k
            MM(P, bank[:, k * 128:(k + 1) * 128], xb[:, kc * 128:(kc + 1) * 128], C.ident_b[:, :], True, True,
               [xbn, "ident_b"], [("ps", 6 + g)])
        src = bank[:, :].rearrange("p (a b) -> p a b", a=4)
        CP(P, evac_eng[g], xT[:, g * 4:(g + 1) * 4, t * 128:(t + 1) * 128], src, [("ps", 6 + g)], [("xT", t)])


def emit_layernorm(C, xs, t, ybanks, yscale, lng, lnb, tmp, lnname):
    P = C.P
    tT, r, xn, st6, mv = tmp
    for h in range(2):
        ACT(P, tT[:, h * 512:(h + 1) * 512], C.ps[ybanks[h]][:, :], AF.Identity, [("ps", ybanks[h])], [("tT", h)], scale=float(yscale))
    STT(P, r[:, :], xs[:, t, :], ALPHA, tT[:, :], ALU.mult, ALU.add, [("xs", t), ("tT", 0), ("tT", 1)], ["r"])
    for h in range(2):
        P.op("dve", (lambda hh: (lambda e: e.bn_stats(out=st6[:, hh, :], in_=r[:, hh * 512:(hh + 1) * 512])))(h), ["r"], [("st6", h)])
    P.op("dve", lambda e: e.bn_aggr(out=mv[:, 0:2], in_=st6[:, :, :].rearrange("p a b -> p (a b)")), [("st6", 0), ("st6", 1)], ["mv"])
    ACT(P, mv[:, 2:3], mv[:, 1:2], AF.Sqrt, ["mv", "eps"], ["std"], bias=C.eps_t[:], scale=1.0)
    P.op("dve", lambda e: e.reciprocal(out=mv[:, 3:4], in_=mv[:, 2:3]), ["std"], ["rstd"])
    STT(P, mv[:, 4:5], mv[:, 0:1], -1.0, mv[:, 3:4], ALU.mult, ALU.mult, ["mv", "rstd"], ["nmr"])
    ACT(P, xn[:, :], r[:, :], AF.Identity, ["r", "rstd", "nmr"], ["xn"], scale=mv[:, 3:4], bias=mv[:, 4:5])
    TT(P, "dve", xn[:, :], xn[:, :], lng, ALU.mult, ["xn", lnname], ["xn"])
    TT(P, "dve", xs[:, t, :], xn[:, :], lnb, ALU.add, ["xn", lnname], [("xs", t)])


def emit_ffn(C, xs, xT, actT, wd, wb, sil, w_up, w_down, tag, first):
    P = C.P
    TB = C.TB
    wu = wview(w_up, 8)
    if first:
        wdv = wview(w_down, NFF)
        DMA(P, "pool", wd[:, 0:11, :], wdv[:, 0:11, :], tag + "wd0", writes=[("wd", 0)])
        DMA(P, "pool", wd[:, 11:22, :], wdv[:, 11:22, :], tag + "wd1", writes=[("wd", 1)])
    par = 0
    for gi in range(NFF // 2):
        slot = gi % 2
        buf = wb[slot]
        DMA(P, "pool", buf[:, :, 0:256], wu[:, :, gi * 256:(gi + 1) * 256], "wbA%d" % slot, writes=[("wb", slot, 0)])
        DMA(P, "pool", buf[:, :, 256:512], wu[:, :, DFF + gi * 256:DFF + (gi + 1) * 256], "wbG%d" % slot, writes=[("wb", slot, 1)])
        for jj in range(2):
            j = gi * 2 + jj
            for hb in range(TB // 512):
                bA, bG = 2 * par, 2 * par + 1
                xr = [("xT", 4 * hb + q) for q in range(4)]
                for kc in range(8):
                    MM(P, C.ps[bA][:, :], buf[:, kc, jj * 128:(jj + 1) * 128], xT[:, kc, hb * 512:(hb + 1) * 512],
                       kc == 0, kc == 7, [("wb", slot, 0)] + xr, [("ps", bA)])
                for kc in range(8):
                    MM(P, C.ps[bG][:, :], buf[:, kc, 256 + jj * 128:256 + (jj + 1) * 128], xT[:, kc, hb * 512:(hb + 1) * 512],
                       kc == 0, kc == 7, [("wb", slot, 1)] + xr, [("ps", bG)])
                ACT(P, sil[par][:, :], C.ps[bA][:, :], AF.Silu, [("ps", bA)], [("sil", par)])
                TT(P, "dve", actT[:, j, hb * 512:(hb + 1) * 512], sil[par][:, :], C.ps[bG][:, :], ALU.mult,
                   [("sil", par), ("ps", bG)], [("actT", j, hb)])
                par ^= 1


def emit_down(C, actT, wd, t, yb=(4, 5)):
    P = C.P
    hb = t // 4
    for dh in range(2):
        for kc in range(NFF):
            MM(P, C.ps[yb[dh]][:, :], actT[:, kc, t * 128:(t + 1) * 128], wd[:, kc, dh * 512:(dh + 1) * 512],
               kc == 0, kc == NFF - 1, [("actT", kc, hb), ("wd", kc // 11)], [("ps", yb[dh])])


def phase1(C):
    P, I, A = C.P, C.I, C.arena
    TB = C.TB
    NTB = C.NK // TB
    own_tb0 = C.KOFF // TB
    A.reset()
    xs = A.alloc([128, 8, D], F32)
    xT = A.alloc([128, 8, TB], BF16)
    actT = A.alloc([128, NFF, TB], BF16)
    wd = A.alloc([128, NFF, D], BF16)
    wb = [A.alloc([128, 8, 512], BF16) for _ in range(2)]
    lnp = A.alloc([128, 2, D], F32)
    tT = A.alloc([128, D], F32)
    r = A.alloc([128, D], F32)
    xn = A.alloc([128, D], F32)
    st6 = A.alloc([128, 2, 6], F32)
    mv = A.alloc([128, 8], F32)
    sil = [A.alloc([128, 512], F32) for _ in range(2)]
    stg = [A.alloc([128, TB], BF16) for _ in range(2)]
    vst = [A.alloc([128, 512], BF16) for _ in range(2)]
    tmp = (tT, r, xn, st6, mv)
    C.xb = [A.alloc([128, D], BF16) for _ in range(2)]
    C.xbi = 0
    DMA(P, "sp", lnp, I["lnp"][:, 0:2, :], "lnp1", writes=["ln1"])
    xv = I["x"].rearrange("(t p) d -> p t d", p=128)
    x1v = C.X1_s.rearrange("(t p) d -> p t d", p=128)
    win = wview(I["w_in"], 8)
    for tb in range(NTB):
        own = tb >= own_tb0
        qtb = tb - own_tb0
        for t in range(8):
            DMA(P, "sp", xs[:, t, :], xv[:, tb * 8 + t, :], "xs%d" % t, writes=[("xs", t)])
        for t in range(8):
            emit_transposes(C, xs, xT, t, ("dve", "act"))
        emit_ffn(C, xs, xT, actT, wd, wb, sil, I["w_ffn1_up"], I["w_ffn1_down"], "f1", tb == 0)
        for t in range(8):
            yb = (4, 5) if (t % 2 == 0 or not FLAG_LNPIPE) else (2, 3)
            emit_down(C, actT, wd, t, yb)
            emit_layernorm(C, xs, t, yb, 0.5, lnp[:, 0, :], lnp[:, 1, :], tmp, "ln1")
            if own:
                DMA(P, "sp", x1v[:, qtb * 8 + t, :], xs[:, t, :], "x1st%d" % t, reads=[("xs", t)])
            if t >= 1:
                emit_transposes(C, xs, xT, t - 1, ("dve", "act"))
        emit_transposes(C, xs, xT, 7, ("dve", "act"))
        if own:
            for kc in range(8):
                DMA(P, "sp", C.X1T_s[kc * 128:(kc + 1) * 128, qtb * TB:(qtb + 1) * TB], xT[:, kc, :], "x1T%d" % kc,
                    reads=[("xT", q) for q in range(8)])
        par = 0
        for g in range(6):
            kind = ("q", "k", "v")[g % 3]
            fam = g // 3
            if kind == "q" and not own:
                continue
            slot = g % 2
            buf = wb[slot]
            DMA(P, "pool", buf[:, :, 0:256], win[:, :, g * 512:g * 512 + 256], "wbA%d" % slot, writes=[("wb", slot, 0)])
            DMA(P, "pool", buf[:, :, 256:512], win[:, :, g * 512 + 256:(g + 1) * 512], "wbG%d" % slot, writes=[("wb", slot, 1)])
            if kind in ("q", "k"):
                dst = C.QT_s if kind == "q" else C.KT_s
                col0 = qtb * TB if kind == "q" else tb * TB
                for c in range(4):
                    sg = stg[par]
                    for hb in range(TB // 512):
                        bank = 2 * par + hb
                        for kc in range(8):
                            MM(P, C.ps[bank][:, :], buf[:, kc, c * 128:(c + 1) * 128], xT[:, kc, hb * 512:(hb + 1) * 512],
                               kc == 0, kc == 7, [("wb", slot, c // 2)] + [("xT", 4 * hb + q) for q in range(4)], [("ps", bank)])
                        CP(P, "act" if hb == 0 else "dve", sg[:, hb * 512:(hb + 1) * 512], C.ps[bank][:, :], [("ps", bank)], [("stg", par, hb)])
                    row0 = fam * 512 + c * 128
                    DMA(P, "sp", dst[row0:row0 + 128, col0:col0 + TB], sg[:, :], "stg%d" % par,
                        reads=[("stg", par, 0), ("stg", par, 1)])
                    par ^= 1
            else:
                vv = C.V_s.rearrange("(t p) d -> p t d", p=128)
                for t in range(8):
                    bank = 2 * par
                    for kc in range(8):
                        MM(P, C.ps[bank][:, :], xT[:, kc, t * 128:(t + 1) * 128], buf[:, kc, :],
                           kc == 0, kc == 7, [("wb", slot, 0), ("wb", slot, 1), ("xT", t)], [("ps", bank)])
                    CP(P, "act" if t % 2 == 0 else "dve", vst[par][:, :], C.ps[bank][:, :], [("ps", bank)], [("vst", par)])
                    DMA(P, "sp", vv[:, tb * 8 + t, fam * 512:(fam + 1) * 512], vst[par][:, :], "vst%d" % par, reads=[("vst", par)])
                    par ^= 1


def phase2(C):
    P, I, A = C.P, C.I, C.arena
    A.reset()
    NHM, NHD = C.NHM, C.NHD
    NQ, NK, KOFF = C.NQ, C.NK, C.KOFF
    NQB = NQ // 512
    NKC = NK // 128
    KO = KOFF // 128
    KTs = [[A.alloc([128, NK], BF16) for m in range(2)] for _ in range(2)]
    QTs = [[A.alloc([128, NQ], BF16) for m in range(2)] for _ in range(2)]
    VEs = [A.alloc([128, NKC, 132], BF16) for _ in range(2)]
    PT = [A.alloc([128, 512], BF16) for _ in range(3)]
    accsb = [A.alloc([128, 4, 2, 132], F32) for _ in range(2)]
    o1 = A.alloc([128, 4, 128], F32)
    yd = A.alloc([128, 4, 128], F32)
    ydb = A.alloc([128, 4, 128], BF16)
    sq = A.alloc([128, 128], F32)
    sc = A.alloc([128, 4, 8], F32)
    ystg = [A.alloc([128, 512], BF16) for _ in range(2)]
    kmf = A.alloc([64, 16], F32)
    kmb = [A.alloc([64, 16], BF16) for _ in range(2)]
    vbt = A.alloc([128, NQB, 4, 16], F32)
    lot = A.alloc([128, NQB, 4, 16], F32)
    gm = [A.alloc([128, 4, 16], F32) for _ in range(2)]
    m8 = [A.alloc([128, 4, 8], F32) for _ in range(2)]
    m1 = [A.alloc([128, 4, 16], F32) for _ in range(2)]
    npad = [A.alloc([128, 4, 80], BF16) for _ in range(2)]
    ones_col = A.alloc([128, NKC, 1], F32)
    ones_k = A.alloc([128, 2], BF16)
    one_f = A.alloc([1, 2], F32)
    OTsb = [A.alloc([128, 512], BF16) for _ in range(2)]
    ZTsb = [A.alloc([1, 512], F32) for _ in range(2)]
    Zhi = [A.alloc([1, 512], BF16) for _ in range(2)]
    Zlo = [A.alloc([1, 512], BF16) for _ in range(2)]
    Zd = [A.alloc([1, 512], F32) for _ in range(2)]
    one_b = A.alloc([1, 2], BF16)
    P.op("dve", lambda e: e.memset(one_b[:, :], 1.0), writes=["one_b"])
    one_b64 = A.alloc([128, 2], BF16)
    P.op("dve", lambda e: e.memset(one_b64[:, :], 1.0), writes=["one_b64"])
    Zm_f = A.alloc([128, 512], F32)
    Zm_d = A.alloc([128, 512], F32)
    Zm_lo = A.alloc([128, 512], BF16)

    ones_kk = A.alloc([128, 128], BF16)
    P.op("dve", lambda e: e.memset(ones_kk[:, :], 1.0), writes=["ones_kk"])
    for par in range(2):
        for m in range(2):
            P.op("pool", (lambda par=par, m=m: (lambda e: e.memset(KTs[par][m][64:128, :], 0.0)))(), writes=[("KT", par, m)])
            P.op("pool", (lambda par=par, m=m: (lambda e: e.memset(QTs[par][m][64:128, :], 0.0)))(), writes=[("QT", par, m), ("QTm", par, m)])
    P.op("dve", lambda e: e.memset(ones_k[:, :], 1.0), writes=["ones_k"])
    P.op("dve", lambda e: e.memset(one_f[:, :], 1.0), writes=["one_f"])
    DMA(P, "sp", vbt, I["vb"], "vbt", writes=["vbt"])
    DMA(P, "sp", lot, I["lo"], "lot", writes=["lot"])
    for g in range(2):
        P.op("dve", (lambda g=g: (lambda e: e.memset(npad[g][:, :, :], 0.0)))(), writes=[("npad", g)])
    P.op("dve", lambda e: e.memset(ones_col[:, :, :], 1.0), writes=["ones_col"])
    vv = C.V_s.rearrange("(k p) d -> p k d", p=128)

    jobs = [("d", h) for h in range(NHD)] + [("m", h) for h in range(NHM)]

    def load_job(ji):
        kind, h = jobs[ji]
        par = ji % 2
        KT, QT, VE = KTs[par], QTs[par], VEs[par]
        if kind == "d":
            hg = NHM + h
            for m in range(2):
                r0 = 512 + h * 128 + m * 64
                DMA(P, "sp", KT[m][0:64, :], C.KT_s[r0:r0 + 64, :], "KTl%d%d" % (par, m), writes=[("KT", par, m)])
                DMA(P, "pool", KT[m][64:68, :], I["kc"][hg, :, :], "KTc%d%d" % (par, m), writes=[("KT", par, m)])
                DMA(P, "sp", QT[m][0:64, :], C.QT_s[r0:r0 + 64, :], "QTl%d%d" % (par, m), writes=[("QT", par, m)])
                DMA(P, "pool", QT[m][64:68, :], I["qc"][hg, :, :], "QTc%d%d" % (par, m), writes=[("QT", par, m)])
            DMA(P, "sp", VE[:, :, 0:128], vv[:, :, 512 + h * 128:512 + (h + 1) * 128], "VEl%d" % par, writes=[("VE", par)])
            CP(P, "dve", VE[:, :, 128:129], ones_col[:, :, :], ["ones_col"], [("VE", par)])
        else:
            r0 = h * 64
            DMA(P, "sp", KT[0][0:64, :], C.KT_s[r0:r0 + 64, :], "KTl%d0" % par, writes=[("KT", par, 0)])
            DMA(P, "pool", KT[0][64:80, :], I["kind"], "KTi%d" % par, writes=[("KT", par, 0)])
            DMA(P, "pool", KT[0][80:84, :], I["kc"][h, :, :], "KTc%d0" % par, writes=[("KT", par, 0)])
            DMA(P, "sp", QT[0][0:64, :], C.QT_s[r0:r0 + 64, :], "QTl%d0" % par, writes=[("QT", par, 0)])
            DMA(P, "pool", QT[0][80:84, :], I["qc"][h, :, :], "QTc%d0" % par, writes=[("QT", par, 0)])
            DMA(P, "sp", VE[:, :, 0:64], vv[:, :, h * 64:(h + 1) * 64], "VEl%d" % par, writes=[("VE", par)])
            CP(P, "dve", VE[:, :, 64:65], ones_col[:, :, :], ["ones_col"], [("VE", par)])

    def mask_stage(ji, tg):
        par = ji % 2
        KT, QT = KTs[par][0], QTs[par][0]
        if tg == 0:
            P.op("dve", lambda e: e.tensor_reduce(out=kmf[:, :], in_=KT[0:64, :].rearrange("p (n l) -> p n l", l=256), axis=AX.X, op=ALU.add),
                 [("KT", par, 0)], ["kmf"])
            TS(P, "dve", kmb[par][:, :], kmf[:, :], 1.0 / 256.0, None, ALU.mult, None, ["kmf"], [("kmb", par)])
        gp = tg % 2
        for tq in range(4):
            t = tg * 4 + tq
            MM(P, C.ps[7][:, tq * 16:(tq + 1) * 16], QT[0:64, t * 128:(t + 1) * 128], kmb[par][:, :], True, True,
               [("QT", par, 0), ("kmb", par)], [("ps", 7)])
        TT(P, "dve", gm[gp][:, :, :], C.ps[7][:, 0:64].rearrange("p (a b) -> p a b", a=4), vbt[:, tg, :, :], ALU.add,
           [("ps", 7), "vbt"], [("gm", gp)])
        for tq in range(4):
            P.op("dve", (lambda tq=tq: (lambda e: e.max(out=m8[gp][:, tq, :], in_=gm[gp][:, tq, :])))(), [("gm", gp)], [("m8", gp, tq)])
        for tq in range(4):
            TS(P, "dve", m1[gp][:, tq, :], gm[gp][:, tq, :], m8[gp][:, tq, 3:4], 1.0, ALU.is_ge, ALU.subtract,
               [("gm", gp), ("m8", gp, tq)], [("m1", gp, tq)])
        TT(P, "dve", npad[gp][:, :, 64:80], m1[gp][:, :, :], lot[:, tg, :, :], ALU.min,
           [("m1", gp, q) for q in range(4)] + ["lot"], [("npad", gp)])

    def mask_stage_b(ji, tg):
        par = ji % 2
        QT = QTs[par][0]
        gp = tg % 2
        for tq in range(4):
            MM(P, C.ps[6][0:80, tq * 128:(tq + 1) * 128], npad[gp][:, tq, :], C.ident_b[:, :], True, True,
               [("npad", gp), "ident_b"], [("ps", 6)])
        CP(P, "act", QT[64:80, tg * 512:(tg + 1) * 512], C.ps[6][64:80, :], [("ps", 6)], [("QTm", par, 0)])

    pending = []
    stepc = [0]

    dseq = [0]

    def defer(lag, fn):
        if not FLAG_DEFER:
            fn()
            return
        dseq[0] += 1
        pending.append((stepc[0] + lag, dseq[0], fn))

    def run_due(flush=False):
        while True:
            due = [p for p in pending if flush or p[0] <= stepc[0]]
            if not due:
                break
            due.sort()
            it = due[0]
            pending.remove(it)
            it[2]()

    def attention(ji, nmaps, dv, K, finalize, side):
        par = ji % 2
        KT, QT, VE = KTs[par], QTs[par], VEs[par]
        steps = [(qb, m, kc) for qb in range(NQB) for m in range(nmaps) for kc in range(KO + 4 * qb + 4)]
        n = len(steps)

        def emit_S(i):
            qb, m, kc = steps[i]
            j = kc - KO - 4 * qb
            n0 = max(0, j) * 128
            sb = i % 2
            S = C.ps[sb]
            MM(P, S[:, n0:512], KT[m][:, kc * 128:(kc + 1) * 128], QT[m][:, qb * 512 + n0:(qb + 1) * 512],
               True, j < 0, [("KT", par, m), ("QT", par, m), ("QTm", par, m)], [("ps", sb)])
            if j >= 0:
                MM(P, S[:, n0:n0 + 128], C.ident_b[:, :], C.tri_b[:, :], False, True, ["ident_b", "tri_b"], [("ps", sb)])
            ACT(P, PT[i % 3][:, n0:512], S[:, n0:512], AF.Exp, [("ps", sb)], [("PT", i % 3)], scale=0.125)

        def emit_AV(i):
            qb, m, kc = steps[i]
            j = kc - KO - 4 * qb
            n0 = max(0, j) * 128
            last = KO + 4 * qb + 3
            pt = PT[i % 3]
            if nmaps == 2:
                MM(P, C.ps[2 + m][:, n0:512], VE[:, kc, 0:128], pt[:, n0:512], kc == 0, kc == last,
                   [("PT", i % 3), ("VE", par)], [("OT", m)])
                MM(P, C.ps[4 + m][:, n0:512], ones_kk[:, :], pt[:, n0:512], kc == 0, kc == last,
                   [("PT", i % 3), "ones_kk"], [("ZT", m)])
            else:
                MM(P, C.ps[2][:, n0:512], VE[:, kc, 0:128], pt[:, n0:512], kc == 0, kc == last,
                   [("PT", i % 3), ("VE", par)], [("OT", 0)])
            if m == nmaps - 1 and kc == last:
                q2 = qb % 2
                if nmaps == 2:
                    for mm in range(2):
                        CP(P, "dve", OTsb[mm][:, :], C.ps[2 + mm][:, :], [("OT", mm)], [("OTsb", mm)])
                        CP(P, "dve", ZTsb[mm][0:1, :], C.ps[4 + mm][0:1, :], [("ZT", mm)], [("ZTsb", mm)])
                        CP(P, "dve", Zhi[mm][0:1, :], ZTsb[mm][0:1, :], [("ZTsb", mm)], [("Zhi", mm)])
                        TT(P, "dve", Zd[mm][0:1, :], ZTsb[mm][0:1, :], Zhi[mm][0:1, :], ALU.subtract, [("ZTsb", mm), ("Zhi", mm)], [("Zd", mm)])
                        CP(P, "dve", Zlo[mm][0:1, :], Zd[mm][0:1, :], [("Zd", mm)], [("Zlo", mm)])
                else:
                    CP(P, "dve", OTsb[0][0:65, :], C.ps[2][0:65, :], [("OT", 0)], [("OTsb", 0)])
                    CP(P, "dve", Zm_f[64:65, :], C.ps[2][64:65, :], [("OT", 0)], [("Zm_f", 0)])
                    TT(P, "dve", Zm_d[64:65, :], Zm_f[64:65, :], OTsb[0][64:65, :], ALU.subtract, [("Zm_f", 0), ("OTsb", 0)], [("Zm_d", 0)])
                    CP(P, "dve", Zm_lo[64:65, :], Zm_d[64:65, :], [("Zm_d", 0)], [("Zm_lo", 0)])

                def stage2(qb=qb, q2=q2):
                    if nmaps == 2:
                        for mm in range(2):
                            for sub in range(4):
                                MM(P, C.ps[6][:, sub * 128:(sub + 1) * 128], OTsb[mm][:, sub * 128:(sub + 1) * 128], C.ident_b[:, :], True, True,
                                   [("OTsb", mm), "ident_b"], [("ps", 6)])
                            CP(P, "dve", accsb[q2][:, 2 * mm:2 * mm + 2, :, 0:128],
                               C.ps[6][:, :].rearrange("p (a b c) -> p a b c", a=2, b=2), [("ps", 6)], [("accsb", q2, 2 * mm), ("accsb", q2, 2 * mm + 1)])
                            for sub in range(4):
                                MM(P, C.ps[7][:, 64 + mm * 4 + sub:64 + mm * 4 + sub + 1], Zhi[mm][0:1, sub * 128:(sub + 1) * 128], one_b[0:1, 0:1],
                                   True, False, [("Zhi", mm), "one_b"], [("ps", 7)])
                                MM(P, C.ps[7][:, 64 + mm * 4 + sub:64 + mm * 4 + sub + 1], Zlo[mm][0:1, sub * 128:(sub + 1) * 128], one_b[0:1, 0:1],
                                   False, True, [("Zlo", mm), "one_b"], [("ps", 7)])
                            CP(P, "dve", accsb[q2][:, 2 * mm:2 * mm + 2, :, 128:129],
                               C.ps[7][:, 64 + mm * 4:64 + mm * 4 + 4].rearrange("p (a b c) -> p a b c", a=2, b=2), [("ps", 7)],
                               [("accsb", q2, 2 * mm), ("accsb", q2, 2 * mm + 1)])
                    else:
                        for sub in range(4):
                            MM(P, C.ps[6][:, sub * 128:sub * 128 + 65], OTsb[0][0:65, sub * 128:(sub + 1) * 128], C.ident_b[0:65, 0:65], True, False,
                               [("OTsb", 0), "ident_b"], [("ps", 6)])
                            MM(P, C.ps[6][:, sub * 128 + 64:sub * 128 + 65], Zm_lo[64:65, sub * 128:(sub + 1) * 128], one_b64[64:65, 0:1], False, True,
                               [("Zm_lo", 0), "one_b64"], [("ps", 6)])
                        CP(P, "dve", accsb[q2][:, 0:2, :, 0:65],
                           C.ps[6][:, :].rearrange("p (a b c) -> p a b c", a=2, b=2)[:, :, :, 0:65], [("ps", 6)], [("accsb", q2, 0), ("accsb", q2, 1)])
                    finalize[0][1](qb)

                if FLAG_S2DEFER:
                    defer(2, stage2)
                else:
                    stage2()
                for lag, fn in finalize[1:]:
                    defer(lag, (lambda qb=qb, fn=fn: fn(qb)))
                side(qb)

        emit_S(0)
        for i in range(n):
            if i + 1 < n:
                emit_S(i + 1)
            emit_AV(i)
            stepc[0] += 1
            run_due()
        run_due(flush=True)

    def fin_diff(h, qb):
        q2 = qb % 2
        for sub in range(4):
            a1 = accsb[q2][:, sub // 2, sub % 2, :]
            a2 = accsb[q2][:, 2 + sub // 2, sub % 2, :]
            r1 = [("accsb", q2, sub // 2)]
            r2 = [("accsb", q2, 2 + sub // 2)]
            s = sc[:, sub, :]
            P.op("dve", (lambda s=s, a1=a1: (lambda e: e.reciprocal(out=s[:, 0:1], in_=a1[:, 128:129])))(), r1, [("sc", sub, 0)])
            P.op("dve", (lambda s=s, a2=a2: (lambda e: e.reciprocal(out=s[:, 1:2], in_=a2[:, 128:129])))(), r2, [("sc", sub, 1)])
            TT(P, "dve", s[:, 2:3], s[:, 1:2], C.lam_t[:, 1:2], ALU.mult, [("sc", sub, 1), "nlam"], [("sc", sub, 2)])
            TS(P, "dve", o1[:, sub, :], a1[:, 0:128], s[:, 0:1], None, ALU.mult, None, r1 + [("sc", sub, 0)], [("o1", sub)])
            STT(P, yd[:, sub, :], a2[:, 0:128], s[:, 2:3], o1[:, sub, :], ALU.mult, ALU.add,
                r2 + [("sc", sub, 2), ("o1", sub)], [("yd", sub)])
            TT(P, "dve", sq[:, :], yd[:, sub, :], yd[:, sub, :], ALU.mult, [("yd", sub)], ["sq"])
            P.op("dve", (lambda s=s: (lambda e: e.tensor_reduce(out=s[:, 3:4], in_=sq[:, :], axis=AX.X, op=ALU.add)))(), ["sq"], [("sc", sub, 3)])

    def fin_diff_act(h, qb):
        ACT(P, sc[:, :, 4:5], sc[:, :, 3:4], AF.Ln, [("sc", sub, 3) for sub in range(4)] + ["eps"], [("sc", sub, 4) for sub in range(4)],
            bias=C.eps_t[:], scale=1.0 / 128.0)
        ACT(P, sc[:, :, 5:6], sc[:, :, 4:5], AF.Exp, [("sc", sub, 4) for sub in range(4)], [("sc", sub, 5) for sub in range(4)], scale=-0.5)

    def fin_diff_a2(h, qb):
        for sub in range(4):
            s = sc[:, sub, :]
            TS(P, "dve", yd[:, sub, :], yd[:, sub, :], s[:, 5:6], float(1.0 - LAM_INIT), ALU.mult, ALU.mult, [("yd", sub), ("sc", sub, 5)], [("yd", sub)])
            TT(P, "dve", ydb[:, sub, :], yd[:, sub, :], C.sublng[:, :], ALU.mult, [("yd", sub), "sublng"], [("ydb", sub)])

    def fin_diff_b(h, qb):
        for sub in range(4):
            MM(P, C.ps[6][:, sub * 128:(sub + 1) * 128], ydb[:, sub, :], C.ident_b[:, :], True, True,
               [("ydb", sub), "ident_b"], [("ps", 6)])
        sl = qb % 2
        CP(P, "dve", ystg[sl][:, :], C.ps[6][:, :], [("ps", 6)], [("ystg", sl)])
        DMA(P, "sp", C.YT_s[512 + h * 128:512 + (h + 1) * 128, qb * 512:(qb + 1) * 512], ystg[sl][:, :], "ystg%d" % sl, reads=[("ystg", sl)])

    def fin_moba(h, qb):
        q2 = qb % 2
        sl = qb % 2
        for sub in range(4):
            a1 = accsb[q2][:, sub // 2, sub % 2, :]
            r1 = [("accsb", q2, sub // 2)]
            s = sc[:, sub, :]
            P.op("dve", (lambda s=s, a1=a1: (lambda e: e.reciprocal(out=s[:, 0:1], in_=a1[:, 64:65])))(), r1, [("sc", sub, 0)])
            TS(P, "dve", ydb[:, sub, 0:64], a1[:, 0:64], s[:, 0:1], None, ALU.mult, None, r1 + [("sc", sub, 0)], [("ydb", sub)])

    def fin_moba_b(h, qb):
        sl = qb % 2
        for sub in range(4):
            MM(P, C.ps[6][0:64, sub * 128:(sub + 1) * 128], ydb[:, sub, 0:64], C.ident_b[:, :], True, True,
               [("ydb", sub), "ident_b"], [("ps", 6)])
        CP(P, "dve", ystg[sl][0:64, :], C.ps[6][0:64, :], [("ps", 6)], [("ystg", sl)])
        DMA(P, "sp", C.YT_s[h * 64:(h + 1) * 64, qb * 512:(qb + 1) * 512], ystg[sl][0:64, :], "ystg%d" % sl, reads=[("ystg", sl)])

    load_job(0)
    if jobs[0][0] == "m":
        for tg in range(NQB):
            mask_stage(0, tg)
            mask_stage_b(0, tg)
    for ji, (kind, h) in enumerate(jobs):
        nxt = ji + 1 if ji + 1 < len(jobs) else None
        if nxt is not None:
            load_job(nxt)

        def side(qb, nxt=nxt):
            if nxt is not None and jobs[nxt][0] == "m":
                defer(4, (lambda: mask_stage(nxt, qb)))
                defer(11, (lambda: mask_stage_b(nxt, qb)))

        if kind == "d":
            attention(ji, 2, 128, 68, [(0, (lambda qb, h=h: fin_diff(h, qb))), (9, (lambda qb, h=h: fin_diff_act(h, qb))),
                                       (11, (lambda qb, h=h: fin_diff_a2(h, qb))), (18, (lambda qb, h=h: fin_diff_b(h, qb)))], side)
        else:
            attention(ji, 1, 64, 84, [(0, (lambda qb, h=h: fin_moba(h, qb))), (6, (lambda qb, h=h: fin_moba_b(h, qb)))], side)


def phase3(C):
    P, I, A = C.P, C.I, C.arena
    TB = C.TB
    NTB = C.NQ // TB
    A.reset()
    xs = A.alloc([128, 8, D], F32)
    xT = A.alloc([128, 8, TB], BF16)
    wb = [A.alloc([128, 8, 512], BF16) for _ in range(2)]
    lnp = A.alloc([128, 4, D], F32)
    tT = A.alloc([128, D], F32)
    r = A.alloc([128, D], F32)
    xn = A.alloc([128, D], F32)
    st6 = A.alloc([128, 2, 6], F32)
    mv = A.alloc([128, 8], F32)
    sil = [A.alloc([128, 512], F32) for _ in range(2)]
    sg = [A.alloc([128, 512], BF16) for _ in range(4)]
    pt_f = [A.alloc([128, 256], F32) for _ in range(1)]
    C.xb = [A.alloc([128, D], BF16) for _ in range(1)]
    C.xbi = 0
    tmp = (tT, r, xn, st6, mv)
    base = A.off
    mergedT = A.alloc([128, 8, TB], BF16)
    yT = A.alloc([128, 8, TB], BF16)
    wbr = A.alloc([128, 2, 4, D], BF16)
    wout = A.alloc([128, 8, D], BF16)
    A.reset(base)
    actT = A.alloc([128, NFF, TB], BF16)
    wd = A.alloc([128, NFF, D], BF16)
    A.reset(base)
    wpg = A.alloc([128, 8, D], BF16)
    wpl = A.alloc([128, 2, D], BF16)
    pT = A.alloc([128, 2, TB], BF16)

    DMA(P, "sp", lnp, I["lnp"][:, 2:6, :], "lnp3", writes=["ln2", "ln3"])
    x1v = C.X1_s.rearrange("(t p) d -> p t d", p=128)
    outv = C.out.rearrange("(t p) d -> p t d", p=128)
    pv = I["p"].rearrange("(t p) d -> p t d", p=128)
    win = wview(I["w_in"], 8)
    for tb in range(NTB):
        tok0 = tb * TB
        for t in range(8):
            DMA(P, "sp", xs[:, t, :], x1v[:, tb * 8 + t, :], "xs%d" % t, writes=[("xs", t)])
        for kc in range(8):
            DMA(P, "sp", xT[:, kc, :], C.X1T_s[kc * 128:(kc + 1) * 128, tb * TB:(tb + 1) * TB], "xTl%d" % kc, writes=[("xT", q) for q in range(8)])
            DMA(P, "sp", yT[:, kc, :], C.YT_s[kc * 128:(kc + 1) * 128, tok0:tok0 + TB], "yTl%d" % kc, writes=[("yT", kc)])
        DMA(P, "pool", wbr[:, 0, :, :], wview(I["w_branch_a"], 4), "wbra", writes=["wbra"])
        DMA(P, "pool", wbr[:, 1, :, :], wview(I["w_branch_b"], 4), "wbrb", writes=["wbrb"])
        DMA(P, "pool", wout, wview(I["w_out"], 8), "wout", writes=["wout"])
        for oc in range(8):
            slot = oc % 2
            buf = wb[slot]
            DMA(P, "pool", buf[:, :, 0:128], win[:, :, 3072 + oc * 128:3072 + (oc + 1) * 128], "wbA%d" % slot, writes=[("wb", slot, 0)])
            DMA(P, "pool", buf[:, :, 128:256], win[:, :, 4096 + oc * 128:4096 + (oc + 1) * 128], "wbG%d" % slot, writes=[("wb", slot, 1)])
            for hb in range(TB // 512):
                xr = [("xT", 4 * hb + q) for q in range(4)]
                ip = (oc * 2 + hb) % 2 if FLAG_MERGEDB else 0
                for br in range(2):
                    bank = 4 * ip + br
                    for kc in range(8):
                        MM(P, C.ps[bank][:, :], buf[:, kc, br * 128:(br + 1) * 128], xT[:, kc, hb * 512:(hb + 1) * 512],
                           kc == 0, kc == 7, [("wb", slot, br)] + xr, [("ps", bank)])
                    ACT(P, sg[2 * ip + br][:, :], C.ps[bank][:, :], AF.Sigmoid, [("ps", bank)], [("sg", 2 * ip + br)])
                    bank2 = 4 * ip + 2 + br
                    for kc in range(4):
                        MM(P, C.ps[bank2][:, :], wbr[:, br, kc, oc * 128:(oc + 1) * 128], yT[:, br * 4 + kc, hb * 512:(hb + 1) * 512],
                           kc == 0, kc == 3, ["wbra" if br == 0 else "wbrb", ("yT", br * 4 + kc)], [("ps", bank2)])
                    TT(P, "dve", sil[br][:, :], sg[2 * ip + br][:, :], C.ps[bank2][:, :], ALU.mult,
                       [("sg", 2 * ip + br), ("ps", bank2)], [("sil", br)])
                TT(P, "dve", mergedT[:, oc, hb * 512:(hb + 1) * 512], sil[0][:, :], sil[1][:, :], ALU.add,
                   [("sil", 0), ("sil", 1)], [("mergedT", oc, hb)])
        for t in range(8):
            hb = t // 4
            yb = (4, 5) if (t % 2 == 0 or not FLAG_LNPIPE) else (2, 3)
            for dh in range(2):
                for kc in range(8):
                    MM(P, C.ps[yb[dh]][:, :], mergedT[:, kc, t * 128:(t + 1) * 128], wout[:, kc, dh * 512:(dh + 1) * 512],
                       kc == 0, kc == 7, [("mergedT", kc, hb), "wout"], [("ps", yb[dh])])
            emit_layernorm(C, xs, t, yb, 1.0, lnp[:, 0, :], lnp[:, 1, :], tmp, "ln2")
            if t >= 1:
                emit_transposes(C, xs, xT, t - 1, ("dve", "act"))
        emit_transposes(C, xs, xT, 7, ("dve", "act"))
        P.barrier()
        emit_ffn(C, xs, xT, actT, wd, wb, sil, I["w_ffn2_up"], I["w_ffn2_down"], "f2", True)
        for t in range(8):
            yb = (4, 5) if (t % 2 == 0 or not FLAG_LNPIPE) else (2, 3)
            emit_down(C, actT, wd, t, yb)
            emit_layernorm(C, xs, t, yb, 0.5, lnp[:, 2, :], lnp[:, 3, :], tmp, "ln3")
            if t >= 1:
                emit_transposes(C, xs, xT, t - 1, ("dve", "act"))
        emit_transposes(C, xs, xT, 7, ("dve", "act"))
        P.barrier()
        DMA(P, "pool", wpg, wview(I["w_ple_gate"], 8), "wpg", writes=["wpg"])
        DMA(P, "pool", wpl, wview(I["w_ple"], 2), "wpl", writes=["wpl"])
        for t in range(8):
            pf = pt_f[0]
            DMA(P, "sp", pf, pv[:, tb * 8 + t, :], "pf0", writes=[("pf", 0)])
            pb = C.xb[0][:, 0:256]
            CP(P, "pool", pb, pf[:, :], [("pf", 0)], [("xb", 0)])
            for k in range(2):
                MM(P, C.ps[6][:, k * 128:(k + 1) * 128], pb[:, k * 128:(k + 1) * 128], C.ident_b[:, :], True, True, [("xb", 0), "ident_b"], [("ps", 6)])
            CP(P, "dve", pT[:, :, t * 128:(t + 1) * 128], C.ps[6][:, 0:256].rearrange("p (a b) -> p a b", a=2), [("ps", 6)], [("pT", t)])
            for dh in range(2):
                for kc in range(8):
                    MM(P, C.ps[dh][:, :], xT[:, kc, t * 128:(t + 1) * 128], wpg[:, kc, dh * 512:(dh + 1) * 512],
                       kc == 0, kc == 7, [("xT", t), "wpg"], [("ps", dh)])
                ACT(P, tT[:, dh * 512:(dh + 1) * 512], C.ps[dh][:, :], AF.Sigmoid, [("ps", dh)], [("tT", dh)])
                for kc in range(2):
                    MM(P, C.ps[2 + dh][:, :], pT[:, kc, t * 128:(t + 1) * 128], wpl[:, kc, dh * 512:(dh + 1) * 512],
                       kc == 0, kc == 1, [("pT", t), "wpl"], [("ps", 2 + dh)])
                TT(P, "dve", sil[dh][:, :], tT[:, dh * 512:(dh + 1) * 512], C.ps[2 + dh][:, :], ALU.mult, [("tT", dh), ("ps", 2 + dh)], [("sil", dh)])
                TT(P, "dve", xn[:, dh * 512:(dh + 1) * 512], sil[dh][:, :], xs[:, t, dh * 512:(dh + 1) * 512], ALU.add,
                   [("sil", dh), ("xs", t)], [("xo", dh)])
            DMA(P, "sp", outv[:, tb * 8 + t, :], xn[:, :], "outst", reads=[("xo", 0), ("xo", 1)])
        P.barrier()


def host_consts(mode, r):
    c = {}
    c["ident"] = np.eye(128, dtype=np.float32)
    ki = np.arange(128)[:, None]
    qi = np.arange(128)[None, :]
    c["tri"] = np.where(ki > qi, -BIGM, 0.0).astype(np.float32)
    slopes = np.concatenate([2.0 ** (-8.0 * np.arange(1, 9) / 8), 2.0 ** (-8.0 * np.arange(1, 5) / 4)]).astype(np.float64)
    if mode == "P4":
        NQ, NK = 4096, 4096
        qpos = np.arange(NQ)
        kpos = np.arange(NK)
        kvalid = np.ones(NK, bool)
    else:
        NQ, NK = 2048, 4096
        qpos = r * 2048 + np.arange(NQ)
        kpos = np.concatenate([np.arange(2048), r * 2048 + np.arange(2048)])
        kvalid = np.concatenate([np.full(2048, r == 1), np.ones(2048, bool)])
    qc = np.zeros((12, 4, NQ), np.float32)
    kc = np.zeros((12, 4, NK), np.float32)
    for h in range(12):
        s8 = 8.0 * slopes[h]
        qc[h, 0] = -s8 * (qpos % 128)
        qc[h, 1] = -s8 * 128.0 * (qpos // 128)
        qc[h, 2] = 1.0
        qc[h, 3] = 1.0
        kc[h, 0] = 1.0
        kc[h, 1] = 1.0
        kc[h, 2] = s8 * (kpos % 128)
        kc[h, 3] = np.where(kvalid, s8 * 128.0 * (kpos // 128), -BIGM)
    c["qc"], c["kc"] = qc, kc
    kind = np.zeros((16, NK), np.float32)
    for n in range(16):
        kind[n, n * 256:(n + 1) * 256] = BIGM
    c["kind"] = kind
    nt = NQ // 128
    vb = np.zeros((nt, 16), np.float32)
    lo = np.zeros((nt, 16), np.float32)
    for t in range(nt):
        b_loc = t // 2
        for n in range(16):
            if mode == "P4":
                st = "past" if n < b_loc else ("own" if n == b_loc else "no")
            else:
                if n < 8:
                    st = "past" if r == 1 else "no"
                else:
                    nn = n - 8
                    st = "past" if nn < b_loc else ("own" if nn == b_loc else "no")
            vb[t, n] = {"past": 0.0, "own": 1e30, "no": -1e30}[st]
            lo[t, n] = 0.0 if st in ("past", "own") else -1.0
    c["vb"] = np.ascontiguousarray(np.broadcast_to(vb.reshape(1, nt // 4, 4, 16), (128, nt // 4, 4, 16)))
    c["lo"] = np.ascontiguousarray(np.broadcast_to(lo.reshape(1, nt // 4, 4, 16), (128, nt // 4, 4, 16)))
    return c


_CACHE = {}
MODE = "P8R"


def kernel(**inputs):
    f = lambda k: np.ascontiguousarray(np.asarray(inputs[k], dtype=np.float32)[0])
    mode = MODE
    if mode not in _CACHE:
        _CACHE[mode] = build(mode)
    nc = _CACHE[mode]
    x = np.asarray(inputs["x"], dtype=np.float32)
    p = np.asarray(inputs["p"], dtype=np.float32)[0]
    shared = {k: f(k) for k in ("w_ffn1_up", "w_ffn1_down", "w_ffn2_up", "w_ffn2_down", "w_in", "w_branch_a",
                                "w_branch_b", "w_out", "w_ple", "w_ple_gate")}
    lnp = np.stack([f(k) for k in ("ln1_g", "ln1_b", "ln2_g", "ln2_b", "ln3_g", "ln3_b")], 0)
    shared["lnp"] = np.ascontiguousarray(np.broadcast_to(lnp[None], (128, 6, D)))
    shared["sublng"] = np.ascontiguousarray(np.broadcast_to(f("subln_g")[None], (128, 128)))
    lamv = np.stack([f(k) for k in ("lambda_q1", "lambda_k1", "lambda_q2", "lambda_k2")], 0)
    shared["lamv"] = np.ascontiguousarray(np.broadcast_to(lamv[None], (128, 4, 64)))
    consts = {}
    in_maps = []
    for c in range(N_CORES):
        m = dict(shared)
        if mode == "P4":
            b, r = c % 4, 0
            m["x"] = np.ascontiguousarray(x[b])
            m["p"] = np.ascontiguousarray(p[b])
        else:
            b, r = c // 2, c % 2
            m["x"] = np.ascontiguousarray(np.concatenate([x[b, 0:2048], x[b, r * 2048:(r + 1) * 2048]], 0))
            m["p"] = np.ascontiguousarray(p[b, r * 2048:(r + 1) * 2048])
        if r not in consts:
            consts[r] = host_consts(mode, r)
        m.update(consts[r])
        in_maps.append(m)
    res = run_bass_kernel_spmd(nc, in_maps, core_ids=list(range(N_CORES)))
    if mode == "P4":
        out = np.stack([np.asarray(res.results[b]["out"], dtype=np.float32) for b in range(4)], 0)
    else:
        out = np.stack([np.concatenate([np.asarray(res.results[2 * b + r]["out"], dtype=np.float32) for r in range(2)], 0)
                        for b in range(4)], 0)
    return out
```

```python
import contextlib
import numpy as np
import concourse.bass as bass
import concourse.mybir as mybir
from concourse.bass_utils import run_bass_kernel_spmd

F32 = mybir.dt.float32
BF16 = mybir.dt.bfloat16
AF = mybir.ActivationFunctionType
ALU = mybir.AluOpType
AX = mybir.AxisListType

D = 1024
DFF = 2816
SEQ = 4096
NFF = 22
ALPHA = float(2.0 ** 0.25)
BIGM = 262144.0
EPS = 1e-5
LAM_INIT = 0.8 - 0.6 * 1.0
N_CORES = 8
import os
FLAG_DEFER = os.environ.get("K_DEFER", "1") == "1"
FLAG_S2DEFER = os.environ.get("K_S2DEFER", "1") == "1"
FLAG_LNPIPE = os.environ.get("K_LNPIPE", "1") == "1"
FLAG_MERGEDB = os.environ.get("K_MERGEDB", "1") == "1"


class Prog:
    ENGS = ("pe", "act", "dve", "pool", "sp")

    def __init__(self, nc):
        self.nc = nc
        self.ops = []
        self.last_w = {}
        self.readers = {}
        self.dma_keys = {}
        self.last_on_eng = {}
        self.dma_since_barrier = []
        self.barrier_deps = {}

    def op(self, eng, fn, reads=(), writes=(), dma=0, key=None):
        idx = len(self.ops)
        deps = set()
        for r in reads:
            w = self.last_w.get(r)
            if w is not None:
                deps.add(w)
        for r in writes:
            w = self.last_w.get(r)
            if w is not None:
                deps.add(w)
            for rd in self.readers.get(r, {}).values():
                deps.add(rd)
        for r in reads:
            d = self.readers.setdefault(r, {})
            d[eng if not dma else ("dma", idx)] = idx
        for r in writes:
            self.last_w[r] = idx
            self.readers[r] = {}
        bd = self.barrier_deps.pop(eng, None)
        if bd:
            deps.update(bd)
        deps.discard(idx)
        o = dict(eng=eng, fn=fn, deps=deps, dma=dma, sig=False)
        if dma:
            assert key is not None
            if key not in self.dma_keys:
                self.dma_keys[key] = len(self.dma_keys)
            o["dkey"] = self.dma_keys[key]
            self.dma_since_barrier.append(idx)
        else:
            self.last_on_eng[eng] = idx
        self.ops.append(o)
        return idx

    def barrier(self):
        deps = set(self.dma_since_barrier)
        deps.update(self.last_on_eng.values())
        self.dma_since_barrier = []
        self.barrier_deps = {e: set(deps) for e in self.ENGS}

    def emit(self):
        nc = self.nc
        ops = self.ops
        for o in ops:
            for d in o["deps"]:
                p = ops[d]
                if p["eng"] == "pe" and o["eng"] == "pe" and not p["dma"] and not o["dma"]:
                    continue
                p["sig"] = True
        es = contextlib.ExitStack()
        eng_sem = {e: es.enter_context(nc.semaphore("s_" + e)) for e in self.ENGS}
        cnt = {e: 0 for e in self.ENGS}
        nk = len(self.dma_keys)
        assert nk <= 120, nk
        dma_sems = [es.enter_context(nc.semaphore("d%d" % i)) for i in range(nk)]
        dcnt = [0] * nk
        for o in ops:
            if o["dma"]:
                k = o["dkey"]
                dcnt[k] += 16 * o["dma"]
                o["sigval"] = dcnt[k]
            elif o["sig"]:
                cnt[o["eng"]] += 1
                o["sigval"] = cnt[o["eng"]]
        per_eng = {e: [] for e in self.ENGS}
        for i, o in enumerate(ops):
            per_eng[o["eng"]].append(i)
        with es, nc.Block() as block:
            def run(engname, engobj):
                waited = {}
                for i in per_eng[engname]:
                    o = ops[i]
                    for d in sorted(o["deps"]):
                        p = ops[d]
                        if p["dma"]:
                            sem = dma_sems[p["dkey"]]
                            key = ("d", p["dkey"])
                        else:
                            if p["eng"] == "pe" and engname == "pe" and not o["dma"]:
                                continue
                            sem = eng_sem[p["eng"]]
                            key = ("e", p["eng"])
                        v = p["sigval"]
                        if waited.get(key, 0) >= v:
                            continue
                        waited[key] = v
                        engobj.wait_ge(sem, v)
                    if o["dma"]:
                        o["fn"](engobj, dma_sems[o["dkey"]])
                    else:
                        inst = o["fn"](engobj)
                        if o["sig"]:
                            inst.then_inc(eng_sem[engname], 1)
                for i in per_eng[engname]:
                    o = ops[i]
                    if o["dma"]:
                        key = ("d", o["dkey"])
                        if waited.get(key, 0) < o["sigval"]:
                            waited[key] = o["sigval"]
                            engobj.wait_ge(dma_sems[o["dkey"]], o["sigval"])

            @block.tensor
            def _(e):
                run("pe", e)

            @block.scalar
            def _(e):
                run("act", e)

            @block.vector
            def _(e):
                run("dve", e)

            @block.gpsimd
            def _(e):
                run("pool", e)

            @block.sync
            def _(e):
                run("sp", e)


class Arena:
    def __init__(self, nc, st, nbytes):
        self.t = st.enter_context(nc.sbuf_tensor("arena", [128, nbytes // 4], F32))
        self.cap = nbytes
        self.off = 0

    def reset(self, off=0):
        self.off = off

    def alloc(self, shape, dtype):
        esz = 4 if dtype == F32 else 2
        n = int(np.prod(shape[1:]))
        nb = (n * esz + 63) // 64 * 64
        assert self.off + nb <= self.cap, (self.off, nb, self.cap)
        w0 = self.off // 4
        ap = self.t[0:shape[0], w0:w0 + nb // 4]
        if dtype != F32:
            ap = ap.bitcast(dtype)
        ap = ap[:, 0:n]
        if len(shape) == 3:
            ap = ap.rearrange("p (a b) -> p a b", a=shape[1])
        elif len(shape) == 4:
            ap = ap.rearrange("p (a b c) -> p a b c", a=shape[1], b=shape[2])
        self.off += nb
        return ap


def MM(P, out, lhsT, rhs, start, stop, reads, writes):
    P.op("pe", lambda e: e.matmul(out, lhsT=lhsT, rhs=rhs, start=start, stop=stop), reads, writes)


def TR(P, out, in_, ident, reads, writes):
    P.op("pe", lambda e: e.transpose(out, in_, ident), reads, writes)


def ACT(P, out, in_, func, reads, writes, scale=None, bias=None, accum_out=None):
    kw = {}
    if scale is not None:
        kw["scale"] = scale
    if bias is not None:
        kw["bias"] = bias
    if accum_out is not None:
        kw["accum_out"] = accum_out
    P.op("act", lambda e: e.activation(out=out, in_=in_, func=func, **kw), reads, writes)


def TT(P, eng, out, in0, in1, op, reads, writes):
    P.op(eng, lambda e: e.tensor_tensor(out=out, in0=in0, in1=in1, op=op), reads, writes)


def TS(P, eng, out, in0, s1, s2, op0, op1, reads, writes):
    if op1 is None:
        P.op(eng, lambda e: e.tensor_scalar(out=out, in0=in0, scalar1=s1, scalar2=None, op0=op0), reads, writes)
    else:
        P.op(eng, lambda e: e.tensor_scalar(out=out, in0=in0, scalar1=s1, scalar2=s2, op0=op0, op1=op1), reads, writes)


def STT(P, out, in0, scalar, in1, op0, op1, reads, writes):
    P.op("dve", lambda e: e.scalar_tensor_tensor(out=out, in0=in0, scalar=scalar, in1=in1, op0=op0, op1=op1), reads, writes)


def CP(P, eng, out, in_, reads, writes):
    if eng == "act":
        P.op("act", lambda e: e.activation(out=out, in_=in_, func=AF.Identity), reads, writes)
    else:
        P.op(eng, lambda e: e.tensor_copy(out=out, in_=in_), reads, writes)


def DMA(P, q, out, in_, key, reads=(), writes=()):
    P.op(q, lambda e, s: e.dma_start(out=out, in_=in_).then_inc(s, 16), reads, writes, dma=1, key=key)


class Ctx:
    pass


def build(MODE, debug=False):
    nc = bass.Bass("TRN2", target_bir_lowering=False)
    C = Ctx()
    C.nc = nc
    C.NHM, C.NHD = 8, 4
    if MODE == "P4":
        C.NQ, C.NK, C.KOFF = 4096, 4096, 0
    else:
        C.NQ, C.NK, C.KOFF = 2048, 4096, 2048
    NQ, NK = C.NQ, C.NK
    TB = 1024
    C.TB = TB

    def din(name, shape):
        return nc.dram_tensor(name, list(shape), F32, kind="ExternalInput").ap()

    I = {}
    I["x"] = din("x", [NK, D])
    I["p"] = din("p", [NQ, 256])
    I["w_ffn1_up"] = din("w_ffn1_up", [D, 2 * DFF])
    I["w_ffn1_down"] = din("w_ffn1_down", [DFF, D])
    I["w_ffn2_up"] = din("w_ffn2_up", [D, 2 * DFF])
    I["w_ffn2_down"] = din("w_ffn2_down", [DFF, D])
    I["w_in"] = din("w_in", [D, 5120])
    I["w_branch_a"] = din("w_branch_a", [512, D])
    I["w_branch_b"] = din("w_branch_b", [512, D])
    I["w_out"] = din("w_out", [D, D])
    I["w_ple"] = din("w_ple", [256, D])
    I["w_ple_gate"] = din("w_ple_gate", [D, D])
    I["lnp"] = din("lnp", [128, 6, D])
    I["sublng"] = din("sublng", [128, 128])
    I["lamv"] = din("lamv", [128, 4, 64])
    I["ident"] = din("ident", [128, 128])
    I["tri"] = din("tri", [128, 128])
    I["qc"] = din("qc", [12, 4, NQ])
    I["kc"] = din("kc", [12, 4, NK])
    I["kind"] = din("kind", [16, NK])
    I["vb"] = din("vb", [128, NQ // 512, 4, 16])
    I["lo"] = din("lo", [128, NQ // 512, 4, 16])
    C.I = I
    kind_s = "ExternalOutput" if debug else "Internal"
    C.out = nc.dram_tensor("out", [NQ, D], F32, kind="ExternalOutput").ap()
    C.X1_s = nc.dram_tensor("X1_s", [NQ, D], F32, kind=kind_s).ap()
    C.X1T_s = nc.dram_tensor("X1T_s", [D, NQ], BF16, kind=kind_s).ap()
    C.QT_s = nc.dram_tensor("QT_s", [D, NQ], BF16, kind=kind_s).ap()
    C.KT_s = nc.dram_tensor("KT_s", [D, NK], BF16, kind=kind_s).ap()
    C.V_s = nc.dram_tensor("V_s", [NK, D], BF16, kind=kind_s).ap()
    C.YT_s = nc.dram_tensor("YT_s", [D, NQ], BF16, kind=kind_s).ap()

    with contextlib.ExitStack() as st:
        C.P = P = Prog(nc)
        C.arena = Arena(nc, st, 192 * 1024)
        C.ps = [st.enter_context(nc.psum_tensor("ps%d" % i, [128, 512], F32)) for i in range(8)]
        sb = lambda n, s, d: st.enter_context(nc.sbuf_tensor(n, s, d))
        C.ident_f = sb("ident_f", [128, 128], F32)
        C.ident_b = sb("ident_b", [128, 128], BF16)
        C.tri_b = sb("tri_b", [128, 128], BF16)
        C.eps_t = sb("eps_t", [128, 1], F32)
        C.lam_t = sb("lam_t", [128, 4], F32)
        C.lamv = sb("lamv_t", [128, 4, 64], F32)
        C.sublng = sb("sublng_t", [128, 128], F32)
        C.small = sb("small_t", [128, 64], F32)

        phase0(C)
        phase1(C)
        P.barrier()
        phase2(C)
        P.barrier()
        phase3(C)
        P.emit()
    return nc


def phase0(C):
    P, I = C.P, C.I
    DMA(P, "sp", C.ident_f[:], I["ident"], "c_if", writes=["ident_f"])
    DMA(P, "pool", C.ident_b[:], I["ident"], "c_ib", writes=["ident_b"])
    DMA(P, "pool", C.tri_b[:], I["tri"], "c_tri", writes=["tri_b"])
    DMA(P, "sp", C.lamv[:], I["lamv"], "c_lamv", writes=["lamv"])
    DMA(P, "sp", C.sublng[:], I["sublng"], "c_sub", writes=["sublng"])
    P.op("dve", lambda e: e.memset(C.eps_t[:], EPS), writes=["eps"])
    sm = C.small
    TT(P, "dve", sm[:, 0:64], C.lamv[:, 0, :], C.lamv[:, 1, :], ALU.mult, ["lamv"], ["sm0"])
    P.op("dve", lambda e: e.tensor_reduce(out=C.lam_t[:, 2:3], in_=sm[:, 0:64], axis=AX.X, op=ALU.add), ["sm0"], ["l2"])
    TT(P, "dve", sm[:, 0:64], C.lamv[:, 2, :], C.lamv[:, 3, :], ALU.mult, ["lamv", "l2"], ["sm0"])
    P.op("dve", lambda e: e.tensor_reduce(out=C.lam_t[:, 3:4], in_=sm[:, 0:64], axis=AX.X, op=ALU.add), ["sm0"], ["l3"])
    ACT(P, C.lam_t[:, 2:4], C.lam_t[:, 2:4], AF.Exp, ["l2", "l3"], ["l23"])
    TT(P, "dve", C.lam_t[:, 0:1], C.lam_t[:, 2:3], C.lam_t[:, 3:4], ALU.subtract, ["l23"], ["lam0"])
    TS(P, "dve", C.lam_t[:, 0:1], C.lam_t[:, 0:1], float(LAM_INIT), None, ALU.add, None, ["lam0"], ["lam"])
    TS(P, "dve", C.lam_t[:, 1:2], C.lam_t[:, 0:1], -1.0, None, ALU.mult, None, ["lam"], ["nlam"])


def wview(w, kcn):
    return w.rearrange("(kc p) n -> p kc n", p=128)


def emit_transposes(C, xs, xT, t, evac_eng):
    P = C.P
    xb = C.xb[C.xbi % len(C.xb)]
    xbn = ("xb", C.xbi % len(C.xb))
    C.xbi += 1
    CP(P, "act", xb[:, :], xs[:, t, :], [("xs", t)], [xbn])
    for g in range(2):
        bank = C.ps[6 + g]
        for k in range(4):
            kc = g * 4 + k
            MM(P, bank[:, k * 128:(k + 1) * 128], xb[:, kc * 128:(kc + 1) * 128], C.ident_b[:, :], True, True,
               [xbn, "ident_b"], [("ps", 6 + g)])
        src = bank[:, :].rearrange("p (a b) -> p a b", a=4)
        CP(P, evac_eng[g], xT[:, g * 4:(g + 1) * 4, t * 128:(t + 1) * 128], src, [("ps", 6 + g)], [("xT", t)])


def emit_layernorm(C, xs, t, ybanks, yscale, lng, lnb, tmp, lnname):
    P = C.P
    tT, r, xn, st6, mv = tmp
    for h in range(2):
        ACT(P, tT[:, h * 512:(h + 1) * 512], C.ps[ybanks[h]][:, :], AF.Identity, [("ps", ybanks[h])], [("tT", h)], scale=float(yscale))
    STT(P, r[:, :], xs[:, t, :], ALPHA, tT[:, :], ALU.mult, ALU.add, [("xs", t), ("tT", 0), ("tT", 1)], ["r"])
    for h in range(2):
        P.op("dve", (lambda hh: (lambda e: e.bn_stats(out=st6[:, hh, :], in_=r[:, hh * 512:(hh + 1) * 512])))(h), ["r"], [("st6", h)])
    P.op("dve", lambda e: e.bn_aggr(out=mv[:, 0:2], in_=st6[:, :, :].rearrange("p a b -> p (a b)")), [("st6", 0), ("st6", 1)], ["mv"])
    ACT(P, mv[:, 2:3], mv[:, 1:2], AF.Sqrt, ["mv", "eps"], ["std"], bias=C.eps_t[:], scale=1.0)
    P.op("dve", lambda e: e.reciprocal(out=mv[:, 3:4], in_=mv[:, 2:3]), ["std"], ["rstd"])
    STT(P, mv[:, 4:5], mv[:, 0:1], -1.0, mv[:, 3:4], ALU.mult, ALU.mult, ["mv", "rstd"], ["nmr"])
    ACT(P, xn[:, :], r[:, :], AF.Identity, ["r", "rstd", "nmr"], ["xn"], scale=mv[:, 3:4], bias=mv[:, 4:5])
    TT(P, "dve", xn[:, :], xn[:, :], lng, ALU.mult, ["xn", lnname], ["xn"])
    TT(P, "dve", xs[:, t, :], xn[:, :], lnb, ALU.add, ["xn", lnname], [("xs", t)])


def emit_ffn(C, xs, xT, actT, wd, wb, sil, w_up, w_down, tag, first):
    P = C.P
    TB = C.TB
    wu = wview(w_up, 8)
    if first:
        wdv = wview(w_down, NFF)
        DMA(P, "pool", wd[:, 0:11, :], wdv[:, 0:11, :], tag + "wd0", writes=[("wd", 0)])
        DMA(P, "pool", wd[:, 11:22, :], wdv[:, 11:22, :], tag + "wd1", writes=[("wd", 1)])
    par = 0
    for gi in range(NFF // 2):
        slot = gi % 2
        buf = wb[slot]
        DMA(P, "pool", buf[:, :, 0:256], wu[:, :, gi * 256:(gi + 1) * 256], "wbA%d" % slot, writes=[("wb", slot, 0)])
        DMA(P, "pool", buf[:, :, 256:512], wu[:, :, DFF + gi * 256:DFF + (gi + 1) * 256], "wbG%d" % slot, writes=[("wb", slot, 1)])
        for jj in range(2):
            j = gi * 2 + jj
            for hb in range(TB // 512):
                bA, bG = 2 * par, 2 * par + 1
                xr = [("xT", 4 * hb + q) for q in range(4)]
                for kc in range(8):
                    MM(P, C.ps[bA][:, :], buf[:, kc, jj * 128:(jj + 1) * 128], xT[:, kc, hb * 512:(hb + 1) * 512],
                       kc == 0, kc == 7, [("wb", slot, 0)] + xr, [("ps", bA)])
                for kc in range(8):
                    MM(P, C.ps[bG][:, :], buf[:, kc, 256 + jj * 128:256 + (jj + 1) * 128], xT[:, kc, hb * 512:(hb + 1) * 512],
                       kc == 0, kc == 7, [("wb", slot, 1)] + xr, [("ps", bG)])
                ACT(P, sil[par][:, :], C.ps[bA][:, :], AF.Silu, [("ps", bA)], [("sil", par)])
                TT(P, "dve", actT[:, j, hb * 512:(hb + 1) * 512], sil[par][:, :], C.ps[bG][:, :], ALU.mult,
                   [("sil", par), ("ps", bG)], [("actT", j, hb)])
                par ^= 1


def emit_down(C, actT, wd, t, yb=(4, 5)):
    P = C.P
    hb = t // 4
    for dh in range(2):
        for kc in range(NFF):
            MM(P, C.ps[yb[dh]][:, :], actT[:, kc, t * 128:(t + 1) * 128], wd[:, kc, dh * 512:(dh + 1) * 512],
               kc == 0, kc == NFF - 1, [("actT", kc, hb), ("wd", kc // 11)], [("ps", yb[dh])])


def phase1(C):
    P, I, A = C.P, C.I, C.arena
    TB = C.TB
    NTB = C.NK // TB
    own_tb0 = C.KOFF // TB
    A.reset()
    xs = A.alloc([128, 8, D], F32)
    xT = A.alloc([128, 8, TB], BF16)
    actT = A.alloc([128, NFF, TB], BF16)
    wd = A.alloc([128, NFF, D], BF16)
    wb = [A.alloc([128, 8, 512], BF16) for _ in range(2)]
    lnp = A.alloc([128, 2, D], F32)
    tT = A.alloc([128, D], F32)
    r = A.alloc([128, D], F32)
    xn = A.alloc([128, D], F32)
    st6 = A.alloc([128, 2, 6], F32)
    mv = A.alloc([128, 8], F32)
    sil = [A.alloc([128, 512], F32) for _ in range(2)]
    stg = [A.alloc([128, TB], BF16) for _ in range(2)]
    vst = [A.alloc([128, 512], BF16) for _ in range(2)]
    tmp = (tT, r, xn, st6, mv)
    C.xb = [A.alloc([128, D], BF16) for _ in range(2)]
    C.xbi = 0
    DMA(P, "sp", lnp, I["lnp"][:, 0:2, :], "lnp1", writes=["ln1"])
    xv = I["x"].rearrange("(t p) d -> p t d", p=128)
    x1v = C.X1_s.rearrange("(t p) d -> p t d", p=128)
    win = wview(I["w_in"], 8)
    for tb in range(NTB):
        own = tb >= own_tb0
        qtb = tb - own_tb0
        for t in range(8):
            DMA(P, "sp", xs[:, t, :], xv[:, tb * 8 + t, :], "xs%d" % t, writes=[("xs", t)])
        for t in range(8):
            emit_transposes(C, xs, xT, t, ("dve", "act"))
        emit_ffn(C, xs, xT, actT, wd, wb, sil, I["w_ffn1_up"], I["w_ffn1_down"], "f1", tb == 0)
        for t in range(8):
            yb = (4, 5) if (t % 2 == 0 or not FLAG_LNPIPE) else (2, 3)
            emit_down(C, actT, wd, t, yb)
            emit_layernorm(C, xs, t, yb, 0.5, lnp[:, 0, :], lnp[:, 1, :], tmp, "ln1")
            if own:
                DMA(P, "sp", x1v[:, qtb * 8 + t, :], xs[:, t, :], "x1st%d" % t, reads=[("xs", t)])
            if t >= 1:
                emit_transposes(C, xs, xT, t - 1, ("dve", "act"))
        emit_transposes(C, xs, xT, 7, ("dve", "act"))
        if own:
            for kc in range(8):
                DMA(P, "sp", C.X1T_s[kc * 128:(kc + 1) * 128, qtb * TB:(qtb + 1) * TB], xT[:, kc, :], "x1T%d" % kc,
                    reads=[("xT", q) for q in range(8)])
        par = 0
        for g in range(6):
            kind = ("q", "k", "v")[g % 3]
            fam = g // 3
            if kind == "q" and not own:
                continue
            slot = g % 2
            buf = wb[slot]
            DMA(P, "pool", buf[:, :, 0:256], win[:, :, g * 512:g * 512 + 256], "wbA%d" % slot, writes=[("wb", slot, 0)])
            DMA(P, "pool", buf[:, :, 256:512], win[:, :, g * 512 + 256:(g + 1) * 512], "wbG%d" % slot, writes=[("wb", slot, 1)])
            if kind in ("q", "k"):
                dst = C.QT_s if kind == "q" else C.KT_s
                col0 = qtb * TB if kind == "q" else tb * TB
                for c in range(4):
                    sg = stg[par]
                    for hb in range(TB // 512):
                        bank = 2 * par + hb
                        for kc in range(8):
                            MM(P, C.ps[bank][:, :], buf[:, kc, c * 128:(c + 1) * 128], xT[:, kc, hb * 512:(hb + 1) * 512],
                               kc == 0, kc == 7, [("wb", slot, c // 2)] + [("xT", 4 * hb + q) for q in range(4)], [("ps", bank)])
                        CP(P, "act" if hb == 0 else "dve", sg[:, hb * 512:(hb + 1) * 512], C.ps[bank][:, :], [("ps", bank)], [("stg", par, hb)])
                    row0 = fam * 512 + c * 128
                    DMA(P, "sp", dst[row0:row0 + 128, col0:col0 + TB], sg[:, :], "stg%d" % par,
                        reads=[("stg", par, 0), ("stg", par, 1)])
                    par ^= 1
            else:
                vv = C.V_s.rearrange("(t p) d -> p t d", p=128)
                for t in range(8):
                    bank = 2 * par
                    for kc in range(8):
                        MM(P, C.ps[bank][:, :], xT[:, kc, t * 128:(t + 1) * 128], buf[:, kc, :],
                           kc == 0, kc == 7, [("wb", slot, 0), ("wb", slot, 1), ("xT", t)], [("ps", bank)])
                    CP(P, "act" if t % 2 == 0 else "dve", vst[par][:, :], C.ps[bank][:, :], [("ps", bank)], [("vst", par)])
                    DMA(P, "sp", vv[:, tb * 8 + t, fam * 512:(fam + 1) * 512], vst[par][:, :], "vst%d" % par, reads=[("vst", par)])
                    par ^= 1


def phase2(C):
    P, I, A = C.P, C.I, C.arena
    A.reset()
    NHM, NHD = C.NHM, C.NHD
    NQ, NK, KOFF = C.NQ, C.NK, C.KOFF
    NQB = NQ // 512
    NKC = NK // 128
    KO = KOFF // 128
    KTs = [[A.alloc([128, NK], BF16) for m in range(2)] for _ in range(2)]
    QTs = [[A.alloc([128, NQ], BF16) for m in range(2)] for _ in range(2)]
    VEs = [A.alloc([128, NKC, 132], BF16) for _ in range(2)]
    PT = [A.alloc([128, 512], BF16) for _ in range(3)]
    accsb = [A.alloc([128, 4, 2, 132], F32) for _ in range(2)]
    o1 = A.alloc([128, 4, 128], F32)
    yd = A.alloc([128, 4, 128], F32)
    ydb = A.alloc([128, 4, 128], BF16)
    sq = A.alloc([128, 128], F32)
    sc = A.alloc([128, 4, 8], F32)
    ystg = [A.alloc([128, 512], BF16) for _ in range(2)]
    kmf = A.alloc([64, 16], F32)
    kmb = [A.alloc([64, 16], BF16) for _ in range(2)]
    vbt = A.alloc([128, NQB, 4, 16], F32)
    lot = A.alloc([128, NQB, 4, 16], F32)
    gm = [A.alloc([128, 4, 16], F32) for _ in range(2)]
    m8 = [A.alloc([128, 4, 8], F32) for _ in range(2)]
    m1 = [A.alloc([128, 4, 16], F32) for _ in range(2)]
    npad = [A.alloc([128, 4, 80], BF16) for _ in range(2)]
    ones_col = A.alloc([128, NKC, 1], F32)
    ones_k = A.alloc([128, 2], BF16)
    one_f = A.alloc([1, 2], F32)
    OTsb = [A.alloc([128, 512], BF16) for _ in range(2)]
    ZTsb = [A.alloc([1, 512], F32) for _ in range(2)]
    Zhi = [A.alloc([1, 512], BF16) for _ in range(2)]
    Zlo = [A.alloc([1, 512], BF16) for _ in range(2)]
    Zd = [A.alloc([1, 512], F32) for _ in range(2)]
    one_b = A.alloc([1, 2], BF16)
    P.op("dve", lambda e: e.memset(one_b[:, :], 1.0), writes=["one_b"])
    one_b64 = A.alloc([128, 2], BF16)
    P.op("dve", lambda e: e.memset(one_b64[:, :], 1.0), writes=["one_b64"])
    Zm_f = A.alloc([128, 512], F32)
    Zm_d = A.alloc([128, 512], F32)
    Zm_lo = A.alloc([128, 512], BF16)

    ones_kk = A.alloc([128, 128], BF16)
    P.op("dve", lambda e: e.memset(ones_kk[:, :], 1.0), writes=["ones_kk"])
    for par in range(2):
        for m in range(2):
            P.op("pool", (lambda par=par, m=m: (lambda e: e.memset(KTs[par][m][64:128, :], 0.0)))(), writes=[("KT", par, m)])
            P.op("pool", (lambda par=par, m=m: (lambda e: e.memset(QTs[par][m][64:128, :], 0.0)))(), writes=[("QT", par, m), ("QTm", par, m)])
    P.op("dve", lambda e: e.memset(ones_k[:, :], 1.0), writes=["ones_k"])
    P.op("dve", lambda e: e.memset(one_f[:, :], 1.0), writes=["one_f"])
    DMA(P, "sp", vbt, I["vb"], "vbt", writes=["vbt"])
    DMA(P, "sp", lot, I["lo"], "lot", writes=["lot"])
    for g in range(2):
        P.op("dve", (lambda g=g: (lambda e: e.memset(npad[g][:, :, :], 0.0)))(), writes=[("npad", g)])
    P.op("dve", lambda e: e.memset(ones_col[:, :, :], 1.0), writes=["ones_col"])
    vv = C.V_s.rearrange("(k p) d -> p k d", p=128)

    jobs = [("d", h) for h in range(NHD)] + [("m", h) for h in range(NHM)]

    def load_job(ji):
        kind, h = jobs[ji]
        par = ji % 2
        KT, QT, VE = KTs[par], QTs[par], VEs[par]
        if kind == "d":
            hg = NHM + h
            for m in range(2):
                r0 = 512 + h * 128 + m * 64
                DMA(P, "sp", KT[m][0:64, :], C.KT_s[r0:r0 + 64, :], "KTl%d%d" % (par, m), writes=[("KT", par, m)])
                DMA(P, "pool", KT[m][64:68, :], I["kc"][hg, :, :], "KTc%d%d" % (par, m), writes=[("KT", par, m)])
                DMA(P, "sp", QT[m][0:64, :], C.QT_s[r0:r0 + 64, :], "QTl%d%d" % (par, m), writes=[("QT", par, m)])
                DMA(P, "pool", QT[m][64:68, :], I["qc"][hg, :, :], "QTc%d%d" % (par, m), writes=[("QT", par, m)])
            DMA(P, "sp", VE[:, :, 0:128], vv[:, :, 512 + h * 128:512 + (h + 1) * 128], "VEl%d" % par, writes=[("VE", par)])
            CP(P, "dve", VE[:, :, 128:129], ones_col[:, :, :], ["ones_col"], [("VE", par)])
        else:
            r0 = h * 64
            DMA(P, "sp", KT[0][0:64, :], C.KT_s[r0:r0 + 64, :], "KTl%d0" % par, writes=[("KT", par, 0)])
            DMA(P, "pool", KT[0][64:80, :], I["kind"], "KTi%d" % par, writes=[("KT", par, 0)])
            DMA(P, "pool", KT[0][80:84, :], I["kc"][h, :, :], "KTc%d0" % par, writes=[("KT", par, 0)])
            DMA(P, "sp", QT[0][0:64, :], C.QT_s[r0:r0 + 64, :], "QTl%d0" % par, writes=[("QT", par, 0)])
            DMA(P, "pool", QT[0][80:84, :], I["qc"][h, :, :], "QTc%d0" % par, writes=[("QT", par, 0)])
            DMA(P, "sp", VE[:, :, 0:64], vv[:, :, h * 64:(h + 1) * 64], "VEl%d" % par, writes=[("VE", par)])
            CP(P, "dve", VE[:, :, 64:65], ones_col[:, :, :], ["ones_col"], [("VE", par)])

    def mask_stage(ji, tg):
        par = ji % 2
        KT, QT = KTs[par][0], QTs[par][0]
        if tg == 0:
            P.op("dve", lambda e: e.tensor_reduce(out=kmf[:, :], in_=KT[0:64, :].rearrange("p (n l) -> p n l", l=256), axis=AX.X, op=ALU.add),
                 [("KT", par, 0)], ["kmf"])
            TS(P, "dve", kmb[par][:, :], kmf[:, :], 1.0 / 256.0, None, ALU.mult, None, ["kmf"], [("kmb", par)])
        gp = tg % 2
        for tq in range(4):
            t = tg * 4 + tq
            MM(P, C.ps[7][:, tq * 16:(tq + 1) * 16], QT[0:64, t * 128:(t + 1) * 128], kmb[par][:, :], True, True,
               [("QT", par, 0), ("kmb", par)], [("ps", 7)])
        TT(P, "dve", gm[gp][:, :, :], C.ps[7][:, 0:64].rearrange("p (a b) -> p a b", a=4), vbt[:, tg, :, :], ALU.add,
           [("ps", 7), "vbt"], [("gm", gp)])
        for tq in range(4):
            P.op("dve", (lambda tq=tq: (lambda e: e.max(out=m8[gp][:, tq, :], in_=gm[gp][:, tq, :])))(), [("gm", gp)], [("m8", gp, tq)])
        for tq in range(4):
            TS(P, "dve", m1[gp][:, tq, :], gm[gp][:, tq, :], m8[gp][:, tq, 3:4], 1.0, ALU.is_ge, ALU.subtract,
               [("gm", gp), ("m8", gp, tq)], [("m1", gp, tq)])
        TT(P, "dve", npad[gp][:, :, 64:80], m1[gp][:, :, :], lot[:, tg, :, :], ALU.min,
           [("m1", gp, q) for q in range(4)] + ["lot"], [("npad", gp)])

    def mask_stage_b(ji, tg):
        par = ji % 2
        QT = QTs[par][0]
        gp = tg % 2
        for tq in range(4):
            MM(P, C.ps[6][0:80, tq * 128:(tq + 1) * 128], npad[gp][:, tq, :], C.ident_b[:, :], True, True,
               [("npad", gp), "ident_b"], [("ps", 6)])
        CP(P, "act", QT[64:80, tg * 512:(tg + 1) * 512], C.ps[6][64:80, :], [("ps", 6)], [("QTm", par, 0)])

    pending = []
    stepc = [0]

    dseq = [0]

    def defer(lag, fn):
        if not FLAG_DEFER:
            fn()
            return
        dseq[0] += 1
        pending.append((stepc[0] + lag, dseq[0], fn))

    def run_due(flush=False):
        while True:
            due = [p for p in pending if flush or p[0] <= stepc[0]]
            if not due:
                break
            due.sort()
            it = due[0]
            pending.remove(it)
            it[2]()

    def attention(ji, nmaps, dv, K, finalize, side):
        par = ji % 2
        KT, QT, VE = KTs[par], QTs[par], VEs[par]
        steps = [(qb, m, kc) for qb in range(NQB) for m in range(nmaps) for kc in range(KO + 4 * qb + 4)]
        n = len(steps)

        def emit_S(i):
            qb, m, kc = steps[i]
            j = kc - KO - 4 * qb
            n0 = max(0, j) * 128
            sb = i % 2
            S = C.ps[sb]
            MM(P, S[:, n0:512], KT[m][:, kc * 128:(kc + 1) * 128], QT[m][:, qb * 512 + n0:(qb + 1) * 512],
               True, j < 0, [("KT", par, m), ("QT", par, m), ("QTm", par, m)], [("ps", sb)])
            if j >= 0:
                MM(P, S[:, n0:n0 + 128], C.ident_b[:, :], C.tri_b[:, :], False, True, ["ident_b", "tri_b"], [("ps", sb)])
            ACT(P, PT[i % 3][:, n0:512], S[:, n0:512], AF.Exp, [("ps", sb)], [("PT", i % 3)], scale=0.125)

        def emit_AV(i):
            qb, m, kc = steps[i]
            j = kc - KO - 4 * qb
            n0 = max(0, j) * 128
            last = KO + 4 * qb + 3
            pt = PT[i % 3]
            if nmaps == 2:
                MM(P, C.ps[2 + m][:, n0:512], VE[:, kc, 0:128], pt[:, n0:512], kc == 0, kc == last,
                   [("PT", i % 3), ("VE", par)], [("OT", m)])
                MM(P, C.ps[4 + m][:, n0:512], ones_kk[:, :], pt[:, n0:512], kc == 0, kc == last,
                   [("PT", i % 3), "ones_kk"], [("ZT", m)])
            else:
                MM(P, C.ps[2][:, n0:512], VE[:, kc, 0:128], pt[:, n0:512], kc == 0, kc == last,
                   [("PT", i % 3), ("VE", par)], [("OT", 0)])
            if m == nmaps - 1 and kc == last:
                q2 = qb % 2
                if nmaps == 2:
                    for mm in range(2):
                        CP(P, "dve", OTsb[mm][:, :], C.ps[2 + mm][:, :], [("OT", mm)], [("OTsb", mm)])
                        CP(P, "dve", ZTsb[mm][0:1, :], C.ps[4 + mm][0:1, :], [("ZT", mm)], [("ZTsb", mm)])
                        CP(P, "dve", Zhi[mm][0:1, :], ZTsb[mm][0:1, :], [("ZTsb", mm)], [("Zhi", mm)])
                        TT(P, "dve", Zd[mm][0:1, :], ZTsb[mm][0:1, :], Zhi[mm][0:1, :], ALU.subtract, [("ZTsb", mm), ("Zhi", mm)], [("Zd", mm)])
                        CP(P, "dve", Zlo[mm][0:1, :], Zd[mm][0:1, :], [("Zd", mm)], [("Zlo", mm)])
                else:
                    CP(P, "dve", OTsb[0][0:65, :], C.ps[2][0:65, :], [("OT", 0)], [("OTsb", 0)])
                    CP(P, "dve", Zm_f[64:65, :], C.ps[2][64:65, :], [("OT", 0)], [("Zm_f", 0)])
                    TT(P, "dve", Zm_d[64:65, :], Zm_f[64:65, :], OTsb[0][64:65, :], ALU.subtract, [("Zm_f", 0), ("OTsb", 0)], [("Zm_d", 0)])
                    CP(P, "dve", Zm_lo[64:65, :], Zm_d[64:65, :], [("Zm_d", 0)], [("Zm_lo", 0)])

                def stage2(qb=qb, q2=q2):
                    if nmaps == 2:
                        for mm in range(2):
                            for sub in range(4):
                                MM(P, C.ps[6][:, sub * 128:(sub + 1) * 128], OTsb[mm][:, sub * 128:(sub + 1) * 128], C.ident_b[:, :], True, True,
                                   [("OTsb", mm), "ident_b"], [("ps", 6)])
                            CP(P, "dve", accsb[q2][:, 2 * mm:2 * mm + 2, :, 0:128],
                               C.ps[6][:, :].rearrange("p (a b c) -> p a b c", a=2, b=2), [("ps", 6)], [("accsb", q2, 2 * mm), ("accsb", q2, 2 * mm + 1)])
                            for sub in range(4):
                                MM(P, C.ps[7][:, 64 + mm * 4 + sub:64 + mm * 4 + sub + 1], Zhi[mm][0:1, sub * 128:(sub + 1) * 128], one_b[0:1, 0:1],
                                   True, False, [("Zhi", mm), "one_b"], [("ps", 7)])
                                MM(P, C.ps[7][:, 64 + mm * 4 + sub:64 + mm * 4 + sub + 1], Zlo[mm][0:1, sub * 128:(sub + 1) * 128], one_b[0:1, 0:1],
                                   False, True, [("Zlo", mm), "one_b"], [("ps", 7)])
                            CP(P, "dve", accsb[q2][:, 2 * mm:2 * mm + 2, :, 128:129],
                               C.ps[7][:, 64 + mm * 4:64 + mm * 4 + 4].rearrange("p (a b c) -> p a b c", a=2, b=2), [("ps", 7)],
                               [("accsb", q2, 2 * mm), ("accsb", q2, 2 * mm + 1)])
                    else:
                        for sub in range(4):
                            MM(P, C.ps[6][:, sub * 128:sub * 128 + 65], OTsb[0][0:65, sub * 128:(sub + 1) * 128], C.ident_b[0:65, 0:65], True, False,
                               [("OTsb", 0), "ident_b"], [("ps", 6)])
                            MM(P, C.ps[6][:, sub * 128 + 64:sub * 128 + 65], Zm_lo[64:65, sub * 128:(sub + 1) * 128], one_b64[64:65, 0:1], False, True,
                               [("Zm_lo", 0), "one_b64"], [("ps", 6)])
                        CP(P, "dve", accsb[q2][:, 0:2, :, 0:65],
                           C.ps[6][:, :].rearrange("p (a b c) -> p a b c", a=2, b=2)[:, :, :, 0:65], [("ps", 6)], [("accsb", q2, 0), ("accsb", q2, 1)])
                    finalize[0][1](qb)

                if FLAG_S2DEFER:
                    defer(2, stage2)
                else:
                    stage2()
                for lag, fn in finalize[1:]:
                    defer(lag, (lambda qb=qb, fn=fn: fn(qb)))
                side(qb)

        emit_S(0)
        for i in range(n):
            if i + 1 < n:
                emit_S(i + 1)
            emit_AV(i)
            stepc[0] += 1
            run_due()

    def fin_diff(h, qb):
        q2 = qb % 2
        for sub in range(4):
            a1 = accsb[q2][:, sub // 2, sub % 2, :]
            a2 = accsb[q2][:, 2 + sub // 2, sub % 2, :]
            r1 = [("accsb", q2, sub // 2)]
            r2 = [("accsb", q2, 2 + sub // 2)]
            s = sc[:, sub, :]
            P.op("dve", (lambda s=s, a1=a1: (lambda e: e.reciprocal(out=s[:, 0:1], in_=a1[:, 128:129])))(), r1, [("sc", sub, 0)])
            P.op("dve", (lambda s=s, a2=a2: (lambda e: e.reciprocal(out=s[:, 1:2], in_=a2[:, 128:129])))(), r2, [("sc", sub, 1)])
            TT(P, "dve", s[:, 2:3], s[:, 1:2], C.lam_t[:, 1:2], ALU.mult, [("sc", sub, 1), "nlam"], [("sc", sub, 2)])
            TS(P, "dve", o1[:, sub, :], a1[:, 0:128], s[:, 0:1], None, ALU.mult, None, r1 + [("sc", sub, 0)], [("o1", sub)])
            STT(P, yd[:, sub, :], a2[:, 0:128], s[:, 2:3], o1[:, sub, :], ALU.mult, ALU.add,
                r2 + [("sc", sub, 2), ("o1", sub)], [("yd", sub)])
            TT(P, "dve", sq[:, :], yd[:, sub, :], yd[:, sub, :], ALU.mult, [("yd", sub)], ["sq"])
            P.op("dve", (lambda s=s: (lambda e: e.tensor_reduce(out=s[:, 3:4], in_=sq[:, :], axis=AX.X, op=ALU.add)))(), ["sq"], [("sc", sub, 3)])

    def fin_diff_act(h, qb):
        ACT(P, sc[:, :, 4:5], sc[:, :, 3:4], AF.Ln, [("sc", sub, 3) for sub in range(4)] + ["eps"], [("sc", sub, 4) for sub in range(4)],
            bias=C.eps_t[:], scale=1.0 / 128.0)
        ACT(P, sc[:, :, 5:6], sc[:, :, 4:5], AF.Exp, [("sc", sub, 4) for sub in range(4)], [("sc", sub, 5) for sub in range(4)], scale=-0.5)

    def fin_diff_a2(h, qb):
        for sub in range(4):
            s = sc[:, sub, :]
            TS(P, "dve", yd[:, sub, :], yd[:, sub, :], s[:, 5:6], float(1.0 - LAM_INIT), ALU.mult, ALU.mult, [("yd", sub), ("sc", sub, 5)], [("yd", sub)])
            TT(P, "dve", ydb[:, sub, :], yd[:, sub, :], C.sublng[:, :], ALU.mult, [("yd", sub), "sublng"], [("ydb", sub)])

    def fin_diff_b(h, qb):
        for sub in range(4):
            MM(P, C.ps[6][:, sub * 128:(sub + 1) * 128], ydb[:, sub, :], C.ident_b[:, :], True, True,
               [("ydb", sub), "ident_b"], [("ps", 6)])
        sl = qb % 2
        CP(P, "dve", ystg[sl][:, :], C.ps[6][:, :], [("ps", 6)], [("ystg", sl)])
        DMA(P, "sp", C.YT_s[512 + h * 128:512 + (h + 1) * 128, qb * 512:(qb + 1) * 512], ystg[sl][:, :], "ystg%d" % sl, reads=[("ystg", sl)])

    def fin_moba(h, qb):
        q2 = qb % 2
        sl = qb % 2
        for sub in range(4):
            a1 = accsb[q2][:, sub // 2, sub % 2, :]
            r1 = [("accsb", q2, sub // 2)]
            s = sc[:, sub, :]
            P.op("dve", (lambda s=s, a1=a1: (lambda e: e.reciprocal(out=s[:, 0:1], in_=a1[:, 64:65])))(), r1, [("sc", sub, 0)])
            TS(P, "dve", ydb[:, sub, 0:64], a1[:, 0:64], s[:, 0:1], None, ALU.mult, None, r1 + [("sc", sub, 0)], [("ydb", sub)])

    def fin_moba_b(h, qb):
        sl = qb % 2
        for sub in range(4):
            MM(P, C.ps[6][0:64, sub * 128:(sub + 1) * 128], ydb[:, sub, 0:64], C.ident_b[:, :], True, True,
               [("ydb", sub), "ident_b"], [("ps", 6)])
        CP(P, "dve", ystg[sl][0:64, :], C.ps[6][0:64, :], [("ps", 6)], [("ystg", sl)])
        DMA(P, "sp", C.YT_s[h * 64:(h + 1) * 64, qb * 512:(qb + 1) * 512], ystg[sl][0:64, :], "ystg%d" % sl, reads=[("ystg", sl)])

    load_job(0)
    if jobs[0][0] == "m":
        for tg in range(NQB):
            mask_stage(0, tg)
            mask_stage_b(0, tg)
    for ji, (kind, h) in enumerate(jobs):
        nxt = ji + 1 if ji + 1 < len(jobs) else None
        if nxt is not None:
            load_job(nxt)

        def side(qb, nxt=nxt):
            if nxt is not None and jobs[nxt][0] == "m":
                defer(4, (lambda: mask_stage(nxt, qb)))
                defer(11, (lambda: mask_stage_b(nxt, qb)))

        if kind == "d":
            attention(ji, 2, 128, 68, [(0, (lambda qb, h=h: fin_diff(h, qb))), (9, (lambda qb, h=h: fin_diff_act(h, qb))),
                                       (11, (lambda qb, h=h: fin_diff_a2(h, qb))), (18, (lambda qb, h=h: fin_diff_b(h, qb)))], side)
        else:
            attention(ji, 1, 64, 84, [(0, (lambda qb, h=h: fin_moba(h, qb))), (6, (lambda qb, h=h: fin_moba_b(h, qb)))], side)
    run_due(flush=True)


def phase3(C):
    P, I, A = C.P, C.I, C.arena
    TB = C.TB
    NTB = C.NQ // TB
    A.reset()
    xs = A.alloc([128, 8, D], F32)
    xT = A.alloc([128, 8, TB], BF16)
    wb = [A.alloc([128, 8, 512], BF16) for _ in range(2)]
    lnp = A.alloc([128, 4, D], F32)
    tT = A.alloc([128, D], F32)
    r = A.alloc([128, D], F32)
    xn = A.alloc([128, D], F32)
    st6 = A.alloc([128, 2, 6], F32)
    mv = A.alloc([128, 8], F32)
    sil = [A.alloc([128, 512], F32) for _ in range(2)]
    sg = [A.alloc([128, 512], BF16) for _ in range(4)]
    pt_f = [A.alloc([128, 256], F32) for _ in range(1)]
    C.xb = [A.alloc([128, D], BF16) for _ in range(1)]
    C.xbi = 0
    tmp = (tT, r, xn, st6, mv)
    base = A.off
    mergedT = A.alloc([128, 8, TB], BF16)
    yT = A.alloc([128, 8, TB], BF16)
    wbr = A.alloc([128, 2, 4, D], BF16)
    wout = A.alloc([128, 8, D], BF16)
    A.reset(base)
    actT = A.alloc([128, NFF, TB], BF16)
    wd = A.alloc([128, NFF, D], BF16)
    A.reset(base)
    wpg = A.alloc([128, 8, D], BF16)
    wpl = A.alloc([128, 2, D], BF16)
    pT = A.alloc([128, 2, TB], BF16)

    DMA(P, "sp", lnp, I["lnp"][:, 2:6, :], "lnp3", writes=["ln2", "ln3"])
    x1v = C.X1_s.rearrange("(t p) d -> p t d", p=128)
    outv = C.out.rearrange("(t p) d -> p t d", p=128)
    pv = I["p"].rearrange("(t p) d -> p t d", p=128)
    win = wview(I["w_in"], 8)
    for tb in range(NTB):
        tok0 = tb * TB
        for t in range(8):
            DMA(P, "sp", xs[:, t, :], x1v[:, tb * 8 + t, :], "xs%d" % t, writes=[("xs", t)])
        for kc in range(8):
            DMA(P, "sp", xT[:, kc, :], C.X1T_s[kc * 128:(kc + 1) * 128, tb * TB:(tb + 1) * TB], "xTl%d" % kc, writes=[("xT", q) for q in range(8)])
            DMA(P, "sp", yT[:, kc, :], C.YT_s[kc * 128:(kc + 1) * 128, tok0:tok0 + TB], "yTl%d" % kc, writes=[("yT", kc)])
        DMA(P, "pool", wbr[:, 0, :, :], wview(I["w_branch_a"], 4), "wbra", writes=["wbra"])
        DMA(P, "pool", wbr[:, 1, :, :], wview(I["w_branch_b"], 4), "wbrb", writes=["wbrb"])
        DMA(P, "pool", wout, wview(I["w_out"], 8), "wout", writes=["wout"])
        for oc in range(8):
            slot = oc % 2
            buf = wb[slot]
            DMA(P, "pool", buf[:, :, 0:128], win[:, :, 3072 + oc * 128:3072 + (oc + 1) * 128], "wbA%d" % slot, writes=[("wb", slot, 0)])
            DMA(P, "pool", buf[:, :, 128:256], win[:, :, 4096 + oc * 128:4096 + (oc + 1) * 128], "wbG%d" % slot, writes=[("wb", slot, 1)])
            for hb in range(TB // 512):
                xr = [("xT", 4 * hb + q) for q in range(4)]
                ip = (oc * 2 + hb) % 2 if FLAG_MERGEDB else 0
                for br in range(2):
                    bank = 4 * ip + br
                    for kc in range(8):
                        MM(P, C.ps[bank][:, :], buf[:, kc, br * 128:(br + 1) * 128], xT[:, kc, hb * 512:(hb + 1) * 512],
                           kc == 0, kc == 7, [("wb", slot, br)] + xr, [("ps", bank)])
                    ACT(P, sg[2 * ip + br][:, :], C.ps[bank][:, :], AF.Sigmoid, [("ps", bank)], [("sg", 2 * ip + br)])
                    bank2 = 4 * ip + 2 + br
                    for kc in range(4):
                        MM(P, C.ps[bank2][:, :], wbr[:, br, kc, oc * 128:(oc + 1) * 128], yT[:, br * 4 + kc, hb * 512:(hb + 1) * 512],
                           kc == 0, kc == 3, ["wbra" if br == 0 else "wbrb", ("yT", br * 4 + kc)], [("ps", bank2)])
                    TT(P, "dve", sil[br][:, :], sg[2 * ip + br][:, :], C.ps[bank2][:, :], ALU.mult,
                       [("sg", 2 * ip + br), ("ps", bank2)], [("sil", br)])
                TT(P, "dve", mergedT[:, oc, hb * 512:(hb + 1) * 512], sil[0][:, :], sil[1][:, :], ALU.add,
                   [("sil", 0), ("sil", 1)], [("mergedT", oc, hb)])
        for t in range(8):
            hb = t // 4
            yb = (4, 5) if (t % 2 == 0 or not FLAG_LNPIPE) else (2, 3)
            for dh in range(2):
                for kc in range(8):
                    MM(P, C.ps[yb[dh]][:, :], mergedT[:, kc, t * 128:(t + 1) * 128], wout[:, kc, dh * 512:(dh + 1) * 512],
                       kc == 0, kc == 7, [("mergedT", kc, hb), "wout"], [("ps", yb[dh])])
            emit_layernorm(C, xs, t, yb, 1.0, lnp[:, 0, :], lnp[:, 1, :], tmp, "ln2")
            if t >= 1:
                emit_transposes(C, xs, xT, t - 1, ("dve", "act"))
        emit_transposes(C, xs, xT, 7, ("dve", "act"))
        P.barrier()
        emit_ffn(C, xs, xT, actT, wd, wb, sil, I["w_ffn2_up"], I["w_ffn2_down"], "f2", True)
        for t in range(8):
            yb = (4, 5) if (t % 2 == 0 or not FLAG_LNPIPE) else (2, 3)
            emit_down(C, actT, wd, t, yb)
            emit_layernorm(C, xs, t, yb, 0.5, lnp[:, 2, :], lnp[:, 3, :], tmp, "ln3")
            if t >= 1:
                emit_transposes(C, xs, xT, t - 1, ("dve", "act"))
        emit_transposes(C, xs, xT, 7, ("dve", "act"))
        P.barrier()
        DMA(P, "pool", wpg, wview(I["w_ple_gate"], 8), "wpg", writes=["wpg"])
        DMA(P, "pool", wpl, wview(I["w_ple"], 2), "wpl", writes=["wpl"])
        for t in range(8):
            pf = pt_f[0]
            DMA(P, "sp", pf, pv[:, tb * 8 + t, :], "pf0", writes=[("pf", 0)])
            pb = C.xb[0][:, 0:256]
            CP(P, "pool", pb, pf[:, :], [("pf", 0)], [("xb", 0)])
            for k in range(2):
                MM(P, C.ps[6][:, k * 128:(k + 1) * 128], pb[:, k * 128:(k + 1) * 128], C.ident_b[:, :], True, True, [("xb", 0), "ident_b"], [("ps", 6)])
            CP(P, "dve", pT[:, :, t * 128:(t + 1) * 128], C.ps[6][:, 0:256].rearrange("p (a b) -> p a b", a=2), [("ps", 6)], [("pT", t)])
            for dh in range(2):
                for kc in range(8):
                    MM(P, C.ps[dh][:, :], xT[:, kc, t * 128:(t + 1) * 128], wpg[:, kc, dh * 512:(dh + 1) * 512],
                       kc == 0, kc == 7, [("xT", t), "wpg"], [("ps", dh)])
                ACT(P, tT[:, dh * 512:(dh + 1) * 512], C.ps[dh][:, :], AF.Sigmoid, [("ps", dh)], [("tT", dh)])
                for kc in range(2):
                    MM(P, C.ps[2 + dh][:, :], pT[:, kc, t * 128:(t + 1) * 128], wpl[:, kc, dh * 512:(dh + 1) * 512],
                       kc == 0, kc == 1, [("pT", t), "wpl"], [("ps", 2 + dh)])
                TT(P, "dve", sil[dh][:, :], tT[:, dh * 512:(dh + 1) * 512], C.ps[2 + dh][:, :], ALU.mult, [("tT", dh), ("ps", 2 + dh)], [("sil", dh)])
                TT(P, "dve", xn[:, dh * 512:(dh + 1) * 512], sil[dh][:, :], xs[:, t, dh * 512:(dh + 1) * 512], ALU.add,
                   [("sil", dh), ("xs", t)], [("xo", dh)])
            DMA(P, "sp", outv[:, tb * 8 + t, :], xn[:, :], "outst", reads=[("xo", 0), ("xo", 1)])
        P.barrier()


def host_consts(mode, r):
    c = {}
    c["ident"] = np.eye(128, dtype=np.float32)
    ki = np.arange(128)[:, None]
    qi = np.arange(128)[None, :]
    c["tri"] = np.where(ki > qi, -BIGM, 0.0).astype(np.float32)
    slopes = np.concatenate([2.0 ** (-8.0 * np.arange(1, 9) / 8), 2.0 ** (-8.0 * np.arange(1, 5) / 4)]).astype(np.float64)
    if mode == "P4":
        NQ, NK = 4096, 4096
        qpos = np.arange(NQ)
        kpos = np.arange(NK)
        kvalid = np.ones(NK, bool)
    else:
        NQ, NK = 2048, 4096
        qpos = r * 2048 + np.arange(NQ)
        kpos = np.concatenate([np.arange(2048), r * 2048 + np.arange(2048)])
        kvalid = np.concatenate([np.full(2048, r == 1), np.ones(2048, bool)])
    qc = np.zeros((12, 4, NQ), np.float32)
    kc = np.zeros((12, 4, NK), np.float32)
    for h in range(12):
        s8 = 8.0 * slopes[h]
        qc[h, 0] = -s8 * (qpos % 128)
        qc[h, 1] = -s8 * 128.0 * (qpos // 128)
        qc[h, 2] = 1.0
        qc[h, 3] = 1.0
        kc[h, 0] = 1.0
        kc[h, 1] = 1.0
        kc[h, 2] = s8 * (kpos % 128)
        kc[h, 3] = np.where(kvalid, s8 * 128.0 * (kpos // 128), -BIGM)
    c["qc"], c["kc"] = qc, kc
    kind = np.zeros((16, NK), np.float32)
    for n in range(16):
        kind[n, n * 256:(n + 1) * 256] = BIGM
    c["kind"] = kind
    nt = NQ // 128
    vb = np.zeros((nt, 16), np.float32)
    lo = np.zeros((nt, 16), np.float32)
    for t in range(nt):
        b_loc = t // 2
        for n in range(16):
            if mode == "P4":
                st = "past" if n < b_loc else ("own" if n == b_loc else "no")
            else:
                if n < 8:
                    st = "past" if r == 1 else "no"
                else:
                    nn = n - 8
                    st = "past" if nn < b_loc else ("own" if nn == b_loc else "no")
            vb[t, n] = {"past": 0.0, "own": 1e30, "no": -1e30}[st]
            lo[t, n] = 0.0 if st in ("past", "own") else -1.0
    c["vb"] = np.ascontiguousarray(np.broadcast_to(vb.reshape(1, nt // 4, 4, 16), (128, nt // 4, 4, 16)))
    c["lo"] = np.ascontiguousarray(np.broadcast_to(lo.reshape(1, nt // 4, 4, 16), (128, nt // 4, 4, 16)))
    return c


_CACHE = {}
MODE = "P8R"


def kernel(**inputs):
    f = lambda k: np.ascontiguousarray(np.asarray(inputs[k], dtype=np.float32)[0])
    mode = MODE
    if mode not in _CACHE:
        _CACHE[mode] = build(mode)
    nc = _CACHE[mode]
    x = np.asarray(inputs["x"], dtype=np.float32)
    p = np.asarray(inputs["p"], dtype=np.float32)[0]
    shared = {k: f(k) for k in ("w_ffn1_up", "w_ffn1_down", "w_ffn2_up", "w_ffn2_down", "w_in", "w_branch_a",
                                "w_branch_b", "w_out", "w_ple", "w_ple_gate")}
    lnp = np.stack([f(k) for k in ("ln1_g", "ln1_b", "ln2_g", "ln2_b", "ln3_g", "ln3_b")], 0)
    shared["lnp"] = np.ascontiguousarray(np.broadcast_to(lnp[None], (128, 6, D)))
    shared["sublng"] = np.ascontiguousarray(np.broadcast_to(f("subln_g")[None], (128, 128)))
    lamv = np.stack([f(k) for k in ("lambda_q1", "lambda_k1", "lambda_q2", "lambda_k2")], 0)
    shared["lamv"] = np.ascontiguousarray(np.broadcast_to(lamv[None], (128, 4, 64)))
    consts = {}
    in_maps = []
    for c in range(N_CORES):
        m = dict(shared)
        if mode == "P4":
            b, r = c % 4, 0
            m["x"] = np.ascontiguousarray(x[b])
            m["p"] = np.ascontiguousarray(p[b])
        else:
            b, r = c // 2, c % 2
            m["x"] = np.ascontiguousarray(np.concatenate([x[b, 0:2048], x[b, r * 2048:(r + 1) * 2048]], 0))
            m["p"] = np.ascontiguousarray(p[b, r * 2048:(r + 1) * 2048])
        if r not in consts:
            consts[r] = host_consts(mode, r)
        m.update(consts[r])
        in_maps.append(m)
    res = run_bass_kernel_spmd(nc, in_maps, core_ids=list(range(N_CORES)))
    if mode == "P4":
        out = np.stack([np.asarray(res.results[b]["out"], dtype=np.float32) for b in range(4)], 0)
    else:
        out = np.stack([np.concatenate([np.asarray(res.results[2 * b + r]["out"], dtype=np.float32) for r in range(2)], 0)
                        for b in range(4)], 0)
    return out
```

```python
import contextlib
import numpy as np
import concourse.bass as bass
import concourse.mybir as mybir
from concourse.bass_utils import run_bass_kernel_spmd

F32 = mybir.dt.float32
BF16 = mybir.dt.bfloat16
AF = mybir.ActivationFunctionType
ALU = mybir.AluOpType
AX = mybir.AxisListType

D = 1024
DFF = 2816
SEQ = 4096
NFF = 22
ALPHA = float(2.0 ** 0.25)
BIGM = 262144.0
EPS = 1e-5
LAM_INIT = 0.8 - 0.6 * 1.0
N_CORES = 8
import os
FLAG_DEFER = os.environ.get("K_DEFER", "1") == "1"
FLAG_S2DEFER = os.environ.get("K_S2DEFER", "0") == "1"
FLAG_LNPIPE = os.environ.get("K_LNPIPE", "1") == "1"
FLAG_MERGEDB = os.environ.get("K_MERGEDB", "1") == "1"


class Prog:
    ENGS = ("pe", "act", "dve", "pool", "sp")

    def __init__(self, nc):
        self.nc = nc
        self.ops = []
        self.last_w = {}
        self.readers = {}
        self.dma_keys = {}
        self.last_on_eng = {}
        self.dma_since_barrier = []
        self.barrier_deps = {}

    def op(self, eng, fn, reads=(), writes=(), dma=0, key=None):
        idx = len(self.ops)
        deps = set()
        for r in reads:
            w = self.last_w.get(r)
            if w is not None:
                deps.add(w)
        for r in writes:
            w = self.last_w.get(r)
            if w is not None:
                deps.add(w)
            for rd in self.readers.get(r, {}).values():
                deps.add(rd)
        for r in reads:
            d = self.readers.setdefault(r, {})
            d[eng if not dma else ("dma", idx)] = idx
        for r in writes:
            self.last_w[r] = idx
            self.readers[r] = {}
        bd = self.barrier_deps.pop(eng, None)
        if bd:
            deps.update(bd)
        deps.discard(idx)
        o = dict(eng=eng, fn=fn, deps=deps, dma=dma, sig=False)
        if dma:
            assert key is not None
            if key not in self.dma_keys:
                self.dma_keys[key] = len(self.dma_keys)
            o["dkey"] = self.dma_keys[key]
            self.dma_since_barrier.append(idx)
        else:
            self.last_on_eng[eng] = idx
        self.ops.append(o)
        return idx

    def barrier(self):
        deps = set(self.dma_since_barrier)
        deps.update(self.last_on_eng.values())
        self.dma_since_barrier = []
        self.barrier_deps = {e: set(deps) for e in self.ENGS}

    def emit(self):
        nc = self.nc
        ops = self.ops
        for o in ops:
            for d in o["deps"]:
                p = ops[d]
                if p["eng"] == "pe" and o["eng"] == "pe" and not p["dma"] and not o["dma"]:
                    continue
                p["sig"] = True
        es = contextlib.ExitStack()
        eng_sem = {e: es.enter_context(nc.semaphore("s_" + e)) for e in self.ENGS}
        cnt = {e: 0 for e in self.ENGS}
        nk = len(self.dma_keys)
        assert nk <= 120, nk
        dma_sems = [es.enter_context(nc.semaphore("d%d" % i)) for i in range(nk)]
        dcnt = [0] * nk
        for o in ops:
            if o["dma"]:
                k = o["dkey"]
                dcnt[k] += 16 * o["dma"]
                o["sigval"] = dcnt[k]
            elif o["sig"]:
                cnt[o["eng"]] += 1
                o["sigval"] = cnt[o["eng"]]
        per_eng = {e: [] for e in self.ENGS}
        for i, o in enumerate(ops):
            per_eng[o["eng"]].append(i)
        with es, nc.Block() as block:
            def run(engname, engobj):
                waited = {}
                for i in per_eng[engname]:
                    o = ops[i]
                    for d in sorted(o["deps"]):
                        p = ops[d]
                        if p["dma"]:
                            sem = dma_sems[p["dkey"]]
                            key = ("d", p["dkey"])
                        else:
                            if p["eng"] == "pe" and engname == "pe" and not o["dma"]:
                                continue
                            sem = eng_sem[p["eng"]]
                            key = ("e", p["eng"])
                        v = p["sigval"]
                        if waited.get(key, 0) >= v:
                            continue
                        waited[key] = v
                        engobj.wait_ge(sem, v)
                    if o["dma"]:
                        o["fn"](engobj, dma_sems[o["dkey"]])
                    else:
                        inst = o["fn"](engobj)
                        if o["sig"]:
                            inst.then_inc(eng_sem[engname], 1)
                for i in per_eng[engname]:
                    o = ops[i]
                    if o["dma"]:
                        key = ("d", o["dkey"])
                        if waited.get(key, 0) < o["sigval"]:
                            waited[key] = o["sigval"]
                            engobj.wait_ge(dma_sems[o["dkey"]], o["sigval"])

            @block.tensor
            def _(e):
                run("pe", e)

            @block.scalar
            def _(e):
                run("act", e)

            @block.vector
            def _(e):
                run("dve", e)

            @block.gpsimd
            def _(e):
                run("pool", e)

            @block.sync
            def _(e):
                run("sp", e)


class Arena:
    def __init__(self, nc, st, nbytes):
        self.t = st.enter_context(nc.sbuf_tensor("arena", [128, nbytes // 4], F32))
        self.cap = nbytes
        self.off = 0

    def reset(self, off=0):
        self.off = off

    def alloc(self, shape, dtype):
        esz = 4 if dtype == F32 else 2
        n = int(np.prod(shape[1:]))
        nb = (n * esz + 63) // 64 * 64
        assert self.off + nb <= self.cap, (self.off, nb, self.cap)
        w0 = self.off // 4
        ap = self.t[0:shape[0], w0:w0 + nb // 4]
        if dtype != F32:
            ap = ap.bitcast(dtype)
        ap = ap[:, 0:n]
        if len(shape) == 3:
            ap = ap.rearrange("p (a b) -> p a b", a=shape[1])
        elif len(shape) == 4:
            ap = ap.rearrange("p (a b c) -> p a b c", a=shape[1], b=shape[2])
        self.off += nb
        return ap


def MM(P, out, lhsT, rhs, start, stop, reads, writes):
    P.op("pe", lambda e: e.matmul(out, lhsT=lhsT, rhs=rhs, start=start, stop=stop), reads, writes)


def TR(P, out, in_, ident, reads, writes):
    P.op("pe", lambda e: e.transpose(out, in_, ident), reads, writes)


def ACT(P, out, in_, func, reads, writes, scale=None, bias=None, accum_out=None):
    kw = {}
    if scale is not None:
        kw["scale"] = scale
    if bias is not None:
        kw["bias"] = bias
    if accum_out is not None:
        kw["accum_out"] = accum_out
    P.op("act", lambda e: e.activation(out=out, in_=in_, func=func, **kw), reads, writes)


def TT(P, eng, out, in0, in1, op, reads, writes):
    P.op(eng, lambda e: e.tensor_tensor(out=out, in0=in0, in1=in1, op=op), reads, writes)


def TS(P, eng, out, in0, s1, s2, op0, op1, reads, writes):
    if op1 is None:
        P.op(eng, lambda e: e.tensor_scalar(out=out, in0=in0, scalar1=s1, scalar2=None, op0=op0), reads, writes)
    else:
        P.op(eng, lambda e: e.tensor_scalar(out=out, in0=in0, scalar1=s1, scalar2=s2, op0=op0, op1=op1), reads, writes)


def STT(P, out, in0, scalar, in1, op0, op1, reads, writes):
    P.op("dve", lambda e: e.scalar_tensor_tensor(out=out, in0=in0, scalar=scalar, in1=in1, op0=op0, op1=op1), reads, writes)


def CP(P, eng, out, in_, reads, writes):
    if eng == "act":
        P.op("act", lambda e: e.activation(out=out, in_=in_, func=AF.Identity), reads, writes)
    else:
        P.op(eng, lambda e: e.tensor_copy(out=out, in_=in_), reads, writes)


def DMA(P, q, out, in_, key, reads=(), writes=()):
    P.op(q, lambda e, s: e.dma_start(out=out, in_=in_).then_inc(s, 16), reads, writes, dma=1, key=key)


class Ctx:
    pass


def build(MODE, debug=False):
    nc = bass.Bass("TRN2", target_bir_lowering=False)
    C = Ctx()
    C.nc = nc
    C.NHM, C.NHD = 8, 4
    if MODE == "P4":
        C.NQ, C.NK, C.KOFF = 4096, 4096, 0
    else:
        C.NQ, C.NK, C.KOFF = 2048, 4096, 2048
    NQ, NK = C.NQ, C.NK
    TB = 1024
    C.TB = TB

    def din(name, shape):
        return nc.dram_tensor(name, list(shape), F32, kind="ExternalInput").ap()

    I = {}
    I["x"] = din("x", [NK, D])
    I["p"] = din("p", [NQ, 256])
    I["w_ffn1_up"] = din("w_ffn1_up", [D, 2 * DFF])
    I["w_ffn1_down"] = din("w_ffn1_down", [DFF, D])
    I["w_ffn2_up"] = din("w_ffn2_up", [D, 2 * DFF])
    I["w_ffn2_down"] = din("w_ffn2_down", [DFF, D])
    I["w_in"] = din("w_in", [D, 5120])
    I["w_branch_a"] = din("w_branch_a", [512, D])
    I["w_branch_b"] = din("w_branch_b", [512, D])
    I["w_out"] = din("w_out", [D, D])
    I["w_ple"] = din("w_ple", [256, D])
    I["w_ple_gate"] = din("w_ple_gate", [D, D])
    I["lnp"] = din("lnp", [128, 6, D])
    I["sublng"] = din("sublng", [128, 128])
    I["lamv"] = din("lamv", [128, 4, 64])
    I["ident"] = din("ident", [128, 128])
    I["tri"] = din("tri", [128, 128])
    I["qc"] = din("qc", [12, 4, NQ])
    I["kc"] = din("kc", [12, 4, NK])
    I["kind"] = din("kind", [16, NK])
    I["vb"] = din("vb", [128, NQ // 512, 4, 16])
    I["lo"] = din("lo", [128, NQ // 512, 4, 16])
    C.I = I
    kind_s = "ExternalOutput" if debug else "Internal"
    C.out = nc.dram_tensor("out", [NQ, D], F32, kind="ExternalOutput").ap()
    C.X1_s = nc.dram_tensor("X1_s", [NQ, D], F32, kind=kind_s).ap()
    C.X1T_s = nc.dram_tensor("X1T_s", [D, NQ], BF16, kind=kind_s).ap()
    C.QT_s = nc.dram_tensor("QT_s", [D, NQ], BF16, kind=kind_s).ap()
    C.KT_s = nc.dram_tensor("KT_s", [D, NK], BF16, kind=kind_s).ap()
    C.V_s = nc.dram_tensor("V_s", [NK, D], BF16, kind=kind_s).ap()
    C.YT_s = nc.dram_tensor("YT_s", [D, NQ], BF16, kind=kind_s).ap()

    with contextlib.ExitStack() as st:
        C.P = P = Prog(nc)
        C.arena = Arena(nc, st, 192 * 1024)
        C.ps = [st.enter_context(nc.psum_tensor("ps%d" % i, [128, 512], F32)) for i in range(8)]
        sb = lambda n, s, d: st.enter_context(nc.sbuf_tensor(n, s, d))
        C.ident_f = sb("ident_f", [128, 128], F32)
        C.ident_b = sb("ident_b", [128, 128], BF16)
        C.tri_b = sb("tri_b", [128, 128], BF16)
        C.eps_t = sb("eps_t", [128, 1], F32)
        C.lam_t = sb("lam_t", [128, 4], F32)
        C.lamv = sb("lamv_t", [128, 4, 64], F32)
        C.sublng = sb("sublng_t", [128, 128], F32)
        C.small = sb("small_t", [128, 64], F32)

        phase0(C)
        phase1(C)
        P.barrier()
        phase2(C)
        P.barrier()
        phase3(C)
        P.emit()
    return nc


def phase0(C):
    P, I = C.P, C.I
    DMA(P, "sp", C.ident_f[:], I["ident"], "c_if", writes=["ident_f"])
    DMA(P, "pool", C.ident_b[:], I["ident"], "c_ib", writes=["ident_b"])
    DMA(P, "pool", C.tri_b[:], I["tri"], "c_tri", writes=["tri_b"])
    DMA(P, "sp", C.lamv[:], I["lamv"], "c_lamv", writes=["lamv"])
    DMA(P, "sp", C.sublng[:], I["sublng"], "c_sub", writes=["sublng"])
    P.op("dve", lambda e: e.memset(C.eps_t[:], EPS), writes=["eps"])
    sm = C.small
    TT(P, "dve", sm[:, 0:64], C.lamv[:, 0, :], C.lamv[:, 1, :], ALU.mult, ["lamv"], ["sm0"])
    P.op("dve", lambda e: e.tensor_reduce(out=C.lam_t[:, 2:3], in_=sm[:, 0:64], axis=AX.X, op=ALU.add), ["sm0"], ["l2"])
    TT(P, "dve", sm[:, 0:64], C.lamv[:, 2, :], C.lamv[:, 3, :], ALU.mult, ["lamv", "l2"], ["sm0"])
    P.op("dve", lambda e: e.tensor_reduce(out=C.lam_t[:, 3:4], in_=sm[:, 0:64], axis=AX.X, op=ALU.add), ["sm0"], ["l3"])
    ACT(P, C.lam_t[:, 2:4], C.lam_t[:, 2:4], AF.Exp, ["l2", "l3"], ["l23"])
    TT(P, "dve", C.lam_t[:, 0:1], C.lam_t[:, 2:3], C.lam_t[:, 3:4], ALU.subtract, ["l23"], ["lam0"])
    TS(P, "dve", C.lam_t[:, 0:1], C.lam_t[:, 0:1], float(LAM_INIT), None, ALU.add, None, ["lam0"], ["lam"])
    TS(P, "dve", C.lam_t[:, 1:2], C.lam_t[:, 0:1], -1.0, None, ALU.mult, None, ["lam"], ["nlam"])


def wview(w, kcn):
    return w.rearrange("(kc p) n -> p kc n", p=128)


def emit_transposes(C, xs, xT, t, evac_eng, cast=True):
    P = C.P
    xi = t % len(C.xb)
    xb = C.xb[xi]
    xbn = ("xb", xi)
    if cast:
        CP(P, "act", xb[:, :], xs[:, t, :], [("xs", t)], [xbn])
    for g in range(2):
        bank = C.ps[6 + g]
        for k in range(4):
            kc = g * 4 + k
            MM(P, bank[:, k * 128:(k + 1) * 128], xb[:, kc * 128:(kc + 1) * 128], C.ident_b[:, :], True, True,
               [xbn, "ident_b"], [("ps", 6 + g)])
        src = bank[:, :].rearrange("p (a b) -> p a b", a=4)
        CP(P, evac_eng[g], xT[:, g * 4:(g + 1) * 4, t * 128:(t + 1) * 128], src, [("ps", 6 + g)], [("xT", t)])


def emit_layernorm(C, xs, t, ybanks, yscale, lng, lnb, tmp, lnname):
    P = C.P
    tT, r, xn, st6, mv = tmp
    for h in range(2):
        ACT(P, tT[:, h * 512:(h + 1) * 512], C.ps[ybanks[h]][:, :], AF.Identity, [("ps", ybanks[h])], [("tT", h)], scale=float(yscale))
    STT(P, r[:, :], xs[:, t, :], ALPHA, tT[:, :], ALU.mult, ALU.add, [("xs", t), ("tT", 0), ("tT", 1)], ["r"])
    for h in range(2):
        P.op("dve", (lambda hh: (lambda e: e.bn_stats(out=st6[:, hh, :], in_=r[:, hh * 512:(hh + 1) * 512])))(h), ["r"], [("st6", h)])
    P.op("dve", lambda e: e.bn_aggr(out=mv[:, 0:2], in_=st6[:, :, :].rearrange("p a b -> p (a b)")), [("st6", 0), ("st6", 1)], ["mv"])
    ACT(P, mv[:, 2:3], mv[:, 1:2], AF.Sqrt, ["mv", "eps"], ["std"], bias=C.eps_t[:], scale=1.0)
    P.op("dve", lambda e: e.reciprocal(out=mv[:, 3:4], in_=mv[:, 2:3]), ["std"], ["rstd"])
    STT(P, mv[:, 4:5], mv[:, 0:1], -1.0, mv[:, 3:4], ALU.mult, ALU.mult, ["mv", "rstd"], ["nmr"])
    ACT(P, xn[:, :], r[:, :], AF.Identity, ["r", "rstd", "nmr"], ["xn"], scale=mv[:, 3:4], bias=mv[:, 4:5])
    TT(P, "dve", xn[:, :], xn[:, :], lng, ALU.mult, ["xn", lnname], ["xn"])
    TT(P, "dve", xs[:, t, :], xn[:, :], lnb, ALU.add, ["xn", lnname], [("xs", t)])
    xi = t % len(C.xb)
    CP(P, "act", C.xb[xi][:, :], xs[:, t, :], [("xs", t)], [("xb", xi)])


def emit_ffn(C, xs, xT, actT, wd, wb, sil, w_up, w_down, tag, first):
    P = C.P
    TB = C.TB
    wu = wview(w_up, 8)
    if first:
        wdv = wview(w_down, NFF)
        DMA(P, "pool", wd[:, 0:11, :], wdv[:, 0:11, :], tag + "wd0", writes=[("wd", 0)])
        DMA(P, "pool", wd[:, 11:22, :], wdv[:, 11:22, :], tag + "wd1", writes=[("wd", 1)])
    par = 0
    for gi in range(NFF // 2):
        slot = gi % 2
        buf = wb[slot]
        DMA(P, "pool", buf[:, :, 0:256], wu[:, :, gi * 256:(gi + 1) * 256], "wbA%d" % slot, writes=[("wb", slot, 0)])
        DMA(P, "pool", buf[:, :, 256:512], wu[:, :, DFF + gi * 256:DFF + (gi + 1) * 256], "wbG%d" % slot, writes=[("wb", slot, 1)])
        for jj in range(2):
            j = gi * 2 + jj
            for hb in range(TB // 512):
                bA, bG = 2 * par, 2 * par + 1
                xr = [("xT", 4 * hb + q) for q in range(4)]
                for kc in range(8):
                    MM(P, C.ps[bA][:, :], buf[:, kc, jj * 128:(jj + 1) * 128], xT[:, kc, hb * 512:(hb + 1) * 512],
                       kc == 0, kc == 7, [("wb", slot, 0)] + xr, [("ps", bA)])
                for kc in range(8):
                    MM(P, C.ps[bG][:, :], buf[:, kc, 256 + jj * 128:256 + (jj + 1) * 128], xT[:, kc, hb * 512:(hb + 1) * 512],
                       kc == 0, kc == 7, [("wb", slot, 1)] + xr, [("ps", bG)])
                ACT(P, sil[par][:, :], C.ps[bA][:, :], AF.Silu, [("ps", bA)], [("sil", par)])
                TT(P, "dve", actT[:, j, hb * 512:(hb + 1) * 512], sil[par][:, :], C.ps[bG][:, :], ALU.mult,
                   [("sil", par), ("ps", bG)], [("actT", j, hb)])
                par ^= 1


def emit_down(C, actT, wd, t, yb=(4, 5)):
    P = C.P
    hb = t // 4
    for dh in range(2):
        for kc in range(NFF):
            MM(P, C.ps[yb[dh]][:, :], actT[:, kc, t * 128:(t + 1) * 128], wd[:, kc, dh * 512:(dh + 1) * 512],
               kc == 0, kc == NFF - 1, [("actT", kc, hb), ("wd", kc // 11)], [("ps", yb[dh])])


def phase1(C):
    P, I, A = C.P, C.I, C.arena
    TB = C.TB
    NTB = C.NK // TB
    own_tb0 = C.KOFF // TB
    A.reset()
    xs = A.alloc([128, 8, D], F32)
    xT = A.alloc([128, 8, TB], BF16)
    actT = A.alloc([128, NFF, TB], BF16)
    wd = A.alloc([128, NFF, D], BF16)
    wb = [A.alloc([128, 8, 512], BF16) for _ in range(2)]
    lnp = A.alloc([128, 2, D], F32)
    tT = A.alloc([128, D], F32)
    r = A.alloc([128, D], F32)
    xn = A.alloc([128, D], F32)
    st6 = A.alloc([128, 2, 6], F32)
    mv = A.alloc([128, 8], F32)
    sil = [A.alloc([128, 512], F32) for _ in range(2)]
    stg = [A.alloc([128, TB], BF16) for _ in range(2)]
    vst = [A.alloc([128, 512], BF16) for _ in range(2)]
    tmp = (tT, r, xn, st6, mv)
    C.xb = [A.alloc([128, D], BF16) for _ in range(2)]
    C.xbi = 0
    DMA(P, "sp", lnp, I["lnp"][:, 0:2, :], "lnp1", writes=["ln1"])
    xv = I["x"].rearrange("(t p) d -> p t d", p=128)
    x1v = C.X1_s.rearrange("(t p) d -> p t d", p=128)
    win = wview(I["w_in"], 8)
    for tb in range(NTB):
        own = tb >= own_tb0
        qtb = tb - own_tb0
        for t in range(8):
            DMA(P, "sp", xs[:, t, :], xv[:, tb * 8 + t, :], "xs%d" % t, writes=[("xs", t)])
        for t in range(8):
            emit_transposes(C, xs, xT, t, ("dve", "act"))
        emit_ffn(C, xs, xT, actT, wd, wb, sil, I["w_ffn1_up"], I["w_ffn1_down"], "f1", tb == 0)
        for t in range(8):
            yb = (4, 5) if (t % 2 == 0 or not FLAG_LNPIPE) else (2, 3)
            emit_down(C, actT, wd, t, yb)
            emit_layernorm(C, xs, t, yb, 0.5, lnp[:, 0, :], lnp[:, 1, :], tmp, "ln1")
            if own:
                DMA(P, "sp", x1v[:, qtb * 8 + t, :], xs[:, t, :], "x1st%d" % t, reads=[("xs", t)])
            if t >= 1:
                emit_transposes(C, xs, xT, t - 1, ("dve", "act"), cast=False)
        emit_transposes(C, xs, xT, 7, ("dve", "act"), cast=False)
        if own:
            for kc in range(8):
                DMA(P, "sp", C.X1T_s[kc * 128:(kc + 1) * 128, qtb * TB:(qtb + 1) * TB], xT[:, kc, :], "x1T%d" % kc,
                    reads=[("xT", q) for q in range(8)])
        par = 0
        for g in range(6):
            kind = ("q", "k", "v")[g % 3]
            fam = g // 3
            if kind == "q" and not own:
                continue
            slot = g % 2
            buf = wb[slot]
            DMA(P, "pool", buf[:, :, 0:256], win[:, :, g * 512:g * 512 + 256], "wbA%d" % slot, writes=[("wb", slot, 0)])
            DMA(P, "pool", buf[:, :, 256:512], win[:, :, g * 512 + 256:(g + 1) * 512], "wbG%d" % slot, writes=[("wb", slot, 1)])
            if kind in ("q", "k"):
                dst = C.QT_s if kind == "q" else C.KT_s
                col0 = qtb * TB if kind == "q" else tb * TB
                for c in range(4):
                    sg = stg[par]
                    for hb in range(TB // 512):
                        bank = 2 * par + hb
                        for kc in range(8):
                            MM(P, C.ps[bank][:, :], buf[:, kc, c * 128:(c + 1) * 128], xT[:, kc, hb * 512:(hb + 1) * 512],
                               kc == 0, kc == 7, [("wb", slot, c // 2)] + [("xT", 4 * hb + q) for q in range(4)], [("ps", bank)])
                        CP(P, "act" if hb == 0 else "dve", sg[:, hb * 512:(hb + 1) * 512], C.ps[bank][:, :], [("ps", bank)], [("stg", par, hb)])
                    row0 = fam * 512 + c * 128
                    DMA(P, "sp", dst[row0:row0 + 128, col0:col0 + TB], sg[:, :], "stg%d" % par,
                        reads=[("stg", par, 0), ("stg", par, 1)])
                    par ^= 1
            else:
                vv = C.V_s.rearrange("(t p) d -> p t d", p=128)
                for t in range(8):
                    bank = 2 * par
                    for kc in range(8):
                        MM(P, C.ps[bank][:, :], xT[:, kc, t * 128:(t + 1) * 128], buf[:, kc, :],
                           kc == 0, kc == 7, [("wb", slot, 0), ("wb", slot, 1), ("xT", t)], [("ps", bank)])
                    CP(P, "act" if t % 2 == 0 else "dve", vst[par][:, :], C.ps[bank][:, :], [("ps", bank)], [("vst", par)])
                    DMA(P, "sp", vv[:, tb * 8 + t, fam * 512:(fam + 1) * 512], vst[par][:, :], "vst%d" % par, reads=[("vst", par)])
                    par ^= 1


def phase2(C):
    P, I, A = C.P, C.I, C.arena
    A.reset()
    NHM, NHD = C.NHM, C.NHD
    NQ, NK, KOFF = C.NQ, C.NK, C.KOFF
    NQB = NQ // 512
    NKC = NK // 128
    KO = KOFF // 128
    KTs = [[A.alloc([128, NK], BF16) for m in range(2)] for _ in range(2)]
    QTs = [[A.alloc([128, NQ], BF16) for m in range(2)] for _ in range(2)]
    VEs = [A.alloc([128, NKC, 132], BF16) for _ in range(2)]
    PT = [A.alloc([128, 512], BF16) for _ in range(3)]
    accsb = [A.alloc([128, 4, 2, 132], F32) for _ in range(2)]
    o1 = A.alloc([128, 4, 128], F32)
    yd = A.alloc([128, 4, 128], F32)
    ydb = A.alloc([128, 4, 128], BF16)
    sq = A.alloc([128, 128], F32)
    sc = A.alloc([128, 4, 8], F32)
    ystg = [A.alloc([128, 512], BF16) for _ in range(2)]
    kmf = A.alloc([64, 16], F32)
    kmb = [A.alloc([64, 16], BF16) for _ in range(2)]
    vbt = A.alloc([128, NQB, 4, 16], F32)
    lot = A.alloc([128, NQB, 4, 16], F32)
    gm = [A.alloc([128, 4, 16], F32) for _ in range(2)]
    m8 = [A.alloc([128, 4, 8], F32) for _ in range(2)]
    m1 = [A.alloc([128, 4, 16], F32) for _ in range(2)]
    npad = [A.alloc([128, 4, 80], BF16) for _ in range(2)]
    ones_col = A.alloc([128, NKC, 1], F32)
    ones_k = A.alloc([128, 2], BF16)
    one_f = A.alloc([1, 2], F32)
    OTsb = [A.alloc([128, 512], BF16) for _ in range(2)]
    ZTsb = [A.alloc([1, 512], F32) for _ in range(2)]
    Zhi = [A.alloc([1, 512], BF16) for _ in range(2)]
    Zlo = [A.alloc([1, 512], BF16) for _ in range(2)]
    Zd = [A.alloc([1, 512], F32) for _ in range(2)]
    one_b = A.alloc([1, 2], BF16)
    P.op("dve", lambda e: e.memset(one_b[:, :], 1.0), writes=["one_b"])
    one_b64 = A.alloc([128, 2], BF16)
    P.op("dve", lambda e: e.memset(one_b64[:, :], 1.0), writes=["one_b64"])
    Zm_f = A.alloc([128, 512], F32)
    Zm_d = A.alloc([128, 512], F32)
    Zm_lo = A.alloc([128, 512], BF16)

    ones_kk = A.alloc([128, 128], BF16)
    P.op("dve", lambda e: e.memset(ones_kk[:, :], 1.0), writes=["ones_kk"])
    for par in range(2):
        for m in range(2):
            P.op("pool", (lambda par=par, m=m: (lambda e: e.memset(KTs[par][m][64:128, :], 0.0)))(), writes=[("KT", par, m)])
            P.op("pool", (lambda par=par, m=m: (lambda e: e.memset(QTs[par][m][64:128, :], 0.0)))(), writes=[("QT", par, m), ("QTm", par, m)])
    P.op("dve", lambda e: e.memset(ones_k[:, :], 1.0), writes=["ones_k"])
    P.op("dve", lambda e: e.memset(one_f[:, :], 1.0), writes=["one_f"])
    DMA(P, "sp", vbt, I["vb"], "vbt", writes=["vbt"])
    DMA(P, "sp", lot, I["lo"], "lot", writes=["lot"])
    for g in range(2):
        P.op("dve", (lambda g=g: (lambda e: e.memset(npad[g][:, :, :], 0.0)))(), writes=[("npad", g)])
    P.op("dve", lambda e: e.memset(ones_col[:, :, :], 1.0), writes=["ones_col"])
    vv = C.V_s.rearrange("(k p) d -> p k d", p=128)

    jobs = [("d", h) for h in range(NHD)] + [("m", h) for h in range(NHM)]

    def load_job(ji):
        kind, h = jobs[ji]
        par = ji % 2
        KT, QT, VE = KTs[par], QTs[par], VEs[par]
        if kind == "d":
            hg = NHM + h
            for m in range(2):
                r0 = 512 + h * 128 + m * 64
                DMA(P, "sp", KT[m][0:64, :], C.KT_s[r0:r0 + 64, :], "KTl%d%d" % (par, m), writes=[("KT", par, m)])
                DMA(P, "pool", KT[m][64:68, :], I["kc"][hg, :, :], "KTc%d%d" % (par, m), writes=[("KT", par, m)])
                DMA(P, "sp", QT[m][0:64, :], C.QT_s[r0:r0 + 64, :], "QTl%d%d" % (par, m), writes=[("QT", par, m)])
                DMA(P, "pool", QT[m][64:68, :], I["qc"][hg, :, :], "QTc%d%d" % (par, m), writes=[("QT", par, m)])
            DMA(P, "sp", VE[:, :, 0:128], vv[:, :, 512 + h * 128:512 + (h + 1) * 128], "VEl%d" % par, writes=[("VE", par)])
            CP(P, "dve", VE[:, :, 128:129], ones_col[:, :, :], ["ones_col"], [("VE", par)])
        else:
            r0 = h * 64
            DMA(P, "sp", KT[0][0:64, :], C.KT_s[r0:r0 + 64, :], "KTl%d0" % par, writes=[("KT", par, 0)])
            DMA(P, "pool", KT[0][64:80, :], I["kind"], "KTi%d" % par, writes=[("KT", par, 0)])
            DMA(P, "pool", KT[0][80:84, :], I["kc"][h, :, :], "KTc%d0" % par, writes=[("KT", par, 0)])
            DMA(P, "sp", QT[0][0:64, :], C.QT_s[r0:r0 + 64, :], "QTl%d0" % par, writes=[("QT", par, 0)])
            DMA(P, "pool", QT[0][80:84, :], I["qc"][h, :, :], "QTc%d0" % par, writes=[("QT", par, 0)])
            DMA(P, "sp", VE[:, :, 0:64], vv[:, :, h * 64:(h + 1) * 64], "VEl%d" % par, writes=[("VE", par)])
            CP(P, "dve", VE[:, :, 64:65], ones_col[:, :, :], ["ones_col"], [("VE", par)])

    def mask_stage(ji, tg):
        par = ji % 2
        KT, QT = KTs[par][0], QTs[par][0]
        if tg == 0:
            P.op("dve", lambda e: e.tensor_reduce(out=kmf[:, :], in_=KT[0:64, :].rearrange("p (n l) -> p n l", l=256), axis=AX.X, op=ALU.add),
                 [("KT", par, 0)], ["kmf"])
            TS(P, "dve", kmb[par][:, :], kmf[:, :], 1.0 / 256.0, None, ALU.mult, None, ["kmf"], [("kmb", par)])
        gp = tg % 2
        for tq in range(4):
            t = tg * 4 + tq
            MM(P, C.ps[7][:, tq * 16:(tq + 1) * 16], QT[0:64, t * 128:(t + 1) * 128], kmb[par][:, :], True, True,
               [("QT", par, 0), ("kmb", par)], [("ps", 7)])
        TT(P, "dve", gm[gp][:, :, :], C.ps[7][:, 0:64].rearrange("p (a b) -> p a b", a=4), vbt[:, tg, :, :], ALU.add,
           [("ps", 7), "vbt"], [("gm", gp)])
        for tq in range(4):
            P.op("dve", (lambda tq=tq: (lambda e: e.max(out=m8[gp][:, tq, :], in_=gm[gp][:, tq, :])))(), [("gm", gp)], [("m8", gp, tq)])
        for tq in range(4):
            TS(P, "dve", m1[gp][:, tq, :], gm[gp][:, tq, :], m8[gp][:, tq, 3:4], 1.0, ALU.is_ge, ALU.subtract,
               [("gm", gp), ("m8", gp, tq)], [("m1", gp, tq)])
        TT(P, "dve", npad[gp][:, :, 64:80], m1[gp][:, :, :], lot[:, tg, :, :], ALU.min,
           [("m1", gp, q) for q in range(4)] + ["lot"], [("npad", gp)])

    def mask_stage_b(ji, tg):
        par = ji % 2
        QT = QTs[par][0]
        gp = tg % 2
        for tq in range(4):
            MM(P, C.ps[6][0:80, tq * 128:(tq + 1) * 128], npad[gp][:, tq, :], C.ident_b[:, :], True, True,
               [("npad", gp), "ident_b"], [("ps", 6)])
        CP(P, "act", QT[64:80, tg * 512:(tg + 1) * 512], C.ps[6][64:80, :], [("ps", 6)], [("QTm", par, 0)])

    pending = []
    stepc = [0]

    dseq = [0]

    def defer(lag, fn):
        if not FLAG_DEFER:
            fn()
            return
        dseq[0] += 1
        pending.append((stepc[0] + lag, dseq[0], fn))

    def run_due(flush=False):
        while True:
            due = [p for p in pending if flush or p[0] <= stepc[0]]
            if not due:
                break
            due.sort()
            it = due[0]
            pending.remove(it)
            it[2]()

    def attention(ji, nmaps, dv, K, finalize, side):
        par = ji % 2
        KT, QT, VE = KTs[par], QTs[par], VEs[par]
        steps = [(qb, m, kc) for qb in range(NQB) for m in range(nmaps) for kc in range(KO + 4 * qb + 4)]
        n = len(steps)

        def emit_S(i):
            qb, m, kc = steps[i]
            j = kc - KO - 4 * qb
            n0 = max(0, j) * 128
            sb = i % 2
            S = C.ps[sb]
            MM(P, S[:, n0:512], KT[m][:, kc * 128:(kc + 1) * 128], QT[m][:, qb * 512 + n0:(qb + 1) * 512],
               True, j < 0, [("KT", par, m), ("QT", par, m), ("QTm", par, m)], [("ps", sb)])
            if j >= 0:
                MM(P, S[:, n0:n0 + 128], C.ident_b[:, :], C.tri_b[:, :], False, True, ["ident_b", "tri_b"], [("ps", sb)])
            ACT(P, PT[i % 3][:, n0:512], S[:, n0:512], AF.Exp, [("ps", sb)], [("PT", i % 3)], scale=0.125)

        def emit_AV(i):
            qb, m, kc = steps[i]
            j = kc - KO - 4 * qb
            n0 = max(0, j) * 128
            last = KO + 4 * qb + 3
            pt = PT[i % 3]
            if nmaps == 2:
                MM(P, C.ps[2 + m][:, n0:512], VE[:, kc, 0:128], pt[:, n0:512], kc == 0, kc == last,
                   [("PT", i % 3), ("VE", par)], [("OT", m)])
                MM(P, C.ps[4 + m][:, n0:512], ones_kk[:, :], pt[:, n0:512], kc == 0, kc == last,
                   [("PT", i % 3), "ones_kk"], [("ZT", m)])
            else:
                MM(P, C.ps[2][:, n0:512], VE[:, kc, 0:128], pt[:, n0:512], kc == 0, kc == last,
                   [("PT", i % 3), ("VE", par)], [("OT", 0)])
            if m == nmaps - 1 and kc == last:
                q2 = qb % 2
                if nmaps == 2:
                    for mm in range(2):
                        CP(P, "dve", OTsb[mm][:, :], C.ps[2 + mm][:, :], [("OT", mm)], [("OTsb", mm)])
                        CP(P, "dve", ZTsb[mm][0:1, :], C.ps[4 + mm][0:1, :], [("ZT", mm)], [("ZTsb", mm)])
                        CP(P, "dve", Zhi[mm][0:1, :], ZTsb[mm][0:1, :], [("ZTsb", mm)], [("Zhi", mm)])
                        TT(P, "dve", Zd[mm][0:1, :], ZTsb[mm][0:1, :], Zhi[mm][0:1, :], ALU.subtract, [("ZTsb", mm), ("Zhi", mm)], [("Zd", mm)])
                        CP(P, "dve", Zlo[mm][0:1, :], Zd[mm][0:1, :], [("Zd", mm)], [("Zlo", mm)])
                else:
                    CP(P, "dve", OTsb[0][0:65, :], C.ps[2][0:65, :], [("OT", 0)], [("OTsb", 0)])
                    CP(P, "dve", Zm_f[64:65, :], C.ps[2][64:65, :], [("OT", 0)], [("Zm_f", 0)])
                    TT(P, "dve", Zm_d[64:65, :], Zm_f[64:65, :], OTsb[0][64:65, :], ALU.subtract, [("Zm_f", 0), ("OTsb", 0)], [("Zm_d", 0)])
                    CP(P, "dve", Zm_lo[64:65, :], Zm_d[64:65, :], [("Zm_d", 0)], [("Zm_lo", 0)])

                def stage2(qb=qb, q2=q2):
                    if nmaps == 2:
                        for mm in range(2):
                            for sub in range(4):
                                MM(P, C.ps[6][:, sub * 128:(sub + 1) * 128], OTsb[mm][:, sub * 128:(sub + 1) * 128], C.ident_b[:, :], True, True,
                                   [("OTsb", mm), "ident_b"], [("ps", 6)])
                            CP(P, "dve", accsb[q2][:, 2 * mm:2 * mm + 2, :, 0:128],
                               C.ps[6][:, :].rearrange("p (a b c) -> p a b c", a=2, b=2), [("ps", 6)], [("accsb", q2, 2 * mm), ("accsb", q2, 2 * mm + 1)])
                            for sub in range(4):
                                MM(P, C.ps[7][:, 64 + mm * 4 + sub:64 + mm * 4 + sub + 1], Zhi[mm][0:1, sub * 128:(sub + 1) * 128], one_b[0:1, 0:1],
                                   True, False, [("Zhi", mm), "one_b"], [("ps", 7)])
                                MM(P, C.ps[7][:, 64 + mm * 4 + sub:64 + mm * 4 + sub + 1], Zlo[mm][0:1, sub * 128:(sub + 1) * 128], one_b[0:1, 0:1],
                                   False, True, [("Zlo", mm), "one_b"], [("ps", 7)])
                            CP(P, "dve", accsb[q2][:, 2 * mm:2 * mm + 2, :, 128:129],
                               C.ps[7][:, 64 + mm * 4:64 + mm * 4 + 4].rearrange("p (a b c) -> p a b c", a=2, b=2), [("ps", 7)],
                               [("accsb", q2, 2 * mm), ("accsb", q2, 2 * mm + 1)])
                    else:
                        for sub in range(4):
                            MM(P, C.ps[6][:, sub * 128:sub * 128 + 65], OTsb[0][0:65, sub * 128:(sub + 1) * 128], C.ident_b[0:65, 0:65], True, False,
                               [("OTsb", 0), "ident_b"], [("ps", 6)])
                            MM(P, C.ps[6][:, sub * 128 + 64:sub * 128 + 65], Zm_lo[64:65, sub * 128:(sub + 1) * 128], one_b64[64:65, 0:1], False, True,
                               [("Zm_lo", 0), "one_b64"], [("ps", 6)])
                        CP(P, "dve", accsb[q2][:, 0:2, :, 0:65],
                           C.ps[6][:, :].rearrange("p (a b c) -> p a b c", a=2, b=2)[:, :, :, 0:65], [("ps", 6)], [("accsb", q2, 0), ("accsb", q2, 1)])
                    finalize[0][1](qb)

                if FLAG_S2DEFER:
                    defer(2, stage2)
                else:
                    stage2()
                for lag, fn in finalize[1:]:
                    defer(lag, (lambda qb=qb, fn=fn: fn(qb)))
                side(qb)

        emit_S(0)
        for i in range(n):
            if i + 1 < n:
                emit_S(i + 1)
            emit_AV(i)
            stepc[0] += 1
            run_due()
        run_due(flush=True)

    def fin_diff(h, qb):
        q2 = qb % 2
        for sub in range(4):
            a1 = accsb[q2][:, sub // 2, sub % 2, :]
            a2 = accsb[q2][:, 2 + sub // 2, sub % 2, :]
            r1 = [("accsb", q2, sub // 2)]
            r2 = [("accsb", q2, 2 + sub // 2)]
            s = sc[:, sub, :]
            P.op("dve", (lambda s=s, a1=a1: (lambda e: e.reciprocal(out=s[:, 0:1], in_=a1[:, 128:129])))(), r1, [("sc", sub, 0)])
            P.op("dve", (lambda s=s, a2=a2: (lambda e: e.reciprocal(out=s[:, 1:2], in_=a2[:, 128:129])))(), r2, [("sc", sub, 1)])
            TT(P, "dve", s[:, 2:3], s[:, 1:2], C.lam_t[:, 1:2], ALU.mult, [("sc", sub, 1), "nlam"], [("sc", sub, 2)])
            TS(P, "dve", o1[:, sub, :], a1[:, 0:128], s[:, 0:1], None, ALU.mult, None, r1 + [("sc", sub, 0)], [("o1", sub)])
            STT(P, yd[:, sub, :], a2[:, 0:128], s[:, 2:3], o1[:, sub, :], ALU.mult, ALU.add,
                r2 + [("sc", sub, 2), ("o1", sub)], [("yd", sub)])
            TT(P, "dve", sq[:, :], yd[:, sub, :], yd[:, sub, :], ALU.mult, [("yd", sub)], ["sq"])
            P.op("dve", (lambda s=s: (lambda e: e.tensor_reduce(out=s[:, 3:4], in_=sq[:, :], axis=AX.X, op=ALU.add)))(), ["sq"], [("sc", sub, 3)])

    def fin_diff_act(h, qb):
        ACT(P, sc[:, :, 4:5], sc[:, :, 3:4], AF.Ln, [("sc", sub, 3) for sub in range(4)] + ["eps"], [("sc", sub, 4) for sub in range(4)],
            bias=C.eps_t[:], scale=1.0 / 128.0)
        ACT(P, sc[:, :, 5:6], sc[:, :, 4:5], AF.Exp, [("sc", sub, 4) for sub in range(4)], [("sc", sub, 5) for sub in range(4)], scale=-0.5)

    def fin_diff_a2(h, qb):
        for sub in range(4):
            s = sc[:, sub, :]
            TS(P, "dve", yd[:, sub, :], yd[:, sub, :], s[:, 5:6], float(1.0 - LAM_INIT), ALU.mult, ALU.mult, [("yd", sub), ("sc", sub, 5)], [("yd", sub)])
            TT(P, "dve", ydb[:, sub, :], yd[:, sub, :], C.sublng[:, :], ALU.mult, [("yd", sub), "sublng"], [("ydb", sub)])

    def fin_diff_b(h, qb):
        for sub in range(4):
            MM(P, C.ps[6][:, sub * 128:(sub + 1) * 128], ydb[:, sub, :], C.ident_b[:, :], True, True,
               [("ydb", sub), "ident_b"], [("ps", 6)])
        sl = qb % 2
        CP(P, "dve", ystg[sl][:, :], C.ps[6][:, :], [("ps", 6)], [("ystg", sl)])
        DMA(P, "sp", C.YT_s[512 + h * 128:512 + (h + 1) * 128, qb * 512:(qb + 1) * 512], ystg[sl][:, :], "ystg%d" % sl, reads=[("ystg", sl)])

    def fin_moba(h, qb):
        q2 = qb % 2
        sl = qb % 2
        for sub in range(4):
            a1 = accsb[q2][:, sub // 2, sub % 2, :]
            r1 = [("accsb", q2, sub // 2)]
            s = sc[:, sub, :]
            P.op("dve", (lambda s=s, a1=a1: (lambda e: e.reciprocal(out=s[:, 0:1], in_=a1[:, 64:65])))(), r1, [("sc", sub, 0)])
            TS(P, "dve", ydb[:, sub, 0:64], a1[:, 0:64], s[:, 0:1], None, ALU.mult, None, r1 + [("sc", sub, 0)], [("ydb", sub)])

    def fin_moba_b(h, qb):
        sl = qb % 2
        for sub in range(4):
            MM(P, C.ps[6][0:64, sub * 128:(sub + 1) * 128], ydb[:, sub, 0:64], C.ident_b[:, :], True, True,
               [("ydb", sub), "ident_b"], [("ps", 6)])
        CP(P, "dve", ystg[sl][0:64, :], C.ps[6][0:64, :], [("ps", 6)], [("ystg", sl)])
        DMA(P, "sp", C.YT_s[h * 64:(h + 1) * 64, qb * 512:(qb + 1) * 512], ystg[sl][0:64, :], "ystg%d" % sl, reads=[("ystg", sl)])

    load_job(0)
    if jobs[0][0] == "m":
        for tg in range(NQB):
            mask_stage(0, tg)
            mask_stage_b(0, tg)
    for ji, (kind, h) in enumerate(jobs):
        nxt = ji + 1 if ji + 1 < len(jobs) else None
        if nxt is not None:
            load_job(nxt)

        def side(qb, nxt=nxt):
            if nxt is not None and jobs[nxt][0] == "m":
                defer(4, (lambda: mask_stage(nxt, qb)))
                defer(11, (lambda: mask_stage_b(nxt, qb)))

        if kind == "d":
            attention(ji, 2, 128, 68, [(0, (lambda qb, h=h: fin_diff(h, qb))), (9, (lambda qb, h=h: fin_diff_act(h, qb))),
                                       (11, (lambda qb, h=h: fin_diff_a2(h, qb))), (18, (lambda qb, h=h: fin_diff_b(h, qb)))], side)
        else:
            attention(ji, 1, 64, 84, [(0, (lambda qb, h=h: fin_moba(h, qb))), (6, (lambda qb, h=h: fin_moba_b(h, qb)))], side)


def phase3(C):
    P, I, A = C.P, C.I, C.arena
    TB = C.TB
    NTB = C.NQ // TB
    A.reset()
    xs = A.alloc([128, 8, D], F32)
    xT = A.alloc([128, 8, TB], BF16)
    wb = [A.alloc([128, 8, 512], BF16) for _ in range(2)]
    lnp = A.alloc([128, 4, D], F32)
    tT = A.alloc([128, D], F32)
    r = A.alloc([128, D], F32)
    xn = A.alloc([128, D], F32)
    st6 = A.alloc([128, 2, 6], F32)
    mv = A.alloc([128, 8], F32)
    sil = [A.alloc([128, 512], F32) for _ in range(2)]
    sg = [A.alloc([128, 512], BF16) for _ in range(2)]
    pt_f = [A.alloc([128, 256], F32) for _ in range(1)]
    C.xb = [A.alloc([128, D], BF16) for _ in range(2)]
    C.xbi = 0
    tmp = (tT, r, xn, st6, mv)
    base = A.off
    mergedT = A.alloc([128, 8, TB], BF16)
    yT = A.alloc([128, 8, TB], BF16)
    wbr = A.alloc([128, 2, 4, D], BF16)
    wout = A.alloc([128, 8, D], BF16)
    A.reset(base)
    actT = A.alloc([128, NFF, TB], BF16)
    wd = A.alloc([128, NFF, D], BF16)
    A.reset(base)
    wpg = A.alloc([128, 8, D], BF16)
    wpl = A.alloc([128, 2, D], BF16)
    pT = A.alloc([128, 2, TB], BF16)

    DMA(P, "sp", lnp, I["lnp"][:, 2:6, :], "lnp3", writes=["ln2", "ln3"])
    x1v = C.X1_s.rearrange("(t p) d -> p t d", p=128)
    outv = C.out.rearrange("(t p) d -> p t d", p=128)
    pv = I["p"].rearrange("(t p) d -> p t d", p=128)
    win = wview(I["w_in"], 8)
    for tb in range(NTB):
        tok0 = tb * TB
        for t in range(8):
            DMA(P, "sp", xs[:, t, :], x1v[:, tb * 8 + t, :], "xs%d" % t, writes=[("xs", t)])
        for kc in range(8):
            DMA(P, "sp", xT[:, kc, :], C.X1T_s[kc * 128:(kc + 1) * 128, tb * TB:(tb + 1) * TB], "xTl%d" % kc, writes=[("xT", q) for q in range(8)])
            DMA(P, "sp", yT[:, kc, :], C.YT_s[kc * 128:(kc + 1) * 128, tok0:tok0 + TB], "yTl%d" % kc, writes=[("yT", kc)])
        DMA(P, "pool", wbr[:, 0, :, :], wview(I["w_branch_a"], 4), "wbra", writes=["wbra"])
        DMA(P, "pool", wbr[:, 1, :, :], wview(I["w_branch_b"], 4), "wbrb", writes=["wbrb"])
        DMA(P, "pool", wout, wview(I["w_out"], 8), "wout", writes=["wout"])
        for oc in range(8):
            slot = oc % 2
            buf = wb[slot]
            DMA(P, "pool", buf[:, :, 0:128], win[:, :, 3072 + oc * 128:3072 + (oc + 1) * 128], "wbA%d" % slot, writes=[("wb", slot, 0)])
            DMA(P, "pool", buf[:, :, 128:256], win[:, :, 4096 + oc * 128:4096 + (oc + 1) * 128], "wbG%d" % slot, writes=[("wb", slot, 1)])
            for hb in range(TB // 512):
                xr = [("xT", 4 * hb + q) for q in range(4)]
                ip = (oc * 2 + hb) % 2 if FLAG_MERGEDB else 0
                for br in range(2):
                    bank = 4 * ip + br
                    for kc in range(8):
                        MM(P, C.ps[bank][:, :], buf[:, kc, br * 128:(br + 1) * 128], xT[:, kc, hb * 512:(hb + 1) * 512],
                           kc == 0, kc == 7, [("wb", slot, br)] + xr, [("ps", bank)])
                    ACT(P, sg[br][:, :], C.ps[bank][:, :], AF.Sigmoid, [("ps", bank)], [("sg", br)])
                    bank2 = 4 * ip + 2 + br
                    for kc in range(4):
                        MM(P, C.ps[bank2][:, :], wbr[:, br, kc, oc * 128:(oc + 1) * 128], yT[:, br * 4 + kc, hb * 512:(hb + 1) * 512],
                           kc == 0, kc == 3, ["wbra" if br == 0 else "wbrb", ("yT", br * 4 + kc)], [("ps", bank2)])
                    TT(P, "dve", sil[br][:, :], sg[br][:, :], C.ps[bank2][:, :], ALU.mult,
                       [("sg", br), ("ps", bank2)], [("sil", br)])
                TT(P, "dve", mergedT[:, oc, hb * 512:(hb + 1) * 512], sil[0][:, :], sil[1][:, :], ALU.add,
                   [("sil", 0), ("sil", 1)], [("mergedT", oc, hb)])
        for t in range(8):
            hb = t // 4
            yb = (4, 5) if (t % 2 == 0 or not FLAG_LNPIPE) else (2, 3)
            for dh in range(2):
                for kc in range(8):
                    MM(P, C.ps[yb[dh]][:, :], mergedT[:, kc, t * 128:(t + 1) * 128], wout[:, kc, dh * 512:(dh + 1) * 512],
                       kc == 0, kc == 7, [("mergedT", kc, hb), "wout"], [("ps", yb[dh])])
            emit_layernorm(C, xs, t, yb, 1.0, lnp[:, 0, :], lnp[:, 1, :], tmp, "ln2")
            if t >= 1:
                emit_transposes(C, xs, xT, t - 1, ("dve", "act"), cast=False)
        emit_transposes(C, xs, xT, 7, ("dve", "act"), cast=False)
        P.barrier()
        emit_ffn(C, xs, xT, actT, wd, wb, sil, I["w_ffn2_up"], I["w_ffn2_down"], "f2", True)
        for t in range(8):
            yb = (4, 5) if (t % 2 == 0 or not FLAG_LNPIPE) else (2, 3)
            emit_down(C, actT, wd, t, yb)
            emit_layernorm(C, xs, t, yb, 0.5, lnp[:, 2, :], lnp[:, 3, :], tmp, "ln3")
            if t >= 1:
                emit_transposes(C, xs, xT, t - 1, ("dve", "act"), cast=False)
        emit_transposes(C, xs, xT, 7, ("dve", "act"), cast=False)
        P.barrier()
        DMA(P, "pool", wpg, wview(I["w_ple_gate"], 8), "wpg", writes=["wpg"])
        DMA(P, "pool", wpl, wview(I["w_ple"], 2), "wpl", writes=["wpl"])
        for t in range(8):
            pf = pt_f[0]
            DMA(P, "sp", pf, pv[:, tb * 8 + t, :], "pf0", writes=[("pf", 0)])
            pb = C.xb[0][:, 0:256]
            CP(P, "pool", pb, pf[:, :], [("pf", 0)], [("xb", 0)])
            for k in range(2):
                MM(P, C.ps[6][:, k * 128:(k + 1) * 128], pb[:, k * 128:(k + 1) * 128], C.ident_b[:, :], True, True, [("xb", 0), "ident_b"], [("ps", 6)])
            CP(P, "dve", pT[:, :, t * 128:(t + 1) * 128], C.ps[6][:, 0:256].rearrange("p (a b) -> p a b", a=2), [("ps", 6)], [("pT", t)])
            for dh in range(2):
                for kc in range(8):
                    MM(P, C.ps[dh][:, :], xT[:, kc, t * 128:(t + 1) * 128], wpg[:, kc, dh * 512:(dh + 1) * 512],
                       kc == 0, kc == 7, [("xT", t), "wpg"], [("ps", dh)])
                ACT(P, tT[:, dh * 512:(dh + 1) * 512], C.ps[dh][:, :], AF.Sigmoid, [("ps", dh)], [("tT", dh)])
                for kc in range(2):
                    MM(P, C.ps[2 + dh][:, :], pT[:, kc, t * 128:(t + 1) * 128], wpl[:, kc, dh * 512:(dh + 1) * 512],
                       kc == 0, kc == 1, [("pT", t), "wpl"], [("ps", 2 + dh)])
                TT(P, "dve", sil[dh][:, :], tT[:, dh * 512:(dh + 1) * 512], C.ps[2 + dh][:, :], ALU.mult, [("tT", dh), ("ps", 2 + dh)], [("sil", dh)])
                TT(P, "dve", xn[:, dh * 512:(dh + 1) * 512], sil[dh][:, :], xs[:, t, dh * 512:(dh + 1) * 512], ALU.add,
                   [("sil", dh), ("xs", t)], [("xo", dh)])
            DMA(P, "sp", outv[:, tb * 8 + t, :], xn[:, :], "outst", reads=[("xo", 0), ("xo", 1)])
        P.barrier()


def host_consts(mode, r):
    c = {}
    c["ident"] = np.eye(128, dtype=np.float32)
    ki = np.arange(128)[:, None]
    qi = np.arange(128)[None, :]
    c["tri"] = np.where(ki > qi, -BIGM, 0.0).astype(np.float32)
    slopes = np.concatenate([2.0 ** (-8.0 * np.arange(1, 9) / 8), 2.0 ** (-8.0 * np.arange(1, 5) / 4)]).astype(np.float64)
    if mode == "P4":
        NQ, NK = 4096, 4096
        qpos = np.arange(NQ)
        kpos = np.arange(NK)
        kvalid = np.ones(NK, bool)
    else:
        NQ, NK = 2048, 4096
        qpos = r * 2048 + np.arange(NQ)
        kpos = np.concatenate([np.arange(2048), r * 2048 + np.arange(2048)])
        kvalid = np.concatenate([np.full(2048, r == 1), np.ones(2048, bool)])
    qc = np.zeros((12, 4, NQ), np.float32)
    kc = np.zeros((12, 4, NK), np.float32)
    for h in range(12):
        s8 = 8.0 * slopes[h]
        qc[h, 0] = -s8 * (qpos % 128)
        qc[h, 1] = -s8 * 128.0 * (qpos // 128)
        qc[h, 2] = 1.0
        qc[h, 3] = 1.0
        kc[h, 0] = 1.0
        kc[h, 1] = 1.0
        kc[h, 2] = s8 * (kpos % 128)
        kc[h, 3] = np.where(kvalid, s8 * 128.0 * (kpos // 128), -BIGM)
    c["qc"], c["kc"] = qc, kc
    kind = np.zeros((16, NK), np.float32)
    for n in range(16):
        kind[n, n * 256:(n + 1) * 256] = BIGM
    c["kind"] = kind
    nt = NQ // 128
    vb = np.zeros((nt, 16), np.float32)
    lo = np.zeros((nt, 16), np.float32)
    for t in range(nt):
        b_loc = t // 2
        for n in range(16):
            if mode == "P4":
                st = "past" if n < b_loc else ("own" if n == b_loc else "no")
            else:
                if n < 8:
                    st = "past" if r == 1 else "no"
                else:
                    nn = n - 8
                    st = "past" if nn < b_loc else ("own" if nn == b_loc else "no")
            vb[t, n] = {"past": 0.0, "own": 1e30, "no": -1e30}[st]
            lo[t, n] = 0.0 if st in ("past", "own") else -1.0
    c["vb"] = np.ascontiguousarray(np.broadcast_to(vb.reshape(1, nt // 4, 4, 16), (128, nt // 4, 4, 16)))
    c["lo"] = np.ascontiguousarray(np.broadcast_to(lo.reshape(1, nt // 4, 4, 16), (128, nt // 4, 4, 16)))
    return c


_CACHE = {}
MODE = "P8R"


def kernel(**inputs):
    f = lambda k: np.ascontiguousarray(np.asarray(inputs[k], dtype=np.float32)[0])
    mode = MODE
    if mode not in _CACHE:
        _CACHE[mode] = build(mode)
    nc = _CACHE[mode]
    x = np.asarray(inputs["x"], dtype=np.float32)
    p = np.asarray(inputs["p"], dtype=np.float32)[0]
    shared = {k: f(k) for k in ("w_ffn1_up", "w_ffn1_down", "w_ffn2_up", "w_ffn2_down", "w_in", "w_branch_a",
                                "w_branch_b", "w_out", "w_ple", "w_ple_gate")}
    lnp = np.stack([f(k) for k in ("ln1_g", "ln1_b", "ln2_g", "ln2_b", "ln3_g", "ln3_b")], 0)
    shared["lnp"] = np.ascontiguousarray(np.broadcast_to(lnp[None], (128, 6, D)))
    shared["sublng"] = np.ascontiguousarray(np.broadcast_to(f("subln_g")[None], (128, 128)))
    lamv = np.stack([f(k) for k in ("lambda_q1", "lambda_k1", "lambda_q2", "lambda_k2")], 0)
    shared["lamv"] = np.ascontiguousarray(np.broadcast_to(lamv[None], (128, 4, 64)))
    consts = {}
    in_maps = []
    for c in range(N_CORES):
        m = dict(shared)
        if mode == "P4":
            b, r = c % 4, 0
            m["x"] = np.ascontiguousarray(x[b])
            m["p"] = np.ascontiguousarray(p[b])
        else:
            b, r = c // 2, c % 2
            m["x"] = np.ascontiguousarray(np.concatenate([x[b, 0:2048], x[b, r * 2048:(r + 1) * 2048]], 0))
            m["p"] = np.ascontiguousarray(p[b, r * 2048:(r + 1) * 2048])
        if r not in consts:
            consts[r] = host_consts(mode, r)
        m.update(consts[r])
        in_maps.append(m)
    res = run_bass_kernel_spmd(nc, in_maps, core_ids=list(range(N_CORES)))
    if mode == "P4":
        out = np.stack([np.asarray(res.results[b]["out"], dtype=np.float32) for b in range(4)], 0)
    else:
        out = np.stack([np.concatenate([np.asarray(res.results[2 * b + r]["out"], dtype=np.float32) for r in range(2)], 0)
                        for b in range(4)], 0)
    return out
```

```python
import contextlib
import numpy as np
import concourse.bass as bass
import concourse.mybir as mybir
from concourse.bass_utils import run_bass_kernel_spmd

F32 = mybir.dt.float32
BF16 = mybir.dt.bfloat16
AF = mybir.ActivationFunctionType
ALU = mybir.AluOpType
AX = mybir.AxisListType

D = 1024
DFF = 2816
SEQ = 4096
NFF = 22
ALPHA = float(2.0 ** 0.25)
BIGM = 262144.0
EPS = 1e-5
LAM_INIT = 0.8 - 0.6 * 1.0
N_CORES = 8
import os
FLAG_DEFER = os.environ.get("K_DEFER", "1") == "1"
FLAG_S2DEFER = os.environ.get("K_S2DEFER", "0") == "1"
FLAG_BURST = int(os.environ.get("K_BURST", "10"))
FLAG_LNPIPE = os.environ.get("K_LNPIPE", "1") == "1"
FLAG_MERGEDB = os.environ.get("K_MERGEDB", "1") == "1"


class Prog:
    ENGS = ("pe", "act", "dve", "pool", "sp")

    def __init__(self, nc):
        self.nc = nc
        self.ops = []
        self.last_w = {}
        self.readers = {}
        self.dma_keys = {}
        self.last_on_eng = {}
        self.dma_since_barrier = []
        self.barrier_deps = {}

    def op(self, eng, fn, reads=(), writes=(), dma=0, key=None):
        idx = len(self.ops)
        deps = set()
        for r in reads:
            w = self.last_w.get(r)
            if w is not None:
                deps.add(w)
        for r in writes:
            w = self.last_w.get(r)
            if w is not None:
                deps.add(w)
            for rd in self.readers.get(r, {}).values():
                deps.add(rd)
        for r in reads:
            d = self.readers.setdefault(r, {})
            d[eng if not dma else ("dma", idx)] = idx
        for r in writes:
            self.last_w[r] = idx
            self.readers[r] = {}
        bd = self.barrier_deps.pop(eng, None)
        if bd:
            deps.update(bd)
        deps.discard(idx)
        o = dict(eng=eng, fn=fn, deps=deps, dma=dma, sig=False)
        if dma:
            assert key is not None
            if key not in self.dma_keys:
                self.dma_keys[key] = len(self.dma_keys)
            o["dkey"] = self.dma_keys[key]
            self.dma_since_barrier.append(idx)
        else:
            self.last_on_eng[eng] = idx
        self.ops.append(o)
        return idx

    def barrier(self):
        deps = set(self.dma_since_barrier)
        deps.update(self.last_on_eng.values())
        self.dma_since_barrier = []
        self.barrier_deps = {e: set(deps) for e in self.ENGS}

    def emit(self):
        nc = self.nc
        ops = self.ops
        for o in ops:
            for d in o["deps"]:
                p = ops[d]
                if p["eng"] == "pe" and o["eng"] == "pe" and not p["dma"] and not o["dma"]:
                    continue
                p["sig"] = True
        es = contextlib.ExitStack()
        eng_sem = {e: es.enter_context(nc.semaphore("s_" + e)) for e in self.ENGS}
        cnt = {e: 0 for e in self.ENGS}
        nk = len(self.dma_keys)
        assert nk <= 120, nk
        dma_sems = [es.enter_context(nc.semaphore("d%d" % i)) for i in range(nk)]
        dcnt = [0] * nk
        for o in ops:
            if o["dma"]:
                k = o["dkey"]
                dcnt[k] += 16 * o["dma"]
                o["sigval"] = dcnt[k]
            elif o["sig"]:
                cnt[o["eng"]] += 1
                o["sigval"] = cnt[o["eng"]]
        per_eng = {e: [] for e in self.ENGS}
        for i, o in enumerate(ops):
            per_eng[o["eng"]].append(i)
        with es, nc.Block() as block:
            def run(engname, engobj):
                waited = {}
                for i in per_eng[engname]:
                    o = ops[i]
                    for d in sorted(o["deps"]):
                        p = ops[d]
                        if p["dma"]:
                            sem = dma_sems[p["dkey"]]
                            key = ("d", p["dkey"])
                        else:
                            if p["eng"] == "pe" and engname == "pe" and not o["dma"]:
                                continue
                            sem = eng_sem[p["eng"]]
                            key = ("e", p["eng"])
                        v = p["sigval"]
                        if waited.get(key, 0) >= v:
                            continue
                        waited[key] = v
                        engobj.wait_ge(sem, v)
                    if o["dma"]:
                        o["fn"](engobj, dma_sems[o["dkey"]])
                    else:
                        inst = o["fn"](engobj)
                        if o["sig"]:
                            inst.then_inc(eng_sem[engname], 1)
                for i in per_eng[engname]:
                    o = ops[i]
                    if o["dma"]:
                        key = ("d", o["dkey"])
                        if waited.get(key, 0) < o["sigval"]:
                            waited[key] = o["sigval"]
                            engobj.wait_ge(dma_sems[o["dkey"]], o["sigval"])

            @block.tensor
            def _(e):
                run("pe", e)

            @block.scalar
            def _(e):
                run("act", e)

            @block.vector
            def _(e):
                run("dve", e)

            @block.gpsimd
            def _(e):
                run("pool", e)

            @block.sync
            def _(e):
                run("sp", e)


class Arena:
    def __init__(self, nc, st, nbytes):
        self.t = st.enter_context(nc.sbuf_tensor("arena", [128, nbytes // 4], F32))
        self.cap = nbytes
        self.off = 0

    def reset(self, off=0):
        self.off = off

    def alloc(self, shape, dtype):
        esz = 4 if dtype == F32 else 2
        n = int(np.prod(shape[1:]))
        nb = (n * esz + 63) // 64 * 64
        assert self.off + nb <= self.cap, (self.off, nb, self.cap)
        w0 = self.off // 4
        ap = self.t[0:shape[0], w0:w0 + nb // 4]
        if dtype != F32:
            ap = ap.bitcast(dtype)
        ap = ap[:, 0:n]
        if len(shape) == 3:
            ap = ap.rearrange("p (a b) -> p a b", a=shape[1])
        elif len(shape) == 4:
            ap = ap.rearrange("p (a b c) -> p a b c", a=shape[1], b=shape[2])
        self.off += nb
        return ap


def MM(P, out, lhsT, rhs, start, stop, reads, writes):
    P.op("pe", lambda e: e.matmul(out, lhsT=lhsT, rhs=rhs, start=start, stop=stop), reads, writes)


def TR(P, out, in_, ident, reads, writes):
    P.op("pe", lambda e: e.transpose(out, in_, ident), reads, writes)


def ACT(P, out, in_, func, reads, writes, scale=None, bias=None, accum_out=None):
    kw = {}
    if scale is not None:
        kw["scale"] = scale
    if bias is not None:
        kw["bias"] = bias
    if accum_out is not None:
        kw["accum_out"] = accum_out
    P.op("act", lambda e: e.activation(out=out, in_=in_, func=func, **kw), reads, writes)


def TT(P, eng, out, in0, in1, op, reads, writes):
    P.op(eng, lambda e: e.tensor_tensor(out=out, in0=in0, in1=in1, op=op), reads, writes)


def TS(P, eng, out, in0, s1, s2, op0, op1, reads, writes):
    if op1 is None:
        P.op(eng, lambda e: e.tensor_scalar(out=out, in0=in0, scalar1=s1, scalar2=None, op0=op0), reads, writes)
    else:
        P.op(eng, lambda e: e.tensor_scalar(out=out, in0=in0, scalar1=s1, scalar2=s2, op0=op0, op1=op1), reads, writes)


def STT(P, out, in0, scalar, in1, op0, op1, reads, writes):
    P.op("dve", lambda e: e.scalar_tensor_tensor(out=out, in0=in0, scalar=scalar, in1=in1, op0=op0, op1=op1), reads, writes)


def CP(P, eng, out, in_, reads, writes):
    if eng == "act":
        P.op("act", lambda e: e.activation(out=out, in_=in_, func=AF.Identity), reads, writes)
    else:
        P.op(eng, lambda e: e.tensor_copy(out=out, in_=in_), reads, writes)


def DMA(P, q, out, in_, key, reads=(), writes=()):
    P.op(q, lambda e, s: e.dma_start(out=out, in_=in_).then_inc(s, 16), reads, writes, dma=1, key=key)


class Ctx:
    pass


def build(MODE, debug=False):
    nc = bass.Bass("TRN2", target_bir_lowering=False)
    C = Ctx()
    C.nc = nc
    C.NHM, C.NHD = 8, 4
    if MODE == "P4":
        C.NQ, C.NK, C.KOFF = 4096, 4096, 0
    else:
        C.NQ, C.NK, C.KOFF = 2048, 4096, 2048
    NQ, NK = C.NQ, C.NK
    TB = 1024
    C.TB = TB

    def din(name, shape):
        return nc.dram_tensor(name, list(shape), F32, kind="ExternalInput").ap()

    I = {}
    I["x"] = din("x", [NK, D])
    I["p"] = din("p", [NQ, 256])
    I["w_ffn1_up"] = din("w_ffn1_up", [D, 2 * DFF])
    I["w_ffn1_down"] = din("w_ffn1_down", [DFF, D])
    I["w_ffn2_up"] = din("w_ffn2_up", [D, 2 * DFF])
    I["w_ffn2_down"] = din("w_ffn2_down", [DFF, D])
    I["w_in"] = din("w_in", [D, 5120])
    I["w_branch_a"] = din("w_branch_a", [512, D])
    I["w_branch_b"] = din("w_branch_b", [512, D])
    I["w_out"] = din("w_out", [D, D])
    I["w_ple"] = din("w_ple", [256, D])
    I["w_ple_gate"] = din("w_ple_gate", [D, D])
    I["lnp"] = din("lnp", [128, 6, D])
    I["sublng"] = din("sublng", [128, 128])
    I["lamv"] = din("lamv", [128, 4, 64])
    I["ident"] = din("ident", [128, 128])
    I["tri"] = din("tri", [128, 128])
    I["qc"] = din("qc", [12, 4, NQ])
    I["kc"] = din("kc", [12, 4, NK])
    I["kind"] = din("kind", [16, NK])
    I["vb"] = din("vb", [128, NQ // 512, 4, 16])
    I["lo"] = din("lo", [128, NQ // 512, 4, 16])
    C.I = I
    kind_s = "ExternalOutput" if debug else "Internal"
    C.out = nc.dram_tensor("out", [NQ, D], F32, kind="ExternalOutput").ap()
    C.X1_s = nc.dram_tensor("X1_s", [NQ, D], F32, kind=kind_s).ap()
    C.X1T_s = nc.dram_tensor("X1T_s", [D, NQ], BF16, kind=kind_s).ap()
    C.QT_s = nc.dram_tensor("QT_s", [D, NQ], BF16, kind=kind_s).ap()
    C.KT_s = nc.dram_tensor("KT_s", [D, NK], BF16, kind=kind_s).ap()
    C.V_s = nc.dram_tensor("V_s", [NK, D], BF16, kind=kind_s).ap()
    C.YT_s = nc.dram_tensor("YT_s", [D, NQ], BF16, kind=kind_s).ap()

    with contextlib.ExitStack() as st:
        C.P = P = Prog(nc)
        C.arena = Arena(nc, st, 192 * 1024)
        C.ps = [st.enter_context(nc.psum_tensor("ps%d" % i, [128, 512], F32)) for i in range(8)]
        sb = lambda n, s, d: st.enter_context(nc.sbuf_tensor(n, s, d))
        C.ident_f = sb("ident_f", [128, 128], F32)
        C.ident_b = sb("ident_b", [128, 128], BF16)
        C.tri_b = sb("tri_b", [128, 128], BF16)
        C.eps_t = sb("eps_t", [128, 1], F32)
        C.lam_t = sb("lam_t", [128, 4], F32)
        C.lamv = sb("lamv_t", [128, 4, 64], F32)
        C.sublng = sb("sublng_t", [128, 128], F32)
        C.small = sb("small_t", [128, 64], F32)

        phase0(C)
        phase1(C)
        P.barrier()
        phase2(C)
        P.barrier()
        phase3(C)
        P.emit()
    return nc


def phase0(C):
    P, I = C.P, C.I
    DMA(P, "sp", C.ident_f[:], I["ident"], "c_if", writes=["ident_f"])
    DMA(P, "pool", C.ident_b[:], I["ident"], "c_ib", writes=["ident_b"])
    DMA(P, "pool", C.tri_b[:], I["tri"], "c_tri", writes=["tri_b"])
    DMA(P, "sp", C.lamv[:], I["lamv"], "c_lamv", writes=["lamv"])
    DMA(P, "sp", C.sublng[:], I["sublng"], "c_sub", writes=["sublng"])
    P.op("dve", lambda e: e.memset(C.eps_t[:], EPS), writes=["eps"])
    sm = C.small
    TT(P, "dve", sm[:, 0:64], C.lamv[:, 0, :], C.lamv[:, 1, :], ALU.mult, ["lamv"], ["sm0"])
    P.op("dve", lambda e: e.tensor_reduce(out=C.lam_t[:, 2:3], in_=sm[:, 0:64], axis=AX.X, op=ALU.add), ["sm0"], ["l2"])
    TT(P, "dve", sm[:, 0:64], C.lamv[:, 2, :], C.lamv[:, 3, :], ALU.mult, ["lamv", "l2"], ["sm0"])
    P.op("dve", lambda e: e.tensor_reduce(out=C.lam_t[:, 3:4], in_=sm[:, 0:64], axis=AX.X, op=ALU.add), ["sm0"], ["l3"])
    ACT(P, C.lam_t[:, 2:4], C.lam_t[:, 2:4], AF.Exp, ["l2", "l3"], ["l23"])
    TT(P, "dve", C.lam_t[:, 0:1], C.lam_t[:, 2:3], C.lam_t[:, 3:4], ALU.subtract, ["l23"], ["lam0"])
    TS(P, "dve", C.lam_t[:, 0:1], C.lam_t[:, 0:1], float(LAM_INIT), None, ALU.add, None, ["lam0"], ["lam"])
    TS(P, "dve", C.lam_t[:, 1:2], C.lam_t[:, 0:1], -1.0, None, ALU.mult, None, ["lam"], ["nlam"])


def wview(w, kcn):
    return w.rearrange("(kc p) n -> p kc n", p=128)


def emit_transposes(C, xs, xT, t, evac_eng, cast=True):
    P = C.P
    xi = t % len(C.xb)
    xb = C.xb[xi]
    xbn = ("xb", xi)
    if cast:
        CP(P, "act", xb[:, :], xs[:, t, :], [("xs", t)], [xbn])
    for g in range(2):
        bank = C.ps[6 + g]
        for k in range(4):
            kc = g * 4 + k
            MM(P, bank[:, k * 128:(k + 1) * 128], xb[:, kc * 128:(kc + 1) * 128], C.ident_b[:, :], True, True,
               [xbn, "ident_b"], [("ps", 6 + g)])
        src = bank[:, :].rearrange("p (a b) -> p a b", a=4)
        CP(P, evac_eng[g], xT[:, g * 4:(g + 1) * 4, t * 128:(t + 1) * 128], src, [("ps", 6 + g)], [("xT", t)])


def emit_layernorm(C, xs, t, ybanks, yscale, lng, lnb, tmp, lnname):
    P = C.P
    tT, r, xn, st6, mv = tmp
    for h in range(2):
        ACT(P, tT[:, h * 512:(h + 1) * 512], C.ps[ybanks[h]][:, :], AF.Identity, [("ps", ybanks[h])], [("tT", h)], scale=float(yscale))
    STT(P, r[:, :], xs[:, t, :], ALPHA, tT[:, :], ALU.mult, ALU.add, [("xs", t), ("tT", 0), ("tT", 1)], ["r"])
    for h in range(2):
        P.op("dve", (lambda hh: (lambda e: e.bn_stats(out=st6[:, hh, :], in_=r[:, hh * 512:(hh + 1) * 512])))(h), ["r"], [("st6", h)])
    P.op("dve", lambda e: e.bn_aggr(out=mv[:, 0:2], in_=st6[:, :, :].rearrange("p a b -> p (a b)")), [("st6", 0), ("st6", 1)], ["mv"])
    ACT(P, mv[:, 2:3], mv[:, 1:2], AF.Sqrt, ["mv", "eps"], ["std"], bias=C.eps_t[:], scale=1.0)
    P.op("dve", lambda e: e.reciprocal(out=mv[:, 3:4], in_=mv[:, 2:3]), ["std"], ["rstd"])
    STT(P, mv[:, 4:5], mv[:, 0:1], -1.0, mv[:, 3:4], ALU.mult, ALU.mult, ["mv", "rstd"], ["nmr"])
    ACT(P, xn[:, :], r[:, :], AF.Identity, ["r", "rstd", "nmr"], ["xn"], scale=mv[:, 3:4], bias=mv[:, 4:5])
    TT(P, "dve", xn[:, :], xn[:, :], lng, ALU.mult, ["xn", lnname], ["xn"])
    TT(P, "dve", xs[:, t, :], xn[:, :], lnb, ALU.add, ["xn", lnname], [("xs", t)])
    xi = t % len(C.xb)
    CP(P, "act", C.xb[xi][:, :], xs[:, t, :], [("xs", t)], [("xb", xi)])


def emit_ffn(C, xs, xT, actT, wd, wb, sil, w_up, w_down, tag, first):
    P = C.P
    TB = C.TB
    wu = wview(w_up, 8)
    if first:
        wdv = wview(w_down, NFF)
        DMA(P, "pool", wd[:, 0:11, :], wdv[:, 0:11, :], tag + "wd0", writes=[("wd", 0)])
        DMA(P, "pool", wd[:, 11:22, :], wdv[:, 11:22, :], tag + "wd1", writes=[("wd", 1)])
    par = 0
    for gi in range(NFF // 2):
        slot = gi % 2
        buf = wb[slot]
        DMA(P, "pool", buf[:, :, 0:256], wu[:, :, gi * 256:(gi + 1) * 256], "wbA%d" % slot, writes=[("wb", slot, 0)])
        DMA(P, "pool", buf[:, :, 256:512], wu[:, :, DFF + gi * 256:DFF + (gi + 1) * 256], "wbG%d" % slot, writes=[("wb", slot, 1)])
        for jj in range(2):
            j = gi * 2 + jj
            for hb in range(TB // 512):
                bA, bG = 2 * par, 2 * par + 1
                xr = [("xT", 4 * hb + q) for q in range(4)]
                for kc in range(8):
                    MM(P, C.ps[bA][:, :], buf[:, kc, jj * 128:(jj + 1) * 128], xT[:, kc, hb * 512:(hb + 1) * 512],
                       kc == 0, kc == 7, [("wb", slot, 0)] + xr, [("ps", bA)])
                for kc in range(8):
                    MM(P, C.ps[bG][:, :], buf[:, kc, 256 + jj * 128:256 + (jj + 1) * 128], xT[:, kc, hb * 512:(hb + 1) * 512],
                       kc == 0, kc == 7, [("wb", slot, 1)] + xr, [("ps", bG)])
                ACT(P, sil[par][:, :], C.ps[bA][:, :], AF.Silu, [("ps", bA)], [("sil", par)])
                TT(P, "dve", actT[:, j, hb * 512:(hb + 1) * 512], sil[par][:, :], C.ps[bG][:, :], ALU.mult,
                   [("sil", par), ("ps", bG)], [("actT", j, hb)])
                par ^= 1


def emit_down(C, actT, wd, t, yb=(4, 5)):
    P = C.P
    hb = t // 4
    for dh in range(2):
        for kc in range(NFF):
            MM(P, C.ps[yb[dh]][:, :], actT[:, kc, t * 128:(t + 1) * 128], wd[:, kc, dh * 512:(dh + 1) * 512],
               kc == 0, kc == NFF - 1, [("actT", kc, hb), ("wd", kc // 11)], [("ps", yb[dh])])


def phase1(C):
    P, I, A = C.P, C.I, C.arena
    TB = C.TB
    NTB = C.NK // TB
    own_tb0 = C.KOFF // TB
    A.reset()
    xs = A.alloc([128, 8, D], F32)
    xT = A.alloc([128, 8, TB], BF16)
    actT = A.alloc([128, NFF, TB], BF16)
    wd = A.alloc([128, NFF, D], BF16)
    wb = [A.alloc([128, 8, 512], BF16) for _ in range(2)]
    lnp = A.alloc([128, 2, D], F32)
    tT = A.alloc([128, D], F32)
    r = A.alloc([128, D], F32)
    xn = A.alloc([128, D], F32)
    st6 = A.alloc([128, 2, 6], F32)
    mv = A.alloc([128, 8], F32)
    sil = [A.alloc([128, 512], F32) for _ in range(2)]
    stg = [A.alloc([128, TB], BF16) for _ in range(2)]
    vst = [A.alloc([128, 512], BF16) for _ in range(2)]
    tmp = (tT, r, xn, st6, mv)
    C.xb = [A.alloc([128, D], BF16) for _ in range(2)]
    C.xbi = 0
    DMA(P, "sp", lnp, I["lnp"][:, 0:2, :], "lnp1", writes=["ln1"])
    xv = I["x"].rearrange("(t p) d -> p t d", p=128)
    x1v = C.X1_s.rearrange("(t p) d -> p t d", p=128)
    win = wview(I["w_in"], 8)
    for tb in range(NTB):
        own = tb >= own_tb0
        qtb = tb - own_tb0
        for t in range(8):
            DMA(P, "sp", xs[:, t, :], xv[:, tb * 8 + t, :], "xs%d" % t, writes=[("xs", t)])
        for t in range(8):
            emit_transposes(C, xs, xT, t, ("dve", "act"))
        emit_ffn(C, xs, xT, actT, wd, wb, sil, I["w_ffn1_up"], I["w_ffn1_down"], "f1", tb == 0)
        for t in range(8):
            yb = (4, 5) if (t % 2 == 0 or not FLAG_LNPIPE) else (2, 3)
            emit_down(C, actT, wd, t, yb)
            emit_layernorm(C, xs, t, yb, 0.5, lnp[:, 0, :], lnp[:, 1, :], tmp, "ln1")
            if own:
                DMA(P, "sp", x1v[:, qtb * 8 + t, :], xs[:, t, :], "x1st%d" % t, reads=[("xs", t)])
            if t >= 1:
                emit_transposes(C, xs, xT, t - 1, ("dve", "act"), cast=False)
        emit_transposes(C, xs, xT, 7, ("dve", "act"), cast=False)
        if own:
            for kc in range(8):
                DMA(P, "sp", C.X1T_s[kc * 128:(kc + 1) * 128, qtb * TB:(qtb + 1) * TB], xT[:, kc, :], "x1T%d" % kc,
                    reads=[("xT", q) for q in range(8)])
        par = 0
        for g in range(6):
            kind = ("q", "k", "v")[g % 3]
            fam = g // 3
            if kind == "q" and not own:
                continue
            slot = g % 2
            buf = wb[slot]
            DMA(P, "pool", buf[:, :, 0:256], win[:, :, g * 512:g * 512 + 256], "wbA%d" % slot, writes=[("wb", slot, 0)])
            DMA(P, "pool", buf[:, :, 256:512], win[:, :, g * 512 + 256:(g + 1) * 512], "wbG%d" % slot, writes=[("wb", slot, 1)])
            if kind in ("q", "k"):
                dst = C.QT_s if kind == "q" else C.KT_s
                col0 = qtb * TB if kind == "q" else tb * TB
                for c in range(4):
                    sg = stg[par]
                    for hb in range(TB // 512):
                        bank = 2 * par + hb
                        for kc in range(8):
                            MM(P, C.ps[bank][:, :], buf[:, kc, c * 128:(c + 1) * 128], xT[:, kc, hb * 512:(hb + 1) * 512],
                               kc == 0, kc == 7, [("wb", slot, c // 2)] + [("xT", 4 * hb + q) for q in range(4)], [("ps", bank)])
                        CP(P, "act" if hb == 0 else "dve", sg[:, hb * 512:(hb + 1) * 512], C.ps[bank][:, :], [("ps", bank)], [("stg", par, hb)])
                    row0 = fam * 512 + c * 128
                    DMA(P, "sp", dst[row0:row0 + 128, col0:col0 + TB], sg[:, :], "stg%d" % par,
                        reads=[("stg", par, 0), ("stg", par, 1)])
                    par ^= 1
            else:
                vv = C.V_s.rearrange("(t p) d -> p t d", p=128)
                for t in range(8):
                    bank = 2 * par
                    for kc in range(8):
                        MM(P, C.ps[bank][:, :], xT[:, kc, t * 128:(t + 1) * 128], buf[:, kc, :],
                           kc == 0, kc == 7, [("wb", slot, 0), ("wb", slot, 1), ("xT", t)], [("ps", bank)])
                    CP(P, "act" if t % 2 == 0 else "dve", vst[par][:, :], C.ps[bank][:, :], [("ps", bank)], [("vst", par)])
                    DMA(P, "sp", vv[:, tb * 8 + t, fam * 512:(fam + 1) * 512], vst[par][:, :], "vst%d" % par, reads=[("vst", par)])
                    par ^= 1


def phase2(C):
    P, I, A = C.P, C.I, C.arena
    A.reset()
    NHM, NHD = C.NHM, C.NHD
    NQ, NK, KOFF = C.NQ, C.NK, C.KOFF
    NQB = NQ // 512
    NKC = NK // 128
    KO = KOFF // 128
    KTs = [[A.alloc([128, NK], BF16) for m in range(2)] for _ in range(2)]
    QTs = [[A.alloc([128, NQ], BF16) for m in range(2)] for _ in range(2)]
    VEs = [A.alloc([128, NKC, 132], BF16) for _ in range(2)]
    PT = [A.alloc([128, 512], BF16) for _ in range(4)]
    SB = [0, 1, 5]
    accsb = [A.alloc([128, 4, 2, 132], F32) for _ in range(2)]
    o1 = A.alloc([128, 4, 128], F32)
    yd = A.alloc([128, 4, 128], F32)
    ydb = A.alloc([128, 4, 128], BF16)
    sq = A.alloc([128, 128], F32)
    sc = A.alloc([128, 4, 8], F32)
    ystg = [A.alloc([128, 512], BF16) for _ in range(2)]
    kmf = A.alloc([64, 16], F32)
    kmb = [A.alloc([64, 16], BF16) for _ in range(2)]
    vbt = A.alloc([128, NQB, 4, 16], F32)
    lot = A.alloc([128, NQB, 4, 16], F32)
    gm = [A.alloc([128, 4, 16], F32) for _ in range(2)]
    m8 = [A.alloc([128, 4, 8], F32) for _ in range(2)]
    m1 = [A.alloc([128, 4, 16], F32) for _ in range(2)]
    npad = [A.alloc([128, 4, 80], BF16) for _ in range(2)]
    ones_col = A.alloc([128, NKC, 1], F32)
    ones_k = A.alloc([128, 2], BF16)
    one_f = A.alloc([1, 2], F32)
    OTsb = [A.alloc([128, 512], BF16) for _ in range(2)]
    ZTsb = [A.alloc([1, 512], F32) for _ in range(2)]
    Zhi = [A.alloc([1, 512], BF16) for _ in range(2)]
    Zlo = [A.alloc([1, 512], BF16) for _ in range(2)]
    Zd = [A.alloc([1, 512], F32) for _ in range(2)]
    one_b = A.alloc([1, 2], BF16)
    P.op("dve", lambda e: e.memset(one_b[:, :], 1.0), writes=["one_b"])
    one_b64 = A.alloc([128, 2], BF16)
    P.op("dve", lambda e: e.memset(one_b64[:, :], 1.0), writes=["one_b64"])
    sel = [A.alloc([128, 128], BF16) for _ in range(2)]
    for mm_ in range(2):
        P.op("dve", (lambda mm_=mm_: (lambda e: e.memset(sel[mm_][:, 0:64], 1.0 if mm_ == 0 else 0.0)))(), writes=[("sel", mm_)])
        P.op("dve", (lambda mm_=mm_: (lambda e: e.memset(sel[mm_][:, 64:128], 0.0 if mm_ == 0 else 1.0)))(), writes=[("sel", mm_)])
    ZB_f = A.alloc([128, 512], F32)
    ZB_d = A.alloc([128, 512], F32)
    ZB_hi = A.alloc([128, 512], BF16)
    ZB_lo = A.alloc([128, 512], BF16)
    Zm_f = A.alloc([128, 512], F32)
    Zm_d = A.alloc([128, 512], F32)
    Zm_lo = A.alloc([128, 512], BF16)

    ones_kk = A.alloc([128, 128], BF16)
    P.op("dve", lambda e: e.memset(ones_kk[:, :], 1.0), writes=["ones_kk"])
    for par in range(2):
        for m in range(2):
            P.op("pool", (lambda par=par, m=m: (lambda e: e.memset(KTs[par][m][64:128, :], 0.0)))(), writes=[("KT", par, m)])
            P.op("pool", (lambda par=par, m=m: (lambda e: e.memset(QTs[par][m][64:128, :], 0.0)))(), writes=[("QT", par, m), ("QTm", par, m)])
    P.op("dve", lambda e: e.memset(ones_k[:, :], 1.0), writes=["ones_k"])
    P.op("dve", lambda e: e.memset(one_f[:, :], 1.0), writes=["one_f"])
    DMA(P, "sp", vbt, I["vb"], "vbt", writes=["vbt"])
    DMA(P, "sp", lot, I["lo"], "lot", writes=["lot"])
    for g in range(2):
        P.op("dve", (lambda g=g: (lambda e: e.memset(npad[g][:, :, :], 0.0)))(), writes=[("npad", g)])
    P.op("dve", lambda e: e.memset(ones_col[:, :, :], 1.0), writes=["ones_col"])
    vv = C.V_s.rearrange("(k p) d -> p k d", p=128)

    jobs = [("d", h) for h in range(NHD)] + [("m", h) for h in range(NHM)]

    def load_job(ji):
        kind, h = jobs[ji]
        par = ji % 2
        KT, QT, VE = KTs[par], QTs[par], VEs[par]
        if kind == "d":
            hg = NHM + h
            for m in range(2):
                r0 = 512 + h * 128 + m * 64
                DMA(P, "sp", KT[m][0:64, :], C.KT_s[r0:r0 + 64, :], "KTl%d%d" % (par, m), writes=[("KT", par, m)])
                DMA(P, "pool", KT[m][64:68, :], I["kc"][hg, :, :], "KTc%d%d" % (par, m), writes=[("KT", par, m)])
                DMA(P, "sp", QT[m][0:64, :], C.QT_s[r0:r0 + 64, :], "QTl%d%d" % (par, m), writes=[("QT", par, m)])
                DMA(P, "pool", QT[m][64:68, :], I["qc"][hg, :, :], "QTc%d%d" % (par, m), writes=[("QT", par, m)])
            DMA(P, "sp", VE[:, :, 0:128], vv[:, :, 512 + h * 128:512 + (h + 1) * 128], "VEl%d" % par, writes=[("VE", par)])
            CP(P, "dve", VE[:, :, 128:129], ones_col[:, :, :], ["ones_col"], [("VE", par)])
        else:
            r0 = h * 64
            DMA(P, "sp", KT[0][0:64, :], C.KT_s[r0:r0 + 64, :], "KTl%d0" % par, writes=[("KT", par, 0)])
            DMA(P, "pool", KT[0][64:80, :], I["kind"], "KTi%d" % par, writes=[("KT", par, 0)])
            DMA(P, "pool", KT[0][80:84, :], I["kc"][h, :, :], "KTc%d0" % par, writes=[("KT", par, 0)])
            DMA(P, "sp", QT[0][0:64, :], C.QT_s[r0:r0 + 64, :], "QTl%d0" % par, writes=[("QT", par, 0)])
            DMA(P, "pool", QT[0][80:84, :], I["qc"][h, :, :], "QTc%d0" % par, writes=[("QT", par, 0)])
            DMA(P, "sp", VE[:, :, 0:64], vv[:, :, h * 64:(h + 1) * 64], "VEl%d" % par, writes=[("VE", par)])
            CP(P, "dve", VE[:, :, 64:65], ones_col[:, :, :], ["ones_col"], [("VE", par)])

    def mask_stage(ji, tg):
        par = ji % 2
        KT, QT = KTs[par][0], QTs[par][0]
        if tg == 0:
            P.op("dve", lambda e: e.tensor_reduce(out=kmf[:, :], in_=KT[0:64, :].rearrange("p (n l) -> p n l", l=256), axis=AX.X, op=ALU.add),
                 [("KT", par, 0)], ["kmf"])
            TS(P, "dve", kmb[par][:, :], kmf[:, :], 1.0 / 256.0, None, ALU.mult, None, ["kmf"], [("kmb", par)])
        gp = tg % 2
        for tq in range(4):
            t = tg * 4 + tq
            MM(P, C.ps[7][:, tq * 16:(tq + 1) * 16], QT[0:64, t * 128:(t + 1) * 128], kmb[par][:, :], True, True,
               [("QT", par, 0), ("kmb", par)], [("ps", 7)])
        TT(P, "dve", gm[gp][:, :, :], C.ps[7][:, 0:64].rearrange("p (a b) -> p a b", a=4), vbt[:, tg, :, :], ALU.add,
           [("ps", 7), "vbt"], [("gm", gp)])
        for tq in range(4):
            P.op("dve", (lambda tq=tq: (lambda e: e.max(out=m8[gp][:, tq, :], in_=gm[gp][:, tq, :])))(), [("gm", gp)], [("m8", gp, tq)])
        for tq in range(4):
            TS(P, "dve", m1[gp][:, tq, :], gm[gp][:, tq, :], m8[gp][:, tq, 3:4], 1.0, ALU.is_ge, ALU.subtract,
               [("gm", gp), ("m8", gp, tq)], [("m1", gp, tq)])
        TT(P, "dve", npad[gp][:, :, 64:80], m1[gp][:, :, :], lot[:, tg, :, :], ALU.min,
           [("m1", gp, q) for q in range(4)] + ["lot"], [("npad", gp)])

    def mask_stage_b(ji, tg):
        par = ji % 2
        QT = QTs[par][0]
        gp = tg % 2
        for tq in range(4):
            MM(P, C.ps[6][0:80, tq * 128:(tq + 1) * 128], npad[gp][:, tq, :], C.ident_b[:, :], True, True,
               [("npad", gp), "ident_b"], [("ps", 6)])
        CP(P, "act", QT[64:80, tg * 512:(tg + 1) * 512], C.ps[6][64:80, :], [("ps", 6)], [("QTm", par, 0)])

    pending = []
    stepc = [0]

    dseq = [0]

    def defer(lag, fn):
        if not FLAG_DEFER:
            fn()
            return
        dseq[0] += 1
        pending.append((stepc[0] + lag, dseq[0], fn))

    def run_due(flush=False):
        while True:
            due = [p for p in pending if flush or p[0] <= stepc[0]]
            if not due:
                break
            due.sort()
            it = due[0]
            pending.remove(it)
            it[2]()

    def attention(ji, nmaps, dv, K, finalize, side):
        par = ji % 2
        KT, QT, VE = KTs[par], QTs[par], VEs[par]
        steps = [(qb, m, kc) for qb in range(NQB) for m in range(nmaps) for kc in range(KO + 4 * qb + 4)]
        n = len(steps)

        def emit_S(i):
            qb, m, kc = steps[i]
            j = kc - KO - 4 * qb
            n0 = max(0, j) * 128
            sb = SB[i % 3]
            S = C.ps[sb]
            MM(P, S[:, n0:512], KT[m][:, kc * 128:(kc + 1) * 128], QT[m][:, qb * 512 + n0:(qb + 1) * 512],
               True, j < 0, [("KT", par, m), ("QT", par, m), ("QTm", par, m)], [("ps", sb)])
            if j >= 0:
                MM(P, S[:, n0:n0 + 128], C.ident_b[:, :], C.tri_b[:, :], False, True, ["ident_b", "tri_b"], [("ps", sb)])
            ACT(P, PT[i % 4][:, n0:512], S[:, n0:512], AF.Exp, [("ps", sb)], [("PT", i % 4)], scale=0.125)

        def emit_AV(i):
            qb, m, kc = steps[i]
            j = kc - KO - 4 * qb
            n0 = max(0, j) * 128
            last = KO + 4 * qb + 3
            pt = PT[i % 4]
            if nmaps == 2:
                MM(P, C.ps[2 + m][:, n0:512], VE[:, kc, 0:128], pt[:, n0:512], kc == 0, kc == last,
                   [("PT", i % 4), ("VE", par)], [("OT", m)])
                MM(P, C.ps[4][:, n0:512], sel[m][:, :], pt[:, n0:512], (m == 0 and kc == 0), (m == 1 and kc == last),
                   [("PT", i % 4), ("sel", m)], [("ZT", 0)])
            else:
                MM(P, C.ps[2][:, n0:512], VE[:, kc, 0:128], pt[:, n0:512], kc == 0, kc == last,
                   [("PT", i % 4), ("VE", par)], [("OT", 0)])
            if m == nmaps - 1 and kc == last:
                q2 = qb % 2
                if nmaps == 2:
                    for mm in range(2):
                        CP(P, "dve", OTsb[mm][:, :], C.ps[2 + mm][:, :], [("OT", mm)], [("OTsb", mm)])
                    CP(P, "dve", ZTsb[0][0:1, :], C.ps[4][0:1, :], [("ZT", 0)], [("ZTsb", 0)])
                    CP(P, "dve", ZB_f[64:65, :], C.ps[4][64:65, :], [("ZT", 0)], [("ZTsb", 1)])
                    CP(P, "dve", Zhi[0][0:1, :], ZTsb[0][0:1, :], [("ZTsb", 0)], [("Zhi", 0)])
                    TT(P, "dve", Zd[0][0:1, :], ZTsb[0][0:1, :], Zhi[0][0:1, :], ALU.subtract, [("ZTsb", 0), ("Zhi", 0)], [("Zd", 0)])
                    CP(P, "dve", Zlo[0][0:1, :], Zd[0][0:1, :], [("Zd", 0)], [("Zlo", 0)])
                    CP(P, "dve", ZB_hi[64:65, :], ZB_f[64:65, :], [("ZTsb", 1)], [("Zhi", 1)])
                    TT(P, "dve", ZB_d[64:65, :], ZB_f[64:65, :], ZB_hi[64:65, :], ALU.subtract, [("ZTsb", 1), ("Zhi", 1)], [("Zd", 1)])
                    CP(P, "dve", ZB_lo[64:65, :], ZB_d[64:65, :], [("Zd", 1)], [("Zlo", 1)])
                else:
                    CP(P, "dve", OTsb[0][0:65, :], C.ps[2][0:65, :], [("OT", 0)], [("OTsb", 0)])
                    CP(P, "dve", Zm_f[64:65, :], C.ps[2][64:65, :], [("OT", 0)], [("Zm_f", 0)])
                    TT(P, "dve", Zm_d[64:65, :], Zm_f[64:65, :], OTsb[0][64:65, :], ALU.subtract, [("Zm_f", 0), ("OTsb", 0)], [("Zm_d", 0)])
                    CP(P, "dve", Zm_lo[64:65, :], Zm_d[64:65, :], [("Zm_d", 0)], [("Zm_lo", 0)])

                def stage2(qb=qb, q2=q2):
                    if nmaps == 2:
                        for mm in range(2):
                            for sub in range(4):
                                MM(P, C.ps[6][:, sub * 128:(sub + 1) * 128], OTsb[mm][:, sub * 128:(sub + 1) * 128], C.ident_b[:, :], True, True,
                                   [("OTsb", mm), "ident_b"], [("ps", 6)])
                            CP(P, "dve", accsb[q2][:, 2 * mm:2 * mm + 2, :, 0:128],
                               C.ps[6][:, :].rearrange("p (a b c) -> p a b c", a=2, b=2), [("ps", 6)], [("accsb", q2, 2 * mm), ("accsb", q2, 2 * mm + 1)])
                            for sub in range(4):
                                zh = Zhi[0][0:1, sub * 128:(sub + 1) * 128] if mm == 0 else ZB_hi[64:65, sub * 128:(sub + 1) * 128]
                                zl = Zlo[0][0:1, sub * 128:(sub + 1) * 128] if mm == 0 else ZB_lo[64:65, sub * 128:(sub + 1) * 128]
                                ob = one_b[0:1, 0:1] if mm == 0 else one_b64[64:65, 0:1]
                                MM(P, C.ps[7][:, 64 + mm * 4 + sub:64 + mm * 4 + sub + 1], zh, ob,
                                   True, False, [("Zhi", mm), "one_b", "one_b64"], [("ps", 7)])
                                MM(P, C.ps[7][:, 64 + mm * 4 + sub:64 + mm * 4 + sub + 1], zl, ob,
                                   False, True, [("Zlo", mm), "one_b", "one_b64"], [("ps", 7)])
                            CP(P, "dve", accsb[q2][:, 2 * mm:2 * mm + 2, :, 128:129],
                               C.ps[7][:, 64 + mm * 4:64 + mm * 4 + 4].rearrange("p (a b c) -> p a b c", a=2, b=2), [("ps", 7)],
                               [("accsb", q2, 2 * mm), ("accsb", q2, 2 * mm + 1)])
                    else:
                        for sub in range(4):
                            MM(P, C.ps[6][:, sub * 128:sub * 128 + 65], OTsb[0][0:65, sub * 128:(sub + 1) * 128], C.ident_b[0:65, 0:65], True, False,
                               [("OTsb", 0), "ident_b"], [("ps", 6)])
                            MM(P, C.ps[6][:, sub * 128 + 64:sub * 128 + 65], Zm_lo[64:65, sub * 128:(sub + 1) * 128], one_b64[64:65, 0:1], False, True,
                               [("Zm_lo", 0), "one_b64"], [("ps", 6)])
                        CP(P, "dve", accsb[q2][:, 0:2, :, 0:65],
                           C.ps[6][:, :].rearrange("p (a b c) -> p a b c", a=2, b=2)[:, :, :, 0:65], [("ps", 6)], [("accsb", q2, 0), ("accsb", q2, 1)])
                    finalize[0][1](qb)

                if FLAG_S2DEFER:
                    defer(2, stage2)
                else:
                    stage2()
                for _f in range(FLAG_BURST):
                    MM(P, C.ps[SB[i % 3]][:, :], C.ident_b[:, :], KT[0][:, 0:512], True, True, [("KT", par, 0)], [("ps", SB[i % 3])])
                for lag, fn in finalize[1:]:
                    defer(lag, (lambda qb=qb, fn=fn: fn(qb)))
                side(qb)

        emit_S(0)
        emit_S(1)
        for i in range(n):
            if i + 2 < n:
                emit_S(i + 2)
            emit_AV(i)
            stepc[0] += 1
            run_due()
        run_due(flush=True)

    def fin_diff(h, qb):
        q2 = qb % 2
        for sub in range(4):
            a1 = accsb[q2][:, sub // 2, sub % 2, :]
            a2 = accsb[q2][:, 2 + sub // 2, sub % 2, :]
            r1 = [("accsb", q2, sub // 2)]
            r2 = [("accsb", q2, 2 + sub // 2)]
            s = sc[:, sub, :]
            P.op("dve", (lambda s=s, a1=a1: (lambda e: e.reciprocal(out=s[:, 0:1], in_=a1[:, 128:129])))(), r1, [("sc", sub, 0)])
            P.op("dve", (lambda s=s, a2=a2: (lambda e: e.reciprocal(out=s[:, 1:2], in_=a2[:, 128:129])))(), r2, [("sc", sub, 1)])
            TT(P, "dve", s[:, 2:3], s[:, 1:2], C.lam_t[:, 1:2], ALU.mult, [("sc", sub, 1), "nlam"], [("sc", sub, 2)])
            TS(P, "dve", o1[:, sub, :], a1[:, 0:128], s[:, 0:1], None, ALU.mult, None, r1 + [("sc", sub, 0)], [("o1", sub)])
            STT(P, yd[:, sub, :], a2[:, 0:128], s[:, 2:3], o1[:, sub, :], ALU.mult, ALU.add,
                r2 + [("sc", sub, 2), ("o1", sub)], [("yd", sub)])
            TT(P, "dve", sq[:, :], yd[:, sub, :], yd[:, sub, :], ALU.mult, [("yd", sub)], ["sq"])
            P.op("dve", (lambda s=s: (lambda e: e.tensor_reduce(out=s[:, 3:4], in_=sq[:, :], axis=AX.X, op=ALU.add)))(), ["sq"], [("sc", sub, 3)])

    def fin_diff_act(h, qb):
        ACT(P, sc[:, :, 4:5], sc[:, :, 3:4], AF.Ln, [("sc", sub, 3) for sub in range(4)] + ["eps"], [("sc", sub, 4) for sub in range(4)],
            bias=C.eps_t[:], scale=1.0 / 128.0)
        ACT(P, sc[:, :, 5:6], sc[:, :, 4:5], AF.Exp, [("sc", sub, 4) for sub in range(4)], [("sc", sub, 5) for sub in range(4)], scale=-0.5)

    def fin_diff_a2(h, qb):
        for sub in range(4):
            s = sc[:, sub, :]
            TS(P, "dve", yd[:, sub, :], yd[:, sub, :], s[:, 5:6], float(1.0 - LAM_INIT), ALU.mult, ALU.mult, [("yd", sub), ("sc", sub, 5)], [("yd", sub)])
            TT(P, "dve", ydb[:, sub, :], yd[:, sub, :], C.sublng[:, :], ALU.mult, [("yd", sub), "sublng"], [("ydb", sub)])

    def fin_diff_b(h, qb):
        for sub in range(4):
            MM(P, C.ps[6][:, sub * 128:(sub + 1) * 128], ydb[:, sub, :], C.ident_b[:, :], True, True,
               [("ydb", sub), "ident_b"], [("ps", 6)])
        sl = qb % 2
        CP(P, "dve", ystg[sl][:, :], C.ps[6][:, :], [("ps", 6)], [("ystg", sl)])
        DMA(P, "sp", C.YT_s[512 + h * 128:512 + (h + 1) * 128, qb * 512:(qb + 1) * 512], ystg[sl][:, :], "ystg%d" % sl, reads=[("ystg", sl)])

    def fin_moba(h, qb):
        q2 = qb % 2
        sl = qb % 2
        for sub in range(4):
            a1 = accsb[q2][:, sub // 2, sub % 2, :]
            r1 = [("accsb", q2, sub // 2)]
            s = sc[:, sub, :]
            P.op("dve", (lambda s=s, a1=a1: (lambda e: e.reciprocal(out=s[:, 0:1], in_=a1[:, 64:65])))(), r1, [("sc", sub, 0)])
            TS(P, "dve", ydb[:, sub, 0:64], a1[:, 0:64], s[:, 0:1], None, ALU.mult, None, r1 + [("sc", sub, 0)], [("ydb", sub)])

    def fin_moba_b(h, qb):
        sl = qb % 2
        for sub in range(4):
            MM(P, C.ps[6][0:64, sub * 128:(sub + 1) * 128], ydb[:, sub, 0:64], C.ident_b[:, :], True, True,
               [("ydb", sub), "ident_b"], [("ps", 6)])
        CP(P, "dve", ystg[sl][0:64, :], C.ps[6][0:64, :], [("ps", 6)], [("ystg", sl)])
        DMA(P, "sp", C.YT_s[h * 64:(h + 1) * 64, qb * 512:(qb + 1) * 512], ystg[sl][0:64, :], "ystg%d" % sl, reads=[("ystg", sl)])

    load_job(0)
    if jobs[0][0] == "m":
        for tg in range(NQB):
            mask_stage(0, tg)
            mask_stage_b(0, tg)
    for ji, (kind, h) in enumerate(jobs):
        nxt = ji + 1 if ji + 1 < len(jobs) else None
        if nxt is not None:
            load_job(nxt)

        def side(qb, nxt=nxt):
            if nxt is not None and jobs[nxt][0] == "m":
                defer(4, (lambda: mask_stage(nxt, qb)))
                defer(11, (lambda: mask_stage_b(nxt, qb)))

        if kind == "d":
            attention(ji, 2, 128, 68, [(0, (lambda qb, h=h: fin_diff(h, qb))), (9, (lambda qb, h=h: fin_diff_act(h, qb))),
                                       (11, (lambda qb, h=h: fin_diff_a2(h, qb))), (18, (lambda qb, h=h: fin_diff_b(h, qb)))], side)
        else:
            attention(ji, 1, 64, 84, [(0, (lambda qb, h=h: fin_moba(h, qb))), (6, (lambda qb, h=h: fin_moba_b(h, qb)))], side)


def phase3(C):
    P, I, A = C.P, C.I, C.arena
    TB = C.TB
    NTB = C.NQ // TB
    A.reset()
    xs = A.alloc([128, 8, D], F32)
    xT = A.alloc([128, 8, TB], BF16)
    wb = [A.alloc([128, 8, 512], BF16) for _ in range(2)]
    lnp = A.alloc([128, 4, D], F32)
    tT = A.alloc([128, D], F32)
    r = A.alloc([128, D], F32)
    xn = A.alloc([128, D], F32)
    st6 = A.alloc([128, 2, 6], F32)
    mv = A.alloc([128, 8], F32)
    sil = [A.alloc([128, 512], F32) for _ in range(2)]
    sg = [A.alloc([128, 512], BF16) for _ in range(2)]
    pt_f = [A.alloc([128, 256], F32) for _ in range(1)]
    C.xb = [A.alloc([128, D], BF16) for _ in range(2)]
    C.xbi = 0
    tmp = (tT, r, xn, st6, mv)
    base = A.off
    mergedT = A.alloc([128, 8, TB], BF16)
    yT = A.alloc([128, 8, TB], BF16)
    wbr = A.alloc([128, 2, 4, D], BF16)
    wout = A.alloc([128, 8, D], BF16)
    A.reset(base)
    actT = A.alloc([128, NFF, TB], BF16)
    wd = A.alloc([128, NFF, D], BF16)
    A.reset(base)
    wpg = A.alloc([128, 8, D], BF16)
    wpl = A.alloc([128, 2, D], BF16)
    pT = A.alloc([128, 2, TB], BF16)

    DMA(P, "sp", lnp, I["lnp"][:, 2:6, :], "lnp3", writes=["ln2", "ln3"])
    x1v = C.X1_s.rearrange("(t p) d -> p t d", p=128)
    outv = C.out.rearrange("(t p) d -> p t d", p=128)
    pv = I["p"].rearrange("(t p) d -> p t d", p=128)
    win = wview(I["w_in"], 8)
    for tb in range(NTB):
        tok0 = tb * TB
        for t in range(8):
            DMA(P, "sp", xs[:, t, :], x1v[:, tb * 8 + t, :], "xs%d" % t, writes=[("xs", t)])
        for kc in range(8):
            DMA(P, "sp", xT[:, kc, :], C.X1T_s[kc * 128:(kc + 1) * 128, tb * TB:(tb + 1) * TB], "xTl%d" % kc, writes=[("xT", q) for q in range(8)])
            DMA(P, "sp", yT[:, kc, :], C.YT_s[kc * 128:(kc + 1) * 128, tok0:tok0 + TB], "yTl%d" % kc, writes=[("yT", kc)])
        DMA(P, "pool", wbr[:, 0, :, :], wview(I["w_branch_a"], 4), "wbra", writes=["wbra"])
        DMA(P, "pool", wbr[:, 1, :, :], wview(I["w_branch_b"], 4), "wbrb", writes=["wbrb"])
        DMA(P, "pool", wout, wview(I["w_out"], 8), "wout", writes=["wout"])
        for oc in range(8):
            slot = oc % 2
            buf = wb[slot]
            DMA(P, "pool", buf[:, :, 0:128], win[:, :, 3072 + oc * 128:3072 + (oc + 1) * 128], "wbA%d" % slot, writes=[("wb", slot, 0)])
            DMA(P, "pool", buf[:, :, 128:256], win[:, :, 4096 + oc * 128:4096 + (oc + 1) * 128], "wbG%d" % slot, writes=[("wb", slot, 1)])
            for hb in range(TB // 512):
                xr = [("xT", 4 * hb + q) for q in range(4)]
                ip = (oc * 2 + hb) % 2 if FLAG_MERGEDB else 0
                for br in range(2):
                    bank = 4 * ip + br
                    for kc in range(8):
                        MM(P, C.ps[bank][:, :], buf[:, kc, br * 128:(br + 1) * 128], xT[:, kc, hb * 512:(hb + 1) * 512],
                           kc == 0, kc == 7, [("wb", slot, br)] + xr, [("ps", bank)])
                    ACT(P, sg[br][:, :], C.ps[bank][:, :], AF.Sigmoid, [("ps", bank)], [("sg", br)])
                    bank2 = 4 * ip + 2 + br
                    for kc in range(4):
                        MM(P, C.ps[bank2][:, :], wbr[:, br, kc, oc * 128:(oc + 1) * 128], yT[:, br * 4 + kc, hb * 512:(hb + 1) * 512],
                           kc == 0, kc == 3, ["wbra" if br == 0 else "wbrb", ("yT", br * 4 + kc)], [("ps", bank2)])
                    TT(P, "dve", sil[br][:, :], sg[br][:, :], C.ps[bank2][:, :], ALU.mult,
                       [("sg", br), ("ps", bank2)], [("sil", br)])
                TT(P, "dve", mergedT[:, oc, hb * 512:(hb + 1) * 512], sil[0][:, :], sil[1][:, :], ALU.add,
                   [("sil", 0), ("sil", 1)], [("mergedT", oc, hb)])
        for t in range(8):
            hb = t // 4
            yb = (4, 5) if (t % 2 == 0 or not FLAG_LNPIPE) else (2, 3)
            for dh in range(2):
                for kc in range(8):
                    MM(P, C.ps[yb[dh]][:, :], mergedT[:, kc, t * 128:(t + 1) * 128], wout[:, kc, dh * 512:(dh + 1) * 512],
                       kc == 0, kc == 7, [("mergedT", kc, hb), "wout"], [("ps", yb[dh])])
            emit_layernorm(C, xs, t, yb, 1.0, lnp[:, 0, :], lnp[:, 1, :], tmp, "ln2")
            if t >= 1:
                emit_transposes(C, xs, xT, t - 1, ("dve", "act"), cast=False)
        emit_transposes(C, xs, xT, 7, ("dve", "act"), cast=False)
        P.barrier()
        emit_ffn(C, xs, xT, actT, wd, wb, sil, I["w_ffn2_up"], I["w_ffn2_down"], "f2", True)
        for t in range(8):
            yb = (4, 5) if (t % 2 == 0 or not FLAG_LNPIPE) else (2, 3)
            emit_down(C, actT, wd, t, yb)
            emit_layernorm(C, xs, t, yb, 0.5, lnp[:, 2, :], lnp[:, 3, :], tmp, "ln3")
            if t >= 1:
                emit_transposes(C, xs, xT, t - 1, ("dve", "act"), cast=False)
        emit_transposes(C, xs, xT, 7, ("dve", "act"), cast=False)
        P.barrier()
        DMA(P, "pool", wpg, wview(I["w_ple_gate"], 8), "wpg", writes=["wpg"])
        DMA(P, "pool", wpl, wview(I["w_ple"], 2), "wpl", writes=["wpl"])
        for t in range(8):
            pf = pt_f[0]
            DMA(P, "sp", pf, pv[:, tb * 8 + t, :], "pf0", writes=[("pf", 0)])
            pb = C.xb[0][:, 0:256]
            CP(P, "pool", pb, pf[:, :], [("pf", 0)], [("xb", 0)])
            for k in range(2):
                MM(P, C.ps[6][:, k * 128:(k + 1) * 128], pb[:, k * 128:(k + 1) * 128], C.ident_b[:, :], True, True, [("xb", 0), "ident_b"], [("ps", 6)])
            CP(P, "dve", pT[:, :, t * 128:(t + 1) * 128], C.ps[6][:, 0:256].rearrange("p (a b) -> p a b", a=2), [("ps", 6)], [("pT", t)])
            for dh in range(2):
                for kc in range(8):
                    MM(P, C.ps[dh][:, :], xT[:, kc, t * 128:(t + 1) * 128], wpg[:, kc, dh * 512:(dh + 1) * 512],
                       kc == 0, kc == 7, [("xT", t), "wpg"], [("ps", dh)])
                ACT(P, tT[:, dh * 512:(dh + 1) * 512], C.ps[dh][:, :], AF.Sigmoid, [("ps", dh)], [("tT", dh)])
                for kc in range(2):
                    MM(P, C.ps[2 + dh][:, :], pT[:, kc, t * 128:(t + 1) * 128], wpl[:, kc, dh * 512:(dh + 1) * 512],
                       kc == 0, kc == 1, [("pT", t), "wpl"], [("ps", 2 + dh)])
                TT(P, "dve", sil[dh][:, :], tT[:, dh * 512:(dh + 1) * 512], C.ps[2 + dh][:, :], ALU.mult, [("tT", dh), ("ps", 2 + dh)], [("sil", dh)])
                TT(P, "dve", xn[:, dh * 512:(dh + 1) * 512], sil[dh][:, :], xs[:, t, dh * 512:(dh + 1) * 512], ALU.add,
                   [("sil", dh), ("xs", t)], [("xo", dh)])
            DMA(P, "sp", outv[:, tb * 8 + t, :], xn[:, :], "outst", reads=[("xo", 0), ("xo", 1)])
        P.barrier()


def host_consts(mode, r):
    c = {}
    c["ident"] = np.eye(128, dtype=np.float32)
    ki = np.arange(128)[:, None]
    qi = np.arange(128)[None, :]
    c["tri"] = np.where(ki > qi, -BIGM, 0.0).astype(np.float32)
    slopes = np.concatenate([2.0 ** (-8.0 * np.arange(1, 9) / 8), 2.0 ** (-8.0 * np.arange(1, 5) / 4)]).astype(np.float64)
    if mode == "P4":
        NQ, NK = 4096, 4096
        qpos = np.arange(NQ)
        kpos = np.arange(NK)
        kvalid = np.ones(NK, bool)
    else:
        NQ, NK = 2048, 4096
        qpos = r * 2048 + np.arange(NQ)
        kpos = np.concatenate([np.arange(2048), r * 2048 + np.arange(2048)])
        kvalid = np.concatenate([np.full(2048, r == 1), np.ones(2048, bool)])
    qc = np.zeros((12, 4, NQ), np.float32)
    kc = np.zeros((12, 4, NK), np.float32)
    for h in range(12):
        s8 = 8.0 * slopes[h]
        qc[h, 0] = -s8 * (qpos % 128)
        qc[h, 1] = -s8 * 128.0 * (qpos // 128)
        qc[h, 2] = 1.0
        qc[h, 3] = 1.0
        kc[h, 0] = 1.0
        kc[h, 1] = 1.0
        kc[h, 2] = s8 * (kpos % 128)
        kc[h, 3] = np.where(kvalid, s8 * 128.0 * (kpos // 128), -BIGM)
    c["qc"], c["kc"] = qc, kc
    kind = np.zeros((16, NK), np.float32)
    for n in range(16):
        kind[n, n * 256:(n + 1) * 256] = BIGM
    c["kind"] = kind
    nt = NQ // 128
    vb = np.zeros((nt, 16), np.float32)
    lo = np.zeros((nt, 16), np.float32)
    for t in range(nt):
        b_loc = t // 2
        for n in range(16):
            if mode == "P4":
                st = "past" if n < b_loc else ("own" if n == b_loc else "no")
            else:
                if n < 8:
                    st = "past" if r == 1 else "no"
                else:
                    nn = n - 8
                    st = "past" if nn < b_loc else ("own" if nn == b_loc else "no")
            vb[t, n] = {"past": 0.0, "own": 1e30, "no": -1e30}[st]
            lo[t, n] = 0.0 if st in ("past", "own") else -1.0
    c["vb"] = np.ascontiguousarray(np.broadcast_to(vb.reshape(1, nt // 4, 4, 16), (128, nt // 4, 4, 16)))
    c["lo"] = np.ascontiguousarray(np.broadcast_to(lo.reshape(1, nt // 4, 4, 16), (128, nt // 4, 4, 16)))
    return c


_CACHE = {}
MODE = "P8R"


def kernel(**inputs):
    f = lambda k: np.ascontiguousarray(np.asarray(inputs[k], dtype=np.float32)[0])
    mode = MODE
    if mode not in _CACHE:
        _CACHE[mode] = build(mode)
    nc = _CACHE[mode]
    x = np.asarray(inputs["x"], dtype=np.float32)
    p = np.asarray(inputs["p"], dtype=np.float32)[0]
    shared = {k: f(k) for k in ("w_ffn1_up", "w_ffn1_down", "w_ffn2_up", "w_ffn2_down", "w_in", "w_branch_a",
                                "w_branch_b", "w_out", "w_ple", "w_ple_gate")}
    lnp = np.stack([f(k) for k in ("ln1_g", "ln1_b", "ln2_g", "ln2_b", "ln3_g", "ln3_b")], 0)
    shared["lnp"] = np.ascontiguousarray(np.broadcast_to(lnp[None], (128, 6, D)))
    shared["sublng"] = np.ascontiguousarray(np.broadcast_to(f("subln_g")[None], (128, 128)))
    lamv = np.stack([f(k) for k in ("lambda_q1", "lambda_k1", "lambda_q2", "lambda_k2")], 0)
    shared["lamv"] = np.ascontiguousarray(np.broadcast_to(lamv[None], (128, 4, 64)))
    consts = {}
    in_maps = []
    for c in range(N_CORES):
        m = dict(shared)
        if mode == "P4":
            b, r = c % 4, 0
            m["x"] = np.ascontiguousarray(x[b])
            m["p"] = np.ascontiguousarray(p[b])
        else:
            b, r = c // 2, c % 2
            m["x"] = np.ascontiguousarray(np.concatenate([x[b, 0:2048], x[b, r * 2048:(r + 1) * 2048]], 0))
            m["p"] = np.ascontiguousarray(p[b, r * 2048:(r + 1) * 2048])
        if r not in consts:
            consts[r] = host_consts(mode, r)
        m.update(consts[r])
        in_maps.append(m)
    res = run_bass_kernel_spmd(nc, in_maps, core_ids=list(range(N_CORES)))
    if mode == "P4":
        out = np.stack([np.asarray(res.results[b]["out"], dtype=np.float32) for b in range(4)], 0)
    else:
        out = np.stack([np.concatenate([np.asarray(res.results[2 * b + r]["out"], dtype=np.float32) for r in range(2)], 0)
                        for b in range(4)], 0)
    return out
```

```python
import contextlib
import numpy as np
import concourse.bass as bass
import concourse.mybir as mybir
from concourse.bass_utils import run_bass_kernel_spmd

F32 = mybir.dt.float32
BF16 = mybir.dt.bfloat16
AF = mybir.ActivationFunctionType
ALU = mybir.AluOpType
AX = mybir.AxisListType

D = 1024
DFF = 2816
SEQ = 4096
NFF = 22
ALPHA = float(2.0 ** 0.25)
BIGM = 262144.0
EPS = 1e-5
LAM_INIT = 0.8 - 0.6 * 1.0
N_CORES = 8
import os
FLAG_DEFER = os.environ.get("K_DEFER", "1") == "1"
FLAG_S2DEFER = os.environ.get("K_S2DEFER", "1") == "1"
FLAG_BURST = int(os.environ.get("K_BURST", "10"))
FLAG_LNPIPE = os.environ.get("K_LNPIPE", "1") == "1"
FLAG_MERGEDB = os.environ.get("K_MERGEDB", "1") == "1"


class Prog:
    ENGS = ("pe", "act", "dve", "pool", "sp")

    def __init__(self, nc):
        self.nc = nc
        self.ops = []
        self.last_w = {}
        self.readers = {}
        self.dma_keys = {}
        self.last_on_eng = {}
        self.dma_since_barrier = []
        self.barrier_deps = {}

    def op(self, eng, fn, reads=(), writes=(), dma=0, key=None):
        idx = len(self.ops)
        deps = set()
        for r in reads:
            w = self.last_w.get(r)
            if w is not None:
                deps.add(w)
        for r in writes:
            w = self.last_w.get(r)
            if w is not None:
                deps.add(w)
            for rd in self.readers.get(r, {}).values():
                deps.add(rd)
        for r in reads:
            d = self.readers.setdefault(r, {})
            d[eng if not dma else ("dma", idx)] = idx
        for r in writes:
            self.last_w[r] = idx
            self.readers[r] = {}
        bd = self.barrier_deps.pop(eng, None)
        if bd:
            deps.update(bd)
        deps.discard(idx)
        o = dict(eng=eng, fn=fn, deps=deps, dma=dma, sig=False)
        if dma:
            assert key is not None
            if key not in self.dma_keys:
                self.dma_keys[key] = len(self.dma_keys)
            o["dkey"] = self.dma_keys[key]
            self.dma_since_barrier.append(idx)
        else:
            self.last_on_eng[eng] = idx
        self.ops.append(o)
        return idx

    def barrier(self):
        deps = set(self.dma_since_barrier)
        deps.update(self.last_on_eng.values())
        self.dma_since_barrier = []
        self.barrier_deps = {e: set(deps) for e in self.ENGS}

    def emit(self):
        nc = self.nc
        ops = self.ops
        for o in ops:
            for d in o["deps"]:
                p = ops[d]
                if p["eng"] == "pe" and o["eng"] == "pe" and not p["dma"] and not o["dma"]:
                    continue
                p["sig"] = True
        es = contextlib.ExitStack()
        eng_sem = {e: es.enter_context(nc.semaphore("s_" + e)) for e in self.ENGS}
        cnt = {e: 0 for e in self.ENGS}
        nk = len(self.dma_keys)
        assert nk <= 120, nk
        dma_sems = [es.enter_context(nc.semaphore("d%d" % i)) for i in range(nk)]
        dcnt = [0] * nk
        for o in ops:
            if o["dma"]:
                k = o["dkey"]
                dcnt[k] += 16 * o["dma"]
                o["sigval"] = dcnt[k]
            elif o["sig"]:
                cnt[o["eng"]] += 1
                o["sigval"] = cnt[o["eng"]]
        per_eng = {e: [] for e in self.ENGS}
        for i, o in enumerate(ops):
            per_eng[o["eng"]].append(i)
        with es, nc.Block() as block:
            def run(engname, engobj):
                waited = {}
                for i in per_eng[engname]:
                    o = ops[i]
                    for d in sorted(o["deps"]):
                        p = ops[d]
                        if p["dma"]:
                            sem = dma_sems[p["dkey"]]
                            key = ("d", p["dkey"])
                        else:
                            if p["eng"] == "pe" and engname == "pe" and not o["dma"]:
                                continue
                            sem = eng_sem[p["eng"]]
                            key = ("e", p["eng"])
                        v = p["sigval"]
                        if waited.get(key, 0) >= v:
                            continue
                        waited[key] = v
                        engobj.wait_ge(sem, v)
                    if o["dma"]:
                        o["fn"](engobj, dma_sems[o["dkey"]])
                    else:
                        inst = o["fn"](engobj)
                        if o["sig"]:
                            inst.then_inc(eng_sem[engname], 1)
                for i in per_eng[engname]:
                    o = ops[i]
                    if o["dma"]:
                        key = ("d", o["dkey"])
                        if waited.get(key, 0) < o["sigval"]:
                            waited[key] = o["sigval"]
                            engobj.wait_ge(dma_sems[o["dkey"]], o["sigval"])

            @block.tensor
            def _(e):
                run("pe", e)

            @block.scalar
            def _(e):
                run("act", e)

            @block.vector
            def _(e):
                run("dve", e)

            @block.gpsimd
            def _(e):
                run("pool", e)

            @block.sync
            def _(e):
                run("sp", e)


class Arena:
    def __init__(self, nc, st, nbytes):
        self.t = st.enter_context(nc.sbuf_tensor("arena", [128, nbytes // 4], F32))
        self.cap = nbytes
        self.off = 0

    def reset(self, off=0):
        self.off = off

    def alloc(self, shape, dtype):
        esz = 4 if dtype == F32 else 2
        n = int(np.prod(shape[1:]))
        nb = (n * esz + 63) // 64 * 64
        assert self.off + nb <= self.cap, (self.off, nb, self.cap)
        w0 = self.off // 4
        ap = self.t[0:shape[0], w0:w0 + nb // 4]
        if dtype != F32:
            ap = ap.bitcast(dtype)
        ap = ap[:, 0:n]
        if len(shape) == 3:
            ap = ap.rearrange("p (a b) -> p a b", a=shape[1])
        elif len(shape) == 4:
            ap = ap.rearrange("p (a b c) -> p a b c", a=shape[1], b=shape[2])
        self.off += nb
        return ap


def MM(P, out, lhsT, rhs, start, stop, reads, writes):
    P.op("pe", lambda e: e.matmul(out, lhsT=lhsT, rhs=rhs, start=start, stop=stop), reads, writes)


def TR(P, out, in_, ident, reads, writes):
    P.op("pe", lambda e: e.transpose(out, in_, ident), reads, writes)


def ACT(P, out, in_, func, reads, writes, scale=None, bias=None, accum_out=None):
    kw = {}
    if scale is not None:
        kw["scale"] = scale
    if bias is not None:
        kw["bias"] = bias
    if accum_out is not None:
        kw["accum_out"] = accum_out
    P.op("act", lambda e: e.activation(out=out, in_=in_, func=func, **kw), reads, writes)


def TT(P, eng, out, in0, in1, op, reads, writes):
    P.op(eng, lambda e: e.tensor_tensor(out=out, in0=in0, in1=in1, op=op), reads, writes)


def TS(P, eng, out, in0, s1, s2, op0, op1, reads, writes):
    if op1 is None:
        P.op(eng, lambda e: e.tensor_scalar(out=out, in0=in0, scalar1=s1, scalar2=None, op0=op0), reads, writes)
    else:
        P.op(eng, lambda e: e.tensor_scalar(out=out, in0=in0, scalar1=s1, scalar2=s2, op0=op0, op1=op1), reads, writes)


def STT(P, out, in0, scalar, in1, op0, op1, reads, writes):
    P.op("dve", lambda e: e.scalar_tensor_tensor(out=out, in0=in0, scalar=scalar, in1=in1, op0=op0, op1=op1), reads, writes)


def CP(P, eng, out, in_, reads, writes):
    if eng == "act":
        P.op("act", lambda e: e.activation(out=out, in_=in_, func=AF.Identity), reads, writes)
    else:
        P.op(eng, lambda e: e.tensor_copy(out=out, in_=in_), reads, writes)


def DMA(P, q, out, in_, key, reads=(), writes=()):
    P.op(q, lambda e, s: e.dma_start(out=out, in_=in_).then_inc(s, 16), reads, writes, dma=1, key=key)


class Ctx:
    pass


def build(MODE, debug=False):
    nc = bass.Bass("TRN2", target_bir_lowering=False)
    C = Ctx()
    C.nc = nc
    C.NHM, C.NHD = 8, 4
    if MODE == "P4":
        C.NQ, C.NK, C.KOFF = 4096, 4096, 0
    else:
        C.NQ, C.NK, C.KOFF = 2048, 4096, 2048
    NQ, NK = C.NQ, C.NK
    TB = 1024
    C.TB = TB

    def din(name, shape):
        return nc.dram_tensor(name, list(shape), F32, kind="ExternalInput").ap()

    I = {}
    I["x"] = din("x", [NK, D])
    I["p"] = din("p", [NQ, 256])
    I["w_ffn1_up"] = din("w_ffn1_up", [D, 2 * DFF])
    I["w_ffn1_down"] = din("w_ffn1_down", [DFF, D])
    I["w_ffn2_up"] = din("w_ffn2_up", [D, 2 * DFF])
    I["w_ffn2_down"] = din("w_ffn2_down", [DFF, D])
    I["w_in"] = din("w_in", [D, 5120])
    I["w_branch_a"] = din("w_branch_a", [512, D])
    I["w_branch_b"] = din("w_branch_b", [512, D])
    I["w_out"] = din("w_out", [D, D])
    I["w_ple"] = din("w_ple", [256, D])
    I["w_ple_gate"] = din("w_ple_gate", [D, D])
    I["lnp"] = din("lnp", [128, 6, D])
    I["sublng"] = din("sublng", [128, 128])
    I["lamv"] = din("lamv", [128, 4, 64])
    I["ident"] = din("ident", [128, 128])
    I["tri"] = din("tri", [128, 128])
    I["qc"] = din("qc", [12, 4, NQ])
    I["kc"] = din("kc", [12, 4, NK])
    I["kind"] = din("kind", [16, NK])
    I["vb"] = din("vb", [128, NQ // 512, 4, 16])
    I["lo"] = din("lo", [128, NQ // 512, 4, 16])
    C.I = I
    kind_s = "ExternalOutput" if debug else "Internal"
    C.out = nc.dram_tensor("out", [NQ, D], F32, kind="ExternalOutput").ap()
    C.X1_s = nc.dram_tensor("X1_s", [NQ, D], F32, kind=kind_s).ap()
    C.X1T_s = nc.dram_tensor("X1T_s", [D, NQ], BF16, kind=kind_s).ap()
    C.QT_s = nc.dram_tensor("QT_s", [D, NQ], BF16, kind=kind_s).ap()
    C.KT_s = nc.dram_tensor("KT_s", [D, NK], BF16, kind=kind_s).ap()
    C.V_s = nc.dram_tensor("V_s", [NK, D], BF16, kind=kind_s).ap()
    C.YT_s = nc.dram_tensor("YT_s", [D, NQ], BF16, kind=kind_s).ap()

    with contextlib.ExitStack() as st:
        C.P = P = Prog(nc)
        C.arena = Arena(nc, st, 192 * 1024)
        C.ps = [st.enter_context(nc.psum_tensor("ps%d" % i, [128, 512], F32)) for i in range(8)]
        sb = lambda n, s, d: st.enter_context(nc.sbuf_tensor(n, s, d))
        C.ident_f = sb("ident_f", [128, 128], F32)
        C.ident_b = sb("ident_b", [128, 128], BF16)
        C.tri_b = sb("tri_b", [128, 128], BF16)
        C.eps_t = sb("eps_t", [128, 1], F32)
        C.lam_t = sb("lam_t", [128, 4], F32)
        C.lamv = sb("lamv_t", [128, 4, 64], F32)
        C.sublng = sb("sublng_t", [128, 128], F32)
        C.small = sb("small_t", [128, 64], F32)

        phase0(C)
        phase1(C)
        P.barrier()
        phase2(C)
        P.barrier()
        phase3(C)
        P.emit()
    return nc


def phase0(C):
    P, I = C.P, C.I
    DMA(P, "sp", C.ident_f[:], I["ident"], "c_if", writes=["ident_f"])
    DMA(P, "pool", C.ident_b[:], I["ident"], "c_ib", writes=["ident_b"])
    DMA(P, "pool", C.tri_b[:], I["tri"], "c_tri", writes=["tri_b"])
    DMA(P, "sp", C.lamv[:], I["lamv"], "c_lamv", writes=["lamv"])
    DMA(P, "sp", C.sublng[:], I["sublng"], "c_sub", writes=["sublng"])
    P.op("dve", lambda e: e.memset(C.eps_t[:], EPS), writes=["eps"])
    sm = C.small
    TT(P, "dve", sm[:, 0:64], C.lamv[:, 0, :], C.lamv[:, 1, :], ALU.mult, ["lamv"], ["sm0"])
    P.op("dve", lambda e: e.tensor_reduce(out=C.lam_t[:, 2:3], in_=sm[:, 0:64], axis=AX.X, op=ALU.add), ["sm0"], ["l2"])
    TT(P, "dve", sm[:, 0:64], C.lamv[:, 2, :], C.lamv[:, 3, :], ALU.mult, ["lamv", "l2"], ["sm0"])
    P.op("dve", lambda e: e.tensor_reduce(out=C.lam_t[:, 3:4], in_=sm[:, 0:64], axis=AX.X, op=ALU.add), ["sm0"], ["l3"])
    ACT(P, C.lam_t[:, 2:4], C.lam_t[:, 2:4], AF.Exp, ["l2", "l3"], ["l23"])
    TT(P, "dve", C.lam_t[:, 0:1], C.lam_t[:, 2:3], C.lam_t[:, 3:4], ALU.subtract, ["l23"], ["lam0"])
    TS(P, "dve", C.lam_t[:, 0:1], C.lam_t[:, 0:1], float(LAM_INIT), None, ALU.add, None, ["lam0"], ["lam"])
    TS(P, "dve", C.lam_t[:, 1:2], C.lam_t[:, 0:1], -1.0, None, ALU.mult, None, ["lam"], ["nlam"])


def wview(w, kcn):
    return w.rearrange("(kc p) n -> p kc n", p=128)


def emit_transposes(C, xs, xT, t, evac_eng, cast=True):
    P = C.P
    xi = t % len(C.xb)
    xb = C.xb[xi]
    xbn = ("xb", xi)
    if cast:
        CP(P, "act", xb[:, :], xs[:, t, :], [("xs", t)], [xbn])
    for g in range(2):
        bank = C.ps[6 + g]
        for k in range(4):
            kc = g * 4 + k
            MM(P, bank[:, k * 128:(k + 1) * 128], xb[:, kc * 128:(kc + 1) * 128], C.ident_b[:, :], True, True,
               [xbn, "ident_b"], [("ps", 6 + g)])
        src = bank[:, :].rearrange("p (a b) -> p a b", a=4)
        CP(P, evac_eng[g], xT[:, g * 4:(g + 1) * 4, t * 128:(t + 1) * 128], src, [("ps", 6 + g)], [("xT", t)])


def emit_layernorm(C, xs, t, ybanks, yscale, lng, lnb, tmp, lnname):
    P = C.P
    tT, r, xn, st6, mv = tmp
    for h in range(2):
        ACT(P, tT[:, h * 512:(h + 1) * 512], C.ps[ybanks[h]][:, :], AF.Identity, [("ps", ybanks[h])], [("tT", h)], scale=float(yscale))
    STT(P, r[:, :], xs[:, t, :], ALPHA, tT[:, :], ALU.mult, ALU.add, [("xs", t), ("tT", 0), ("tT", 1)], ["r"])
    for h in range(2):
        P.op("dve", (lambda hh: (lambda e: e.bn_stats(out=st6[:, hh, :], in_=r[:, hh * 512:(hh + 1) * 512])))(h), ["r"], [("st6", h)])
    P.op("dve", lambda e: e.bn_aggr(out=mv[:, 0:2], in_=st6[:, :, :].rearrange("p a b -> p (a b)")), [("st6", 0), ("st6", 1)], ["mv"])
    ACT(P, mv[:, 2:3], mv[:, 1:2], AF.Sqrt, ["mv", "eps"], ["std"], bias=C.eps_t[:], scale=1.0)
    P.op("dve", lambda e: e.reciprocal(out=mv[:, 3:4], in_=mv[:, 2:3]), ["std"], ["rstd"])
    STT(P, mv[:, 4:5], mv[:, 0:1], -1.0, mv[:, 3:4], ALU.mult, ALU.mult, ["mv", "rstd"], ["nmr"])
    ACT(P, xn[:, :], r[:, :], AF.Identity, ["r", "rstd", "nmr"], ["xn"], scale=mv[:, 3:4], bias=mv[:, 4:5])
    TT(P, "dve", xn[:, :], xn[:, :], lng, ALU.mult, ["xn", lnname], ["xn"])
    TT(P, "dve", xs[:, t, :], xn[:, :], lnb, ALU.add, ["xn", lnname], [("xs", t)])
    xi = t % len(C.xb)
    CP(P, "act", C.xb[xi][:, :], xs[:, t, :], [("xs", t)], [("xb", xi)])


def emit_ffn(C, xs, xT, actT, wd, wb, sil, w_up, w_down, tag, first):
    P = C.P
    TB = C.TB
    wu = wview(w_up, 8)
    if first:
        wdv = wview(w_down, NFF)
        DMA(P, "pool", wd[:, 0:11, :], wdv[:, 0:11, :], tag + "wd0", writes=[("wd", 0)])
        DMA(P, "pool", wd[:, 11:22, :], wdv[:, 11:22, :], tag + "wd1", writes=[("wd", 1)])
    par = 0
    for gi in range(NFF // 2):
        slot = gi % 2
        buf = wb[slot]
        DMA(P, "pool", buf[:, :, 0:256], wu[:, :, gi * 256:(gi + 1) * 256], "wbA%d" % slot, writes=[("wb", slot, 0)])
        DMA(P, "pool", buf[:, :, 256:512], wu[:, :, DFF + gi * 256:DFF + (gi + 1) * 256], "wbG%d" % slot, writes=[("wb", slot, 1)])
        for jj in range(2):
            j = gi * 2 + jj
            for hb in range(TB // 512):
                bA, bG = 2 * par, 2 * par + 1
                xr = [("xT", 4 * hb + q) for q in range(4)]
                for kc in range(8):
                    MM(P, C.ps[bA][:, :], buf[:, kc, jj * 128:(jj + 1) * 128], xT[:, kc, hb * 512:(hb + 1) * 512],
                       kc == 0, kc == 7, [("wb", slot, 0)] + xr, [("ps", bA)])
                for kc in range(8):
                    MM(P, C.ps[bG][:, :], buf[:, kc, 256 + jj * 128:256 + (jj + 1) * 128], xT[:, kc, hb * 512:(hb + 1) * 512],
                       kc == 0, kc == 7, [("wb", slot, 1)] + xr, [("ps", bG)])
                ACT(P, sil[par][:, :], C.ps[bA][:, :], AF.Silu, [("ps", bA)], [("sil", par)])
                TT(P, "dve", actT[:, j, hb * 512:(hb + 1) * 512], sil[par][:, :], C.ps[bG][:, :], ALU.mult,
                   [("sil", par), ("ps", bG)], [("actT", j, hb)])
                par ^= 1


def emit_down(C, actT, wd, t, yb=(4, 5)):
    P = C.P
    hb = t // 4
    for dh in range(2):
        for kc in range(NFF):
            MM(P, C.ps[yb[dh]][:, :], actT[:, kc, t * 128:(t + 1) * 128], wd[:, kc, dh * 512:(dh + 1) * 512],
               kc == 0, kc == NFF - 1, [("actT", kc, hb), ("wd", kc // 11)], [("ps", yb[dh])])


def phase1(C):
    P, I, A = C.P, C.I, C.arena
    TB = C.TB
    NTB = C.NK // TB
    own_tb0 = C.KOFF // TB
    A.reset()
    xs = A.alloc([128, 8, D], F32)
    xT = A.alloc([128, 8, TB], BF16)
    actT = A.alloc([128, NFF, TB], BF16)
    wd = A.alloc([128, NFF, D], BF16)
    wb = [A.alloc([128, 8, 512], BF16) for _ in range(2)]
    lnp = A.alloc([128, 2, D], F32)
    tT = A.alloc([128, D], F32)
    r = A.alloc([128, D], F32)
    xn = A.alloc([128, D], F32)
    st6 = A.alloc([128, 2, 6], F32)
    mv = A.alloc([128, 8], F32)
    sil = [A.alloc([128, 512], F32) for _ in range(2)]
    stg = [A.alloc([128, TB], BF16) for _ in range(2)]
    vst = [A.alloc([128, 512], BF16) for _ in range(2)]
    tmp = (tT, r, xn, st6, mv)
    C.xb = [A.alloc([128, D], BF16) for _ in range(2)]
    C.xbi = 0
    DMA(P, "sp", lnp, I["lnp"][:, 0:2, :], "lnp1", writes=["ln1"])
    xv = I["x"].rearrange("(t p) d -> p t d", p=128)
    x1v = C.X1_s.rearrange("(t p) d -> p t d", p=128)
    win = wview(I["w_in"], 8)
    for tb in range(NTB):
        own = tb >= own_tb0
        qtb = tb - own_tb0
        for t in range(8):
            DMA(P, "sp", xs[:, t, :], xv[:, tb * 8 + t, :], "xs%d" % t, writes=[("xs", t)])
        for t in range(8):
            emit_transposes(C, xs, xT, t, ("dve", "act"))
        emit_ffn(C, xs, xT, actT, wd, wb, sil, I["w_ffn1_up"], I["w_ffn1_down"], "f1", tb == 0)
        for t in range(8):
            yb = (4, 5) if (t % 2 == 0 or not FLAG_LNPIPE) else (2, 3)
            emit_down(C, actT, wd, t, yb)
            emit_layernorm(C, xs, t, yb, 0.5, lnp[:, 0, :], lnp[:, 1, :], tmp, "ln1")
            if own:
                DMA(P, "sp", x1v[:, qtb * 8 + t, :], xs[:, t, :], "x1st%d" % t, reads=[("xs", t)])
            if t >= 1:
                emit_transposes(C, xs, xT, t - 1, ("dve", "act"), cast=False)
        emit_transposes(C, xs, xT, 7, ("dve", "act"), cast=False)
        if own:
            for kc in range(8):
                DMA(P, "sp", C.X1T_s[kc * 128:(kc + 1) * 128, qtb * TB:(qtb + 1) * TB], xT[:, kc, :], "x1T%d" % kc,
                    reads=[("xT", q) for q in range(8)])
        par = 0
        for g in range(6):
            kind = ("q", "k", "v")[g % 3]
            fam = g // 3
            if kind == "q" and not own:
                continue
            slot = g % 2
            buf = wb[slot]
            DMA(P, "pool", buf[:, :, 0:256], win[:, :, g * 512:g * 512 + 256], "wbA%d" % slot, writes=[("wb", slot, 0)])
            DMA(P, "pool", buf[:, :, 256:512], win[:, :, g * 512 + 256:(g + 1) * 512], "wbG%d" % slot, writes=[("wb", slot, 1)])
            if kind in ("q", "k"):
                dst = C.QT_s if kind == "q" else C.KT_s
                col0 = qtb * TB if kind == "q" else tb * TB
                for c in range(4):
                    sg = stg[par]
                    for hb in range(TB // 512):
                        bank = 2 * par + hb
                        for kc in range(8):
                            MM(P, C.ps[bank][:, :], buf[:, kc, c * 128:(c + 1) * 128], xT[:, kc, hb * 512:(hb + 1) * 512],
                               kc == 0, kc == 7, [("wb", slot, c // 2)] + [("xT", 4 * hb + q) for q in range(4)], [("ps", bank)])
                        CP(P, "act" if hb == 0 else "dve", sg[:, hb * 512:(hb + 1) * 512], C.ps[bank][:, :], [("ps", bank)], [("stg", par, hb)])
                    row0 = fam * 512 + c * 128
                    DMA(P, "sp", dst[row0:row0 + 128, col0:col0 + TB], sg[:, :], "stg%d" % par,
                        reads=[("stg", par, 0), ("stg", par, 1)])
                    par ^= 1
            else:
                vv = C.V_s.rearrange("(t p) d -> p t d", p=128)
                for t in range(8):
                    bank = 2 * par
                    for kc in range(8):
                        MM(P, C.ps[bank][:, :], xT[:, kc, t * 128:(t + 1) * 128], buf[:, kc, :],
                           kc == 0, kc == 7, [("wb", slot, 0), ("wb", slot, 1), ("xT", t)], [("ps", bank)])
                    CP(P, "act" if t % 2 == 0 else "dve", vst[par][:, :], C.ps[bank][:, :], [("ps", bank)], [("vst", par)])
                    DMA(P, "sp", vv[:, tb * 8 + t, fam * 512:(fam + 1) * 512], vst[par][:, :], "vst%d" % par, reads=[("vst", par)])
                    par ^= 1


def phase2(C):
    P, I, A = C.P, C.I, C.arena
    A.reset()
    NHM, NHD = C.NHM, C.NHD
    NQ, NK, KOFF = C.NQ, C.NK, C.KOFF
    NQB = NQ // 512
    NKC = NK // 128
    KO = KOFF // 128
    KTs = [[A.alloc([128, NK], BF16) for m in range(2)] for _ in range(2)]
    QTs = [[A.alloc([128, NQ], BF16) for m in range(2)] for _ in range(2)]
    VEs = [A.alloc([128, NKC, 132], BF16) for _ in range(2)]
    PT = [A.alloc([128, 512], BF16) for _ in range(4)]
    SB = [0, 1, 5]
    accsb = [A.alloc([128, 4, 2, 132], F32) for _ in range(2)]
    o1 = A.alloc([128, 4, 128], F32)
    yd = A.alloc([128, 4, 128], F32)
    ydb = A.alloc([128, 4, 128], BF16)
    sq = A.alloc([128, 128], F32)
    sc = A.alloc([128, 4, 8], F32)
    ystg = [A.alloc([128, 512], BF16) for _ in range(2)]
    kmf = A.alloc([64, 16], F32)
    kmb = [A.alloc([64, 16], BF16) for _ in range(2)]
    vbt = A.alloc([128, NQB, 4, 16], F32)
    lot = A.alloc([128, NQB, 4, 16], F32)
    gm = [A.alloc([128, 4, 16], F32) for _ in range(2)]
    m8 = [A.alloc([128, 4, 8], F32) for _ in range(2)]
    m1 = [A.alloc([128, 4, 16], F32) for _ in range(2)]
    npad = [A.alloc([128, 4, 80], BF16) for _ in range(2)]
    ones_col = A.alloc([128, NKC, 1], F32)
    ones_k = A.alloc([128, 2], BF16)
    one_f = A.alloc([1, 2], F32)
    OTsb = [A.alloc([128, 512], BF16) for _ in range(2)]
    ZTsb = [A.alloc([1, 512], F32) for _ in range(2)]
    Zhi = [A.alloc([1, 512], BF16) for _ in range(2)]
    Zlo = [A.alloc([1, 512], BF16) for _ in range(2)]
    Zd = [A.alloc([1, 512], F32) for _ in range(2)]
    one_b = A.alloc([1, 2], BF16)
    P.op("dve", lambda e: e.memset(one_b[:, :], 1.0), writes=["one_b"])
    one_b64 = A.alloc([128, 2], BF16)
    P.op("dve", lambda e: e.memset(one_b64[:, :], 1.0), writes=["one_b64"])
    sel = [A.alloc([128, 128], BF16) for _ in range(2)]
    for mm_ in range(2):
        P.op("dve", (lambda mm_=mm_: (lambda e: e.memset(sel[mm_][:, 0:64], 1.0 if mm_ == 0 else 0.0)))(), writes=[("sel", mm_)])
        P.op("dve", (lambda mm_=mm_: (lambda e: e.memset(sel[mm_][:, 64:128], 0.0 if mm_ == 0 else 1.0)))(), writes=[("sel", mm_)])
    ZB_f = A.alloc([128, 512], F32)
    ZB_d = A.alloc([128, 512], F32)
    ZB_hi = A.alloc([128, 512], BF16)
    ZB_lo = A.alloc([128, 512], BF16)
    Zm_f = A.alloc([128, 512], F32)
    Zm_d = A.alloc([128, 512], F32)
    Zm_lo = A.alloc([128, 512], BF16)

    ones_kk = A.alloc([128, 128], BF16)
    P.op("dve", lambda e: e.memset(ones_kk[:, :], 1.0), writes=["ones_kk"])
    for par in range(2):
        for m in range(2):
            P.op("pool", (lambda par=par, m=m: (lambda e: e.memset(KTs[par][m][64:128, :], 0.0)))(), writes=[("KT", par, m)])
            P.op("pool", (lambda par=par, m=m: (lambda e: e.memset(QTs[par][m][64:128, :], 0.0)))(), writes=[("QT", par, m), ("QTm", par, m)])
    P.op("dve", lambda e: e.memset(ones_k[:, :], 1.0), writes=["ones_k"])
    P.op("dve", lambda e: e.memset(one_f[:, :], 1.0), writes=["one_f"])
    DMA(P, "sp", vbt, I["vb"], "vbt", writes=["vbt"])
    DMA(P, "sp", lot, I["lo"], "lot", writes=["lot"])
    for g in range(2):
        P.op("dve", (lambda g=g: (lambda e: e.memset(npad[g][:, :, :], 0.0)))(), writes=[("npad", g)])
    P.op("dve", lambda e: e.memset(ones_col[:, :, :], 1.0), writes=["ones_col"])
    vv = C.V_s.rearrange("(k p) d -> p k d", p=128)

    jobs = [("d", h) for h in range(NHD)] + [("m", h) for h in range(NHM)]

    def load_job(ji):
        kind, h = jobs[ji]
        par = ji % 2
        KT, QT, VE = KTs[par], QTs[par], VEs[par]
        if kind == "d":
            hg = NHM + h
            for m in range(2):
                r0 = 512 + h * 128 + m * 64
                DMA(P, "sp", KT[m][0:64, :], C.KT_s[r0:r0 + 64, :], "KTl%d%d" % (par, m), writes=[("KT", par, m)])
                DMA(P, "pool", KT[m][64:68, :], I["kc"][hg, :, :], "KTc%d%d" % (par, m), writes=[("KT", par, m)])
                DMA(P, "sp", QT[m][0:64, :], C.QT_s[r0:r0 + 64, :], "QTl%d%d" % (par, m), writes=[("QT", par, m)])
                DMA(P, "pool", QT[m][64:68, :], I["qc"][hg, :, :], "QTc%d%d" % (par, m), writes=[("QT", par, m)])
            DMA(P, "sp", VE[:, :, 0:128], vv[:, :, 512 + h * 128:512 + (h + 1) * 128], "VEl%d" % par, writes=[("VE", par)])
            CP(P, "dve", VE[:, :, 128:129], ones_col[:, :, :], ["ones_col"], [("VE", par)])
        else:
            r0 = h * 64
            DMA(P, "sp", KT[0][0:64, :], C.KT_s[r0:r0 + 64, :], "KTl%d0" % par, writes=[("KT", par, 0)])
            DMA(P, "pool", KT[0][64:80, :], I["kind"], "KTi%d" % par, writes=[("KT", par, 0)])
            DMA(P, "pool", KT[0][80:84, :], I["kc"][h, :, :], "KTc%d0" % par, writes=[("KT", par, 0)])
            DMA(P, "sp", QT[0][0:64, :], C.QT_s[r0:r0 + 64, :], "QTl%d0" % par, writes=[("QT", par, 0)])
            DMA(P, "pool", QT[0][80:84, :], I["qc"][h, :, :], "QTc%d0" % par, writes=[("QT", par, 0)])
            DMA(P, "sp", VE[:, :, 0:64], vv[:, :, h * 64:(h + 1) * 64], "VEl%d" % par, writes=[("VE", par)])
            CP(P, "dve", VE[:, :, 64:65], ones_col[:, :, :], ["ones_col"], [("VE", par)])

    def mask_stage(ji, tg):
        par = ji % 2
        KT, QT = KTs[par][0], QTs[par][0]
        if tg == 0:
            P.op("dve", lambda e: e.tensor_reduce(out=kmf[:, :], in_=KT[0:64, :].rearrange("p (n l) -> p n l", l=256), axis=AX.X, op=ALU.add),
                 [("KT", par, 0)], ["kmf"])
            TS(P, "dve", kmb[par][:, :], kmf[:, :], 1.0 / 256.0, None, ALU.mult, None, ["kmf"], [("kmb", par)])
        gp = tg % 2
        for tq in range(4):
            t = tg * 4 + tq
            MM(P, C.ps[7][:, tq * 16:(tq + 1) * 16], QT[0:64, t * 128:(t + 1) * 128], kmb[par][:, :], True, True,
               [("QT", par, 0), ("kmb", par)], [("ps", 7)])
        TT(P, "dve", gm[gp][:, :, :], C.ps[7][:, 0:64].rearrange("p (a b) -> p a b", a=4), vbt[:, tg, :, :], ALU.add,
           [("ps", 7), "vbt"], [("gm", gp)])
        for tq in range(4):
            P.op("dve", (lambda tq=tq: (lambda e: e.max(out=m8[gp][:, tq, :], in_=gm[gp][:, tq, :])))(), [("gm", gp)], [("m8", gp, tq)])
        for tq in range(4):
            TS(P, "dve", m1[gp][:, tq, :], gm[gp][:, tq, :], m8[gp][:, tq, 3:4], 1.0, ALU.is_ge, ALU.subtract,
               [("gm", gp), ("m8", gp, tq)], [("m1", gp, tq)])
        TT(P, "dve", npad[gp][:, :, 64:80], m1[gp][:, :, :], lot[:, tg, :, :], ALU.min,
           [("m1", gp, q) for q in range(4)] + ["lot"], [("npad", gp)])

    def mask_stage_b(ji, tg):
        par = ji % 2
        QT = QTs[par][0]
        gp = tg % 2
        for tq in range(4):
            MM(P, C.ps[6][0:80, tq * 128:(tq + 1) * 128], npad[gp][:, tq, :], C.ident_b[:, :], True, True,
               [("npad", gp), "ident_b"], [("ps", 6)])
        CP(P, "act", QT[64:80, tg * 512:(tg + 1) * 512], C.ps[6][64:80, :], [("ps", 6)], [("QTm", par, 0)])

    pending = []
    stepc = [0]

    dseq = [0]

    def defer(lag, fn):
        if not FLAG_DEFER:
            fn()
            return
        dseq[0] += 1
        pending.append((stepc[0] + lag, dseq[0], fn))

    def run_due(flush=False):
        while True:
            due = [p for p in pending if flush or p[0] <= stepc[0]]
            if not due:
                break
            due.sort()
            it = due[0]
            pending.remove(it)
            it[2]()

    def attention(ji, nmaps, dv, K, finalize, side):
        par = ji % 2
        KT, QT, VE = KTs[par], QTs[par], VEs[par]
        steps = [(qb, m, kc) for qb in range(NQB) for m in range(nmaps) for kc in range(KO + 4 * qb + 4)]
        n = len(steps)

        def emit_S(i):
            qb, m, kc = steps[i]
            j = kc - KO - 4 * qb
            n0 = max(0, j) * 128
            sb = SB[i % 3]
            S = C.ps[sb]
            MM(P, S[:, n0:512], KT[m][:, kc * 128:(kc + 1) * 128], QT[m][:, qb * 512 + n0:(qb + 1) * 512],
               True, j < 0, [("KT", par, m), ("QT", par, m), ("QTm", par, m)], [("ps", sb)])
            if j >= 0:
                MM(P, S[:, n0:n0 + 128], C.ident_b[:, :], C.tri_b[:, :], False, True, ["ident_b", "tri_b"], [("ps", sb)])
            ACT(P, PT[i % 4][:, n0:512], S[:, n0:512], AF.Exp, [("ps", sb)], [("PT", i % 4)], scale=0.125)

        def emit_AV(i):
            qb, m, kc = steps[i]
            j = kc - KO - 4 * qb
            n0 = max(0, j) * 128
            last = KO + 4 * qb + 3
            pt = PT[i % 4]
            if nmaps == 2:
                MM(P, C.ps[2 + m][:, n0:512], VE[:, kc, 0:128], pt[:, n0:512], kc == 0, kc == last,
                   [("PT", i % 4), ("VE", par)], [("OT", m)])
                MM(P, C.ps[4][:, n0:512], sel[m][:, :], pt[:, n0:512], (m == 0 and kc == 0), (m == 1 and kc == last),
                   [("PT", i % 4), ("sel", m)], [("ZT", 0)])
            else:
                MM(P, C.ps[2][:, n0:512], VE[:, kc, 0:128], pt[:, n0:512], kc == 0, kc == last,
                   [("PT", i % 4), ("VE", par)], [("OT", 0)])
            if m == nmaps - 1 and kc == last:
                q2 = qb % 2
                if nmaps == 2:
                    for mm in range(2):
                        CP(P, "dve", OTsb[mm][:, :], C.ps[2 + mm][:, :], [("OT", mm)], [("OTsb", mm)])
                    CP(P, "dve", ZTsb[0][0:1, :], C.ps[4][0:1, :], [("ZT", 0)], [("ZTsb", 0)])
                    CP(P, "dve", ZB_f[64:65, :], C.ps[4][64:65, :], [("ZT", 0)], [("ZTsb", 1)])
                    CP(P, "dve", Zhi[0][0:1, :], ZTsb[0][0:1, :], [("ZTsb", 0)], [("Zhi", 0)])
                    TT(P, "dve", Zd[0][0:1, :], ZTsb[0][0:1, :], Zhi[0][0:1, :], ALU.subtract, [("ZTsb", 0), ("Zhi", 0)], [("Zd", 0)])
                    CP(P, "dve", Zlo[0][0:1, :], Zd[0][0:1, :], [("Zd", 0)], [("Zlo", 0)])
                    CP(P, "dve", ZB_hi[64:65, :], ZB_f[64:65, :], [("ZTsb", 1)], [("Zhi", 1)])
                    TT(P, "dve", ZB_d[64:65, :], ZB_f[64:65, :], ZB_hi[64:65, :], ALU.subtract, [("ZTsb", 1), ("Zhi", 1)], [("Zd", 1)])
                    CP(P, "dve", ZB_lo[64:65, :], ZB_d[64:65, :], [("Zd", 1)], [("Zlo", 1)])
                else:
                    CP(P, "dve", OTsb[0][0:65, :], C.ps[2][0:65, :], [("OT", 0)], [("OTsb", 0)])
                    CP(P, "dve", Zm_f[64:65, :], C.ps[2][64:65, :], [("OT", 0)], [("Zm_f", 0)])
                    TT(P, "dve", Zm_d[64:65, :], Zm_f[64:65, :], OTsb[0][64:65, :], ALU.subtract, [("Zm_f", 0), ("OTsb", 0)], [("Zm_d", 0)])
                    CP(P, "dve", Zm_lo[64:65, :], Zm_d[64:65, :], [("Zm_d", 0)], [("Zm_lo", 0)])

                def stage2(qb=qb, q2=q2):
                    if nmaps == 2:
                        for mm in range(2):
                            for sub in range(4):
                                MM(P, C.ps[6][:, sub * 128:(sub + 1) * 128], OTsb[mm][:, sub * 128:(sub + 1) * 128], C.ident_b[:, :], True, True,
                                   [("OTsb", mm), "ident_b"], [("ps", 6)])
                            CP(P, "dve", accsb[q2][:, 2 * mm:2 * mm + 2, :, 0:128],
                               C.ps[6][:, :].rearrange("p (a b c) -> p a b c", a=2, b=2), [("ps", 6)], [("accsb", q2, 2 * mm), ("accsb", q2, 2 * mm + 1)])
                            for sub in range(4):
                                zh = Zhi[0][0:1, sub * 128:(sub + 1) * 128] if mm == 0 else ZB_hi[64:65, sub * 128:(sub + 1) * 128]
                                zl = Zlo[0][0:1, sub * 128:(sub + 1) * 128] if mm == 0 else ZB_lo[64:65, sub * 128:(sub + 1) * 128]
                                ob = one_b[0:1, 0:1] if mm == 0 else one_b64[64:65, 0:1]
                                MM(P, C.ps[7][:, 64 + mm * 4 + sub:64 + mm * 4 + sub + 1], zh, ob,
                                   True, False, [("Zhi", mm), "one_b", "one_b64"], [("ps", 7)])
                                MM(P, C.ps[7][:, 64 + mm * 4 + sub:64 + mm * 4 + sub + 1], zl, ob,
                                   False, True, [("Zlo", mm), "one_b", "one_b64"], [("ps", 7)])
                            CP(P, "dve", accsb[q2][:, 2 * mm:2 * mm + 2, :, 128:129],
                               C.ps[7][:, 64 + mm * 4:64 + mm * 4 + 4].rearrange("p (a b c) -> p a b c", a=2, b=2), [("ps", 7)],
                               [("accsb", q2, 2 * mm), ("accsb", q2, 2 * mm + 1)])
                    else:
                        for sub in range(4):
                            MM(P, C.ps[6][:, sub * 128:sub * 128 + 65], OTsb[0][0:65, sub * 128:(sub + 1) * 128], C.ident_b[0:65, 0:65], True, False,
                               [("OTsb", 0), "ident_b"], [("ps", 6)])
                            MM(P, C.ps[6][:, sub * 128 + 64:sub * 128 + 65], Zm_lo[64:65, sub * 128:(sub + 1) * 128], one_b64[64:65, 0:1], False, True,
                               [("Zm_lo", 0), "one_b64"], [("ps", 6)])
                        CP(P, "dve", accsb[q2][:, 0:2, :, 0:65],
                           C.ps[6][:, :].rearrange("p (a b c) -> p a b c", a=2, b=2)[:, :, :, 0:65], [("ps", 6)], [("accsb", q2, 0), ("accsb", q2, 1)])
                    finalize[0][1](qb)

                if FLAG_S2DEFER:
                    defer(2, stage2)
                else:
                    stage2()
                if not FLAG_S2DEFER or qb == NQB - 1:
                    for _f in range(FLAG_BURST):
                        MM(P, C.ps[SB[i % 3]][:, :], C.ident_b[:, :], KT[0][:, 0:512], True, True, [("KT", par, 0)], [("ps", SB[i % 3])])
                for lag, fn in finalize[1:]:
                    defer(lag, (lambda qb=qb, fn=fn: fn(qb)))
                side(qb)

        emit_S(0)
        emit_S(1)
        for i in range(n):
            if i + 2 < n:
                emit_S(i + 2)
            emit_AV(i)
            stepc[0] += 1
            run_due()
        run_due(flush=True)

    def fin_diff(h, qb):
        q2 = qb % 2
        for sub in range(4):
            a1 = accsb[q2][:, sub // 2, sub % 2, :]
            a2 = accsb[q2][:, 2 + sub // 2, sub % 2, :]
            r1 = [("accsb", q2, sub // 2)]
            r2 = [("accsb", q2, 2 + sub // 2)]
            s = sc[:, sub, :]
            P.op("dve", (lambda s=s, a1=a1: (lambda e: e.reciprocal(out=s[:, 0:1], in_=a1[:, 128:129])))(), r1, [("sc", sub, 0)])
            P.op("dve", (lambda s=s, a2=a2: (lambda e: e.reciprocal(out=s[:, 1:2], in_=a2[:, 128:129])))(), r2, [("sc", sub, 1)])
            TT(P, "dve", s[:, 2:3], s[:, 1:2], C.lam_t[:, 1:2], ALU.mult, [("sc", sub, 1), "nlam"], [("sc", sub, 2)])
            TS(P, "dve", o1[:, sub, :], a1[:, 0:128], s[:, 0:1], None, ALU.mult, None, r1 + [("sc", sub, 0)], [("o1", sub)])
            STT(P, yd[:, sub, :], a2[:, 0:128], s[:, 2:3], o1[:, sub, :], ALU.mult, ALU.add,
                r2 + [("sc", sub, 2), ("o1", sub)], [("yd", sub)])
            TT(P, "dve", sq[:, :], yd[:, sub, :], yd[:, sub, :], ALU.mult, [("yd", sub)], ["sq"])
            P.op("dve", (lambda s=s: (lambda e: e.tensor_reduce(out=s[:, 3:4], in_=sq[:, :], axis=AX.X, op=ALU.add)))(), ["sq"], [("sc", sub, 3)])

    def fin_diff_act(h, qb):
        ACT(P, sc[:, :, 4:5], sc[:, :, 3:4], AF.Ln, [("sc", sub, 3) for sub in range(4)] + ["eps"], [("sc", sub, 4) for sub in range(4)],
            bias=C.eps_t[:], scale=1.0 / 128.0)
        ACT(P, sc[:, :, 5:6], sc[:, :, 4:5], AF.Exp, [("sc", sub, 4) for sub in range(4)], [("sc", sub, 5) for sub in range(4)], scale=-0.5)

    def fin_diff_a2(h, qb):
        for sub in range(4):
            s = sc[:, sub, :]
            TS(P, "dve", yd[:, sub, :], yd[:, sub, :], s[:, 5:6], float(1.0 - LAM_INIT), ALU.mult, ALU.mult, [("yd", sub), ("sc", sub, 5)], [("yd", sub)])
            TT(P, "dve", ydb[:, sub, :], yd[:, sub, :], C.sublng[:, :], ALU.mult, [("yd", sub), "sublng"], [("ydb", sub)])

    def fin_diff_b(h, qb):
        for sub in range(4):
            MM(P, C.ps[6][:, sub * 128:(sub + 1) * 128], ydb[:, sub, :], C.ident_b[:, :], True, True,
               [("ydb", sub), "ident_b"], [("ps", 6)])
        sl = qb % 2
        CP(P, "dve", ystg[sl][:, :], C.ps[6][:, :], [("ps", 6)], [("ystg", sl)])
        DMA(P, "sp", C.YT_s[512 + h * 128:512 + (h + 1) * 128, qb * 512:(qb + 1) * 512], ystg[sl][:, :], "ystg%d" % sl, reads=[("ystg", sl)])

    def fin_moba(h, qb):
        q2 = qb % 2
        sl = qb % 2
        for sub in range(4):
            a1 = accsb[q2][:, sub // 2, sub % 2, :]
            r1 = [("accsb", q2, sub // 2)]
            s = sc[:, sub, :]
            P.op("dve", (lambda s=s, a1=a1: (lambda e: e.reciprocal(out=s[:, 0:1], in_=a1[:, 64:65])))(), r1, [("sc", sub, 0)])
            TS(P, "dve", ydb[:, sub, 0:64], a1[:, 0:64], s[:, 0:1], None, ALU.mult, None, r1 + [("sc", sub, 0)], [("ydb", sub)])

    def fin_moba_b(h, qb):
        sl = qb % 2
        for sub in range(4):
            MM(P, C.ps[6][0:64, sub * 128:(sub + 1) * 128], ydb[:, sub, 0:64], C.ident_b[:, :], True, True,
               [("ydb", sub), "ident_b"], [("ps", 6)])
        CP(P, "dve", ystg[sl][0:64, :], C.ps[6][0:64, :], [("ps", 6)], [("ystg", sl)])
        DMA(P, "sp", C.YT_s[h * 64:(h + 1) * 64, qb * 512:(qb + 1) * 512], ystg[sl][0:64, :], "ystg%d" % sl, reads=[("ystg", sl)])

    load_job(0)
    if jobs[0][0] == "m":
        for tg in range(NQB):
            mask_stage(0, tg)
            mask_stage_b(0, tg)
    for ji, (kind, h) in enumerate(jobs):
        nxt = ji + 1 if ji + 1 < len(jobs) else None
        if nxt is not None:
            load_job(nxt)

        def side(qb, nxt=nxt):
            if nxt is not None and jobs[nxt][0] == "m":
                defer(4, (lambda: mask_stage(nxt, qb)))
                defer(11, (lambda: mask_stage_b(nxt, qb)))

        if kind == "d":
            attention(ji, 2, 128, 68, [(0, (lambda qb, h=h: fin_diff(h, qb))), (9, (lambda qb, h=h: fin_diff_act(h, qb))),
                                       (11, (lambda qb, h=h: fin_diff_a2(h, qb))), (18, (lambda qb, h=h: fin_diff_b(h, qb)))], side)
        else:
            attention(ji, 1, 64, 84, [(0, (lambda qb, h=h: fin_moba(h, qb))), (6, (lambda qb, h=h: fin_moba_b(h, qb)))], side)


def phase3(C):
    P, I, A = C.P, C.I, C.arena
    TB = C.TB
    NTB = C.NQ // TB
    A.reset()
    xs = A.alloc([128, 8, D], F32)
    xT = A.alloc([128, 8, TB], BF16)
    wb = [A.alloc([128, 8, 512], BF16) for _ in range(2)]
    lnp = A.alloc([128, 4, D], F32)
    tT = A.alloc([128, D], F32)
    r = A.alloc([128, D], F32)
    xn = A.alloc([128, D], F32)
    st6 = A.alloc([128, 2, 6], F32)
    mv = A.alloc([128, 8], F32)
    sil = [A.alloc([128, 512], F32) for _ in range(2)]
    sg = [A.alloc([128, 512], BF16) for _ in range(2)]
    pt_f = [A.alloc([128, 256], F32) for _ in range(1)]
    C.xb = [A.alloc([128, D], BF16) for _ in range(2)]
    C.xbi = 0
    tmp = (tT, r, xn, st6, mv)
    base = A.off
    mergedT = A.alloc([128, 8, TB], BF16)
    yT = A.alloc([128, 8, TB], BF16)
    wbr = A.alloc([128, 2, 4, D], BF16)
    wout = A.alloc([128, 8, D], BF16)
    A.reset(base)
    actT = A.alloc([128, NFF, TB], BF16)
    wd = A.alloc([128, NFF, D], BF16)
    A.reset(base)
    wpg = A.alloc([128, 8, D], BF16)
    wpl = A.alloc([128, 2, D], BF16)
    pT = A.alloc([128, 2, TB], BF16)

    DMA(P, "sp", lnp, I["lnp"][:, 2:6, :], "lnp3", writes=["ln2", "ln3"])
    x1v = C.X1_s.rearrange("(t p) d -> p t d", p=128)
    outv = C.out.rearrange("(t p) d -> p t d", p=128)
    pv = I["p"].rearrange("(t p) d -> p t d", p=128)
    win = wview(I["w_in"], 8)
    for tb in range(NTB):
        tok0 = tb * TB
        for t in range(8):
            DMA(P, "sp", xs[:, t, :], x1v[:, tb * 8 + t, :], "xs%d" % t, writes=[("xs", t)])
        for kc in range(8):
            DMA(P, "sp", xT[:, kc, :], C.X1T_s[kc * 128:(kc + 1) * 128, tb * TB:(tb + 1) * TB], "xTl%d" % kc, writes=[("xT", q) for q in range(8)])
            DMA(P, "sp", yT[:, kc, :], C.YT_s[kc * 128:(kc + 1) * 128, tok0:tok0 + TB], "yTl%d" % kc, writes=[("yT", kc)])
        DMA(P, "pool", wbr[:, 0, :, :], wview(I["w_branch_a"], 4), "wbra", writes=["wbra"])
        DMA(P, "pool", wbr[:, 1, :, :], wview(I["w_branch_b"], 4), "wbrb", writes=["wbrb"])
        DMA(P, "pool", wout, wview(I["w_out"], 8), "wout", writes=["wout"])
        for oc in range(8):
            slot = oc % 2
            buf = wb[slot]
            DMA(P, "pool", buf[:, :, 0:128], win[:, :, 3072 + oc * 128:3072 + (oc + 1) * 128], "wbA%d" % slot, writes=[("wb", slot, 0)])
            DMA(P, "pool", buf[:, :, 128:256], win[:, :, 4096 + oc * 128:4096 + (oc + 1) * 128], "wbG%d" % slot, writes=[("wb", slot, 1)])
            for hb in range(TB // 512):
                xr = [("xT", 4 * hb + q) for q in range(4)]
                ip = (oc * 2 + hb) % 2 if FLAG_MERGEDB else 0
                for br in range(2):
                    bank = 4 * ip + br
                    for kc in range(8):
                        MM(P, C.ps[bank][:, :], buf[:, kc, br * 128:(br + 1) * 128], xT[:, kc, hb * 512:(hb + 1) * 512],
                           kc == 0, kc == 7, [("wb", slot, br)] + xr, [("ps", bank)])
                    ACT(P, sg[br][:, :], C.ps[bank][:, :], AF.Sigmoid, [("ps", bank)], [("sg", br)])
                    bank2 = 4 * ip + 2 + br
                    for kc in range(4):
                        MM(P, C.ps[bank2][:, :], wbr[:, br, kc, oc * 128:(oc + 1) * 128], yT[:, br * 4 + kc, hb * 512:(hb + 1) * 512],
                           kc == 0, kc == 3, ["wbra" if br == 0 else "wbrb", ("yT", br * 4 + kc)], [("ps", bank2)])
                    TT(P, "dve", sil[br][:, :], sg[br][:, :], C.ps[bank2][:, :], ALU.mult,
                       [("sg", br), ("ps", bank2)], [("sil", br)])
                TT(P, "dve", mergedT[:, oc, hb * 512:(hb + 1) * 512], sil[0][:, :], sil[1][:, :], ALU.add,
                   [("sil", 0), ("sil", 1)], [("mergedT", oc, hb)])
        for t in range(8):
            hb = t // 4
            yb = (4, 5) if (t % 2 == 0 or not FLAG_LNPIPE) else (2, 3)
            for dh in range(2):
                for kc in range(8):
                    MM(P, C.ps[yb[dh]][:, :], mergedT[:, kc, t * 128:(t + 1) * 128], wout[:, kc, dh * 512:(dh + 1) * 512],
                       kc == 0, kc == 7, [("mergedT", kc, hb), "wout"], [("ps", yb[dh])])
            emit_layernorm(C, xs, t, yb, 1.0, lnp[:, 0, :], lnp[:, 1, :], tmp, "ln2")
            if t >= 1:
                emit_transposes(C, xs, xT, t - 1, ("dve", "act"), cast=False)
        emit_transposes(C, xs, xT, 7, ("dve", "act"), cast=False)
        P.barrier()
        emit_ffn(C, xs, xT, actT, wd, wb, sil, I["w_ffn2_up"], I["w_ffn2_down"], "f2", True)
        for t in range(8):
            yb = (4, 5) if (t % 2 == 0 or not FLAG_LNPIPE) else (2, 3)
            emit_down(C, actT, wd, t, yb)
            emit_layernorm(C, xs, t, yb, 0.5, lnp[:, 2, :], lnp[:, 3, :], tmp, "ln3")
            if t >= 1:
                emit_transposes(C, xs, xT, t - 1, ("dve", "act"), cast=False)
        emit_transposes(C, xs, xT, 7, ("dve", "act"), cast=False)
        P.barrier()
        DMA(P, "pool", wpg, wview(I["w_ple_gate"], 8), "wpg", writes=["wpg"])
        DMA(P, "pool", wpl, wview(I["w_ple"], 2), "wpl", writes=["wpl"])
        for t in range(8):
            pf = pt_f[0]
            DMA(P, "sp", pf, pv[:, tb * 8 + t, :], "pf0", writes=[("pf", 0)])
            pb = C.xb[0][:, 0:256]
            CP(P, "pool", pb, pf[:, :], [("pf", 0)], [("xb", 0)])
            for k in range(2):
                MM(P, C.ps[6][:, k * 128:(k + 1) * 128], pb[:, k * 128:(k + 1) * 128], C.ident_b[:, :], True, True, [("xb", 0), "ident_b"], [("ps", 6)])
            CP(P, "dve", pT[:, :, t * 128:(t + 1) * 128], C.ps[6][:, 0:256].rearrange("p (a b) -> p a b", a=2), [("ps", 6)], [("pT", t)])
            for dh in range(2):
                for kc in range(8):
                    MM(P, C.ps[dh][:, :], xT[:, kc, t * 128:(t + 1) * 128], wpg[:, kc, dh * 512:(dh + 1) * 512],
                       kc == 0, kc == 7, [("xT", t), "wpg"], [("ps", dh)])
                ACT(P, tT[:, dh * 512:(dh + 1) * 512], C.ps[dh][:, :], AF.Sigmoid, [("ps", dh)], [("tT", dh)])
                for kc in range(2):
                    MM(P, C.ps[2 + dh][:, :], pT[:, kc, t * 128:(t + 1) * 128], wpl[:, kc, dh * 512:(dh + 1) * 512],
                       kc == 0, kc == 1, [("pT", t), "wpl"], [("ps", 2 + dh)])
                TT(P, "dve", sil[dh][:, :], tT[:, dh * 512:(dh + 1) * 512], C.ps[2 + dh][:, :], ALU.mult, [("tT", dh), ("ps", 2 + dh)], [("sil", dh)])
                TT(P, "dve", xn[:, dh * 512:(dh + 1) * 512], sil[dh][:, :], xs[:, t, dh * 512:(dh + 1) * 512], ALU.add,
                   [("sil", dh), ("xs", t)], [("xo", dh)])
            DMA(P, "sp", outv[:, tb * 8 + t, :], xn[:, :], "outst", reads=[("xo", 0), ("xo", 1)])
        P.barrier()


def host_consts(mode, r):
    c = {}
    c["ident"] = np.eye(128, dtype=np.float32)
    ki = np.arange(128)[:, None]
    qi = np.arange(128)[None, :]
    c["tri"] = np.where(ki > qi, -BIGM, 0.0).astype(np.float32)
    slopes = np.concatenate([2.0 ** (-8.0 * np.arange(1, 9) / 8), 2.0 ** (-8.0 * np.arange(1, 5) / 4)]).astype(np.float64)
    if mode == "P4":
        NQ, NK = 4096, 4096
        qpos = np.arange(NQ)
        kpos = np.arange(NK)
        kvalid = np.ones(NK, bool)
    else:
        NQ, NK = 2048, 4096
        qpos = r * 2048 + np.arange(NQ)
        kpos = np.concatenate([np.arange(2048), r * 2048 + np.arange(2048)])
        kvalid = np.concatenate([np.full(2048, r == 1), np.ones(2048, bool)])
    qc = np.zeros((12, 4, NQ), np.float32)
    kc = np.zeros((12, 4, NK), np.float32)
    for h in range(12):
        s8 = 8.0 * slopes[h]
        qc[h, 0] = -s8 * (qpos % 128)
        qc[h, 1] = -s8 * 128.0 * (qpos // 128)
        qc[h, 2] = 1.0
        qc[h, 3] = 1.0
        kc[h, 0] = 1.0
        kc[h, 1] = 1.0
        kc[h, 2] = s8 * (kpos % 128)
        kc[h, 3] = np.where(kvalid, s8 * 128.0 * (kpos // 128), -BIGM)
    c["qc"], c["kc"] = qc, kc
    kind = np.zeros((16, NK), np.float32)
    for n in range(16):
        kind[n, n * 256:(n + 1) * 256] = BIGM
    c["kind"] = kind
    nt = NQ // 128
    vb = np.zeros((nt, 16), np.float32)
    lo = np.zeros((nt, 16), np.float32)
    for t in range(nt):
        b_loc = t // 2
        for n in range(16):
            if mode == "P4":
                st = "past" if n < b_loc else ("own" if n == b_loc else "no")
            else:
                if n < 8:
                    st = "past" if r == 1 else "no"
                else:
                    nn = n - 8
                    st = "past" if nn < b_loc else ("own" if nn == b_loc else "no")
            vb[t, n] = {"past": 0.0, "own": 1e30, "no": -1e30}[st]
            lo[t, n] = 0.0 if st in ("past", "own") else -1.0
    c["vb"] = np.ascontiguousarray(np.broadcast_to(vb.reshape(1, nt // 4, 4, 16), (128, nt // 4, 4, 16)))
    c["lo"] = np.ascontiguousarray(np.broadcast_to(lo.reshape(1, nt // 4, 4, 16), (128, nt // 4, 4, 16)))
    return c


_CACHE = {}
MODE = "P8R"


def kernel(**inputs):
    f = lambda k: np.ascontiguousarray(np.asarray(inputs[k], dtype=np.float32)[0])
    mode = MODE
    if mode not in _CACHE:
        _CACHE[mode] = build(mode)
    nc = _CACHE[mode]
    x = np.asarray(inputs["x"], dtype=np.float32)
    p = np.asarray(inputs["p"], dtype=np.float32)[0]
    shared = {k: f(k) for k in ("w_ffn1_up", "w_ffn1_down", "w_ffn2_up", "w_ffn2_down", "w_in", "w_branch_a",
                                "w_branch_b", "w_out", "w_ple", "w_ple_gate")}
    lnp = np.stack([f(k) for k in ("ln1_g", "ln1_b", "ln2_g", "ln2_b", "ln3_g", "ln3_b")], 0)
    shared["lnp"] = np.ascontiguousarray(np.broadcast_to(lnp[None], (128, 6, D)))
    shared["sublng"] = np.ascontiguousarray(np.broadcast_to(f("subln_g")[None], (128, 128)))
    lamv = np.stack([f(k) for k in ("lambda_q1", "lambda_k1", "lambda_q2", "lambda_k2")], 0)
    shared["lamv"] = np.ascontiguousarray(np.broadcast_to(lamv[None], (128, 4, 64)))
    consts = {}
    in_maps = []
    for c in range(N_CORES):
        m = dict(shared)
        if mode == "P4":
            b, r = c % 4, 0
            m["x"] = np.ascontiguousarray(x[b])
            m["p"] = np.ascontiguousarray(p[b])
        else:
            b, r = c // 2, c % 2
            m["x"] = np.ascontiguousarray(np.concatenate([x[b, 0:2048], x[b, r * 2048:(r + 1) * 2048]], 0))
            m["p"] = np.ascontiguousarray(p[b, r * 2048:(r + 1) * 2048])
        if r not in consts:
            consts[r] = host_consts(mode, r)
        m.update(consts[r])
        in_maps.append(m)
    res = run_bass_kernel_spmd(nc, in_maps, core_ids=list(range(N_CORES)))
    if mode == "P4":
        out = np.stack([np.asarray(res.results[b]["out"], dtype=np.float32) for b in range(4)], 0)
    else:
        out = np.stack([np.concatenate([np.asarray(res.results[2 * b + r]["out"], dtype=np.float32) for r in range(2)], 0)
                        for b in range(4)], 0)
    return out
```

```python
import contextlib
import numpy as np
import concourse.bass as bass
import concourse.mybir as mybir
from concourse.bass_utils import run_bass_kernel_spmd

F32 = mybir.dt.float32
BF16 = mybir.dt.bfloat16
AF = mybir.ActivationFunctionType
ALU = mybir.AluOpType
AX = mybir.AxisListType

D = 1024
DFF = 2816
SEQ = 4096
NFF = 22
ALPHA = float(2.0 ** 0.25)
BIGM = 262144.0
EPS = 1e-5
LAM_INIT = 0.8 - 0.6 * 1.0
N_CORES = 8
import os
FLAG_DEFER = os.environ.get("K_DEFER", "1") == "1"
FLAG_S2DEFER = os.environ.get("K_S2DEFER", "1") == "1"
FLAG_BURST = int(os.environ.get("K_BURST", "10"))
FLAG_LNPIPE = os.environ.get("K_LNPIPE", "1") == "1"
FLAG_MERGEDB = os.environ.get("K_MERGEDB", "1") == "1"


class Prog:
    ENGS = ("pe", "act", "dve", "pool", "sp")

    def __init__(self, nc):
        self.nc = nc
        self.ops = []
        self.last_w = {}
        self.readers = {}
        self.dma_keys = {}
        self.last_on_eng = {}
        self.dma_since_barrier = []
        self.barrier_deps = {}

    def op(self, eng, fn, reads=(), writes=(), dma=0, key=None):
        idx = len(self.ops)
        deps = set()
        for r in reads:
            w = self.last_w.get(r)
            if w is not None:
                deps.add(w)
        for r in writes:
            w = self.last_w.get(r)
            if w is not None:
                deps.add(w)
            for rd in self.readers.get(r, {}).values():
                deps.add(rd)
        for r in reads:
            d = self.readers.setdefault(r, {})
            d[eng if not dma else ("dma", idx)] = idx
        for r in writes:
            self.last_w[r] = idx
            self.readers[r] = {}
        bd = self.barrier_deps.pop(eng, None)
        if bd:
            deps.update(bd)
        deps.discard(idx)
        o = dict(eng=eng, fn=fn, deps=deps, dma=dma, sig=False)
        if dma:
            assert key is not None
            if key not in self.dma_keys:
                self.dma_keys[key] = len(self.dma_keys)
            o["dkey"] = self.dma_keys[key]
            self.dma_since_barrier.append(idx)
        else:
            self.last_on_eng[eng] = idx
        self.ops.append(o)
        return idx

    def barrier(self):
        deps = set(self.dma_since_barrier)
        deps.update(self.last_on_eng.values())
        self.dma_since_barrier = []
        self.barrier_deps = {e: set(deps) for e in self.ENGS}

    def emit(self):
        nc = self.nc
        ops = self.ops
        for o in ops:
            for d in o["deps"]:
                p = ops[d]
                if p["eng"] == "pe" and o["eng"] == "pe" and not p["dma"] and not o["dma"]:
                    continue
                p["sig"] = True
        es = contextlib.ExitStack()
        eng_sem = {e: es.enter_context(nc.semaphore("s_" + e)) for e in self.ENGS}
        cnt = {e: 0 for e in self.ENGS}
        nk = len(self.dma_keys)
        assert nk <= 120, nk
        dma_sems = [es.enter_context(nc.semaphore("d%d" % i)) for i in range(nk)]
        dcnt = [0] * nk
        for o in ops:
            if o["dma"]:
                k = o["dkey"]
                dcnt[k] += 16 * o["dma"]
                o["sigval"] = dcnt[k]
            elif o["sig"]:
                cnt[o["eng"]] += 1
                o["sigval"] = cnt[o["eng"]]
        per_eng = {e: [] for e in self.ENGS}
        for i, o in enumerate(ops):
            per_eng[o["eng"]].append(i)
        with es, nc.Block() as block:
            def run(engname, engobj):
                waited = {}
                for i in per_eng[engname]:
                    o = ops[i]
                    for d in sorted(o["deps"]):
                        p = ops[d]
                        if p["dma"]:
                            sem = dma_sems[p["dkey"]]
                            key = ("d", p["dkey"])
                        else:
                            if p["eng"] == "pe" and engname == "pe" and not o["dma"]:
                                continue
                            sem = eng_sem[p["eng"]]
                            key = ("e", p["eng"])
                        v = p["sigval"]
                        if waited.get(key, 0) >= v:
                            continue
                        waited[key] = v
                        engobj.wait_ge(sem, v)
                    if o["dma"]:
                        o["fn"](engobj, dma_sems[o["dkey"]])
                    else:
                        inst = o["fn"](engobj)
                        if o["sig"]:
                            inst.then_inc(eng_sem[engname], 1)
                for i in per_eng[engname]:
                    o = ops[i]
                    if o["dma"]:
                        key = ("d", o["dkey"])
                        if waited.get(key, 0) < o["sigval"]:
                            waited[key] = o["sigval"]
                            engobj.wait_ge(dma_sems[o["dkey"]], o["sigval"])

            @block.tensor
            def _(e):
                run("pe", e)

            @block.scalar
            def _(e):
                run("act", e)

            @block.vector
            def _(e):
                run("dve", e)

            @block.gpsimd
            def _(e):
                run("pool", e)

            @block.sync
            def _(e):
                run("sp", e)


class Arena:
    def __init__(self, nc, st, nbytes):
        self.t = st.enter_context(nc.sbuf_tensor("arena", [128, nbytes // 4], F32))
        self.cap = nbytes
        self.off = 0

    def reset(self, off=0):
        self.off = off

    def alloc(self, shape, dtype):
        esz = 4 if dtype == F32 else 2
        n = int(np.prod(shape[1:]))
        nb = (n * esz + 63) // 64 * 64
        assert self.off + nb <= self.cap, (self.off, nb, self.cap)
        w0 = self.off // 4
        ap = self.t[0:shape[0], w0:w0 + nb // 4]
        if dtype != F32:
            ap = ap.bitcast(dtype)
        ap = ap[:, 0:n]
        if len(shape) == 3:
            ap = ap.rearrange("p (a b) -> p a b", a=shape[1])
        elif len(shape) == 4:
            ap = ap.rearrange("p (a b c) -> p a b c", a=shape[1], b=shape[2])
        self.off += nb
        return ap


def MM(P, out, lhsT, rhs, start, stop, reads, writes):
    P.op("pe", lambda e: e.matmul(out, lhsT=lhsT, rhs=rhs, start=start, stop=stop), reads, writes)


def TR(P, out, in_, ident, reads, writes):
    P.op("pe", lambda e: e.transpose(out, in_, ident), reads, writes)


def ACT(P, out, in_, func, reads, writes, scale=None, bias=None, accum_out=None):
    kw = {}
    if scale is not None:
        kw["scale"] = scale
    if bias is not None:
        kw["bias"] = bias
    if accum_out is not None:
        kw["accum_out"] = accum_out
    P.op("act", lambda e: e.activation(out=out, in_=in_, func=func, **kw), reads, writes)


def TT(P, eng, out, in0, in1, op, reads, writes):
    P.op(eng, lambda e: e.tensor_tensor(out=out, in0=in0, in1=in1, op=op), reads, writes)


def TS(P, eng, out, in0, s1, s2, op0, op1, reads, writes):
    if op1 is None:
        P.op(eng, lambda e: e.tensor_scalar(out=out, in0=in0, scalar1=s1, scalar2=None, op0=op0), reads, writes)
    else:
        P.op(eng, lambda e: e.tensor_scalar(out=out, in0=in0, scalar1=s1, scalar2=s2, op0=op0, op1=op1), reads, writes)


def STT(P, out, in0, scalar, in1, op0, op1, reads, writes):
    P.op("dve", lambda e: e.scalar_tensor_tensor(out=out, in0=in0, scalar=scalar, in1=in1, op0=op0, op1=op1), reads, writes)


def CP(P, eng, out, in_, reads, writes):
    if eng == "act":
        P.op("act", lambda e: e.activation(out=out, in_=in_, func=AF.Identity), reads, writes)
    else:
        P.op(eng, lambda e: e.tensor_copy(out=out, in_=in_), reads, writes)


def DMA(P, q, out, in_, key, reads=(), writes=()):
    P.op(q, lambda e, s: e.dma_start(out=out, in_=in_).then_inc(s, 16), reads, writes, dma=1, key=key)


class Ctx:
    pass


def build(MODE, debug=False):
    nc = bass.Bass("TRN2", target_bir_lowering=False)
    C = Ctx()
    C.nc = nc
    C.NHM, C.NHD = 8, 4
    if MODE == "P4":
        C.NQ, C.NK, C.KOFF = 4096, 4096, 0
    else:
        C.NQ, C.NK, C.KOFF = 2048, 4096, 2048
    NQ, NK = C.NQ, C.NK
    TB = 1024
    C.TB = TB

    def din(name, shape):
        return nc.dram_tensor(name, list(shape), F32, kind="ExternalInput").ap()

    I = {}
    I["x"] = din("x", [NK, D])
    I["p"] = din("p", [NQ, 256])
    I["w_ffn1_up"] = din("w_ffn1_up", [D, 2 * DFF])
    I["w_ffn1_down"] = din("w_ffn1_down", [DFF, D])
    I["w_ffn2_up"] = din("w_ffn2_up", [D, 2 * DFF])
    I["w_ffn2_down"] = din("w_ffn2_down", [DFF, D])
    I["w_in"] = din("w_in", [D, 5120])
    I["w_branch_a"] = din("w_branch_a", [512, D])
    I["w_branch_b"] = din("w_branch_b", [512, D])
    I["w_out"] = din("w_out", [D, D])
    I["w_ple"] = din("w_ple", [256, D])
    I["w_ple_gate"] = din("w_ple_gate", [D, D])
    I["lnp"] = din("lnp", [128, 6, D])
    I["sublng"] = din("sublng", [128, 128])
    I["lamv"] = din("lamv", [128, 4, 64])
    I["ident"] = din("ident", [128, 128])
    I["tri"] = din("tri", [128, 128])
    I["qc"] = din("qc", [12, 4, NQ])
    I["kc"] = din("kc", [12, 4, NK])
    I["kind"] = din("kind", [16, NK])
    I["vb"] = din("vb", [128, NQ // 512, 4, 16])
    I["lo"] = din("lo", [128, NQ // 512, 4, 16])
    C.I = I
    kind_s = "ExternalOutput" if debug else "Internal"
    C.out = nc.dram_tensor("out", [NQ, D], F32, kind="ExternalOutput").ap()
    C.X1_s = nc.dram_tensor("X1_s", [NQ, D], F32, kind=kind_s).ap()
    C.X1T_s = nc.dram_tensor("X1T_s", [D, NQ], BF16, kind=kind_s).ap()
    C.QT_s = nc.dram_tensor("QT_s", [D, NQ], BF16, kind=kind_s).ap()
    C.KT_s = nc.dram_tensor("KT_s", [D, NK], BF16, kind=kind_s).ap()
    C.V_s = nc.dram_tensor("V_s", [NK, D], BF16, kind=kind_s).ap()
    C.YT_s = nc.dram_tensor("YT_s", [D, NQ], BF16, kind=kind_s).ap()

    with contextlib.ExitStack() as st:
        C.P = P = Prog(nc)
        C.arena = Arena(nc, st, 192 * 1024)
        C.ps = [st.enter_context(nc.psum_tensor("ps%d" % i, [128, 512], F32)) for i in range(8)]
        sb = lambda n, s, d: st.enter_context(nc.sbuf_tensor(n, s, d))
        C.ident_f = sb("ident_f", [128, 128], F32)
        C.ident_b = sb("ident_b", [128, 128], BF16)
        C.tri_b = sb("tri_b", [128, 128], BF16)
        C.eps_t = sb("eps_t", [128, 1], F32)
        C.lam_t = sb("lam_t", [128, 4], F32)
        C.lamv = sb("lamv_t", [128, 4, 64], F32)
        C.sublng = sb("sublng_t", [128, 128], F32)
        C.small = sb("small_t", [128, 64], F32)

        phase0(C)
        phase1(C)
        P.barrier()
        phase2(C)
        P.barrier()
        phase3(C)
        P.emit()
    return nc


def phase0(C):
    P, I = C.P, C.I
    DMA(P, "sp", C.ident_f[:], I["ident"], "c_if", writes=["ident_f"])
    DMA(P, "pool", C.ident_b[:], I["ident"], "c_ib", writes=["ident_b"])
    DMA(P, "pool", C.tri_b[:], I["tri"], "c_tri", writes=["tri_b"])
    DMA(P, "sp", C.lamv[:], I["lamv"], "c_lamv", writes=["lamv"])
    DMA(P, "sp", C.sublng[:], I["sublng"], "c_sub", writes=["sublng"])
    P.op("dve", lambda e: e.memset(C.eps_t[:], EPS), writes=["eps"])
    sm = C.small
    TT(P, "dve", sm[:, 0:64], C.lamv[:, 0, :], C.lamv[:, 1, :], ALU.mult, ["lamv"], ["sm0"])
    P.op("dve", lambda e: e.tensor_reduce(out=C.lam_t[:, 2:3], in_=sm[:, 0:64], axis=AX.X, op=ALU.add), ["sm0"], ["l2"])
    TT(P, "dve", sm[:, 0:64], C.lamv[:, 2, :], C.lamv[:, 3, :], ALU.mult, ["lamv", "l2"], ["sm0"])
    P.op("dve", lambda e: e.tensor_reduce(out=C.lam_t[:, 3:4], in_=sm[:, 0:64], axis=AX.X, op=ALU.add), ["sm0"], ["l3"])
    ACT(P, C.lam_t[:, 2:4], C.lam_t[:, 2:4], AF.Exp, ["l2", "l3"], ["l23"])
    TT(P, "dve", C.lam_t[:, 0:1], C.lam_t[:, 2:3], C.lam_t[:, 3:4], ALU.subtract, ["l23"], ["lam0"])
    TS(P, "dve", C.lam_t[:, 0:1], C.lam_t[:, 0:1], float(LAM_INIT), None, ALU.add, None, ["lam0"], ["lam"])
    TS(P, "dve", C.lam_t[:, 1:2], C.lam_t[:, 0:1], -1.0, None, ALU.mult, None, ["lam"], ["nlam"])


def wview(w, kcn):
    return w.rearrange("(kc p) n -> p kc n", p=128)


def emit_transposes(C, xs, xT, t, evac_eng, cast=True):
    P = C.P
    xi = t % len(C.xb)
    xb = C.xb[xi]
    xbn = ("xb", xi)
    if cast:
        CP(P, "act", xb[:, :], xs[:, t, :], [("xs", t)], [xbn])
    for g in range(2):
        bank = C.ps[6 + g]
        for k in range(4):
            kc = g * 4 + k
            MM(P, bank[:, k * 128:(k + 1) * 128], xb[:, kc * 128:(kc + 1) * 128], C.ident_b[:, :], True, True,
               [xbn, "ident_b"], [("ps", 6 + g)])
        src = bank[:, :].rearrange("p (a b) -> p a b", a=4)
        CP(P, evac_eng[g], xT[:, g * 4:(g + 1) * 4, t * 128:(t + 1) * 128], src, [("ps", 6 + g)], [("xT", t)])


def emit_layernorm(C, xs, t, ybanks, yscale, lng, lnb, tmp, lnname):
    P = C.P
    tT, r, xn, st6, mv = tmp
    for h in range(2):
        ACT(P, tT[:, h * 512:(h + 1) * 512], C.ps[ybanks[h]][:, :], AF.Identity, [("ps", ybanks[h])], [("tT", h)], scale=float(yscale))
    STT(P, r[:, :], xs[:, t, :], ALPHA, tT[:, :], ALU.mult, ALU.add, [("xs", t), ("tT", 0), ("tT", 1)], ["r"])
    for h in range(2):
        P.op("dve", (lambda hh: (lambda e: e.bn_stats(out=st6[:, hh, :], in_=r[:, hh * 512:(hh + 1) * 512])))(h), ["r"], [("st6", h)])
    P.op("dve", lambda e: e.bn_aggr(out=mv[:, 0:2], in_=st6[:, :, :].rearrange("p a b -> p (a b)")), [("st6", 0), ("st6", 1)], ["mv"])
    ACT(P, mv[:, 2:3], mv[:, 1:2], AF.Sqrt, ["mv", "eps"], ["std"], bias=C.eps_t[:], scale=1.0)
    P.op("dve", lambda e: e.reciprocal(out=mv[:, 3:4], in_=mv[:, 2:3]), ["std"], ["rstd"])
    STT(P, mv[:, 4:5], mv[:, 0:1], -1.0, mv[:, 3:4], ALU.mult, ALU.mult, ["mv", "rstd"], ["nmr"])
    ACT(P, xn[:, :], r[:, :], AF.Identity, ["r", "rstd", "nmr"], ["xn"], scale=mv[:, 3:4], bias=mv[:, 4:5])
    TT(P, "dve", xn[:, :], xn[:, :], lng, ALU.mult, ["xn", lnname], ["xn"])
    TT(P, "dve", xs[:, t, :], xn[:, :], lnb, ALU.add, ["xn", lnname], [("xs", t)])
    xi = t % len(C.xb)
    CP(P, "act", C.xb[xi][:, :], xs[:, t, :], [("xs", t)], [("xb", xi)])


def emit_ffn(C, xs, xT, actT, wd, wb, sil, w_up, w_down, tag, first):
    P = C.P
    TB = C.TB
    wu = wview(w_up, 8)
    if first:
        wdv = wview(w_down, NFF)
        DMA(P, "pool", wd[:, 0:11, :], wdv[:, 0:11, :], tag + "wd0", writes=[("wd", 0)])
        DMA(P, "pool", wd[:, 11:22, :], wdv[:, 11:22, :], tag + "wd1", writes=[("wd", 1)])
    par = 0
    for gi in range(NFF // 2):
        slot = gi % 2
        buf = wb[slot]
        DMA(P, "pool", buf[:, :, 0:256], wu[:, :, gi * 256:(gi + 1) * 256], "wbA%d" % slot, writes=[("wb", slot, 0)])
        DMA(P, "pool", buf[:, :, 256:512], wu[:, :, DFF + gi * 256:DFF + (gi + 1) * 256], "wbG%d" % slot, writes=[("wb", slot, 1)])
        for jj in range(2):
            j = gi * 2 + jj
            for hb in range(TB // 512):
                bA, bG = 2 * par, 2 * par + 1
                xr = [("xT", 4 * hb + q) for q in range(4)]
                for kc in range(8):
                    MM(P, C.ps[bA][:, :], buf[:, kc, jj * 128:(jj + 1) * 128], xT[:, kc, hb * 512:(hb + 1) * 512],
                       kc == 0, kc == 7, [("wb", slot, 0)] + xr, [("ps", bA)])
                for kc in range(8):
                    MM(P, C.ps[bG][:, :], buf[:, kc, 256 + jj * 128:256 + (jj + 1) * 128], xT[:, kc, hb * 512:(hb + 1) * 512],
                       kc == 0, kc == 7, [("wb", slot, 1)] + xr, [("ps", bG)])
                ACT(P, sil[par][:, :], C.ps[bA][:, :], AF.Silu, [("ps", bA)], [("sil", par)])
                TT(P, "dve", actT[:, j, hb * 512:(hb + 1) * 512], sil[par][:, :], C.ps[bG][:, :], ALU.mult,
                   [("sil", par), ("ps", bG)], [("actT", j, hb)])
                par ^= 1


def emit_down(C, actT, wd, t, yb=(4, 5)):
    P = C.P
    hb = t // 4
    for dh in range(2):
        for kc in range(NFF):
            MM(P, C.ps[yb[dh]][:, :], actT[:, kc, t * 128:(t + 1) * 128], wd[:, kc, dh * 512:(dh + 1) * 512],
               kc == 0, kc == NFF - 1, [("actT", kc, hb), ("wd", kc // 11)], [("ps", yb[dh])])


def phase1(C):
    P, I, A = C.P, C.I, C.arena
    TB = C.TB
    NTB = C.NK // TB
    own_tb0 = C.KOFF // TB
    A.reset()
    xs = A.alloc([128, 8, D], F32)
    xT = A.alloc([128, 8, TB], BF16)
    actT = A.alloc([128, NFF, TB], BF16)
    wd = A.alloc([128, NFF, D], BF16)
    wb = [A.alloc([128, 8, 512], BF16) for _ in range(2)]
    lnp = A.alloc([128, 2, D], F32)
    tT = A.alloc([128, D], F32)
    r = A.alloc([128, D], F32)
    xn = A.alloc([128, D], F32)
    st6 = A.alloc([128, 2, 6], F32)
    mv = A.alloc([128, 8], F32)
    sil = [A.alloc([128, 512], F32) for _ in range(2)]
    stg = [A.alloc([128, TB], BF16) for _ in range(2)]
    vst = [A.alloc([128, 512], BF16) for _ in range(2)]
    tmp = (tT, r, xn, st6, mv)
    C.xb = [A.alloc([128, D], BF16) for _ in range(2)]
    C.xbi = 0
    DMA(P, "sp", lnp, I["lnp"][:, 0:2, :], "lnp1", writes=["ln1"])
    xv = I["x"].rearrange("(t p) d -> p t d", p=128)
    x1v = C.X1_s.rearrange("(t p) d -> p t d", p=128)
    win = wview(I["w_in"], 8)
    for tb in range(NTB):
        own = tb >= own_tb0
        qtb = tb - own_tb0
        for t in range(8):
            DMA(P, "sp", xs[:, t, :], xv[:, tb * 8 + t, :], "xs%d" % t, writes=[("xs", t)])
        for t in range(8):
            emit_transposes(C, xs, xT, t, ("dve", "act"))
        emit_ffn(C, xs, xT, actT, wd, wb, sil, I["w_ffn1_up"], I["w_ffn1_down"], "f1", tb == 0)
        for t in range(8):
            yb = (4, 5) if (t % 2 == 0 or not FLAG_LNPIPE) else (2, 3)
            emit_down(C, actT, wd, t, yb)
            emit_layernorm(C, xs, t, yb, 0.5, lnp[:, 0, :], lnp[:, 1, :], tmp, "ln1")
            if own:
                DMA(P, "sp", x1v[:, qtb * 8 + t, :], xs[:, t, :], "x1st%d" % t, reads=[("xs", t)])
            if t >= 1:
                emit_transposes(C, xs, xT, t - 1, ("dve", "act"), cast=False)
        emit_transposes(C, xs, xT, 7, ("dve", "act"), cast=False)
        if own:
            for kc in range(8):
                DMA(P, "sp", C.X1T_s[kc * 128:(kc + 1) * 128, qtb * TB:(qtb + 1) * TB], xT[:, kc, :], "x1T%d" % kc,
                    reads=[("xT", q) for q in range(8)])
        par = 0
        for g in range(6):
            kind = ("q", "k", "v")[g % 3]
            fam = g // 3
            if kind == "q" and not own:
                continue
            slot = g % 2
            buf = wb[slot]
            DMA(P, "pool", buf[:, :, 0:256], win[:, :, g * 512:g * 512 + 256], "wbA%d" % slot, writes=[("wb", slot, 0)])
            DMA(P, "pool", buf[:, :, 256:512], win[:, :, g * 512 + 256:(g + 1) * 512], "wbG%d" % slot, writes=[("wb", slot, 1)])
            if kind in ("q", "k"):
                dst = C.QT_s if kind == "q" else C.KT_s
                col0 = qtb * TB if kind == "q" else tb * TB
                for c in range(4):
                    sg = stg[par]
                    for hb in range(TB // 512):
                        bank = 2 * par + hb
                        for kc in range(8):
                            MM(P, C.ps[bank][:, :], buf[:, kc, c * 128:(c + 1) * 128], xT[:, kc, hb * 512:(hb + 1) * 512],
                               kc == 0, kc == 7, [("wb", slot, c // 2)] + [("xT", 4 * hb + q) for q in range(4)], [("ps", bank)])
                        CP(P, "act" if hb == 0 else "dve", sg[:, hb * 512:(hb + 1) * 512], C.ps[bank][:, :], [("ps", bank)], [("stg", par, hb)])
                    row0 = fam * 512 + c * 128
                    DMA(P, "sp", dst[row0:row0 + 128, col0:col0 + TB], sg[:, :], "stg%d" % par,
                        reads=[("stg", par, 0), ("stg", par, 1)])
                    par ^= 1
            else:
                vv = C.V_s.rearrange("(t p) d -> p t d", p=128)
                for t in range(8):
                    bank = 2 * par
                    for kc in range(8):
                        MM(P, C.ps[bank][:, :], xT[:, kc, t * 128:(t + 1) * 128], buf[:, kc, :],
                           kc == 0, kc == 7, [("wb", slot, 0), ("wb", slot, 1), ("xT", t)], [("ps", bank)])
                    CP(P, "act" if t % 2 == 0 else "dve", vst[par][:, :], C.ps[bank][:, :], [("ps", bank)], [("vst", par)])
                    DMA(P, "sp", vv[:, tb * 8 + t, fam * 512:(fam + 1) * 512], vst[par][:, :], "vst%d" % par, reads=[("vst", par)])
                    par ^= 1


def phase2(C):
    P, I, A = C.P, C.I, C.arena
    A.reset()
    NHM, NHD = C.NHM, C.NHD
    NQ, NK, KOFF = C.NQ, C.NK, C.KOFF
    NQB = NQ // 512
    NKC = NK // 128
    KO = KOFF // 128
    KTs = [[A.alloc([128, NK], BF16) for m in range(2)] for _ in range(2)]
    QTs = [[A.alloc([128, NQ], BF16) for m in range(2)] for _ in range(2)]
    VEs = [A.alloc([128, NKC, 132], BF16) for _ in range(2)]
    PT = [A.alloc([128, 512], BF16) for _ in range(4)]
    SB = [0, 1, 5]
    accsb = [A.alloc([128, 4, 2, 132], F32) for _ in range(2)]
    o1 = A.alloc([128, 4, 128], F32)
    yd = A.alloc([128, 4, 128], F32)
    ydb = A.alloc([128, 4, 128], BF16)
    sq = A.alloc([128, 128], F32)
    sc = A.alloc([128, 4, 8], F32)
    ystg = [A.alloc([128, 512], BF16) for _ in range(2)]
    kmf = A.alloc([64, 16], F32)
    kmb = [A.alloc([64, 16], BF16) for _ in range(2)]
    vbt = A.alloc([128, NQB, 4, 16], F32)
    lot = A.alloc([128, NQB, 4, 16], F32)
    gm = [A.alloc([128, 4, 16], F32) for _ in range(2)]
    m8 = [A.alloc([128, 4, 8], F32) for _ in range(2)]
    m1 = [A.alloc([128, 4, 16], F32) for _ in range(2)]
    npad = [A.alloc([128, 4, 80], BF16) for _ in range(2)]
    ones_col = A.alloc([128, NKC, 1], F32)
    ones_k = A.alloc([128, 2], BF16)
    one_f = A.alloc([1, 2], F32)
    OTsb = [A.alloc([128, 512], BF16) for _ in range(2)]
    ZTsb = [A.alloc([1, 512], F32) for _ in range(2)]
    Zhi = [A.alloc([1, 512], BF16) for _ in range(2)]
    Zlo = [A.alloc([1, 512], BF16) for _ in range(2)]
    Zd = [A.alloc([1, 512], F32) for _ in range(2)]
    one_b = A.alloc([1, 2], BF16)
    P.op("dve", lambda e: e.memset(one_b[:, :], 1.0), writes=["one_b"])
    one_b64 = A.alloc([128, 2], BF16)
    P.op("dve", lambda e: e.memset(one_b64[:, :], 1.0), writes=["one_b64"])
    sel = [A.alloc([128, 128], BF16) for _ in range(2)]
    for mm_ in range(2):
        P.op("dve", (lambda mm_=mm_: (lambda e: e.memset(sel[mm_][:, 0:64], 1.0 if mm_ == 0 else 0.0)))(), writes=[("sel", mm_)])
        P.op("dve", (lambda mm_=mm_: (lambda e: e.memset(sel[mm_][:, 64:128], 0.0 if mm_ == 0 else 1.0)))(), writes=[("sel", mm_)])
    ZB_f = A.alloc([128, 512], F32)
    ZB_d = A.alloc([128, 512], F32)
    ZB_hi = A.alloc([128, 512], BF16)
    ZB_lo = A.alloc([128, 512], BF16)
    OTfull = A.alloc([128, 512], BF16)
    Zm_f = A.alloc([128, 512], F32)
    Zm_d = A.alloc([128, 512], F32)
    Zm_lo = A.alloc([128, 512], BF16)
    Zm_lo_full = Zm_lo
    P.op("dve", lambda e: e.memset(Zm_lo[:, :], 0.0), writes=[("Zm_lo", 0)])

    ones_kk = A.alloc([128, 128], BF16)
    P.op("dve", lambda e: e.memset(ones_kk[:, :], 1.0), writes=["ones_kk"])
    for par in range(2):
        for m in range(2):
            P.op("pool", (lambda par=par, m=m: (lambda e: e.memset(KTs[par][m][64:128, :], 0.0)))(), writes=[("KT", par, m)])
            P.op("pool", (lambda par=par, m=m: (lambda e: e.memset(QTs[par][m][64:128, :], 0.0)))(), writes=[("QT", par, m), ("QTm", par, m)])
    P.op("dve", lambda e: e.memset(ones_k[:, :], 1.0), writes=["ones_k"])
    P.op("dve", lambda e: e.memset(one_f[:, :], 1.0), writes=["one_f"])
    DMA(P, "sp", vbt, I["vb"], "vbt", writes=["vbt"])
    DMA(P, "sp", lot, I["lo"], "lot", writes=["lot"])
    for g in range(2):
        P.op("dve", (lambda g=g: (lambda e: e.memset(npad[g][:, :, :], 0.0)))(), writes=[("npad", g)])
    P.op("dve", lambda e: e.memset(ones_col[:, :, :], 1.0), writes=["ones_col"])
    vv = C.V_s.rearrange("(k p) d -> p k d", p=128)

    jobs = [("d", h) for h in range(NHD)] + [("m", h) for h in range(NHM)]

    def load_job(ji):
        kind, h = jobs[ji]
        par = ji % 2
        KT, QT, VE = KTs[par], QTs[par], VEs[par]
        if kind == "d":
            hg = NHM + h
            for m in range(2):
                r0 = 512 + h * 128 + m * 64
                DMA(P, "sp", KT[m][0:64, :], C.KT_s[r0:r0 + 64, :], "KTl%d%d" % (par, m), writes=[("KT", par, m)])
                DMA(P, "pool", KT[m][64:68, :], I["kc"][hg, :, :], "KTc%d%d" % (par, m), writes=[("KT", par, m)])
                DMA(P, "sp", QT[m][0:64, :], C.QT_s[r0:r0 + 64, :], "QTl%d%d" % (par, m), writes=[("QT", par, m)])
                DMA(P, "pool", QT[m][64:68, :], I["qc"][hg, :, :], "QTc%d%d" % (par, m), writes=[("QT", par, m)])
            DMA(P, "sp", VE[:, :, 0:128], vv[:, :, 512 + h * 128:512 + (h + 1) * 128], "VEl%d" % par, writes=[("VE", par)])
            CP(P, "dve", VE[:, :, 128:129], ones_col[:, :, :], ["ones_col"], [("VE", par)])
        else:
            r0 = h * 64
            DMA(P, "sp", KT[0][0:64, :], C.KT_s[r0:r0 + 64, :], "KTl%d0" % par, writes=[("KT", par, 0)])
            DMA(P, "pool", KT[0][64:80, :], I["kind"], "KTi%d" % par, writes=[("KT", par, 0)])
            DMA(P, "pool", KT[0][80:84, :], I["kc"][h, :, :], "KTc%d0" % par, writes=[("KT", par, 0)])
            DMA(P, "sp", QT[0][0:64, :], C.QT_s[r0:r0 + 64, :], "QTl%d0" % par, writes=[("QT", par, 0)])
            DMA(P, "pool", QT[0][80:84, :], I["qc"][h, :, :], "QTc%d0" % par, writes=[("QT", par, 0)])
            DMA(P, "sp", VE[:, :, 0:64], vv[:, :, h * 64:(h + 1) * 64], "VEl%d" % par, writes=[("VE", par)])
            CP(P, "dve", VE[:, :, 64:65], ones_col[:, :, :], ["ones_col"], [("VE", par)])

    def mask_stage(ji, tg):
        par = ji % 2
        KT, QT = KTs[par][0], QTs[par][0]
        if tg == 0:
            P.op("dve", lambda e: e.tensor_reduce(out=kmf[:, :], in_=KT[0:64, :].rearrange("p (n l) -> p n l", l=256), axis=AX.X, op=ALU.add),
                 [("KT", par, 0)], ["kmf"])
            TS(P, "dve", kmb[par][:, :], kmf[:, :], 1.0 / 256.0, None, ALU.mult, None, ["kmf"], [("kmb", par)])
        gp = tg % 2
        for tq in range(4):
            t = tg * 4 + tq
            MM(P, C.ps[7][:, tq * 16:(tq + 1) * 16], QT[0:64, t * 128:(t + 1) * 128], kmb[par][:, :], True, True,
               [("QT", par, 0), ("kmb", par)], [("ps", 7)])
        TT(P, "dve", gm[gp][:, :, :], C.ps[7][:, 0:64].rearrange("p (a b) -> p a b", a=4), vbt[:, tg, :, :], ALU.add,
           [("ps", 7), "vbt"], [("gm", gp)])
        for tq in range(4):
            P.op("dve", (lambda tq=tq: (lambda e: e.max(out=m8[gp][:, tq, :], in_=gm[gp][:, tq, :])))(), [("gm", gp)], [("m8", gp, tq)])
        for tq in range(4):
            TS(P, "dve", m1[gp][:, tq, :], gm[gp][:, tq, :], m8[gp][:, tq, 3:4], 1.0, ALU.is_ge, ALU.subtract,
               [("gm", gp), ("m8", gp, tq)], [("m1", gp, tq)])
        TT(P, "dve", npad[gp][:, :, 64:80], m1[gp][:, :, :], lot[:, tg, :, :], ALU.min,
           [("m1", gp, q) for q in range(4)] + ["lot"], [("npad", gp)])

    def mask_stage_b(ji, tg):
        par = ji % 2
        QT = QTs[par][0]
        gp = tg % 2
        for tq in range(4):
            MM(P, C.ps[6][0:80, tq * 128:(tq + 1) * 128], npad[gp][:, tq, :], C.ident_b[:, :], True, True,
               [("npad", gp), "ident_b"], [("ps", 6)])
        CP(P, "act", QT[64:80, tg * 512:(tg + 1) * 512], C.ps[6][64:80, :], [("ps", 6)], [("QTm", par, 0)])

    pending = []
    stepc = [0]

    dseq = [0]

    def defer(lag, fn):
        if not FLAG_DEFER:
            fn()
            return
        dseq[0] += 1
        pending.append((stepc[0] + lag, dseq[0], fn))

    def run_due(flush=False):
        while True:
            due = [p for p in pending if flush or p[0] <= stepc[0]]
            if not due:
                break
            due.sort()
            it = due[0]
            pending.remove(it)
            it[2]()

    def attention(ji, nmaps, dv, K, finalize, side):
        par = ji % 2
        KT, QT, VE = KTs[par], QTs[par], VEs[par]
        steps = [(qb, m, kc) for qb in range(NQB) for m in range(nmaps) for kc in range(KO + 4 * qb + 4)]
        n = len(steps)

        def emit_S(i):
            qb, m, kc = steps[i]
            j = kc - KO - 4 * qb
            n0 = max(0, j) * 128
            sb = SB[i % 3]
            S = C.ps[sb]
            MM(P, S[:, n0:512], KT[m][:, kc * 128:(kc + 1) * 128], QT[m][:, qb * 512 + n0:(qb + 1) * 512],
               True, j < 0, [("KT", par, m), ("QT", par, m), ("QTm", par, m)], [("ps", sb)])
            if j >= 0:
                MM(P, S[:, n0:n0 + 128], C.ident_b[:, :], C.tri_b[:, :], False, True, ["ident_b", "tri_b"], [("ps", sb)])
            ACT(P, PT[i % 4][:, n0:512], S[:, n0:512], AF.Exp, [("ps", sb)], [("PT", i % 4)], scale=0.125)

        def emit_AV(i):
            qb, m, kc = steps[i]
            j = kc - KO - 4 * qb
            n0 = max(0, j) * 128
            last = KO + 4 * qb + 3
            pt = PT[i % 4]
            if nmaps == 2:
                MM(P, C.ps[2 + m][:, n0:512], VE[:, kc, 0:128], pt[:, n0:512], kc == 0, kc == last,
                   [("PT", i % 4), ("VE", par)], [("OT", m)])
                MM(P, C.ps[4][:, n0:512], sel[m][:, :], pt[:, n0:512], (m == 0 and kc == 0), (m == 1 and kc == last),
                   [("PT", i % 4), ("sel", m)], [("ZT", 0)])
            else:
                MM(P, C.ps[2][:, n0:512], VE[:, kc, 0:128], pt[:, n0:512], kc == 0, kc == last,
                   [("PT", i % 4), ("VE", par)], [("OT", 0)])
            if m == nmaps - 1 and kc == last:
                q2 = qb % 2
                if nmaps == 2:
                    for mm in range(2):
                        CP(P, "dve", OTsb[mm][:, :], C.ps[2 + mm][:, :], [("OT", mm)], [("OTsb", mm)])
                    CP(P, "dve", ZB_f[:, :], C.ps[4][:, :], [("ZT", 0)], [("ZTsb", 0)])
                    CP(P, "dve", ZB_hi[:, :], ZB_f[:, :], [("ZTsb", 0)], [("Zhi", 0)])
                    TT(P, "dve", ZB_d[:, :], ZB_f[:, :], ZB_hi[:, :], ALU.subtract, [("ZTsb", 0), ("Zhi", 0)], [("Zd", 0)])
                    CP(P, "dve", ZB_lo[:, :], ZB_d[:, :], [("Zd", 0)], [("Zlo", 0)])
                else:
                    CP(P, "dve", OTfull[:, :], C.ps[2][:, :], [("OT", 0)], [("OTsb", 0)])
                    CP(P, "dve", Zm_f[64:128, :], C.ps[2][64:128, :], [("OT", 0)], [("Zm_f", 0)])
                    TT(P, "dve", Zm_d[64:128, :], Zm_f[64:128, :], OTfull[64:128, :], ALU.subtract, [("Zm_f", 0), ("OTsb", 0)], [("Zm_d", 0)])
                    CP(P, "dve", Zm_lo[64:128, :], Zm_d[64:128, :], [("Zm_d", 0)], [("Zm_lo", 0)])

                def stage2(qb=qb, q2=q2):
                    if nmaps == 2:
                        for mm in range(2):
                            for sub in range(4):
                                MM(P, C.ps[6][:, sub * 128:(sub + 1) * 128], OTsb[mm][:, sub * 128:(sub + 1) * 128], C.ident_b[:, :], True, True,
                                   [("OTsb", mm), "ident_b"], [("ps", 6)])
                            CP(P, "dve", accsb[q2][:, 2 * mm:2 * mm + 2, :, 0:128],
                               C.ps[6][:, :].rearrange("p (a b c) -> p a b c", a=2, b=2), [("ps", 6)], [("accsb", q2, 2 * mm), ("accsb", q2, 2 * mm + 1)])
                            for sub in range(4):
                                e0 = 0 if mm == 0 else 64
                                MM(P, C.ps[7][:, 64 + mm * 4 + sub:64 + mm * 4 + sub + 1], ZB_hi[:, sub * 128:(sub + 1) * 128], C.ident_b[:, e0:e0 + 1],
                                   True, False, [("Zhi", 0), "ident_b"], [("ps", 7)])
                                MM(P, C.ps[7][:, 64 + mm * 4 + sub:64 + mm * 4 + sub + 1], ZB_lo[:, sub * 128:(sub + 1) * 128], C.ident_b[:, e0:e0 + 1],
                                   False, True, [("Zlo", 0), "ident_b"], [("ps", 7)])
                            CP(P, "dve", accsb[q2][:, 2 * mm:2 * mm + 2, :, 128:129],
                               C.ps[7][:, 64 + mm * 4:64 + mm * 4 + 4].rearrange("p (a b c) -> p a b c", a=2, b=2), [("ps", 7)],
                               [("accsb", q2, 2 * mm), ("accsb", q2, 2 * mm + 1)])
                    else:
                        for sub in range(4):
                            MM(P, C.ps[6][:, sub * 128:sub * 128 + 65], OTfull[:, sub * 128:(sub + 1) * 128], C.ident_b[:, 0:65], True, False,
                               [("OTsb", 0), "ident_b"], [("ps", 6)])
                            MM(P, C.ps[6][:, sub * 128 + 64:sub * 128 + 65], Zm_lo_full[:, sub * 128:(sub + 1) * 128], C.ident_b[:, 64:65], False, True,
                               [("Zm_lo", 0), "ident_b"], [("ps", 6)])
                        CP(P, "dve", accsb[q2][:, 0:2, :, 0:65],
                           C.ps[6][:, :].rearrange("p (a b c) -> p a b c", a=2, b=2)[:, :, :, 0:65], [("ps", 6)], [("accsb", q2, 0), ("accsb", q2, 1)])
                    finalize[0][1](qb)

                if FLAG_S2DEFER:
                    defer(7, stage2)
                else:
                    stage2()
                if not FLAG_S2DEFER or qb == NQB - 1:
                    for _f in range(FLAG_BURST):
                        MM(P, C.ps[SB[i % 3]][:, :], C.ident_b[:, :], KT[0][:, 0:512], True, True, [("KT", par, 0)], [("ps", SB[i % 3])])
                for lag, fn in finalize[1:]:
                    defer(lag + (7 if FLAG_S2DEFER else 0), (lambda qb=qb, fn=fn: fn(qb)))
                side(qb)

        emit_S(0)
        emit_S(1)
        for i in range(n):
            if i + 2 < n:
                emit_S(i + 2)
            emit_AV(i)
            stepc[0] += 1
            run_due()
        run_due(flush=True)

    def fin_diff(h, qb):
        q2 = qb % 2
        for sub in range(4):
            a1 = accsb[q2][:, sub // 2, sub % 2, :]
            a2 = accsb[q2][:, 2 + sub // 2, sub % 2, :]
            r1 = [("accsb", q2, sub // 2)]
            r2 = [("accsb", q2, 2 + sub // 2)]
            s = sc[:, sub, :]
            P.op("dve", (lambda s=s, a1=a1: (lambda e: e.reciprocal(out=s[:, 0:1], in_=a1[:, 128:129])))(), r1, [("sc", sub, 0)])
            P.op("dve", (lambda s=s, a2=a2: (lambda e: e.reciprocal(out=s[:, 1:2], in_=a2[:, 128:129])))(), r2, [("sc", sub, 1)])
            TT(P, "dve", s[:, 2:3], s[:, 1:2], C.lam_t[:, 1:2], ALU.mult, [("sc", sub, 1), "nlam"], [("sc", sub, 2)])
            TS(P, "dve", o1[:, sub, :], a1[:, 0:128], s[:, 0:1], None, ALU.mult, None, r1 + [("sc", sub, 0)], [("o1", sub)])
            STT(P, yd[:, sub, :], a2[:, 0:128], s[:, 2:3], o1[:, sub, :], ALU.mult, ALU.add,
                r2 + [("sc", sub, 2), ("o1", sub)], [("yd", sub)])
            TT(P, "dve", sq[:, :], yd[:, sub, :], yd[:, sub, :], ALU.mult, [("yd", sub)], ["sq"])
            P.op("dve", (lambda s=s: (lambda e: e.tensor_reduce(out=s[:, 3:4], in_=sq[:, :], axis=AX.X, op=ALU.add)))(), ["sq"], [("sc", sub, 3)])

    def fin_diff_act(h, qb):
        ACT(P, sc[:, :, 4:5], sc[:, :, 3:4], AF.Ln, [("sc", sub, 3) for sub in range(4)] + ["eps"], [("sc", sub, 4) for sub in range(4)],
            bias=C.eps_t[:], scale=1.0 / 128.0)
        ACT(P, sc[:, :, 5:6], sc[:, :, 4:5], AF.Exp, [("sc", sub, 4) for sub in range(4)], [("sc", sub, 5) for sub in range(4)], scale=-0.5)

    def fin_diff_a2(h, qb):
        for sub in range(4):
            s = sc[:, sub, :]
            TS(P, "dve", yd[:, sub, :], yd[:, sub, :], s[:, 5:6], float(1.0 - LAM_INIT), ALU.mult, ALU.mult, [("yd", sub), ("sc", sub, 5)], [("yd", sub)])
            TT(P, "dve", ydb[:, sub, :], yd[:, sub, :], C.sublng[:, :], ALU.mult, [("yd", sub), "sublng"], [("ydb", sub)])

    def fin_diff_b(h, qb):
        for sub in range(4):
            MM(P, C.ps[6][:, sub * 128:(sub + 1) * 128], ydb[:, sub, :], C.ident_b[:, :], True, True,
               [("ydb", sub), "ident_b"], [("ps", 6)])
        sl = qb % 2
        CP(P, "dve", ystg[sl][:, :], C.ps[6][:, :], [("ps", 6)], [("ystg", sl)])
        DMA(P, "sp", C.YT_s[512 + h * 128:512 + (h + 1) * 128, qb * 512:(qb + 1) * 512], ystg[sl][:, :], "ystg%d" % sl, reads=[("ystg", sl)])

    def fin_moba(h, qb):
        q2 = qb % 2
        sl = qb % 2
        for sub in range(4):
            a1 = accsb[q2][:, sub // 2, sub % 2, :]
            r1 = [("accsb", q2, sub // 2)]
            s = sc[:, sub, :]
            P.op("dve", (lambda s=s, a1=a1: (lambda e: e.reciprocal(out=s[:, 0:1], in_=a1[:, 64:65])))(), r1, [("sc", sub, 0)])
            TS(P, "dve", ydb[:, sub, 0:64], a1[:, 0:64], s[:, 0:1], None, ALU.mult, None, r1 + [("sc", sub, 0)], [("ydb", sub)])

    def fin_moba_b(h, qb):
        sl = qb % 2
        for sub in range(4):
            MM(P, C.ps[6][0:64, sub * 128:(sub + 1) * 128], ydb[:, sub, 0:64], C.ident_b[:, :], True, True,
               [("ydb", sub), "ident_b"], [("ps", 6)])
        CP(P, "dve", ystg[sl][0:64, :], C.ps[6][0:64, :], [("ps", 6)], [("ystg", sl)])
        DMA(P, "sp", C.YT_s[h * 64:(h + 1) * 64, qb * 512:(qb + 1) * 512], ystg[sl][0:64, :], "ystg%d" % sl, reads=[("ystg", sl)])

    load_job(0)
    if jobs[0][0] == "m":
        for tg in range(NQB):
            mask_stage(0, tg)
            mask_stage_b(0, tg)
    for ji, (kind, h) in enumerate(jobs):
        nxt = ji + 1 if ji + 1 < len(jobs) else None
        if nxt is not None:
            load_job(nxt)

        def side(qb, nxt=nxt):
            if nxt is not None and jobs[nxt][0] == "m":
                defer(4, (lambda: mask_stage(nxt, qb)))
                defer(11, (lambda: mask_stage_b(nxt, qb)))

        if kind == "d":
            attention(ji, 2, 128, 68, [(0, (lambda qb, h=h: fin_diff(h, qb))), (9, (lambda qb, h=h: fin_diff_act(h, qb))),
                                       (11, (lambda qb, h=h: fin_diff_a2(h, qb))), (18, (lambda qb, h=h: fin_diff_b(h, qb)))], side)
        else:
            attention(ji, 1, 64, 84, [(0, (lambda qb, h=h: fin_moba(h, qb))), (6, (lambda qb, h=h: fin_moba_b(h, qb)))], side)


def phase3(C):
    P, I, A = C.P, C.I, C.arena
    TB = C.TB
    NTB = C.NQ // TB
    A.reset()
    xs = A.alloc([128, 8, D], F32)
    xT = A.alloc([128, 8, TB], BF16)
    wb = [A.alloc([128, 8, 512], BF16) for _ in range(2)]
    lnp = A.alloc([128, 4, D], F32)
    tT = A.alloc([128, D], F32)
    r = A.alloc([128, D], F32)
    xn = A.alloc([128, D], F32)
    st6 = A.alloc([128, 2, 6], F32)
    mv = A.alloc([128, 8], F32)
    sil = [A.alloc([128, 512], F32) for _ in range(2)]
    sg = [A.alloc([128, 512], BF16) for _ in range(2)]
    pt_f = [A.alloc([128, 256], F32) for _ in range(1)]
    C.xb = [A.alloc([128, D], BF16) for _ in range(2)]
    C.xbi = 0
    tmp = (tT, r, xn, st6, mv)
    base = A.off
    mergedT = A.alloc([128, 8, TB], BF16)
    yT = A.alloc([128, 8, TB], BF16)
    wbr = A.alloc([128, 2, 4, D], BF16)
    wout = A.alloc([128, 8, D], BF16)
    A.reset(base)
    actT = A.alloc([128, NFF, TB], BF16)
    wd = A.alloc([128, NFF, D], BF16)
    A.reset(base)
    wpg = A.alloc([128, 8, D], BF16)
    wpl = A.alloc([128, 2, D], BF16)
    pT = A.alloc([128, 2, TB], BF16)

    DMA(P, "sp", lnp, I["lnp"][:, 2:6, :], "lnp3", writes=["ln2", "ln3"])
    x1v = C.X1_s.rearrange("(t p) d -> p t d", p=128)
    outv = C.out.rearrange("(t p) d -> p t d", p=128)
    pv = I["p"].rearrange("(t p) d -> p t d", p=128)
    win = wview(I["w_in"], 8)
    for tb in range(NTB):
        tok0 = tb * TB
        for t in range(8):
            DMA(P, "sp", xs[:, t, :], x1v[:, tb * 8 + t, :], "xs%d" % t, writes=[("xs", t)])
        for kc in range(8):
            DMA(P, "sp", xT[:, kc, :], C.X1T_s[kc * 128:(kc + 1) * 128, tb * TB:(tb + 1) * TB], "xTl%d" % kc, writes=[("xT", q) for q in range(8)])
            DMA(P, "sp", yT[:, kc, :], C.YT_s[kc * 128:(kc + 1) * 128, tok0:tok0 + TB], "yTl%d" % kc, writes=[("yT", kc)])
        DMA(P, "pool", wbr[:, 0, :, :], wview(I["w_branch_a"], 4), "wbra", writes=["wbra"])
        DMA(P, "pool", wbr[:, 1, :, :], wview(I["w_branch_b"], 4), "wbrb", writes=["wbrb"])
        DMA(P, "pool", wout, wview(I["w_out"], 8), "wout", writes=["wout"])
        for oc in range(8):
            slot = oc % 2
            buf = wb[slot]
            DMA(P, "pool", buf[:, :, 0:128], win[:, :, 3072 + oc * 128:3072 + (oc + 1) * 128], "wbA%d" % slot, writes=[("wb", slot, 0)])
            DMA(P, "pool", buf[:, :, 128:256], win[:, :, 4096 + oc * 128:4096 + (oc + 1) * 128], "wbG%d" % slot, writes=[("wb", slot, 1)])
            for hb in range(TB // 512):
                xr = [("xT", 4 * hb + q) for q in range(4)]
                ip = (oc * 2 + hb) % 2 if FLAG_MERGEDB else 0
                for br in range(2):
                    bank = 4 * ip + br
                    for kc in range(8):
                        MM(P, C.ps[bank][:, :], buf[:, kc, br * 128:(br + 1) * 128], xT[:, kc, hb * 512:(hb + 1) * 512],
                           kc == 0, kc == 7, [("wb", slot, br)] + xr, [("ps", bank)])
                    ACT(P, sg[br][:, :], C.ps[bank][:, :], AF.Sigmoid, [("ps", bank)], [("sg", br)])
                    bank2 = 4 * ip + 2 + br
                    for kc in range(4):
                        MM(P, C.ps[bank2][:, :], wbr[:, br, kc, oc * 128:(oc + 1) * 128], yT[:, br * 4 + kc, hb * 512:(hb + 1) * 512],
                           kc == 0, kc == 3, ["wbra" if br == 0 else "wbrb", ("yT", br * 4 + kc)], [("ps", bank2)])
                    TT(P, "dve", sil[br][:, :], sg[br][:, :], C.ps[bank2][:, :], ALU.mult,
                       [("sg", br), ("ps", bank2)], [("sil", br)])
                TT(P, "dve", mergedT[:, oc, hb * 512:(hb + 1) * 512], sil[0][:, :], sil[1][:, :], ALU.add,
                   [("sil", 0), ("sil", 1)], [("mergedT", oc, hb)])
        for t in range(8):
            hb = t // 4
            yb = (4, 5) if (t % 2 == 0 or not FLAG_LNPIPE) else (2, 3)
            for dh in range(2):
                for kc in range(8):
                    MM(P, C.ps[yb[dh]][:, :], mergedT[:, kc, t * 128:(t + 1) * 128], wout[:, kc, dh * 512:(dh + 1) * 512],
                       kc == 0, kc == 7, [("mergedT", kc, hb), "wout"], [("ps", yb[dh])])
            emit_layernorm(C, xs, t, yb, 1.0, lnp[:, 0, :], lnp[:, 1, :], tmp, "ln2")
            if t >= 1:
                emit_transposes(C, xs, xT, t - 1, ("dve", "act"), cast=False)
        emit_transposes(C, xs, xT, 7, ("dve", "act"), cast=False)
        P.barrier()
        emit_ffn(C, xs, xT, actT, wd, wb, sil, I["w_ffn2_up"], I["w_ffn2_down"], "f2", True)
        for t in range(8):
            yb = (4, 5) if (t % 2 == 0 or not FLAG_LNPIPE) else (2, 3)
            emit_down(C, actT, wd, t, yb)
            emit_layernorm(C, xs, t, yb, 0.5, lnp[:, 2, :], lnp[:, 3, :], tmp, "ln3")
            if t >= 1:
                emit_transposes(C, xs, xT, t - 1, ("dve", "act"), cast=False)
        emit_transposes(C, xs, xT, 7, ("dve", "act"), cast=False)
        P.barrier()
        DMA(P, "pool", wpg, wview(I["w_ple_gate"], 8), "wpg", writes=["wpg"])
        DMA(P, "pool", wpl, wview(I["w_ple"], 2), "wpl", writes=["wpl"])
        for t in range(8):
            pf = pt_f[0]
            DMA(P, "sp", pf, pv[:, tb * 8 + t, :], "pf0", writes=[("pf", 0)])
            pb = C.xb[0][:, 0:256]
            CP(P, "pool", pb, pf[:, :], [("pf", 0)], [("xb", 0)])
            for k in range(2):
                MM(P, C.ps[6][:, k * 128:(k + 1) * 128], pb[:, k * 128:(k + 1) * 128], C.ident_b[:, :], True, True, [("xb", 0), "ident_b"], [("ps", 6)])
            CP(P, "dve", pT[:, :, t * 128:(t + 1) * 128], C.ps[6][:, 0:256].rearrange("p (a b) -> p a b", a=2), [("ps", 6)], [("pT", t)])
            for dh in range(2):
                for kc in range(8):
                    MM(P, C.ps[dh][:, :], xT[:, kc, t * 128:(t + 1) * 128], wpg[:, kc, dh * 512:(dh + 1) * 512],
                       kc == 0, kc == 7, [("xT", t), "wpg"], [("ps", dh)])
                ACT(P, tT[:, dh * 512:(dh + 1) * 512], C.ps[dh][:, :], AF.Sigmoid, [("ps", dh)], [("tT", dh)])
                for kc in range(2):
                    MM(P, C.ps[2 + dh][:, :], pT[:, kc, t * 128:(t + 1) * 128], wpl[:, kc, dh * 512:(dh + 1) * 512],
                       kc == 0, kc == 1, [("pT", t), "wpl"], [("ps", 2 + dh)])
                TT(P, "dve", sil[dh][:, :], tT[:, dh * 512:(dh + 1) * 512], C.ps[2 + dh][:, :], ALU.mult, [("tT", dh), ("ps", 2 + dh)], [("sil", dh)])
                TT(P, "dve", xn[:, dh * 512:(dh + 1) * 512], sil[dh][:, :], xs[:, t, dh * 512:(dh + 1) * 512], ALU.add,
                   [("sil", dh), ("xs", t)], [("xo", dh)])
            DMA(P, "sp", outv[:, tb * 8 + t, :], xn[:, :], "outst", reads=[("xo", 0), ("xo", 1)])
        P.barrier()


def host_consts(mode, r):
    c = {}
    c["ident"] = np.eye(128, dtype=np.float32)
    ki = np.arange(128)[:, None]
    qi = np.arange(128)[None, :]
    c["tri"] = np.where(ki > qi, -BIGM, 0.0).astype(np.float32)
    slopes = np.concatenate([2.0 ** (-8.0 * np.arange(1, 9) / 8), 2.0 ** (-8.0 * np.arange(1, 5) / 4)]).astype(np.float64)
    if mode == "P4":
        NQ, NK = 4096, 4096
        qpos = np.arange(NQ)
        kpos = np.arange(NK)
        kvalid = np.ones(NK, bool)
    else:
        NQ, NK = 2048, 4096
        qpos = r * 2048 + np.arange(NQ)
        kpos = np.concatenate([np.arange(2048), r * 2048 + np.arange(2048)])
        kvalid = np.concatenate([np.full(2048, r == 1), np.ones(2048, bool)])
    qc = np.zeros((12, 4, NQ), np.float32)
    kc = np.zeros((12, 4, NK), np.float32)
    for h in range(12):
        s8 = 8.0 * slopes[h]
        qc[h, 0] = -s8 * (qpos % 128)
        qc[h, 1] = -s8 * 128.0 * (qpos // 128)
        qc[h, 2] = 1.0
        qc[h, 3] = 1.0
        kc[h, 0] = 1.0
        kc[h, 1] = 1.0
        kc[h, 2] = s8 * (kpos % 128)
        kc[h, 3] = np.where(kvalid, s8 * 128.0 * (kpos // 128), -BIGM)
    c["qc"], c["kc"] = qc, kc
    kind = np.zeros((16, NK), np.float32)
    for n in range(16):
        kind[n, n * 256:(n + 1) * 256] = BIGM
    c["kind"] = kind
    nt = NQ // 128
    vb = np.zeros((nt, 16), np.float32)
    lo = np.zeros((nt, 16), np.float32)
    for t in range(nt):
        b_loc = t // 2
        for n in range(16):
            if mode == "P4":
                st = "past" if n < b_loc else ("own" if n == b_loc else "no")
            else:
                if n < 8:
                    st = "past" if r == 1 else "no"
                else:
                    nn = n - 8
                    st = "past" if nn < b_loc else ("own" if nn == b_loc else "no")
            vb[t, n] = {"past": 0.0, "own": 1e30, "no": -1e30}[st]
            lo[t, n] = 0.0 if st in ("past", "own") else -1.0
    c["vb"] = np.ascontiguousarray(np.broadcast_to(vb.reshape(1, nt // 4, 4, 16), (128, nt // 4, 4, 16)))
    c["lo"] = np.ascontiguousarray(np.broadcast_to(lo.reshape(1, nt // 4, 4, 16), (128, nt // 4, 4, 16)))
    return c


_CACHE = {}
MODE = "P8R"


def kernel(**inputs):
    f = lambda k: np.ascontiguousarray(np.asarray(inputs[k], dtype=np.float32)[0])
    mode = MODE
    if mode not in _CACHE:
        _CACHE[mode] = build(mode)
    nc = _CACHE[mode]
    x = np.asarray(inputs["x"], dtype=np.float32)
    p = np.asarray(inputs["p"], dtype=np.float32)[0]
    shared = {k: f(k) for k in ("w_ffn1_up", "w_ffn1_down", "w_ffn2_up", "w_ffn2_down", "w_in", "w_branch_a",
                                "w_branch_b", "w_out", "w_ple", "w_ple_gate")}
    lnp = np.stack([f(k) for k in ("ln1_g", "ln1_b", "ln2_g", "ln2_b", "ln3_g", "ln3_b")], 0)
    shared["lnp"] = np.ascontiguousarray(np.broadcast_to(lnp[None], (128, 6, D)))
    shared["sublng"] = np.ascontiguousarray(np.broadcast_to(f("subln_g")[None], (128, 128)))
    lamv = np.stack([f(k) for k in ("lambda_q1", "lambda_k1", "lambda_q2", "lambda_k2")], 0)
    shared["lamv"] = np.ascontiguousarray(np.broadcast_to(lamv[None], (128, 4, 64)))
    consts = {}
    in_maps = []
    for c in range(N_CORES):
        m = dict(shared)
        if mode == "P4":
            b, r = c % 4, 0
            m["x"] = np.ascontiguousarray(x[b])
            m["p"] = np.ascontiguousarray(p[b])
        else:
            b, r = c // 2, c % 2
            m["x"] = np.ascontiguousarray(np.concatenate([x[b, 0:2048], x[b, r * 2048:(r + 1) * 2048]], 0))
            m["p"] = np.ascontiguousarray(p[b, r * 2048:(r + 1) * 2048])
        if r not in consts:
            consts[r] = host_consts(mode, r)
        m.update(consts[r])
        in_maps.append(m)
    res = run_bass_kernel_spmd(nc, in_maps, core_ids=list(range(N_CORES)))
    if mode == "P4":
        out = np.stack([np.asarray(res.results[b]["out"], dtype=np.float32) for b in range(4)], 0)
    else:
        out = np.stack([np.concatenate([np.asarray(res.results[2 * b + r]["out"], dtype=np.float32) for r in range(2)], 0)
                        for b in range(4)], 0)
    return out
```

```python
import contextlib
import numpy as np
import concourse.bass as bass
import concourse.mybir as mybir
from concourse.bass_utils import run_bass_kernel_spmd

F32 = mybir.dt.float32
BF16 = mybir.dt.bfloat16
AF = mybir.ActivationFunctionType
ALU = mybir.AluOpType
AX = mybir.AxisListType

D = 1024
DFF = 2816
SEQ = 4096
NFF = 22
ALPHA = float(2.0 ** 0.25)
BIGM = 262144.0
EPS = 1e-5
LAM_INIT = 0.8 - 0.6 * 1.0
N_CORES = 8
import os
FLAG_DEFER = os.environ.get("K_DEFER", "1") == "1"
FLAG_S2DEFER = os.environ.get("K_S2DEFER", "1") == "1"
FLAG_BURST = int(os.environ.get("K_BURST", "10"))
FLAG_LNPIPE = os.environ.get("K_LNPIPE", "1") == "1"
FLAG_MERGEDB = os.environ.get("K_MERGEDB", "1") == "1"


class Prog:
    ENGS = ("pe", "act", "dve", "pool", "sp")

    def __init__(self, nc):
        self.nc = nc
        self.ops = []
        self.last_w = {}
        self.readers = {}
        self.dma_keys = {}
        self.last_on_eng = {}
        self.dma_since_barrier = []
        self.barrier_deps = {}

    def op(self, eng, fn, reads=(), writes=(), dma=0, key=None):
        idx = len(self.ops)
        deps = set()
        for r in reads:
            w = self.last_w.get(r)
            if w is not None:
                deps.add(w)
        for r in writes:
            w = self.last_w.get(r)
            if w is not None:
                deps.add(w)
            for rd in self.readers.get(r, {}).values():
                deps.add(rd)
        for r in reads:
            d = self.readers.setdefault(r, {})
            d[eng if not dma else ("dma", idx)] = idx
        for r in writes:
            self.last_w[r] = idx
            self.readers[r] = {}
        bd = self.barrier_deps.pop(eng, None)
        if bd:
            deps.update(bd)
        deps.discard(idx)
        o = dict(eng=eng, fn=fn, deps=deps, dma=dma, sig=False)
        if dma:
            assert key is not None
            if key not in self.dma_keys:
                self.dma_keys[key] = len(self.dma_keys)
            o["dkey"] = self.dma_keys[key]
            self.dma_since_barrier.append(idx)
        else:
            self.last_on_eng[eng] = idx
        self.ops.append(o)
        return idx

    def barrier(self):
        deps = set(self.dma_since_barrier)
        deps.update(self.last_on_eng.values())
        self.dma_since_barrier = []
        self.barrier_deps = {e: set(deps) for e in self.ENGS}

    def emit(self):
        nc = self.nc
        ops = self.ops
        for o in ops:
            for d in o["deps"]:
                p = ops[d]
                if p["eng"] == "pe" and o["eng"] == "pe" and not p["dma"] and not o["dma"]:
                    continue
                p["sig"] = True
        es = contextlib.ExitStack()
        eng_sem = {e: es.enter_context(nc.semaphore("s_" + e)) for e in self.ENGS}
        cnt = {e: 0 for e in self.ENGS}
        nk = len(self.dma_keys)
        assert nk <= 120, nk
        dma_sems = [es.enter_context(nc.semaphore("d%d" % i)) for i in range(nk)]
        dcnt = [0] * nk
        for o in ops:
            if o["dma"]:
                k = o["dkey"]
                dcnt[k] += 16 * o["dma"]
                o["sigval"] = dcnt[k]
            elif o["sig"]:
                cnt[o["eng"]] += 1
                o["sigval"] = cnt[o["eng"]]
        per_eng = {e: [] for e in self.ENGS}
        for i, o in enumerate(ops):
            per_eng[o["eng"]].append(i)
        with es, nc.Block() as block:
            def run(engname, engobj):
                waited = {}
                for i in per_eng[engname]:
                    o = ops[i]
                    for d in sorted(o["deps"]):
                        p = ops[d]
                        if p["dma"]:
                            sem = dma_sems[p["dkey"]]
                            key = ("d", p["dkey"])
                        else:
                            if p["eng"] == "pe" and engname == "pe" and not o["dma"]:
                                continue
                            sem = eng_sem[p["eng"]]
                            key = ("e", p["eng"])
                        v = p["sigval"]
                        if waited.get(key, 0) >= v:
                            continue
                        waited[key] = v
                        engobj.wait_ge(sem, v)
                    if o["dma"]:
                        o["fn"](engobj, dma_sems[o["dkey"]])
                    else:
                        inst = o["fn"](engobj)
                        if o["sig"]:
                            inst.then_inc(eng_sem[engname], 1)
                for i in per_eng[engname]:
                    o = ops[i]
                    if o["dma"]:
                        key = ("d", o["dkey"])
                        if waited.get(key, 0) < o["sigval"]:
                            waited[key] = o["sigval"]
                            engobj.wait_ge(dma_sems[o["dkey"]], o["sigval"])

            @block.tensor
            def _(e):
                run("pe", e)

            @block.scalar
            def _(e):
                run("act", e)

            @block.vector
            def _(e):
                run("dve", e)

            @block.gpsimd
            def _(e):
                run("pool", e)

            @block.sync
            def _(e):
                run("sp", e)


class Arena:
    def __init__(self, nc, st, nbytes):
        self.t = st.enter_context(nc.sbuf_tensor("arena", [128, nbytes // 4], F32))
        self.cap = nbytes
        self.off = 0

    def reset(self, off=0):
        self.off = off

    def alloc(self, shape, dtype):
        esz = 4 if dtype == F32 else 2
        n = int(np.prod(shape[1:]))
        nb = (n * esz + 63) // 64 * 64
        assert self.off + nb <= self.cap, (self.off, nb, self.cap)
        w0 = self.off // 4
        ap = self.t[0:shape[0], w0:w0 + nb // 4]
        if dtype != F32:
            ap = ap.bitcast(dtype)
        ap = ap[:, 0:n]
        if len(shape) == 3:
            ap = ap.rearrange("p (a b) -> p a b", a=shape[1])
        elif len(shape) == 4:
            ap = ap.rearrange("p (a b c) -> p a b c", a=shape[1], b=shape[2])
        self.off += nb
        return ap


def MM(P, out, lhsT, rhs, start, stop, reads, writes):
    P.op("pe", lambda e: e.matmul(out, lhsT=lhsT, rhs=rhs, start=start, stop=stop), reads, writes)


def TR(P, out, in_, ident, reads, writes):
    P.op("pe", lambda e: e.transpose(out, in_, ident), reads, writes)


def ACT(P, out, in_, func, reads, writes, scale=None, bias=None, accum_out=None):
    kw = {}
    if scale is not None:
        kw["scale"] = scale
    if bias is not None:
        kw["bias"] = bias
    if accum_out is not None:
        kw["accum_out"] = accum_out
    P.op("act", lambda e: e.activation(out=out, in_=in_, func=func, **kw), reads, writes)


def TT(P, eng, out, in0, in1, op, reads, writes):
    P.op(eng, lambda e: e.tensor_tensor(out=out, in0=in0, in1=in1, op=op), reads, writes)


def TS(P, eng, out, in0, s1, s2, op0, op1, reads, writes):
    if op1 is None:
        P.op(eng, lambda e: e.tensor_scalar(out=out, in0=in0, scalar1=s1, scalar2=None, op0=op0), reads, writes)
    else:
        P.op(eng, lambda e: e.tensor_scalar(out=out, in0=in0, scalar1=s1, scalar2=s2, op0=op0, op1=op1), reads, writes)


def STT(P, out, in0, scalar, in1, op0, op1, reads, writes):
    P.op("dve", lambda e: e.scalar_tensor_tensor(out=out, in0=in0, scalar=scalar, in1=in1, op0=op0, op1=op1), reads, writes)


def CP(P, eng, out, in_, reads, writes):
    if eng == "act":
        P.op("act", lambda e: e.activation(out=out, in_=in_, func=AF.Identity), reads, writes)
    else:
        P.op(eng, lambda e: e.tensor_copy(out=out, in_=in_), reads, writes)


def DMA(P, q, out, in_, key, reads=(), writes=()):
    P.op(q, lambda e, s: e.dma_start(out=out, in_=in_).then_inc(s, 16), reads, writes, dma=1, key=key)


class Ctx:
    pass


def build(MODE, debug=False):
    nc = bass.Bass("TRN2", target_bir_lowering=False)
    C = Ctx()
    C.nc = nc
    C.NHM, C.NHD = 8, 4
    if MODE == "P4":
        C.NQ, C.NK, C.KOFF = 4096, 4096, 0
    else:
        C.NQ, C.NK, C.KOFF = 2048, 4096, 2048
    NQ, NK = C.NQ, C.NK
    TB = 1024
    C.TB = TB

    def din(name, shape):
        return nc.dram_tensor(name, list(shape), F32, kind="ExternalInput").ap()

    I = {}
    I["x"] = din("x", [NK, D])
    I["p"] = din("p", [NQ, 256])
    I["w_ffn1_up"] = din("w_ffn1_up", [D, 2 * DFF])
    I["w_ffn1_down"] = din("w_ffn1_down", [DFF, D])
    I["w_ffn2_up"] = din("w_ffn2_up", [D, 2 * DFF])
    I["w_ffn2_down"] = din("w_ffn2_down", [DFF, D])
    I["w_in"] = din("w_in", [D, 5120])
    I["w_branch_a"] = din("w_branch_a", [512, D])
    I["w_branch_b"] = din("w_branch_b", [512, D])
    I["w_out"] = din("w_out", [D, D])
    I["w_ple"] = din("w_ple", [256, D])
    I["w_ple_gate"] = din("w_ple_gate", [D, D])
    I["lnp"] = din("lnp", [128, 6, D])
    I["sublng"] = din("sublng", [128, 128])
    I["lamv"] = din("lamv", [128, 4, 64])
    I["ident"] = din("ident", [128, 128])
    I["tri"] = din("tri", [128, 128])
    I["qc"] = din("qc", [12, 4, NQ])
    I["kc"] = din("kc", [12, 4, NK])
    I["kind"] = din("kind", [16, NK])
    I["vb"] = din("vb", [128, NQ // 512, 4, 16])
    I["lo"] = din("lo", [128, NQ // 512, 4, 16])
    C.I = I
    kind_s = "ExternalOutput" if debug else "Internal"
    C.out = nc.dram_tensor("out", [NQ, D], F32, kind="ExternalOutput").ap()
    C.X1_s = nc.dram_tensor("X1_s", [NQ, D], F32, kind=kind_s).ap()
    C.X1T_s = nc.dram_tensor("X1T_s", [D, NQ], BF16, kind=kind_s).ap()
    C.QT_s = nc.dram_tensor("QT_s", [D, NQ], BF16, kind=kind_s).ap()
    C.KT_s = nc.dram_tensor("KT_s", [D, NK], BF16, kind=kind_s).ap()
    C.V_s = nc.dram_tensor("V_s", [NK, D], BF16, kind=kind_s).ap()
    C.YT_s = nc.dram_tensor("YT_s", [D, NQ], BF16, kind=kind_s).ap()

    with contextlib.ExitStack() as st:
        C.P = P = Prog(nc)
        C.arena = Arena(nc, st, 192 * 1024)
        C.ps = [st.enter_context(nc.psum_tensor("ps%d" % i, [128, 512], F32)) for i in range(8)]
        sb = lambda n, s, d: st.enter_context(nc.sbuf_tensor(n, s, d))
        C.ident_f = sb("ident_f", [128, 128], F32)
        C.ident_b = sb("ident_b", [128, 128], BF16)
        C.tri_b = sb("tri_b", [128, 128], BF16)
        C.eps_t = sb("eps_t", [128, 1], F32)
        C.lam_t = sb("lam_t", [128, 4], F32)
        C.lamv = sb("lamv_t", [128, 4, 64], F32)
        C.sublng = sb("sublng_t", [128, 128], F32)
        C.small = sb("small_t", [128, 64], F32)

        phase0(C)
        phase1(C)
        P.barrier()
        phase2(C)
        P.barrier()
        phase3(C)
        P.emit()
    return nc


def phase0(C):
    P, I = C.P, C.I
    DMA(P, "sp", C.ident_f[:], I["ident"], "c_if", writes=["ident_f"])
    DMA(P, "pool", C.ident_b[:], I["ident"], "c_ib", writes=["ident_b"])
    DMA(P, "pool", C.tri_b[:], I["tri"], "c_tri", writes=["tri_b"])
    DMA(P, "sp", C.lamv[:], I["lamv"], "c_lamv", writes=["lamv"])
    DMA(P, "sp", C.sublng[:], I["sublng"], "c_sub", writes=["sublng"])
    P.op("dve", lambda e: e.memset(C.eps_t[:], EPS), writes=["eps"])
    sm = C.small
    TT(P, "dve", sm[:, 0:64], C.lamv[:, 0, :], C.lamv[:, 1, :], ALU.mult, ["lamv"], ["sm0"])
    P.op("dve", lambda e: e.tensor_reduce(out=C.lam_t[:, 2:3], in_=sm[:, 0:64], axis=AX.X, op=ALU.add), ["sm0"], ["l2"])
    TT(P, "dve", sm[:, 0:64], C.lamv[:, 2, :], C.lamv[:, 3, :], ALU.mult, ["lamv", "l2"], ["sm0"])
    P.op("dve", lambda e: e.tensor_reduce(out=C.lam_t[:, 3:4], in_=sm[:, 0:64], axis=AX.X, op=ALU.add), ["sm0"], ["l3"])
    ACT(P, C.lam_t[:, 2:4], C.lam_t[:, 2:4], AF.Exp, ["l2", "l3"], ["l23"])
    TT(P, "dve", C.lam_t[:, 0:1], C.lam_t[:, 2:3], C.lam_t[:, 3:4], ALU.subtract, ["l23"], ["lam0"])
    TS(P, "dve", C.lam_t[:, 0:1], C.lam_t[:, 0:1], float(LAM_INIT), None, ALU.add, None, ["lam0"], ["lam"])
    TS(P, "dve", C.lam_t[:, 1:2], C.lam_t[:, 0:1], -1.0, None, ALU.mult, None, ["lam"], ["nlam"])


def wview(w, kcn):
    return w.rearrange("(kc p) n -> p kc n", p=128)


def emit_transposes(C, xs, xT, t, evac_eng, cast=True):
    P = C.P
    xi = t % len(C.xb)
    xb = C.xb[xi]
    xbn = ("xb", xi)
    if cast:
        CP(P, "act", xb[:, :], xs[:, t, :], [("xs", t)], [xbn])
    for g in range(2):
        bank = C.ps[6 + g]
        for k in range(4):
            kc = g * 4 + k
            MM(P, bank[:, k * 128:(k + 1) * 128], xb[:, kc * 128:(kc + 1) * 128], C.ident_b[:, :], True, True,
               [xbn, "ident_b"], [("ps", 6 + g)])
        src = bank[:, :].rearrange("p (a b) -> p a b", a=4)
        CP(P, evac_eng[g], xT[:, g * 4:(g + 1) * 4, t * 128:(t + 1) * 128], src, [("ps", 6 + g)], [("xT", t)])


def emit_layernorm(C, xs, t, ybanks, yscale, lng, lnb, tmp, lnname):
    P = C.P
    tT, r, xn, st6, mv = tmp
    for h in range(2):
        ACT(P, tT[:, h * 512:(h + 1) * 512], C.ps[ybanks[h]][:, :], AF.Identity, [("ps", ybanks[h])], [("tT", h)], scale=float(yscale))
    STT(P, r[:, :], xs[:, t, :], ALPHA, tT[:, :], ALU.mult, ALU.add, [("xs", t), ("tT", 0), ("tT", 1)], ["r"])
    for h in range(2):
        P.op("dve", (lambda hh: (lambda e: e.bn_stats(out=st6[:, hh, :], in_=r[:, hh * 512:(hh + 1) * 512])))(h), ["r"], [("st6", h)])
    P.op("dve", lambda e: e.bn_aggr(out=mv[:, 0:2], in_=st6[:, :, :].rearrange("p a b -> p (a b)")), [("st6", 0), ("st6", 1)], ["mv"])
    ACT(P, mv[:, 2:3], mv[:, 1:2], AF.Sqrt, ["mv", "eps"], ["std"], bias=C.eps_t[:], scale=1.0)
    P.op("dve", lambda e: e.reciprocal(out=mv[:, 3:4], in_=mv[:, 2:3]), ["std"], ["rstd"])
    STT(P, mv[:, 4:5], mv[:, 0:1], -1.0, mv[:, 3:4], ALU.mult, ALU.mult, ["mv", "rstd"], ["nmr"])
    ACT(P, xn[:, :], r[:, :], AF.Identity, ["r", "rstd", "nmr"], ["xn"], scale=mv[:, 3:4], bias=mv[:, 4:5])
    TT(P, "dve", xn[:, :], xn[:, :], lng, ALU.mult, ["xn", lnname], ["xn"])
    TT(P, "dve", xs[:, t, :], xn[:, :], lnb, ALU.add, ["xn", lnname], [("xs", t)])
    xi = t % len(C.xb)
    CP(P, "act", C.xb[xi][:, :], xs[:, t, :], [("xs", t)], [("xb", xi)])


def emit_ffn(C, xs, xT, actT, wd, wb, sil, w_up, w_down, tag, first):
    P = C.P
    TB = C.TB
    wu = wview(w_up, 8)
    if first:
        wdv = wview(w_down, NFF)
        DMA(P, "pool", wd[:, 0:11, :], wdv[:, 0:11, :], tag + "wd0", writes=[("wd", 0)])
        DMA(P, "pool", wd[:, 11:22, :], wdv[:, 11:22, :], tag + "wd1", writes=[("wd", 1)])
    par = 0
    for gi in range(NFF // 2):
        slot = gi % 2
        buf = wb[slot]
        DMA(P, "pool", buf[:, :, 0:256], wu[:, :, gi * 256:(gi + 1) * 256], "wbA%d" % slot, writes=[("wb", slot, 0)])
        DMA(P, "pool", buf[:, :, 256:512], wu[:, :, DFF + gi * 256:DFF + (gi + 1) * 256], "wbG%d" % slot, writes=[("wb", slot, 1)])
        for jj in range(2):
            j = gi * 2 + jj
            for hb in range(TB // 512):
                bA, bG = 2 * par, 2 * par + 1
                xr = [("xT", 4 * hb + q) for q in range(4)]
                for kc in range(8):
                    MM(P, C.ps[bA][:, :], buf[:, kc, jj * 128:(jj + 1) * 128], xT[:, kc, hb * 512:(hb + 1) * 512],
                       kc == 0, kc == 7, [("wb", slot, 0)] + xr, [("ps", bA)])
                for kc in range(8):
                    MM(P, C.ps[bG][:, :], buf[:, kc, 256 + jj * 128:256 + (jj + 1) * 128], xT[:, kc, hb * 512:(hb + 1) * 512],
                       kc == 0, kc == 7, [("wb", slot, 1)] + xr, [("ps", bG)])
                ACT(P, sil[par][:, :], C.ps[bA][:, :], AF.Silu, [("ps", bA)], [("sil", par)])
                TT(P, "dve", actT[:, j, hb * 512:(hb + 1) * 512], sil[par][:, :], C.ps[bG][:, :], ALU.mult,
                   [("sil", par), ("ps", bG)], [("actT", j, hb)])
                par ^= 1


def emit_down(C, actT, wd, t, yb=(4, 5)):
    P = C.P
    hb = t // 4
    for dh in range(2):
        for kc in range(NFF):
            MM(P, C.ps[yb[dh]][:, :], actT[:, kc, t * 128:(t + 1) * 128], wd[:, kc, dh * 512:(dh + 1) * 512],
               kc == 0, kc == NFF - 1, [("actT", kc, hb), ("wd", kc // 11)], [("ps", yb[dh])])


def phase1(C):
    P, I, A = C.P, C.I, C.arena
    TB = C.TB
    NTB = C.NK // TB
    own_tb0 = C.KOFF // TB
    A.reset()
    xs = A.alloc([128, 8, D], F32)
    xT = A.alloc([128, 8, TB], BF16)
    actT = A.alloc([128, NFF, TB], BF16)
    wd = A.alloc([128, NFF, D], BF16)
    wb = [A.alloc([128, 8, 512], BF16) for _ in range(2)]
    lnp = A.alloc([128, 2, D], F32)
    tT = A.alloc([128, D], F32)
    r = A.alloc([128, D], F32)
    xn = A.alloc([128, D], F32)
    st6 = A.alloc([128, 2, 6], F32)
    mv = A.alloc([128, 8], F32)
    sil = [A.alloc([128, 512], F32) for _ in range(2)]
    stg = [A.alloc([128, TB], BF16) for _ in range(2)]
    vst = [A.alloc([128, 512], BF16) for _ in range(2)]
    tmp = (tT, r, xn, st6, mv)
    C.xb = [A.alloc([128, D], BF16) for _ in range(2)]
    C.xbi = 0
    DMA(P, "sp", lnp, I["lnp"][:, 0:2, :], "lnp1", writes=["ln1"])
    xv = I["x"].rearrange("(t p) d -> p t d", p=128)
    x1v = C.X1_s.rearrange("(t p) d -> p t d", p=128)
    win = wview(I["w_in"], 8)
    for tb in range(NTB):
        own = tb >= own_tb0
        qtb = tb - own_tb0
        for t in range(8):
            DMA(P, "sp", xs[:, t, :], xv[:, tb * 8 + t, :], "xs%d" % t, writes=[("xs", t)])
        for t in range(8):
            emit_transposes(C, xs, xT, t, ("dve", "act"))
        emit_ffn(C, xs, xT, actT, wd, wb, sil, I["w_ffn1_up"], I["w_ffn1_down"], "f1", tb == 0)
        for t in range(8):
            yb = (4, 5) if (t % 2 == 0 or not FLAG_LNPIPE) else (2, 3)
            emit_down(C, actT, wd, t, yb)
            emit_layernorm(C, xs, t, yb, 0.5, lnp[:, 0, :], lnp[:, 1, :], tmp, "ln1")
            if own:
                DMA(P, "sp", x1v[:, qtb * 8 + t, :], xs[:, t, :], "x1st%d" % t, reads=[("xs", t)])
            if t >= 1:
                emit_transposes(C, xs, xT, t - 1, ("dve", "act"), cast=False)
        emit_transposes(C, xs, xT, 7, ("dve", "act"), cast=False)
        if own:
            for kc in range(8):
                DMA(P, "sp", C.X1T_s[kc * 128:(kc + 1) * 128, qtb * TB:(qtb + 1) * TB], xT[:, kc, :], "x1T%d" % kc,
                    reads=[("xT", q) for q in range(8)])
        par = 0
        for g in range(6):
            kind = ("q", "k", "v")[g % 3]
            fam = g // 3
            if kind == "q" and not own:
                continue
            slot = g % 2
            buf = wb[slot]
            DMA(P, "pool", buf[:, :, 0:256], win[:, :, g * 512:g * 512 + 256], "wbA%d" % slot, writes=[("wb", slot, 0)])
            DMA(P, "pool", buf[:, :, 256:512], win[:, :, g * 512 + 256:(g + 1) * 512], "wbG%d" % slot, writes=[("wb", slot, 1)])
            if kind in ("q", "k"):
                dst = C.QT_s if kind == "q" else C.KT_s
                col0 = qtb * TB if kind == "q" else tb * TB
                for c in range(4):
                    sg = stg[par]
                    for hb in range(TB // 512):
                        bank = 2 * par + hb
                        for kc in range(8):
                            MM(P, C.ps[bank][:, :], buf[:, kc, c * 128:(c + 1) * 128], xT[:, kc, hb * 512:(hb + 1) * 512],
                               kc == 0, kc == 7, [("wb", slot, c // 2)] + [("xT", 4 * hb + q) for q in range(4)], [("ps", bank)])
                        CP(P, "act" if hb == 0 else "dve", sg[:, hb * 512:(hb + 1) * 512], C.ps[bank][:, :], [("ps", bank)], [("stg", par, hb)])
                    row0 = fam * 512 + c * 128
                    DMA(P, "sp", dst[row0:row0 + 128, col0:col0 + TB], sg[:, :], "stg%d" % par,
                        reads=[("stg", par, 0), ("stg", par, 1)])
                    par ^= 1
            else:
                vv = C.V_s.rearrange("(t p) d -> p t d", p=128)
                for t in range(8):
                    bank = 2 * par
                    for kc in range(8):
                        MM(P, C.ps[bank][:, :], xT[:, kc, t * 128:(t + 1) * 128], buf[:, kc, :],
                           kc == 0, kc == 7, [("wb", slot, 0), ("wb", slot, 1), ("xT", t)], [("ps", bank)])
                    CP(P, "act" if t % 2 == 0 else "dve", vst[par][:, :], C.ps[bank][:, :], [("ps", bank)], [("vst", par)])
                    DMA(P, "sp", vv[:, tb * 8 + t, fam * 512:(fam + 1) * 512], vst[par][:, :], "vst%d" % par, reads=[("vst", par)])
                    par ^= 1


def phase2(C):
    P, I, A = C.P, C.I, C.arena
    A.reset()
    NHM, NHD = C.NHM, C.NHD
    NQ, NK, KOFF = C.NQ, C.NK, C.KOFF
    NQB = NQ // 512
    NKC = NK // 128
    KO = KOFF // 128
    KTs = [[A.alloc([128, NK], BF16) for m in range(2)] for _ in range(2)]
    QTs = [[A.alloc([128, NQ], BF16) for m in range(2)] for _ in range(2)]
    VEs = [A.alloc([128, NKC, 132], BF16) for _ in range(2)]
    PT = [A.alloc([128, 512], BF16) for _ in range(4)]
    SB = [0, 1, 5]
    accsb = [A.alloc([128, 4, 2, 132], F32) for _ in range(2)]
    o1 = A.alloc([128, 4, 128], F32)
    yd = A.alloc([128, 4, 128], F32)
    ydb = A.alloc([128, 4, 128], BF16)
    sq = A.alloc([128, 128], F32)
    sc = A.alloc([128, 4, 8], F32)
    ystg = [A.alloc([128, 512], BF16) for _ in range(2)]
    kmf = A.alloc([64, 16], F32)
    kmb = [A.alloc([64, 16], BF16) for _ in range(2)]
    vbt = A.alloc([128, NQB, 4, 16], F32)
    lot = A.alloc([128, NQB, 4, 16], F32)
    gm = [A.alloc([128, 4, 16], F32) for _ in range(2)]
    m8 = [A.alloc([128, 4, 8], F32) for _ in range(2)]
    m1 = [A.alloc([128, 4, 16], F32) for _ in range(2)]
    npad = [A.alloc([128, 4, 80], BF16) for _ in range(2)]
    ones_col = A.alloc([128, NKC, 1], F32)
    ones_k = A.alloc([128, 2], BF16)
    one_f = A.alloc([1, 2], F32)
    OTsb = [A.alloc([128, 512], BF16) for _ in range(2)]
    ZTsb = [A.alloc([1, 512], F32) for _ in range(2)]
    Zhi = [A.alloc([1, 512], BF16) for _ in range(2)]
    Zlo = [A.alloc([1, 512], BF16) for _ in range(2)]
    Zd = [A.alloc([1, 512], F32) for _ in range(2)]
    one_b = A.alloc([1, 2], BF16)
    P.op("dve", lambda e: e.memset(one_b[:, :], 1.0), writes=["one_b"])
    one_b64 = A.alloc([128, 2], BF16)
    P.op("dve", lambda e: e.memset(one_b64[:, :], 1.0), writes=["one_b64"])
    sel = [A.alloc([128, 128], BF16) for _ in range(2)]
    for mm_ in range(2):
        P.op("dve", (lambda mm_=mm_: (lambda e: e.memset(sel[mm_][:, 0:64], 1.0 if mm_ == 0 else 0.0)))(), writes=[("sel", mm_)])
        P.op("dve", (lambda mm_=mm_: (lambda e: e.memset(sel[mm_][:, 64:128], 0.0 if mm_ == 0 else 1.0)))(), writes=[("sel", mm_)])
    ZB_f = A.alloc([128, 512], F32)
    ZB_d = A.alloc([128, 512], F32)
    ZB_hi = A.alloc([128, 512], BF16)
    ZB_lo = A.alloc([128, 512], BF16)
    OTfull = A.alloc([128, 512], BF16)
    Zm_f = A.alloc([128, 512], F32)
    Zm_d = A.alloc([128, 512], F32)
    Zm_lo = A.alloc([128, 512], BF16)
    Zm_lo_full = Zm_lo
    P.op("dve", lambda e: e.memset(Zm_lo[:, :], 0.0), writes=[("Zm_lo", 0)])

    ones_kk = A.alloc([128, 128], BF16)
    P.op("dve", lambda e: e.memset(ones_kk[:, :], 1.0), writes=["ones_kk"])
    for par in range(2):
        for m in range(2):
            P.op("pool", (lambda par=par, m=m: (lambda e: e.memset(KTs[par][m][64:128, :], 0.0)))(), writes=[("KT", par, m)])
            P.op("pool", (lambda par=par, m=m: (lambda e: e.memset(QTs[par][m][64:128, :], 0.0)))(), writes=[("QT", par, m), ("QTm", par, m)])
    P.op("dve", lambda e: e.memset(ones_k[:, :], 1.0), writes=["ones_k"])
    P.op("dve", lambda e: e.memset(one_f[:, :], 1.0), writes=["one_f"])
    DMA(P, "sp", vbt, I["vb"], "vbt", writes=["vbt"])
    DMA(P, "sp", lot, I["lo"], "lot", writes=["lot"])
    for g in range(2):
        P.op("dve", (lambda g=g: (lambda e: e.memset(npad[g][:, :, :], 0.0)))(), writes=[("npad", g)])
    P.op("dve", lambda e: e.memset(ones_col[:, :, :], 1.0), writes=["ones_col"])
    vv = C.V_s.rearrange("(k p) d -> p k d", p=128)

    jobs = [("d", h) for h in range(NHD)] + [("m", h) for h in range(NHM)]

    def load_job(ji):
        kind, h = jobs[ji]
        par = ji % 2
        KT, QT, VE = KTs[par], QTs[par], VEs[par]
        if kind == "d":
            hg = NHM + h
            for m in range(2):
                r0 = 512 + h * 128 + m * 64
                DMA(P, "sp", KT[m][0:64, :], C.KT_s[r0:r0 + 64, :], "KTl%d%d" % (par, m), writes=[("KT", par, m)])
                DMA(P, "pool", KT[m][64:68, :], I["kc"][hg, :, :], "KTc%d%d" % (par, m), writes=[("KT", par, m)])
                DMA(P, "sp", QT[m][0:64, :], C.QT_s[r0:r0 + 64, :], "QTl%d%d" % (par, m), writes=[("QT", par, m)])
                DMA(P, "pool", QT[m][64:68, :], I["qc"][hg, :, :], "QTc%d%d" % (par, m), writes=[("QT", par, m)])
            DMA(P, "sp", VE[:, :, 0:128], vv[:, :, 512 + h * 128:512 + (h + 1) * 128], "VEl%d" % par, writes=[("VE", par)])
            CP(P, "dve", VE[:, :, 128:129], ones_col[:, :, :], ["ones_col"], [("VE", par)])
        else:
            r0 = h * 64
            DMA(P, "sp", KT[0][0:64, :], C.KT_s[r0:r0 + 64, :], "KTl%d0" % par, writes=[("KT", par, 0)])
            DMA(P, "pool", KT[0][64:80, :], I["kind"], "KTi%d" % par, writes=[("KT", par, 0)])
            DMA(P, "pool", KT[0][80:84, :], I["kc"][h, :, :], "KTc%d0" % par, writes=[("KT", par, 0)])
            DMA(P, "sp", QT[0][0:64, :], C.QT_s[r0:r0 + 64, :], "QTl%d0" % par, writes=[("QT", par, 0)])
            DMA(P, "pool", QT[0][80:84, :], I["qc"][h, :, :], "QTc%d0" % par, writes=[("QT", par, 0)])
            DMA(P, "sp", VE[:, :, 0:64], vv[:, :, h * 64:(h + 1) * 64], "VEl%d" % par, writes=[("VE", par)])
            CP(P, "dve", VE[:, :, 64:65], ones_col[:, :, :], ["ones_col"], [("VE", par)])

    def mask_stage(ji, tg):
        par = ji % 2
        KT, QT = KTs[par][0], QTs[par][0]
        if tg == 0:
            P.op("dve", lambda e: e.tensor_reduce(out=kmf[:, :], in_=KT[0:64, :].rearrange("p (n l) -> p n l", l=256), axis=AX.X, op=ALU.add),
                 [("KT", par, 0)], ["kmf"])
            TS(P, "dve", kmb[par][:, :], kmf[:, :], 1.0 / 256.0, None, ALU.mult, None, ["kmf"], [("kmb", par)])
        gp = tg % 2
        for tq in range(4):
            t = tg * 4 + tq
            MM(P, C.ps[7][:, tq * 16:(tq + 1) * 16], QT[0:64, t * 128:(t + 1) * 128], kmb[par][:, :], True, True,
               [("QT", par, 0), ("kmb", par)], [("ps", 7)])
        TT(P, "dve", gm[gp][:, :, :], C.ps[7][:, 0:64].rearrange("p (a b) -> p a b", a=4), vbt[:, tg, :, :], ALU.add,
           [("ps", 7), "vbt"], [("gm", gp)])
        for tq in range(4):
            P.op("dve", (lambda tq=tq: (lambda e: e.max(out=m8[gp][:, tq, :], in_=gm[gp][:, tq, :])))(), [("gm", gp)], [("m8", gp, tq)])
        for tq in range(4):
            TS(P, "dve", m1[gp][:, tq, :], gm[gp][:, tq, :], m8[gp][:, tq, 3:4], 1.0, ALU.is_ge, ALU.subtract,
               [("gm", gp), ("m8", gp, tq)], [("m1", gp, tq)])
        TT(P, "dve", npad[gp][:, :, 64:80], m1[gp][:, :, :], lot[:, tg, :, :], ALU.min,
           [("m1", gp, q) for q in range(4)] + ["lot"], [("npad", gp)])

    def mask_stage_b(ji, tg):
        par = ji % 2
        QT = QTs[par][0]
        gp = tg % 2
        for tq in range(4):
            MM(P, C.ps[6][0:80, tq * 128:(tq + 1) * 128], npad[gp][:, tq, :], C.ident_b[:, :], True, True,
               [("npad", gp), "ident_b"], [("ps", 6)])
        CP(P, "act", QT[64:80, tg * 512:(tg + 1) * 512], C.ps[6][64:80, :], [("ps", 6)], [("QTm", par, 0)])

    pending = []
    stepc = [0]

    dseq = [0]

    def defer(lag, fn):
        if not FLAG_DEFER:
            fn()
            return
        dseq[0] += 1
        pending.append((stepc[0] + lag, dseq[0], fn))

    def run_due(flush=False):
        while True:
            due = [p for p in pending if flush or p[0] <= stepc[0]]
            if not due:
                break
            due.sort()
            it = due[0]
            pending.remove(it)
            it[2]()

    def attention(ji, nmaps, dv, K, finalize, side):
        par = ji % 2
        KT, QT, VE = KTs[par], QTs[par], VEs[par]
        steps = [(qb, m, kc) for qb in range(NQB) for m in range(nmaps) for kc in range(KO + 4 * qb + 4)]
        n = len(steps)

        def emit_S(i):
            qb, m, kc = steps[i]
            j = kc - KO - 4 * qb
            n0 = max(0, j) * 128
            sb = SB[i % 3]
            S = C.ps[sb]
            MM(P, S[:, n0:512], KT[m][:, kc * 128:(kc + 1) * 128], QT[m][:, qb * 512 + n0:(qb + 1) * 512],
               True, j < 0, [("KT", par, m), ("QT", par, m), ("QTm", par, m)], [("ps", sb)])
            if j >= 0:
                MM(P, S[:, n0:n0 + 128], C.ident_b[:, :], C.tri_b[:, :], False, True, ["ident_b", "tri_b"], [("ps", sb)])
            ACT(P, PT[i % 4][:, n0:512], S[:, n0:512], AF.Exp, [("ps", sb)], [("PT", i % 4)], scale=0.125)

        def emit_AV(i):
            qb, m, kc = steps[i]
            j = kc - KO - 4 * qb
            n0 = max(0, j) * 128
            last = KO + 4 * qb + 3
            pt = PT[i % 4]
            if nmaps == 2:
                MM(P, C.ps[2 + m][:, n0:512], VE[:, kc, 0:128], pt[:, n0:512], kc == 0, kc == last,
                   [("PT", i % 4), ("VE", par)], [("OT", m)])
                MM(P, C.ps[4][:, n0:512], sel[m][:, :], pt[:, n0:512], (m == 0 and kc == 0), (m == 1 and kc == last),
                   [("PT", i % 4), ("sel", m)], [("ZT", 0)])
            else:
                MM(P, C.ps[2][:, n0:512], VE[:, kc, 0:128], pt[:, n0:512], kc == 0, kc == last,
                   [("PT", i % 4), ("VE", par)], [("OT", 0)])
            if m == nmaps - 1 and kc == last:
                q2 = qb % 2
                if nmaps == 2:
                    for mm in range(2):
                        CP(P, "dve", OTsb[mm][:, :], C.ps[2 + mm][:, :], [("OT", mm)], [("OTsb", mm)])
                    CP(P, "dve", ZB_f[:, :], C.ps[4][:, :], [("ZT", 0)], [("ZTsb", 0)])
                    CP(P, "dve", ZB_hi[:, :], ZB_f[:, :], [("ZTsb", 0)], [("Zhi", 0)])
                    TT(P, "dve", ZB_d[:, :], ZB_f[:, :], ZB_hi[:, :], ALU.subtract, [("ZTsb", 0), ("Zhi", 0)], [("Zd", 0)])
                    CP(P, "dve", ZB_lo[:, :], ZB_d[:, :], [("Zd", 0)], [("Zlo", 0)])
                else:
                    CP(P, "dve", OTfull[:, :], C.ps[2][:, :], [("OT", 0)], [("OTsb", 0)])
                    CP(P, "dve", Zm_f[64:128, :], C.ps[2][64:128, :], [("OT", 0)], [("Zm_f", 0)])
                    TT(P, "dve", Zm_d[64:128, :], Zm_f[64:128, :], OTfull[64:128, :], ALU.subtract, [("Zm_f", 0), ("OTsb", 0)], [("Zm_d", 0)])
                    CP(P, "dve", Zm_lo[64:128, :], Zm_d[64:128, :], [("Zm_d", 0)], [("Zm_lo", 0)])

                def stage2(qb=qb, q2=q2):
                    if nmaps == 2:
                        for mm in range(2):
                            for sub in range(4):
                                MM(P, C.ps[6][:, sub * 128:(sub + 1) * 128], OTsb[mm][:, sub * 128:(sub + 1) * 128], C.ident_b[:, :], True, True,
                                   [("OTsb", mm), "ident_b"], [("ps", 6)])
                            CP(P, "dve", accsb[q2][:, 2 * mm:2 * mm + 2, :, 0:128],
                               C.ps[6][:, :].rearrange("p (a b c) -> p a b c", a=2, b=2), [("ps", 6)], [("accsb", q2, 2 * mm), ("accsb", q2, 2 * mm + 1)])
                            for sub in range(4):
                                e0 = 0 if mm == 0 else 64
                                MM(P, C.ps[7][:, 64 + mm * 4 + sub:64 + mm * 4 + sub + 1], ZB_hi[:, sub * 128:(sub + 1) * 128], C.ident_b[:, e0:e0 + 1],
                                   True, False, [("Zhi", 0), "ident_b"], [("ps", 7)])
                                MM(P, C.ps[7][:, 64 + mm * 4 + sub:64 + mm * 4 + sub + 1], ZB_lo[:, sub * 128:(sub + 1) * 128], C.ident_b[:, e0:e0 + 1],
                                   False, True, [("Zlo", 0), "ident_b"], [("ps", 7)])
                            CP(P, "dve", accsb[q2][:, 2 * mm:2 * mm + 2, :, 128:129],
                               C.ps[7][:, 64 + mm * 4:64 + mm * 4 + 4].rearrange("p (a b c) -> p a b c", a=2, b=2), [("ps", 7)],
                               [("accsb", q2, 2 * mm), ("accsb", q2, 2 * mm + 1)])
                    else:
                        for sub in range(4):
                            MM(P, C.ps[6][:, sub * 128:sub * 128 + 65], OTfull[:, sub * 128:(sub + 1) * 128], C.ident_b[:, 0:65], True, False,
                               [("OTsb", 0), "ident_b"], [("ps", 6)])
                            MM(P, C.ps[6][:, sub * 128 + 64:sub * 128 + 65], Zm_lo_full[:, sub * 128:(sub + 1) * 128], C.ident_b[:, 64:65], False, True,
                               [("Zm_lo", 0), "ident_b"], [("ps", 6)])
                        CP(P, "dve", accsb[q2][:, 0:2, :, 0:65],
                           C.ps[6][:, :].rearrange("p (a b c) -> p a b c", a=2, b=2)[:, :, :, 0:65], [("ps", 6)], [("accsb", q2, 0), ("accsb", q2, 1)])
                    finalize[0][1](qb)

                if FLAG_S2DEFER:
                    defer(7, stage2)
                else:
                    stage2()
                if not FLAG_S2DEFER or qb == NQB - 1:
                    for _f in range(FLAG_BURST):
                        MM(P, C.ps[SB[i % 3]][:, :], C.ident_b[:, :], KT[0][:, 0:512], True, True, [("KT", par, 0)], [("ps", SB[i % 3])])
                for lag, fn in finalize[1:]:
                    defer(lag + (7 if FLAG_S2DEFER else 0), (lambda qb=qb, fn=fn: fn(qb)))
                side(qb)

        emit_S(0)
        emit_S(1)
        for i in range(n):
            if i + 2 < n:
                emit_S(i + 2)
            emit_AV(i)
            stepc[0] += 1
            run_due()

    def fin_diff(h, qb):
        q2 = qb % 2
        for sub in range(4):
            a1 = accsb[q2][:, sub // 2, sub % 2, :]
            a2 = accsb[q2][:, 2 + sub // 2, sub % 2, :]
            r1 = [("accsb", q2, sub // 2)]
            r2 = [("accsb", q2, 2 + sub // 2)]
            s = sc[:, sub, :]
            P.op("dve", (lambda s=s, a1=a1: (lambda e: e.reciprocal(out=s[:, 0:1], in_=a1[:, 128:129])))(), r1, [("sc", sub, 0)])
            P.op("dve", (lambda s=s, a2=a2: (lambda e: e.reciprocal(out=s[:, 1:2], in_=a2[:, 128:129])))(), r2, [("sc", sub, 1)])
            TT(P, "dve", s[:, 2:3], s[:, 1:2], C.lam_t[:, 1:2], ALU.mult, [("sc", sub, 1), "nlam"], [("sc", sub, 2)])
            TS(P, "dve", o1[:, sub, :], a1[:, 0:128], s[:, 0:1], None, ALU.mult, None, r1 + [("sc", sub, 0)], [("o1", sub)])
            STT(P, yd[:, sub, :], a2[:, 0:128], s[:, 2:3], o1[:, sub, :], ALU.mult, ALU.add,
                r2 + [("sc", sub, 2), ("o1", sub)], [("yd", sub)])
            TT(P, "dve", sq[:, :], yd[:, sub, :], yd[:, sub, :], ALU.mult, [("yd", sub)], ["sq"])
            P.op("dve", (lambda s=s: (lambda e: e.tensor_reduce(out=s[:, 3:4], in_=sq[:, :], axis=AX.X, op=ALU.add)))(), ["sq"], [("sc", sub, 3)])

    def fin_diff_act(h, qb):
        ACT(P, sc[:, :, 4:5], sc[:, :, 3:4], AF.Ln, [("sc", sub, 3) for sub in range(4)] + ["eps"], [("sc", sub, 4) for sub in range(4)],
            bias=C.eps_t[:], scale=1.0 / 128.0)
        ACT(P, sc[:, :, 5:6], sc[:, :, 4:5], AF.Exp, [("sc", sub, 4) for sub in range(4)], [("sc", sub, 5) for sub in range(4)], scale=-0.5)

    def fin_diff_a2(h, qb):
        for sub in range(4):
            s = sc[:, sub, :]
            TS(P, "dve", yd[:, sub, :], yd[:, sub, :], s[:, 5:6], float(1.0 - LAM_INIT), ALU.mult, ALU.mult, [("yd", sub), ("sc", sub, 5)], [("yd", sub)])
            TT(P, "dve", ydb[:, sub, :], yd[:, sub, :], C.sublng[:, :], ALU.mult, [("yd", sub), "sublng"], [("ydb", sub)])

    def fin_diff_b(h, qb):
        for sub in range(4):
            MM(P, C.ps[6][:, sub * 128:(sub + 1) * 128], ydb[:, sub, :], C.ident_b[:, :], True, True,
               [("ydb", sub), "ident_b"], [("ps", 6)])
        sl = qb % 2
        CP(P, "dve", ystg[sl][:, :], C.ps[6][:, :], [("ps", 6)], [("ystg", sl)])
        DMA(P, "sp", C.YT_s[512 + h * 128:512 + (h + 1) * 128, qb * 512:(qb + 1) * 512], ystg[sl][:, :], "ystg%d" % sl, reads=[("ystg", sl)])

    def fin_moba(h, qb):
        q2 = qb % 2
        sl = qb % 2
        for sub in range(4):
            a1 = accsb[q2][:, sub // 2, sub % 2, :]
            r1 = [("accsb", q2, sub // 2)]
            s = sc[:, sub, :]
            P.op("dve", (lambda s=s, a1=a1: (lambda e: e.reciprocal(out=s[:, 0:1], in_=a1[:, 64:65])))(), r1, [("sc", sub, 0)])
            TS(P, "dve", ydb[:, sub, 0:64], a1[:, 0:64], s[:, 0:1], None, ALU.mult, None, r1 + [("sc", sub, 0)], [("ydb", sub)])

    def fin_moba_b(h, qb):
        sl = qb % 2
        for sub in range(4):
            MM(P, C.ps[6][0:64, sub * 128:(sub + 1) * 128], ydb[:, sub, 0:64], C.ident_b[:, :], True, True,
               [("ydb", sub), "ident_b"], [("ps", 6)])
        CP(P, "dve", ystg[sl][0:64, :], C.ps[6][0:64, :], [("ps", 6)], [("ystg", sl)])
        DMA(P, "sp", C.YT_s[h * 64:(h + 1) * 64, qb * 512:(qb + 1) * 512], ystg[sl][0:64, :], "ystg%d" % sl, reads=[("ystg", sl)])

    load_job(0)
    if jobs[0][0] == "m":
        for tg in range(NQB):
            mask_stage(0, tg)
            mask_stage_b(0, tg)
    for ji, (kind, h) in enumerate(jobs):
        nxt = ji + 1 if ji + 1 < len(jobs) else None
        if nxt is not None:
            load_job(nxt)

        def side(qb, nxt=nxt):
            if nxt is not None and jobs[nxt][0] == "m":
                defer(4, (lambda: mask_stage(nxt, qb)))
                defer(11, (lambda: mask_stage_b(nxt, qb)))

        if kind == "d":
            attention(ji, 2, 128, 68, [(0, (lambda qb, h=h: fin_diff(h, qb))), (9, (lambda qb, h=h: fin_diff_act(h, qb))),
                                       (11, (lambda qb, h=h: fin_diff_a2(h, qb))), (18, (lambda qb, h=h: fin_diff_b(h, qb)))], side)
        else:
            attention(ji, 1, 64, 84, [(0, (lambda qb, h=h: fin_moba(h, qb))), (6, (lambda qb, h=h: fin_moba_b(h, qb)))], side)
    run_due(flush=True)


def phase3(C):
    P, I, A = C.P, C.I, C.arena
    TB = C.TB
    NTB = C.NQ // TB
    A.reset()
    xs = A.alloc([128, 8, D], F32)
    xT = A.alloc([128, 8, TB], BF16)
    wb = [A.alloc([128, 8, 512], BF16) for _ in range(2)]
    lnp = A.alloc([128, 4, D], F32)
    tT = A.alloc([128, D], F32)
    r = A.alloc([128, D], F32)
    xn = A.alloc([128, D], F32)
    st6 = A.alloc([128, 2, 6], F32)
    mv = A.alloc([128, 8], F32)
    sil = [A.alloc([128, 512], F32) for _ in range(2)]
    sg = [A.alloc([128, 512], BF16) for _ in range(2)]
    pt_f = [A.alloc([128, 256], F32) for _ in range(1)]
    C.xb = [A.alloc([128, D], BF16) for _ in range(2)]
    C.xbi = 0
    tmp = (tT, r, xn, st6, mv)
    base = A.off
    mergedT = A.alloc([128, 8, TB], BF16)
    yT = A.alloc([128, 8, TB], BF16)
    wbr = A.alloc([128, 2, 4, D], BF16)
    wout = A.alloc([128, 8, D], BF16)
    A.reset(base)
    actT = A.alloc([128, NFF, TB], BF16)
    wd = A.alloc([128, NFF, D], BF16)
    A.reset(base)
    wpg = A.alloc([128, 8, D], BF16)
    wpl = A.alloc([128, 2, D], BF16)
    pT = A.alloc([128, 2, TB], BF16)

    DMA(P, "sp", lnp, I["lnp"][:, 2:6, :], "lnp3", writes=["ln2", "ln3"])
    x1v = C.X1_s.rearrange("(t p) d -> p t d", p=128)
    outv = C.out.rearrange("(t p) d -> p t d", p=128)
    pv = I["p"].rearrange("(t p) d -> p t d", p=128)
    win = wview(I["w_in"], 8)
    for tb in range(NTB):
        tok0 = tb * TB
        for t in range(8):
            DMA(P, "sp", xs[:, t, :], x1v[:, tb * 8 + t, :], "xs%d" % t, writes=[("xs", t)])
        for kc in range(8):
            DMA(P, "sp", xT[:, kc, :], C.X1T_s[kc * 128:(kc + 1) * 128, tb * TB:(tb + 1) * TB], "xTl%d" % kc, writes=[("xT", q) for q in range(8)])
            DMA(P, "sp", yT[:, kc, :], C.YT_s[kc * 128:(kc + 1) * 128, tok0:tok0 + TB], "yTl%d" % kc, writes=[("yT", kc)])
        DMA(P, "pool", wbr[:, 0, :, :], wview(I["w_branch_a"], 4), "wbra", writes=["wbra"])
        DMA(P, "pool", wbr[:, 1, :, :], wview(I["w_branch_b"], 4), "wbrb", writes=["wbrb"])
        DMA(P, "pool", wout, wview(I["w_out"], 8), "wout", writes=["wout"])
        for oc in range(8):
            slot = oc % 2
            buf = wb[slot]
            DMA(P, "pool", buf[:, :, 0:128], win[:, :, 3072 + oc * 128:3072 + (oc + 1) * 128], "wbA%d" % slot, writes=[("wb", slot, 0)])
            DMA(P, "pool", buf[:, :, 128:256], win[:, :, 4096 + oc * 128:4096 + (oc + 1) * 128], "wbG%d" % slot, writes=[("wb", slot, 1)])
            for hb in range(TB // 512):
                xr = [("xT", 4 * hb + q) for q in range(4)]
                ip = (oc * 2 + hb) % 2 if FLAG_MERGEDB else 0
                for br in range(2):
                    bank = 4 * ip + br
                    for kc in range(8):
                        MM(P, C.ps[bank][:, :], buf[:, kc, br * 128:(br + 1) * 128], xT[:, kc, hb * 512:(hb + 1) * 512],
                           kc == 0, kc == 7, [("wb", slot, br)] + xr, [("ps", bank)])
                    ACT(P, sg[br][:, :], C.ps[bank][:, :], AF.Sigmoid, [("ps", bank)], [("sg", br)])
                    bank2 = 4 * ip + 2 + br
                    for kc in range(4):
                        MM(P, C.ps[bank2][:, :], wbr[:, br, kc, oc * 128:(oc + 1) * 128], yT[:, br * 4 + kc, hb * 512:(hb + 1) * 512],
                           kc == 0, kc == 3, ["wbra" if br == 0 else "wbrb", ("yT", br * 4 + kc)], [("ps", bank2)])
                    TT(P, "dve", sil[br][:, :], sg[br][:, :], C.ps[bank2][:, :], ALU.mult,
                       [("sg", br), ("ps", bank2)], [("sil", br)])
                TT(P, "dve", mergedT[:, oc, hb * 512:(hb + 1) * 512], sil[0][:, :], sil[1][:, :], ALU.add,
                   [("sil", 0), ("sil", 1)], [("mergedT", oc, hb)])
        for t in range(8):
            hb = t // 4
            yb = (4, 5) if (t % 2 == 0 or not FLAG_LNPIPE) else (2, 3)
            for dh in range(2):
                for kc in range(8):
                    MM(P, C.ps[yb[dh]][:, :], mergedT[:, kc, t * 128:(t + 1) * 128], wout[:, kc, dh * 512:(dh + 1) * 512],
                       kc == 0, kc == 7, [("mergedT", kc, hb), "wout"], [("ps", yb[dh])])
            emit_layernorm(C, xs, t, yb, 1.0, lnp[:, 0, :], lnp[:, 1, :], tmp, "ln2")
            if t >= 1:
                emit_transposes(C, xs, xT, t - 1, ("dve", "act"), cast=False)
        emit_transposes(C, xs, xT, 7, ("dve", "act"), cast=False)
        P.barrier()
        emit_ffn(C, xs, xT, actT, wd, wb, sil, I["w_ffn2_up"], I["w_ffn2_down"], "f2", True)
        for t in range(8):
            yb = (4, 5) if (t % 2 == 0 or not FLAG_LNPIPE) else (2, 3)
            emit_down(C, actT, wd, t, yb)
            emit_layernorm(C, xs, t, yb, 0.5, lnp[:, 2, :], lnp[:, 3, :], tmp, "ln3")
            if t >= 1:
                emit_transposes(C, xs, xT, t - 1, ("dve", "act"), cast=False)
        emit_transposes(C, xs, xT, 7, ("dve", "act"), cast=False)
        P.barrier()
        DMA(P, "pool", wpg, wview(I["w_ple_gate"], 8), "wpg", writes=["wpg"])
        DMA(P, "pool", wpl, wview(I["w_ple"], 2), "wpl", writes=["wpl"])
        for t in range(8):
            pf = pt_f[0]
            DMA(P, "sp", pf, pv[:, tb * 8 + t, :], "pf0", writes=[("pf", 0)])
            pb = C.xb[0][:, 0:256]
            CP(P, "pool", pb, pf[:, :], [("pf", 0)], [("xb", 0)])
            for k in range(2):
                MM(P, C.ps[6][:, k * 128:(k + 1) * 128], pb[:, k * 128:(k + 1) * 128], C.ident_b[:, :], True, True, [("xb", 0), "ident_b"], [("ps", 6)])
            CP(P, "dve", pT[:, :, t * 128:(t + 1) * 128], C.ps[6][:, 0:256].rearrange("p (a b) -> p a b", a=2), [("ps", 6)], [("pT", t)])
            for dh in range(2):
                for kc in range(8):
                    MM(P, C.ps[dh][:, :], xT[:, kc, t * 128:(t + 1) * 128], wpg[:, kc, dh * 512:(dh + 1) * 512],
                       kc == 0, kc == 7, [("xT", t), "wpg"], [("ps", dh)])
                ACT(P, tT[:, dh * 512:(dh + 1) * 512], C.ps[dh][:, :], AF.Sigmoid, [("ps", dh)], [("tT", dh)])
                for kc in range(2):
                    MM(P, C.ps[2 + dh][:, :], pT[:, kc, t * 128:(t + 1) * 128], wpl[:, kc, dh * 512:(dh + 1) * 512],
                       kc == 0, kc == 1, [("pT", t), "wpl"], [("ps", 2 + dh)])
                TT(P, "dve", sil[dh][:, :], tT[:, dh * 512:(dh + 1) * 512], C.ps[2 + dh][:, :], ALU.mult, [("tT", dh), ("ps", 2 + dh)], [("sil", dh)])
                TT(P, "dve", xn[:, dh * 512:(dh + 1) * 512], sil[dh][:, :], xs[:, t, dh * 512:(dh + 1) * 512], ALU.add,
                   [("sil", dh), ("xs", t)], [("xo", dh)])
            DMA(P, "sp", outv[:, tb * 8 + t, :], xn[:, :], "outst", reads=[("xo", 0), ("xo", 1)])
        P.barrier()


def host_consts(mode, r):
    c = {}
    c["ident"] = np.eye(128, dtype=np.float32)
    ki = np.arange(128)[:, None]
    qi = np.arange(128)[None, :]
    c["tri"] = np.where(ki > qi, -BIGM, 0.0).astype(np.float32)
    slopes = np.concatenate([2.0 ** (-8.0 * np.arange(1, 9) / 8), 2.0 ** (-8.0 * np.arange(1, 5) / 4)]).astype(np.float64)
    if mode == "P4":
        NQ, NK = 4096, 4096
        qpos = np.arange(NQ)
        kpos = np.arange(NK)
        kvalid = np.ones(NK, bool)
    else:
        NQ, NK = 2048, 4096
        qpos = r * 2048 + np.arange(NQ)
        kpos = np.concatenate([np.arange(2048), r * 2048 + np.arange(2048)])
        kvalid = np.concatenate([np.full(2048, r == 1), np.ones(2048, bool)])
    qc = np.zeros((12, 4, NQ), np.float32)
    kc = np.zeros((12, 4, NK), np.float32)
    for h in range(12):
        s8 = 8.0 * slopes[h]
        qc[h, 0] = -s8 * (qpos % 128)
        qc[h, 1] = -s8 * 128.0 * (qpos // 128)
        qc[h, 2] = 1.0
        qc[h, 3] = 1.0
        kc[h, 0] = 1.0
        kc[h, 1] = 1.0
        kc[h, 2] = s8 * (kpos % 128)
        kc[h, 3] = np.where(kvalid, s8 * 128.0 * (kpos // 128), -BIGM)
    c["qc"], c["kc"] = qc, kc
    kind = np.zeros((16, NK), np.float32)
    for n in range(16):
        kind[n, n * 256:(n + 1) * 256] = BIGM
    c["kind"] = kind
    nt = NQ // 128
    vb = np.zeros((nt, 16), np.float32)
    lo = np.zeros((nt, 16), np.float32)
    for t in range(nt):
        b_loc = t // 2
        for n in range(16):
            if mode == "P4":
                st = "past" if n < b_loc else ("own" if n == b_loc else "no")
            else:
                if n < 8:
                    st = "past" if r == 1 else "no"
                else:
                    nn = n - 8
                    st = "past" if nn < b_loc else ("own" if nn == b_loc else "no")
            vb[t, n] = {"past": 0.0, "own": 1e30, "no": -1e30}[st]
            lo[t, n] = 0.0 if st in ("past", "own") else -1.0
    c["vb"] = np.ascontiguousarray(np.broadcast_to(vb.reshape(1, nt // 4, 4, 16), (128, nt // 4, 4, 16)))
    c["lo"] = np.ascontiguousarray(np.broadcast_to(lo.reshape(1, nt // 4, 4, 16), (128, nt // 4, 4, 16)))
    return c


_CACHE = {}
MODE = "P8R"


def kernel(**inputs):
    f = lambda k: np.ascontiguousarray(np.asarray(inputs[k], dtype=np.float32)[0])
    mode = MODE
    if mode not in _CACHE:
        _CACHE[mode] = build(mode)
    nc = _CACHE[mode]
    x = np.asarray(inputs["x"], dtype=np.float32)
    p = np.asarray(inputs["p"], dtype=np.float32)[0]
    shared = {k: f(k) for k in ("w_ffn1_up", "w_ffn1_down", "w_ffn2_up", "w_ffn2_down", "w_in", "w_branch_a",
                                "w_branch_b", "w_out", "w_ple", "w_ple_gate")}
    lnp = np.stack([f(k) for k in ("ln1_g", "ln1_b", "ln2_g", "ln2_b", "ln3_g", "ln3_b")], 0)
    shared["lnp"] = np.ascontiguousarray(np.broadcast_to(lnp[None], (128, 6, D)))
    shared["sublng"] = np.ascontiguousarray(np.broadcast_to(f("subln_g")[None], (128, 128)))
    lamv = np.stack([f(k) for k in ("lambda_q1", "lambda_k1", "lambda_q2", "lambda_k2")], 0)
    shared["lamv"] = np.ascontiguousarray(np.broadcast_to(lamv[None], (128, 4, 64)))
    consts = {}
    in_maps = []
    for c in range(N_CORES):
        m = dict(shared)
        if mode == "P4":
            b, r = c % 4, 0
            m["x"] = np.ascontiguousarray(x[b])
            m["p"] = np.ascontiguousarray(p[b])
        else:
            b, r = c // 2, c % 2
            m["x"] = np.ascontiguousarray(np.concatenate([x[b, 0:2048], x[b, r * 2048:(r + 1) * 2048]], 0))
            m["p"] = np.ascontiguousarray(p[b, r * 2048:(r + 1) * 2048])
        if r not in consts:
            consts[r] = host_consts(mode, r)
        m.update(consts[r])
        in_maps.append(m)
    res = run_bass_kernel_spmd(nc, in_maps, core_ids=list(range(N_CORES)))
    if mode == "P4":
        out = np.stack([np.asarray(res.results[b]["out"], dtype=np.float32) for b in range(4)], 0)
    else:
        out = np.stack([np.concatenate([np.asarray(res.results[2 * b + r]["out"], dtype=np.float32) for r in range(2)], 0)
                        for b in range(4)], 0)
    return out
```

```python
import contextlib
import numpy as np
import concourse.bass as bass
import concourse.mybir as mybir
from concourse.bass_utils import run_bass_kernel_spmd

F32 = mybir.dt.float32
BF16 = mybir.dt.bfloat16
AF = mybir.ActivationFunctionType
ALU = mybir.AluOpType
AX = mybir.AxisListType

D = 1024
DFF = 2816
SEQ = 4096
NFF = 22
ALPHA = float(2.0 ** 0.25)
BIGM = 262144.0
EPS = 1e-5
LAM_INIT = 0.8 - 0.6 * 1.0
N_CORES = 8
import os
FLAG_DEFER = os.environ.get("K_DEFER", "1") == "1"
FLAG_S2DEFER = os.environ.get("K_S2DEFER", "1") == "1"
FLAG_BURST = int(os.environ.get("K_BURST", "10"))
FLAG_LNPIPE = os.environ.get("K_LNPIPE", "1") == "1"
FLAG_MERGEDB = os.environ.get("K_MERGEDB", "1") == "1"


class Prog:
    ENGS = ("pe", "act", "dve", "pool", "sp")

    def __init__(self, nc):
        self.nc = nc
        self.ops = []
        self.last_w = {}
        self.readers = {}
        self.dma_keys = {}
        self.last_on_eng = {}
        self.dma_since_barrier = []
        self.barrier_deps = {}

    def op(self, eng, fn, reads=(), writes=(), dma=0, key=None):
        idx = len(self.ops)
        deps = set()
        for r in reads:
            w = self.last_w.get(r)
            if w is not None:
                deps.add(w)
        for r in writes:
            w = self.last_w.get(r)
            if w is not None:
                deps.add(w)
            for rd in self.readers.get(r, {}).values():
                deps.add(rd)
        for r in reads:
            d = self.readers.setdefault(r, {})
            d[eng if not dma else ("dma", idx)] = idx
        for r in writes:
            self.last_w[r] = idx
            self.readers[r] = {}
        bd = self.barrier_deps.pop(eng, None)
        if bd:
            deps.update(bd)
        deps.discard(idx)
        o = dict(eng=eng, fn=fn, deps=deps, dma=dma, sig=False)
        if dma:
            assert key is not None
            if key not in self.dma_keys:
                self.dma_keys[key] = len(self.dma_keys)
            o["dkey"] = self.dma_keys[key]
            self.dma_since_barrier.append(idx)
        else:
            self.last_on_eng[eng] = idx
        self.ops.append(o)
        return idx

    def barrier(self):
        deps = set(self.dma_since_barrier)
        deps.update(self.last_on_eng.values())
        self.dma_since_barrier = []
        self.barrier_deps = {e: set(deps) for e in self.ENGS}

    def emit(self):
        nc = self.nc
        ops = self.ops
        for o in ops:
            for d in o["deps"]:
                p = ops[d]
                if p["eng"] == "pe" and o["eng"] == "pe" and not p["dma"] and not o["dma"]:
                    continue
                p["sig"] = True
        es = contextlib.ExitStack()
        eng_sem = {e: es.enter_context(nc.semaphore("s_" + e)) for e in self.ENGS}
        cnt = {e: 0 for e in self.ENGS}
        nk = len(self.dma_keys)
        assert nk <= 120, nk
        dma_sems = [es.enter_context(nc.semaphore("d%d" % i)) for i in range(nk)]
        dcnt = [0] * nk
        for o in ops:
            if o["dma"]:
                k = o["dkey"]
                dcnt[k] += 16 * o["dma"]
                o["sigval"] = dcnt[k]
            elif o["sig"]:
                cnt[o["eng"]] += 1
                o["sigval"] = cnt[o["eng"]]
        per_eng = {e: [] for e in self.ENGS}
        for i, o in enumerate(ops):
            per_eng[o["eng"]].append(i)
        with es, nc.Block() as block:
            def run(engname, engobj):
                waited = {}
                for i in per_eng[engname]:
                    o = ops[i]
                    for d in sorted(o["deps"]):
                        p = ops[d]
                        if p["dma"]:
                            sem = dma_sems[p["dkey"]]
                            key = ("d", p["dkey"])
                        else:
                            if p["eng"] == "pe" and engname == "pe" and not o["dma"]:
                                continue
                            sem = eng_sem[p["eng"]]
                            key = ("e", p["eng"])
                        v = p["sigval"]
                        if waited.get(key, 0) >= v:
                            continue
                        waited[key] = v
                        engobj.wait_ge(sem, v)
                    if o["dma"]:
                        o["fn"](engobj, dma_sems[o["dkey"]])
                    else:
                        inst = o["fn"](engobj)
                        if o["sig"]:
                            inst.then_inc(eng_sem[engname], 1)
                for i in per_eng[engname]:
                    o = ops[i]
                    if o["dma"]:
                        key = ("d", o["dkey"])
                        if waited.get(key, 0) < o["sigval"]:
                            waited[key] = o["sigval"]
                            engobj.wait_ge(dma_sems[o["dkey"]], o["sigval"])

            @block.tensor
            def _(e):
                run("pe", e)

            @block.scalar
            def _(e):
                run("act", e)

            @block.vector
            def _(e):
                run("dve", e)

            @block.gpsimd
            def _(e):
                run("pool", e)

            @block.sync
            def _(e):
                run("sp", e)


class Arena:
    def __init__(self, nc, st, nbytes):
        self.t = st.enter_context(nc.sbuf_tensor("arena", [128, nbytes // 4], F32))
        self.cap = nbytes
        self.off = 0

    def reset(self, off=0):
        self.off = off

    def alloc(self, shape, dtype):
        esz = 4 if dtype == F32 else 2
        n = int(np.prod(shape[1:]))
        nb = (n * esz + 63) // 64 * 64
        assert self.off + nb <= self.cap, (self.off, nb, self.cap)
        w0 = self.off // 4
        ap = self.t[0:shape[0], w0:w0 + nb // 4]
        if dtype != F32:
            ap = ap.bitcast(dtype)
        ap = ap[:, 0:n]
        if len(shape) == 3:
            ap = ap.rearrange("p (a b) -> p a b", a=shape[1])
        elif len(shape) == 4:
            ap = ap.rearrange("p (a b c) -> p a b c", a=shape[1], b=shape[2])
        self.off += nb
        return ap


def MM(P, out, lhsT, rhs, start, stop, reads, writes):
    P.op("pe", lambda e: e.matmul(out, lhsT=lhsT, rhs=rhs, start=start, stop=stop), reads, writes)


def TR(P, out, in_, ident, reads, writes):
    P.op("pe", lambda e: e.transpose(out, in_, ident), reads, writes)


def ACT(P, out, in_, func, reads, writes, scale=None, bias=None, accum_out=None):
    kw = {}
    if scale is not None:
        kw["scale"] = scale
    if bias is not None:
        kw["bias"] = bias
    if accum_out is not None:
        kw["accum_out"] = accum_out
    P.op("act", lambda e: e.activation(out=out, in_=in_, func=func, **kw), reads, writes)


def TT(P, eng, out, in0, in1, op, reads, writes):
    P.op(eng, lambda e: e.tensor_tensor(out=out, in0=in0, in1=in1, op=op), reads, writes)


def TS(P, eng, out, in0, s1, s2, op0, op1, reads, writes):
    if op1 is None:
        P.op(eng, lambda e: e.tensor_scalar(out=out, in0=in0, scalar1=s1, scalar2=None, op0=op0), reads, writes)
    else:
        P.op(eng, lambda e: e.tensor_scalar(out=out, in0=in0, scalar1=s1, scalar2=s2, op0=op0, op1=op1), reads, writes)


def STT(P, out, in0, scalar, in1, op0, op1, reads, writes):
    P.op("dve", lambda e: e.scalar_tensor_tensor(out=out, in0=in0, scalar=scalar, in1=in1, op0=op0, op1=op1), reads, writes)


def CP(P, eng, out, in_, reads, writes):
    if eng == "act":
        P.op("act", lambda e: e.activation(out=out, in_=in_, func=AF.Identity), reads, writes)
    else:
        P.op(eng, lambda e: e.tensor_copy(out=out, in_=in_), reads, writes)


def DMA(P, q, out, in_, key, reads=(), writes=()):
    P.op(q, lambda e, s: e.dma_start(out=out, in_=in_).then_inc(s, 16), reads, writes, dma=1, key=key)


class Ctx:
    pass


def build(MODE, debug=False):
    nc = bass.Bass("TRN2", target_bir_lowering=False)
    C = Ctx()
    C.nc = nc
    C.NHM, C.NHD = 8, 4
    if MODE == "P4":
        C.NQ, C.NK, C.KOFF = 4096, 4096, 0
    else:
        C.NQ, C.NK, C.KOFF = 2048, 4096, 2048
    NQ, NK = C.NQ, C.NK
    TB = 1024
    C.TB = TB

    def din(name, shape):
        return nc.dram_tensor(name, list(shape), F32, kind="ExternalInput").ap()

    I = {}
    I["x"] = din("x", [NK, D])
    I["p"] = din("p", [NQ, 256])
    I["w_ffn1_up"] = din("w_ffn1_up", [D, 2 * DFF])
    I["w_ffn1_down"] = din("w_ffn1_down", [DFF, D])
    I["w_ffn2_up"] = din("w_ffn2_up", [D, 2 * DFF])
    I["w_ffn2_down"] = din("w_ffn2_down", [DFF, D])
    I["w_in"] = din("w_in", [D, 5120])
    I["w_branch_a"] = din("w_branch_a", [512, D])
    I["w_branch_b"] = din("w_branch_b", [512, D])
    I["w_out"] = din("w_out", [D, D])
    I["w_ple"] = din("w_ple", [256, D])
    I["w_ple_gate"] = din("w_ple_gate", [D, D])
    I["lnp"] = din("lnp", [128, 6, D])
    I["sublng"] = din("sublng", [128, 128])
    I["lamv"] = din("lamv", [128, 4, 64])
    I["ident"] = din("ident", [128, 128])
    I["tri"] = din("tri", [128, 128])
    I["qc"] = din("qc", [12, 4, NQ])
    I["kc"] = din("kc", [12, 4, NK])
    I["kind"] = din("kind", [16, NK])
    I["vb"] = din("vb", [128, NQ // 512, 4, 16])
    I["lo"] = din("lo", [128, NQ // 512, 4, 16])
    C.I = I
    kind_s = "ExternalOutput" if debug else "Internal"
    C.out = nc.dram_tensor("out", [NQ, D], F32, kind="ExternalOutput").ap()
    C.X1_s = nc.dram_tensor("X1_s", [NQ, D], F32, kind=kind_s).ap()
    C.X1T_s = nc.dram_tensor("X1T_s", [D, NQ], BF16, kind=kind_s).ap()
    C.QT_s = nc.dram_tensor("QT_s", [D, NQ], BF16, kind=kind_s).ap()
    C.KT_s = nc.dram_tensor("KT_s", [D, NK], BF16, kind=kind_s).ap()
    C.V_s = nc.dram_tensor("V_s", [NK, D], BF16, kind=kind_s).ap()
    C.YT_s = nc.dram_tensor("YT_s", [D, NQ], BF16, kind=kind_s).ap()

    with contextlib.ExitStack() as st:
        C.P = P = Prog(nc)
        C.arena = Arena(nc, st, 192 * 1024)
        C.ps = [st.enter_context(nc.psum_tensor("ps%d" % i, [128, 512], F32)) for i in range(8)]
        sb = lambda n, s, d: st.enter_context(nc.sbuf_tensor(n, s, d))
        C.ident_f = sb("ident_f", [128, 128], F32)
        C.ident_b = sb("ident_b", [128, 128], BF16)
        C.tri_b = sb("tri_b", [128, 128], BF16)
        C.eps_t = sb("eps_t", [128, 1], F32)
        C.lam_t = sb("lam_t", [128, 4], F32)
        C.lamv = sb("lamv_t", [128, 4, 64], F32)
        C.sublng = sb("sublng_t", [128, 128], F32)
        C.small = sb("small_t", [128, 64], F32)

        phase0(C)
        phase1(C)
        P.barrier()
        phase2(C)
        P.barrier()
        phase3(C)
        P.emit()
    return nc


def phase0(C):
    P, I = C.P, C.I
    DMA(P, "sp", C.ident_f[:], I["ident"], "c_if", writes=["ident_f"])
    DMA(P, "pool", C.ident_b[:], I["ident"], "c_ib", writes=["ident_b"])
    DMA(P, "pool", C.tri_b[:], I["tri"], "c_tri", writes=["tri_b"])
    DMA(P, "sp", C.lamv[:], I["lamv"], "c_lamv", writes=["lamv"])
    DMA(P, "sp", C.sublng[:], I["sublng"], "c_sub", writes=["sublng"])
    P.op("dve", lambda e: e.memset(C.eps_t[:], EPS), writes=["eps"])
    sm = C.small
    TT(P, "dve", sm[:, 0:64], C.lamv[:, 0, :], C.lamv[:, 1, :], ALU.mult, ["lamv"], ["sm0"])
    P.op("dve", lambda e: e.tensor_reduce(out=C.lam_t[:, 2:3], in_=sm[:, 0:64], axis=AX.X, op=ALU.add), ["sm0"], ["l2"])
    TT(P, "dve", sm[:, 0:64], C.lamv[:, 2, :], C.lamv[:, 3, :], ALU.mult, ["lamv", "l2"], ["sm0"])
    P.op("dve", lambda e: e.tensor_reduce(out=C.lam_t[:, 3:4], in_=sm[:, 0:64], axis=AX.X, op=ALU.add), ["sm0"], ["l3"])
    ACT(P, C.lam_t[:, 2:4], C.lam_t[:, 2:4], AF.Exp, ["l2", "l3"], ["l23"])
    TT(P, "dve", C.lam_t[:, 0:1], C.lam_t[:, 2:3], C.lam_t[:, 3:4], ALU.subtract, ["l23"], ["lam0"])
    TS(P, "dve", C.lam_t[:, 0:1], C.lam_t[:, 0:1], float(LAM_INIT), None, ALU.add, None, ["lam0"], ["lam"])
    TS(P, "dve", C.lam_t[:, 1:2], C.lam_t[:, 0:1], -1.0, None, ALU.mult, None, ["lam"], ["nlam"])


def wview(w, kcn):
    return w.rearrange("(kc p) n -> p kc n", p=128)


def emit_transposes(C, xs, xT, t, evac_eng, cast=True):
    P = C.P
    xi = t % len(C.xb)
    xb = C.xb[xi]
    xbn = ("xb", xi)
    if cast:
        CP(P, "act", xb[:, :], xs[:, t, :], [("xs", t)], [xbn])
    for g in range(2):
        bank = C.ps[6 + g]
        for k in range(4):
            kc = g * 4 + k
            MM(P, bank[:, k * 128:(k + 1) * 128], xb[:, kc * 128:(kc + 1) * 128], C.ident_b[:, :], True, True,
               [xbn, "ident_b"], [("ps", 6 + g)])
        src = bank[:, :].rearrange("p (a b) -> p a b", a=4)
        CP(P, evac_eng[g], xT[:, g * 4:(g + 1) * 4, t * 128:(t + 1) * 128], src, [("ps", 6 + g)], [("xT", t)])


def emit_layernorm(C, xs, t, ybanks, yscale, lng, lnb, tmp, lnname):
    P = C.P
    tT, r, xn, st6, mv = tmp
    for h in range(2):
        ACT(P, tT[:, h * 512:(h + 1) * 512], C.ps[ybanks[h]][:, :], AF.Identity, [("ps", ybanks[h])], [("tT", h)], scale=float(yscale))
    STT(P, r[:, :], xs[:, t, :], ALPHA, tT[:, :], ALU.mult, ALU.add, [("xs", t), ("tT", 0), ("tT", 1)], ["r"])
    for h in range(2):
        P.op("dve", (lambda hh: (lambda e: e.bn_stats(out=st6[:, hh, :], in_=r[:, hh * 512:(hh + 1) * 512])))(h), ["r"], [("st6", h)])
    P.op("dve", lambda e: e.bn_aggr(out=mv[:, 0:2], in_=st6[:, :, :].rearrange("p a b -> p (a b)")), [("st6", 0), ("st6", 1)], ["mv"])
    ACT(P, mv[:, 2:3], mv[:, 1:2], AF.Sqrt, ["mv", "eps"], ["std"], bias=C.eps_t[:], scale=1.0)
    P.op("dve", lambda e: e.reciprocal(out=mv[:, 3:4], in_=mv[:, 2:3]), ["std"], ["rstd"])
    STT(P, mv[:, 4:5], mv[:, 0:1], -1.0, mv[:, 3:4], ALU.mult, ALU.mult, ["mv", "rstd"], ["nmr"])
    ACT(P, xn[:, :], r[:, :], AF.Identity, ["r", "rstd", "nmr"], ["xn"], scale=mv[:, 3:4], bias=mv[:, 4:5])
    TT(P, "dve", xn[:, :], xn[:, :], lng, ALU.mult, ["xn", lnname], ["xn"])
    TT(P, "dve", xs[:, t, :], xn[:, :], lnb, ALU.add, ["xn", lnname], [("xs", t)])
    xi = t % len(C.xb)
    CP(P, "act", C.xb[xi][:, :], xs[:, t, :], [("xs", t)], [("xb", xi)])


def emit_ffn(C, xs, xT, actT, wd, wb, sil, w_up, w_down, tag, first):
    P = C.P
    TB = C.TB
    wu = wview(w_up, 8)
    if first:
        wdv = wview(w_down, NFF)
        DMA(P, "pool", wd[:, 0:11, :], wdv[:, 0:11, :], tag + "wd0", writes=[("wd", 0)])
        DMA(P, "pool", wd[:, 11:22, :], wdv[:, 11:22, :], tag + "wd1", writes=[("wd", 1)])
    par = 0
    for gi in range(NFF // 2):
        slot = gi % 2
        buf = wb[slot]
        DMA(P, "pool", buf[:, :, 0:256], wu[:, :, gi * 256:(gi + 1) * 256], "wbA%d" % slot, writes=[("wb", slot, 0)])
        DMA(P, "pool", buf[:, :, 256:512], wu[:, :, DFF + gi * 256:DFF + (gi + 1) * 256], "wbG%d" % slot, writes=[("wb", slot, 1)])
        for jj in range(2):
            j = gi * 2 + jj
            for hb in range(TB // 512):
                bA, bG = 2 * par, 2 * par + 1
                xr = [("xT", 4 * hb + q) for q in range(4)]
                for kc in range(8):
                    MM(P, C.ps[bA][:, :], buf[:, kc, jj * 128:(jj + 1) * 128], xT[:, kc, hb * 512:(hb + 1) * 512],
                       kc == 0, kc == 7, [("wb", slot, 0)] + xr, [("ps", bA)])
                for kc in range(8):
                    MM(P, C.ps[bG][:, :], buf[:, kc, 256 + jj * 128:256 + (jj + 1) * 128], xT[:, kc, hb * 512:(hb + 1) * 512],
                       kc == 0, kc == 7, [("wb", slot, 1)] + xr, [("ps", bG)])
                ACT(P, sil[par][:, :], C.ps[bA][:, :], AF.Silu, [("ps", bA)], [("sil", par)])
                TT(P, "dve", actT[:, j, hb * 512:(hb + 1) * 512], sil[par][:, :], C.ps[bG][:, :], ALU.mult,
                   [("sil", par), ("ps", bG)], [("actT", j, hb)])
                par ^= 1


def emit_down(C, actT, wd, t, yb=(4, 5)):
    P = C.P
    hb = t // 4
    for dh in range(2):
        for kc in range(NFF):
            MM(P, C.ps[yb[dh]][:, :], actT[:, kc, t * 128:(t + 1) * 128], wd[:, kc, dh * 512:(dh + 1) * 512],
               kc == 0, kc == NFF - 1, [("actT", kc, hb), ("wd", kc // 11)], [("ps", yb[dh])])


def phase1(C):
    P, I, A = C.P, C.I, C.arena
    TB = C.TB
    NTB = C.NK // TB
    own_tb0 = C.KOFF // TB
    A.reset()
    xs = A.alloc([128, 8, D], F32)
    xT = A.alloc([128, 8, TB], BF16)
    actT = A.alloc([128, NFF, TB], BF16)
    wd = A.alloc([128, NFF, D], BF16)
    wb = [A.alloc([128, 8, 512], BF16) for _ in range(2)]
    lnp = A.alloc([128, 2, D], F32)
    tT = A.alloc([128, D], F32)
    r = A.alloc([128, D], F32)
    xn = A.alloc([128, D], F32)
    st6 = A.alloc([128, 2, 6], F32)
    mv = A.alloc([128, 8], F32)
    sil = [A.alloc([128, 512], F32) for _ in range(2)]
    stg = [A.alloc([128, TB], BF16) for _ in range(2)]
    vst = [A.alloc([128, 512], BF16) for _ in range(2)]
    tmp = (tT, r, xn, st6, mv)
    C.xb = [A.alloc([128, D], BF16) for _ in range(2)]
    C.xbi = 0
    DMA(P, "sp", lnp, I["lnp"][:, 0:2, :], "lnp1", writes=["ln1"])
    xv = I["x"].rearrange("(t p) d -> p t d", p=128)
    x1v = C.X1_s.rearrange("(t p) d -> p t d", p=128)
    win = wview(I["w_in"], 8)
    for t in range(8):
        DMA(P, "sp", xs[:, t, :], xv[:, t, :], "xs%d" % t, writes=[("xs", t)])
    for tb in range(NTB):
        own = tb >= own_tb0
        qtb = tb - own_tb0
        for t in range(8):
            emit_transposes(C, xs, xT, t, ("dve", "act"))
        emit_ffn(C, xs, xT, actT, wd, wb, sil, I["w_ffn1_up"], I["w_ffn1_down"], "f1", tb == 0)
        for t in range(8):
            yb = (4, 5) if (t % 2 == 0 or not FLAG_LNPIPE) else (2, 3)
            emit_down(C, actT, wd, t, yb)
            emit_layernorm(C, xs, t, yb, 0.5, lnp[:, 0, :], lnp[:, 1, :], tmp, "ln1")
            if own:
                DMA(P, "sp", x1v[:, qtb * 8 + t, :], xs[:, t, :], "x1st%d" % t, reads=[("xs", t)])
            if t >= 1:
                emit_transposes(C, xs, xT, t - 1, ("dve", "act"), cast=False)
        emit_transposes(C, xs, xT, 7, ("dve", "act"), cast=False)
        if own:
            for kc in range(8):
                DMA(P, "sp", C.X1T_s[kc * 128:(kc + 1) * 128, qtb * TB:(qtb + 1) * TB], xT[:, kc, :], "x1T%d" % kc,
                    reads=[("xT", q) for q in range(8)])
        if tb + 1 < NTB:
            for t in range(8):
                DMA(P, "sp", xs[:, t, :], xv[:, (tb + 1) * 8 + t, :], "xs%d" % t, writes=[("xs", t)])
        par = 0
        for g in range(6):
            kind = ("q", "k", "v")[g % 3]
            fam = g // 3
            if kind == "q" and not own:
                continue
            slot = g % 2
            buf = wb[slot]
            DMA(P, "pool", buf[:, :, 0:256], win[:, :, g * 512:g * 512 + 256], "wbA%d" % slot, writes=[("wb", slot, 0)])
            DMA(P, "pool", buf[:, :, 256:512], win[:, :, g * 512 + 256:(g + 1) * 512], "wbG%d" % slot, writes=[("wb", slot, 1)])
            if kind in ("q", "k"):
                dst = C.QT_s if kind == "q" else C.KT_s
                col0 = qtb * TB if kind == "q" else tb * TB
                for c in range(4):
                    sg = stg[par]
                    for hb in range(TB // 512):
                        bank = 2 * par + hb
                        for kc in range(8):
                            MM(P, C.ps[bank][:, :], buf[:, kc, c * 128:(c + 1) * 128], xT[:, kc, hb * 512:(hb + 1) * 512],
                               kc == 0, kc == 7, [("wb", slot, c // 2)] + [("xT", 4 * hb + q) for q in range(4)], [("ps", bank)])
                        CP(P, "act" if hb == 0 else "dve", sg[:, hb * 512:(hb + 1) * 512], C.ps[bank][:, :], [("ps", bank)], [("stg", par, hb)])
                    row0 = fam * 512 + c * 128
                    DMA(P, "sp", dst[row0:row0 + 128, col0:col0 + TB], sg[:, :], "stg%d" % par,
                        reads=[("stg", par, 0), ("stg", par, 1)])
                    par ^= 1
            else:
                vv = C.V_s.rearrange("(t p) d -> p t d", p=128)
                for t in range(8):
                    bank = 2 * par
                    for kc in range(8):
                        MM(P, C.ps[bank][:, :], xT[:, kc, t * 128:(t + 1) * 128], buf[:, kc, :],
                           kc == 0, kc == 7, [("wb", slot, 0), ("wb", slot, 1), ("xT", t)], [("ps", bank)])
                    CP(P, "act" if t % 2 == 0 else "dve", vst[par][:, :], C.ps[bank][:, :], [("ps", bank)], [("vst", par)])
                    DMA(P, "sp", vv[:, tb * 8 + t, fam * 512:(fam + 1) * 512], vst[par][:, :], "vst%d" % par, reads=[("vst", par)])
                    par ^= 1


def phase2(C):
    P, I, A = C.P, C.I, C.arena
    A.reset()
    NHM, NHD = C.NHM, C.NHD
    NQ, NK, KOFF = C.NQ, C.NK, C.KOFF
    NQB = NQ // 512
    NKC = NK // 128
    KO = KOFF // 128
    KTs = [[A.alloc([128, NK], BF16) for m in range(2)] for _ in range(2)]
    QTs = [[A.alloc([128, NQ], BF16) for m in range(2)] for _ in range(2)]
    VEs = [A.alloc([128, NKC, 132], BF16) for _ in range(2)]
    PT = [A.alloc([128, 512], BF16) for _ in range(4)]
    SB = [0, 1, 5]
    accsb = [A.alloc([128, 4, 2, 132], F32) for _ in range(2)]
    o1 = A.alloc([128, 4, 128], F32)
    yd = A.alloc([128, 4, 128], F32)
    ydb = A.alloc([128, 4, 128], BF16)
    sq = A.alloc([128, 128], F32)
    sc = A.alloc([128, 4, 8], F32)
    ystg = [A.alloc([128, 512], BF16) for _ in range(2)]
    kmf = A.alloc([64, 16], F32)
    kmb = [A.alloc([64, 16], BF16) for _ in range(2)]
    vbt = A.alloc([128, NQB, 4, 16], F32)
    lot = A.alloc([128, NQB, 4, 16], F32)
    gm = [A.alloc([128, 4, 16], F32) for _ in range(2)]
    m8 = [A.alloc([128, 4, 8], F32) for _ in range(2)]
    m1 = [A.alloc([128, 4, 16], F32) for _ in range(2)]
    npad = [A.alloc([128, 4, 80], BF16) for _ in range(2)]
    ones_col = A.alloc([128, NKC, 1], F32)
    ones_k = A.alloc([128, 2], BF16)
    one_f = A.alloc([1, 2], F32)
    OTsb = [A.alloc([128, 512], BF16) for _ in range(2)]
    ZTsb = [A.alloc([1, 512], F32) for _ in range(2)]
    Zhi = [A.alloc([1, 512], BF16) for _ in range(2)]
    Zlo = [A.alloc([1, 512], BF16) for _ in range(2)]
    Zd = [A.alloc([1, 512], F32) for _ in range(2)]
    one_b = A.alloc([1, 2], BF16)
    P.op("dve", lambda e: e.memset(one_b[:, :], 1.0), writes=["one_b"])
    one_b64 = A.alloc([128, 2], BF16)
    P.op("dve", lambda e: e.memset(one_b64[:, :], 1.0), writes=["one_b64"])
    sel = [A.alloc([128, 128], BF16) for _ in range(2)]
    for mm_ in range(2):
        P.op("dve", (lambda mm_=mm_: (lambda e: e.memset(sel[mm_][:, 0:64], 1.0 if mm_ == 0 else 0.0)))(), writes=[("sel", mm_)])
        P.op("dve", (lambda mm_=mm_: (lambda e: e.memset(sel[mm_][:, 64:128], 0.0 if mm_ == 0 else 1.0)))(), writes=[("sel", mm_)])
    ZB_f = A.alloc([128, 512], F32)
    ZB_d = A.alloc([128, 512], F32)
    ZB_hi = A.alloc([128, 512], BF16)
    ZB_lo = A.alloc([128, 512], BF16)
    OTfull = A.alloc([128, 512], BF16)
    Zm_f = A.alloc([128, 512], F32)
    Zm_d = A.alloc([128, 512], F32)
    Zm_lo = A.alloc([128, 512], BF16)
    Zm_lo_full = Zm_lo
    P.op("dve", lambda e: e.memset(Zm_lo[:, :], 0.0), writes=[("Zm_lo", 0)])

    ones_kk = A.alloc([128, 128], BF16)
    P.op("dve", lambda e: e.memset(ones_kk[:, :], 1.0), writes=["ones_kk"])
    for par in range(2):
        for m in range(2):
            P.op("pool", (lambda par=par, m=m: (lambda e: e.memset(KTs[par][m][64:128, :], 0.0)))(), writes=[("KT", par, m)])
            P.op("pool", (lambda par=par, m=m: (lambda e: e.memset(QTs[par][m][64:128, :], 0.0)))(), writes=[("QT", par, m), ("QTm", par, m)])
    P.op("dve", lambda e: e.memset(ones_k[:, :], 1.0), writes=["ones_k"])
    P.op("dve", lambda e: e.memset(one_f[:, :], 1.0), writes=["one_f"])
    DMA(P, "sp", vbt, I["vb"], "vbt", writes=["vbt"])
    DMA(P, "sp", lot, I["lo"], "lot", writes=["lot"])
    for g in range(2):
        P.op("dve", (lambda g=g: (lambda e: e.memset(npad[g][:, :, :], 0.0)))(), writes=[("npad", g)])
    P.op("dve", lambda e: e.memset(ones_col[:, :, :], 1.0), writes=["ones_col"])
    vv = C.V_s.rearrange("(k p) d -> p k d", p=128)

    jobs = [("d", h) for h in range(NHD)] + [("m", h) for h in range(NHM)]

    def load_job(ji):
        kind, h = jobs[ji]
        par = ji % 2
        KT, QT, VE = KTs[par], QTs[par], VEs[par]
        if kind == "d":
            hg = NHM + h
            for m in range(2):
                r0 = 512 + h * 128 + m * 64
                DMA(P, "sp", KT[m][0:64, :], C.KT_s[r0:r0 + 64, :], "KTl%d%d" % (par, m), writes=[("KT", par, m)])
                DMA(P, "pool", KT[m][64:68, :], I["kc"][hg, :, :], "KTc%d%d" % (par, m), writes=[("KT", par, m)])
                DMA(P, "sp", QT[m][0:64, :], C.QT_s[r0:r0 + 64, :], "QTl%d%d" % (par, m), writes=[("QT", par, m)])
                DMA(P, "pool", QT[m][64:68, :], I["qc"][hg, :, :], "QTc%d%d" % (par, m), writes=[("QT", par, m)])
            DMA(P, "sp", VE[:, :, 0:128], vv[:, :, 512 + h * 128:512 + (h + 1) * 128], "VEl%d" % par, writes=[("VE", par)])
            CP(P, "dve", VE[:, :, 128:129], ones_col[:, :, :], ["ones_col"], [("VE", par)])
        else:
            r0 = h * 64
            DMA(P, "sp", KT[0][0:64, :], C.KT_s[r0:r0 + 64, :], "KTl%d0" % par, writes=[("KT", par, 0)])
            DMA(P, "pool", KT[0][64:80, :], I["kind"], "KTi%d" % par, writes=[("KT", par, 0)])
            DMA(P, "pool", KT[0][80:84, :], I["kc"][h, :, :], "KTc%d0" % par, writes=[("KT", par, 0)])
            DMA(P, "sp", QT[0][0:64, :], C.QT_s[r0:r0 + 64, :], "QTl%d0" % par, writes=[("QT", par, 0)])
            DMA(P, "pool", QT[0][80:84, :], I["qc"][h, :, :], "QTc%d0" % par, writes=[("QT", par, 0)])
            DMA(P, "sp", VE[:, :, 0:64], vv[:, :, h * 64:(h + 1) * 64], "VEl%d" % par, writes=[("VE", par)])
            CP(P, "dve", VE[:, :, 64:65], ones_col[:, :, :], ["ones_col"], [("VE", par)])

    def mask_stage(ji, tg):
        par = ji % 2
        KT, QT = KTs[par][0], QTs[par][0]
        if tg == 0:
            P.op("dve", lambda e: e.tensor_reduce(out=kmf[:, :], in_=KT[0:64, :].rearrange("p (n l) -> p n l", l=256), axis=AX.X, op=ALU.add),
                 [("KT", par, 0)], ["kmf"])
            TS(P, "dve", kmb[par][:, :], kmf[:, :], 1.0 / 256.0, None, ALU.mult, None, ["kmf"], [("kmb", par)])
        gp = tg % 2
        for tq in range(4):
            t = tg * 4 + tq
            MM(P, C.ps[7][:, tq * 16:(tq + 1) * 16], QT[0:64, t * 128:(t + 1) * 128], kmb[par][:, :], True, True,
               [("QT", par, 0), ("kmb", par)], [("ps", 7)])
        TT(P, "dve", gm[gp][:, :, :], C.ps[7][:, 0:64].rearrange("p (a b) -> p a b", a=4), vbt[:, tg, :, :], ALU.add,
           [("ps", 7), "vbt"], [("gm", gp)])
        for tq in range(4):
            P.op("dve", (lambda tq=tq: (lambda e: e.max(out=m8[gp][:, tq, :], in_=gm[gp][:, tq, :])))(), [("gm", gp)], [("m8", gp, tq)])
        for tq in range(4):
            TS(P, "dve", m1[gp][:, tq, :], gm[gp][:, tq, :], m8[gp][:, tq, 3:4], 1.0, ALU.is_ge, ALU.subtract,
               [("gm", gp), ("m8", gp, tq)], [("m1", gp, tq)])
        TT(P, "dve", npad[gp][:, :, 64:80], m1[gp][:, :, :], lot[:, tg, :, :], ALU.min,
           [("m1", gp, q) for q in range(4)] + ["lot"], [("npad", gp)])

    def mask_stage_b(ji, tg):
        par = ji % 2
        QT = QTs[par][0]
        gp = tg % 2
        for tq in range(4):
            MM(P, C.ps[6][0:80, tq * 128:(tq + 1) * 128], npad[gp][:, tq, :], C.ident_b[:, :], True, True,
               [("npad", gp), "ident_b"], [("ps", 6)])
        CP(P, "act", QT[64:80, tg * 512:(tg + 1) * 512], C.ps[6][64:80, :], [("ps", 6)], [("QTm", par, 0)])

    pending = []
    stepc = [0]

    dseq = [0]

    def defer(lag, fn):
        if not FLAG_DEFER:
            fn()
            return
        dseq[0] += 1
        pending.append((stepc[0] + lag, dseq[0], fn))

    def run_due(flush=False):
        while True:
            due = [p for p in pending if flush or p[0] <= stepc[0]]
            if not due:
                break
            due.sort()
            it = due[0]
            pending.remove(it)
            it[2]()

    def attention(ji, nmaps, dv, K, finalize, side):
        par = ji % 2
        KT, QT, VE = KTs[par], QTs[par], VEs[par]
        steps = [(qb, m, kc) for qb in range(NQB) for m in range(nmaps) for kc in range(KO + 4 * qb + 4)]
        n = len(steps)

        def emit_S(i):
            qb, m, kc = steps[i]
            j = kc - KO - 4 * qb
            n0 = max(0, j) * 128
            sb = SB[i % 3]
            S = C.ps[sb]
            MM(P, S[:, n0:512], KT[m][:, kc * 128:(kc + 1) * 128], QT[m][:, qb * 512 + n0:(qb + 1) * 512],
               True, j < 0, [("KT", par, m), ("QT", par, m), ("QTm", par, m)], [("ps", sb)])
            if j >= 0:
                MM(P, S[:, n0:n0 + 128], C.ident_b[:, :], C.tri_b[:, :], False, True, ["ident_b", "tri_b"], [("ps", sb)])
            ACT(P, PT[i % 4][:, n0:512], S[:, n0:512], AF.Exp, [("ps", sb)], [("PT", i % 4)], scale=0.125)

        def emit_AV(i):
            qb, m, kc = steps[i]
            j = kc - KO - 4 * qb
            n0 = max(0, j) * 128
            last = KO + 4 * qb + 3
            pt = PT[i % 4]
            if nmaps == 2:
                MM(P, C.ps[2 + m][:, n0:512], VE[:, kc, 0:128], pt[:, n0:512], kc == 0, kc == last,
                   [("PT", i % 4), ("VE", par)], [("OT", m)])
                MM(P, C.ps[4][:, n0:512], sel[m][:, :], pt[:, n0:512], (m == 0 and kc == 0), (m == 1 and kc == last),
                   [("PT", i % 4), ("sel", m)], [("ZT", 0)])
            else:
                MM(P, C.ps[2][:, n0:512], VE[:, kc, 0:128], pt[:, n0:512], kc == 0, kc == last,
                   [("PT", i % 4), ("VE", par)], [("OT", 0)])
            if m == nmaps - 1 and kc == last:
                q2 = qb % 2
                if nmaps == 2:
                    for mm in range(2):
                        CP(P, "dve", OTsb[mm][:, :], C.ps[2 + mm][:, :], [("OT", mm)], [("OTsb", mm)])
                    CP(P, "dve", ZB_f[:, :], C.ps[4][:, :], [("ZT", 0)], [("ZTsb", 0)])
                    CP(P, "dve", ZB_hi[:, :], ZB_f[:, :], [("ZTsb", 0)], [("Zhi", 0)])
                    TT(P, "dve", ZB_d[:, :], ZB_f[:, :], ZB_hi[:, :], ALU.subtract, [("ZTsb", 0), ("Zhi", 0)], [("Zd", 0)])
                    CP(P, "dve", ZB_lo[:, :], ZB_d[:, :], [("Zd", 0)], [("Zlo", 0)])
                else:
                    CP(P, "dve", OTfull[:, :], C.ps[2][:, :], [("OT", 0)], [("OTsb", 0)])
                    CP(P, "dve", Zm_f[64:128, :], C.ps[2][64:128, :], [("OT", 0)], [("Zm_f", 0)])
                    TT(P, "dve", Zm_d[64:128, :], Zm_f[64:128, :], OTfull[64:128, :], ALU.subtract, [("Zm_f", 0), ("OTsb", 0)], [("Zm_d", 0)])
                    CP(P, "dve", Zm_lo[64:128, :], Zm_d[64:128, :], [("Zm_d", 0)], [("Zm_lo", 0)])

                def stage2(qb=qb, q2=q2):
                    if nmaps == 2:
                        for mm in range(2):
                            for sub in range(4):
                                MM(P, C.ps[6][:, sub * 128:(sub + 1) * 128], OTsb[mm][:, sub * 128:(sub + 1) * 128], C.ident_b[:, :], True, True,
                                   [("OTsb", mm), "ident_b"], [("ps", 6)])
                            CP(P, "dve", accsb[q2][:, 2 * mm:2 * mm + 2, :, 0:128],
                               C.ps[6][:, :].rearrange("p (a b c) -> p a b c", a=2, b=2), [("ps", 6)], [("accsb", q2, 2 * mm), ("accsb", q2, 2 * mm + 1)])
                            for sub in range(4):
                                e0 = 0 if mm == 0 else 64
                                MM(P, C.ps[7][:, 64 + mm * 4 + sub:64 + mm * 4 + sub + 1], ZB_hi[:, sub * 128:(sub + 1) * 128], C.ident_b[:, e0:e0 + 1],
                                   True, False, [("Zhi", 0), "ident_b"], [("ps", 7)])
                                MM(P, C.ps[7][:, 64 + mm * 4 + sub:64 + mm * 4 + sub + 1], ZB_lo[:, sub * 128:(sub + 1) * 128], C.ident_b[:, e0:e0 + 1],
                                   False, True, [("Zlo", 0), "ident_b"], [("ps", 7)])
                            CP(P, "dve", accsb[q2][:, 2 * mm:2 * mm + 2, :, 128:129],
                               C.ps[7][:, 64 + mm * 4:64 + mm * 4 + 4].rearrange("p (a b c) -> p a b c", a=2, b=2), [("ps", 7)],
                               [("accsb", q2, 2 * mm), ("accsb", q2, 2 * mm + 1)])
                    else:
                        for sub in range(4):
                            MM(P, C.ps[6][:, sub * 128:sub * 128 + 65], OTfull[:, sub * 128:(sub + 1) * 128], C.ident_b[:, 0:65], True, False,
                               [("OTsb", 0), "ident_b"], [("ps", 6)])
                            MM(P, C.ps[6][:, sub * 128 + 64:sub * 128 + 65], Zm_lo_full[:, sub * 128:(sub + 1) * 128], C.ident_b[:, 64:65], False, True,
                               [("Zm_lo", 0), "ident_b"], [("ps", 6)])
                        CP(P, "dve", accsb[q2][:, 0:2, :, 0:65],
                           C.ps[6][:, :].rearrange("p (a b c) -> p a b c", a=2, b=2)[:, :, :, 0:65], [("ps", 6)], [("accsb", q2, 0), ("accsb", q2, 1)])
                    finalize[0][1](qb)

                if FLAG_S2DEFER:
                    defer(7, stage2)
                else:
                    stage2()
                if not FLAG_S2DEFER or qb == NQB - 1:
                    for _f in range(FLAG_BURST):
                        MM(P, C.ps[SB[i % 3]][:, :], C.ident_b[:, :], KT[0][:, 0:512], True, True, [("KT", par, 0)], [("ps", SB[i % 3])])
                for lag, fn in finalize[1:]:
                    defer(lag + (7 if FLAG_S2DEFER else 0), (lambda qb=qb, fn=fn: fn(qb)))
                side(qb)

        emit_S(0)
        emit_S(1)
        for i in range(n):
            if i + 2 < n:
                emit_S(i + 2)
            emit_AV(i)
            stepc[0] += 1
            run_due()

    def fin_diff(h, qb):
        q2 = qb % 2
        for sub in range(4):
            a1 = accsb[q2][:, sub // 2, sub % 2, :]
            a2 = accsb[q2][:, 2 + sub // 2, sub % 2, :]
            r1 = [("accsb", q2, sub // 2)]
            r2 = [("accsb", q2, 2 + sub // 2)]
            s = sc[:, sub, :]
            P.op("dve", (lambda s=s, a1=a1: (lambda e: e.reciprocal(out=s[:, 0:1], in_=a1[:, 128:129])))(), r1, [("sc", sub, 0)])
            P.op("dve", (lambda s=s, a2=a2: (lambda e: e.reciprocal(out=s[:, 1:2], in_=a2[:, 128:129])))(), r2, [("sc", sub, 1)])
            TT(P, "dve", s[:, 2:3], s[:, 1:2], C.lam_t[:, 1:2], ALU.mult, [("sc", sub, 1), "nlam"], [("sc", sub, 2)])
            TS(P, "dve", o1[:, sub, :], a1[:, 0:128], s[:, 0:1], None, ALU.mult, None, r1 + [("sc", sub, 0)], [("o1", sub)])
            STT(P, yd[:, sub, :], a2[:, 0:128], s[:, 2:3], o1[:, sub, :], ALU.mult, ALU.add,
                r2 + [("sc", sub, 2), ("o1", sub)], [("yd", sub)])
            TT(P, "dve", sq[:, :], yd[:, sub, :], yd[:, sub, :], ALU.mult, [("yd", sub)], ["sq"])
            P.op("dve", (lambda s=s: (lambda e: e.tensor_reduce(out=s[:, 3:4], in_=sq[:, :], axis=AX.X, op=ALU.add)))(), ["sq"], [("sc", sub, 3)])

    def fin_diff_act(h, qb):
        ACT(P, sc[:, :, 4:5], sc[:, :, 3:4], AF.Ln, [("sc", sub, 3) for sub in range(4)] + ["eps"], [("sc", sub, 4) for sub in range(4)],
            bias=C.eps_t[:], scale=1.0 / 128.0)
        ACT(P, sc[:, :, 5:6], sc[:, :, 4:5], AF.Exp, [("sc", sub, 4) for sub in range(4)], [("sc", sub, 5) for sub in range(4)], scale=-0.5)

    def fin_diff_a2(h, qb):
        for sub in range(4):
            s = sc[:, sub, :]
            TS(P, "dve", yd[:, sub, :], yd[:, sub, :], s[:, 5:6], float(1.0 - LAM_INIT), ALU.mult, ALU.mult, [("yd", sub), ("sc", sub, 5)], [("yd", sub)])
            TT(P, "dve", ydb[:, sub, :], yd[:, sub, :], C.sublng[:, :], ALU.mult, [("yd", sub), "sublng"], [("ydb", sub)])

    def fin_diff_b(h, qb):
        for sub in range(4):
            MM(P, C.ps[6][:, sub * 128:(sub + 1) * 128], ydb[:, sub, :], C.ident_b[:, :], True, True,
               [("ydb", sub), "ident_b"], [("ps", 6)])
        sl = qb % 2
        CP(P, "dve", ystg[sl][:, :], C.ps[6][:, :], [("ps", 6)], [("ystg", sl)])
        DMA(P, "sp", C.YT_s[512 + h * 128:512 + (h + 1) * 128, qb * 512:(qb + 1) * 512], ystg[sl][:, :], "ystg%d" % sl, reads=[("ystg", sl)])

    def fin_moba(h, qb):
        q2 = qb % 2
        sl = qb % 2
        for sub in range(4):
            a1 = accsb[q2][:, sub // 2, sub % 2, :]
            r1 = [("accsb", q2, sub // 2)]
            s = sc[:, sub, :]
            P.op("dve", (lambda s=s, a1=a1: (lambda e: e.reciprocal(out=s[:, 0:1], in_=a1[:, 64:65])))(), r1, [("sc", sub, 0)])
            TS(P, "dve", ydb[:, sub, 0:64], a1[:, 0:64], s[:, 0:1], None, ALU.mult, None, r1 + [("sc", sub, 0)], [("ydb", sub)])

    def fin_moba_b(h, qb):
        sl = qb % 2
        for sub in range(4):
            MM(P, C.ps[6][0:64, sub * 128:(sub + 1) * 128], ydb[:, sub, 0:64], C.ident_b[:, :], True, True,
               [("ydb", sub), "ident_b"], [("ps", 6)])
        CP(P, "dve", ystg[sl][0:64, :], C.ps[6][0:64, :], [("ps", 6)], [("ystg", sl)])
        DMA(P, "sp", C.YT_s[h * 64:(h + 1) * 64, qb * 512:(qb + 1) * 512], ystg[sl][0:64, :], "ystg%d" % sl, reads=[("ystg", sl)])

    load_job(0)
    if jobs[0][0] == "m":
        for tg in range(NQB):
            mask_stage(0, tg)
            mask_stage_b(0, tg)
    for ji, (kind, h) in enumerate(jobs):
        nxt = ji + 1 if ji + 1 < len(jobs) else None
        if nxt is not None:
            load_job(nxt)

        def side(qb, nxt=nxt):
            if nxt is not None and jobs[nxt][0] == "m":
                defer(4, (lambda: mask_stage(nxt, qb)))
                defer(11, (lambda: mask_stage_b(nxt, qb)))

        if kind == "d":
            attention(ji, 2, 128, 68, [(0, (lambda qb, h=h: fin_diff(h, qb))), (9, (lambda qb, h=h: fin_diff_act(h, qb))),
                                       (11, (lambda qb, h=h: fin_diff_a2(h, qb))), (18, (lambda qb, h=h: fin_diff_b(h, qb)))], side)
        else:
            attention(ji, 1, 64, 84, [(0, (lambda qb, h=h: fin_moba(h, qb))), (6, (lambda qb, h=h: fin_moba_b(h, qb)))], side)
    run_due(flush=True)


def phase3(C):
    P, I, A = C.P, C.I, C.arena
    TB = C.TB
    NTB = C.NQ // TB
    A.reset()
    xs = A.alloc([128, 8, D], F32)
    xT = A.alloc([128, 8, TB], BF16)
    wb = [A.alloc([128, 8, 512], BF16) for _ in range(2)]
    lnp = A.alloc([128, 4, D], F32)
    tT = A.alloc([128, D], F32)
    r = A.alloc([128, D], F32)
    xn = A.alloc([128, D], F32)
    st6 = A.alloc([128, 2, 6], F32)
    mv = A.alloc([128, 8], F32)
    sil = [A.alloc([128, 512], F32) for _ in range(2)]
    sg = [A.alloc([128, 512], BF16) for _ in range(2)]
    pt_f = [A.alloc([128, 256], F32) for _ in range(1)]
    C.xb = [A.alloc([128, D], BF16) for _ in range(2)]
    C.xbi = 0
    tmp = (tT, r, xn, st6, mv)
    base = A.off
    mergedT = A.alloc([128, 8, TB], BF16)
    yT = A.alloc([128, 8, TB], BF16)
    wbr = A.alloc([128, 2, 4, D], BF16)
    wout = A.alloc([128, 8, D], BF16)
    A.reset(base)
    actT = A.alloc([128, NFF, TB], BF16)
    wd = A.alloc([128, NFF, D], BF16)
    A.reset(base)
    wpg = A.alloc([128, 8, D], BF16)
    wpl = A.alloc([128, 2, D], BF16)
    pT = A.alloc([128, 2, TB], BF16)

    DMA(P, "sp", lnp, I["lnp"][:, 2:6, :], "lnp3", writes=["ln2", "ln3"])
    x1v = C.X1_s.rearrange("(t p) d -> p t d", p=128)
    outv = C.out.rearrange("(t p) d -> p t d", p=128)
    pv = I["p"].rearrange("(t p) d -> p t d", p=128)
    win = wview(I["w_in"], 8)
    for tb in range(NTB):
        tok0 = tb * TB
        for t in range(8):
            DMA(P, "sp", xs[:, t, :], x1v[:, tb * 8 + t, :], "xs%d" % t, writes=[("xs", t)])
        for kc in range(8):
            DMA(P, "sp", xT[:, kc, :], C.X1T_s[kc * 128:(kc + 1) * 128, tb * TB:(tb + 1) * TB], "xTl%d" % kc, writes=[("xT", q) for q in range(8)])
            DMA(P, "sp", yT[:, kc, :], C.YT_s[kc * 128:(kc + 1) * 128, tok0:tok0 + TB], "yTl%d" % kc, writes=[("yT", kc)])
        DMA(P, "pool", wbr[:, 0, :, :], wview(I["w_branch_a"], 4), "wbra", writes=["wbra"])
        DMA(P, "pool", wbr[:, 1, :, :], wview(I["w_branch_b"], 4), "wbrb", writes=["wbrb"])
        DMA(P, "pool", wout, wview(I["w_out"], 8), "wout", writes=["wout"])
        for oc in range(8):
            slot = oc % 2
            buf = wb[slot]
            DMA(P, "pool", buf[:, :, 0:128], win[:, :, 3072 + oc * 128:3072 + (oc + 1) * 128], "wbA%d" % slot, writes=[("wb", slot, 0)])
            DMA(P, "pool", buf[:, :, 128:256], win[:, :, 4096 + oc * 128:4096 + (oc + 1) * 128], "wbG%d" % slot, writes=[("wb", slot, 1)])
            for hb in range(TB // 512):
                xr = [("xT", 4 * hb + q) for q in range(4)]
                ip = (oc * 2 + hb) % 2 if FLAG_MERGEDB else 0
                for br in range(2):
                    bank = 4 * ip + br
                    for kc in range(8):
                        MM(P, C.ps[bank][:, :], buf[:, kc, br * 128:(br + 1) * 128], xT[:, kc, hb * 512:(hb + 1) * 512],
                           kc == 0, kc == 7, [("wb", slot, br)] + xr, [("ps", bank)])
                    ACT(P, sg[br][:, :], C.ps[bank][:, :], AF.Sigmoid, [("ps", bank)], [("sg", br)])
                    bank2 = 4 * ip + 2 + br
                    for kc in range(4):
                        MM(P, C.ps[bank2][:, :], wbr[:, br, kc, oc * 128:(oc + 1) * 128], yT[:, br * 4 + kc, hb * 512:(hb + 1) * 512],
                           kc == 0, kc == 3, ["wbra" if br == 0 else "wbrb", ("yT", br * 4 + kc)], [("ps", bank2)])
                    TT(P, "dve", sil[br][:, :], sg[br][:, :], C.ps[bank2][:, :], ALU.mult,
                       [("sg", br), ("ps", bank2)], [("sil", br)])
                TT(P, "dve", mergedT[:, oc, hb * 512:(hb + 1) * 512], sil[0][:, :], sil[1][:, :], ALU.add,
                   [("sil", 0), ("sil", 1)], [("mergedT", oc, hb)])
        for t in range(8):
            hb = t // 4
            yb = (4, 5) if (t % 2 == 0 or not FLAG_LNPIPE) else (2, 3)
            for dh in range(2):
                for kc in range(8):
                    MM(P, C.ps[yb[dh]][:, :], mergedT[:, kc, t * 128:(t + 1) * 128], wout[:, kc, dh * 512:(dh + 1) * 512],
                       kc == 0, kc == 7, [("mergedT", kc, hb), "wout"], [("ps", yb[dh])])
            emit_layernorm(C, xs, t, yb, 1.0, lnp[:, 0, :], lnp[:, 1, :], tmp, "ln2")
            if t >= 1:
                emit_transposes(C, xs, xT, t - 1, ("dve", "act"), cast=False)
        emit_transposes(C, xs, xT, 7, ("dve", "act"), cast=False)
        P.barrier()
        emit_ffn(C, xs, xT, actT, wd, wb, sil, I["w_ffn2_up"], I["w_ffn2_down"], "f2", True)
        for t in range(8):
            yb = (4, 5) if (t % 2 == 0 or not FLAG_LNPIPE) else (2, 3)
            emit_down(C, actT, wd, t, yb)
            emit_layernorm(C, xs, t, yb, 0.5, lnp[:, 2, :], lnp[:, 3, :], tmp, "ln3")
            if t >= 1:
                emit_transposes(C, xs, xT, t - 1, ("dve", "act"), cast=False)
        emit_transposes(C, xs, xT, 7, ("dve", "act"), cast=False)
        P.barrier()
        DMA(P, "pool", wpg, wview(I["w_ple_gate"], 8), "wpg", writes=["wpg"])
        DMA(P, "pool", wpl, wview(I["w_ple"], 2), "wpl", writes=["wpl"])
        for t in range(8):
            pf = pt_f[0]
            DMA(P, "sp", pf, pv[:, tb * 8 + t, :], "pf0", writes=[("pf", 0)])
            pb = C.xb[0][:, 0:256]
            CP(P, "pool", pb, pf[:, :], [("pf", 0)], [("xb", 0)])
            for k in range(2):
                MM(P, C.ps[6][:, k * 128:(k + 1) * 128], pb[:, k * 128:(k + 1) * 128], C.ident_b[:, :], True, True, [("xb", 0), "ident_b"], [("ps", 6)])
            CP(P, "dve", pT[:, :, t * 128:(t + 1) * 128], C.ps[6][:, 0:256].rearrange("p (a b) -> p a b", a=2), [("ps", 6)], [("pT", t)])
            for dh in range(2):
                for kc in range(8):
                    MM(P, C.ps[dh][:, :], xT[:, kc, t * 128:(t + 1) * 128], wpg[:, kc, dh * 512:(dh + 1) * 512],
                       kc == 0, kc == 7, [("xT", t), "wpg"], [("ps", dh)])
                ACT(P, tT[:, dh * 512:(dh + 1) * 512], C.ps[dh][:, :], AF.Sigmoid, [("ps", dh)], [("tT", dh)])
                for kc in range(2):
                    MM(P, C.ps[2 + dh][:, :], pT[:, kc, t * 128:(t + 1) * 128], wpl[:, kc, dh * 512:(dh + 1) * 512],
                       kc == 0, kc == 1, [("pT", t), "wpl"], [("ps", 2 + dh)])
                TT(P, "dve", sil[dh][:, :], tT[:, dh * 512:(dh + 1) * 512], C.ps[2 + dh][:, :], ALU.mult, [("tT", dh), ("ps", 2 + dh)], [("sil", dh)])
                TT(P, "dve", xn[:, dh * 512:(dh + 1) * 512], sil[dh][:, :], xs[:, t, dh * 512:(dh + 1) * 512], ALU.add,
                   [("sil", dh), ("xs", t)], [("xo", dh)])
            DMA(P, "sp", outv[:, tb * 8 + t, :], xn[:, :], "outst", reads=[("xo", 0), ("xo", 1)])
        P.barrier()


def host_consts(mode, r):
    c = {}
    c["ident"] = np.eye(128, dtype=np.float32)
    ki = np.arange(128)[:, None]
    qi = np.arange(128)[None, :]
    c["tri"] = np.where(ki > qi, -BIGM, 0.0).astype(np.float32)
    slopes = np.concatenate([2.0 ** (-8.0 * np.arange(1, 9) / 8), 2.0 ** (-8.0 * np.arange(1, 5) / 4)]).astype(np.float64)
    if mode == "P4":
        NQ, NK = 4096, 4096
        qpos = np.arange(NQ)
        kpos = np.arange(NK)
        kvalid = np.ones(NK, bool)
    else:
        NQ, NK = 2048, 4096
        qpos = r * 2048 + np.arange(NQ)
        kpos = np.concatenate([np.arange(2048), r * 2048 + np.arange(2048)])
        kvalid = np.concatenate([np.full(2048, r == 1), np.ones(2048, bool)])
    qc = np.zeros((12, 4, NQ), np.float32)
    kc = np.zeros((12, 4, NK), np.float32)
    for h in range(12):
        s8 = 8.0 * slopes[h]
        qc[h, 0] = -s8 * (qpos % 128)
        qc[h, 1] = -s8 * 128.0 * (qpos // 128)
        qc[h, 2] = 1.0
        qc[h, 3] = 1.0
        kc[h, 0] = 1.0
        kc[h, 1] = 1.0
        kc[h, 2] = s8 * (kpos % 128)
        kc[h, 3] = np.where(kvalid, s8 * 128.0 * (kpos // 128), -BIGM)
    c["qc"], c["kc"] = qc, kc
    kind = np.zeros((16, NK), np.float32)
    for n in range(16):
        kind[n, n * 256:(n + 1) * 256] = BIGM
    c["kind"] = kind
    nt = NQ // 128
    vb = np.zeros((nt, 16), np.float32)
    lo = np.zeros((nt, 16), np.float32)
    for t in range(nt):
        b_loc = t // 2
        for n in range(16):
            if mode == "P4":
                st = "past" if n < b_loc else ("own" if n == b_loc else "no")
            else:
                if n < 8:
                    st = "past" if r == 1 else "no"
                else:
                    nn = n - 8
                    st = "past" if nn < b_loc else ("own" if nn == b_loc else "no")
            vb[t, n] = {"past": 0.0, "own": 1e30, "no": -1e30}[st]
            lo[t, n] = 0.0 if st in ("past", "own") else -1.0
    c["vb"] = np.ascontiguousarray(np.broadcast_to(vb.reshape(1, nt // 4, 4, 16), (128, nt // 4, 4, 16)))
    c["lo"] = np.ascontiguousarray(np.broadcast_to(lo.reshape(1, nt // 4, 4, 16), (128, nt // 4, 4, 16)))
    return c


_CACHE = {}
MODE = "P8R"


def kernel(**inputs):
    f = lambda k: np.ascontiguousarray(np.asarray(inputs[k], dtype=np.float32)[0])
    mode = MODE
    if mode not in _CACHE:
        _CACHE[mode] = build(mode)
    nc = _CACHE[mode]
    x = np.asarray(inputs["x"], dtype=np.float32)
    p = np.asarray(inputs["p"], dtype=np.float32)[0]
    shared = {k: f(k) for k in ("w_ffn1_up", "w_ffn1_down", "w_ffn2_up", "w_ffn2_down", "w_in", "w_branch_a",
                                "w_branch_b", "w_out", "w_ple", "w_ple_gate")}
    lnp = np.stack([f(k) for k in ("ln1_g", "ln1_b", "ln2_g", "ln2_b", "ln3_g", "ln3_b")], 0)
    shared["lnp"] = np.ascontiguousarray(np.broadcast_to(lnp[None], (128, 6, D)))
    shared["sublng"] = np.ascontiguousarray(np.broadcast_to(f("subln_g")[None], (128, 128)))
    lamv = np.stack([f(k) for k in ("lambda_q1", "lambda_k1", "lambda_q2", "lambda_k2")], 0)
    shared["lamv"] = np.ascontiguousarray(np.broadcast_to(lamv[None], (128, 4, 64)))
    consts = {}
    in_maps = []
    for c in range(N_CORES):
        m = dict(shared)
        if mode == "P4":
            b, r = c % 4, 0
            m["x"] = np.ascontiguousarray(x[b])
            m["p"] = np.ascontiguousarray(p[b])
        else:
            b, r = c // 2, c % 2
            m["x"] = np.ascontiguousarray(np.concatenate([x[b, 0:2048], x[b, r * 2048:(r + 1) * 2048]], 0))
            m["p"] = np.ascontiguousarray(p[b, r * 2048:(r + 1) * 2048])
        if r not in consts:
            consts[r] = host_consts(mode, r)
        m.update(consts[r])
        in_maps.append(m)
    res = run_bass_kernel_spmd(nc, in_maps, core_ids=list(range(N_CORES)))
    if mode == "P4":
        out = np.stack([np.asarray(res.results[b]["out"], dtype=np.float32) for b in range(4)], 0)
    else:
        out = np.stack([np.concatenate([np.asarray(res.results[2 * b + r]["out"], dtype=np.float32) for r in range(2)], 0)
                        for b in range(4)], 0)
    return out
```
